# Optimizing a Trainium2 kernel written in Bass

```python
import jax
import jax.numpy as jnp
from jax import lax
import numpy as np

D_MODEL = 4096
BATCH = 2
SEQ = 8192
DEPTH = 2

GRID_W = 64
CTX_LEN = 256
N_EVEN = (DEPTH + 1) // 2
N_ODD = DEPTH // 2
N_MOD = 6
W_A = D_MODEL // 2
W_B = D_MODEL // 2
A_CONV = 31
B_CONV = 3
AB_IN = 2 * W_A + 3 * W_B
M_HEADS = 8
M_DQK = D_MODEL // (2 * M_HEADS)
M_DV = D_MODEL // M_HEADS
M_CHUNK = 128
N_DIR = 2
Q_COLS = M_HEADS * M_DQK
V_COLS = M_HEADS * M_DV
G_COLS = N_DIR * 2 * M_HEADS
C_IN = 2 * Q_COLS + 2 * V_COLS + G_COLS
D_FF = 11008
FFN_CONV = 3
EPS = 1e-6

kernel_name = 'hybrid_conformer_shortconv_mlstm_dit'


def rms_norm(x, g):
    xf = x.astype(jnp.float32)
    y = xf * lax.rsqrt(jnp.mean(xf * xf, axis=-1, keepdims=True) + EPS)
    return (y * g).astype(x.dtype)


def layer_norm(x, g, b):
    xf = x.astype(jnp.float32)
    mu = jnp.mean(xf, axis=-1, keepdims=True)
    xc = xf - mu
    y = xc * lax.rsqrt(jnp.mean(xc * xc, axis=-1, keepdims=True) + EPS)
    return (y * g + b).astype(x.dtype)


def modulate(h, shift, scale):
    return h * (1 + scale[:, None, :]) + shift[:, None, :]


def dwconv2d(x4, w):
    return lax.conv_general_dilated(x4, w[:, :, None, :], (1, 1), 'SAME',
                                    dimension_numbers=('NHWC', 'HWIO', 'NHWC'),
                                    feature_group_count=x4.shape[-1])


def conv_grid(x, w, rows, vertical):
    b, s, ch = x.shape
    k = w[:, None, :] if vertical else w[None, :, :]
    return dwconv2d(x.reshape(b, rows, GRID_W, ch), k).reshape(b, s, ch)


def conv_seq(x, w):
    return dwconv2d(x[:, None], w[None])[:, 0]


def mixer_ab(h, w_in, a_conv_w, a_conv_b, a_ln_g, a_ln_b, b_conv_w, w_out, rows):
    a_val, a_gate, b_b, b_c, b_x = jnp.split(
        h @ w_in, [W_A, 2 * W_A, 2 * W_A + W_B, 2 * W_A + 2 * W_B], axis=-1)
    u = a_val * jax.nn.sigmoid(a_gate)
    z = b_c * b_x
    if rows is None:
        u = conv_seq(u, a_conv_w)
        z = conv_seq(z, b_conv_w)
    else:
        u = conv_grid(u, a_conv_w, rows, False)
        z = conv_grid(z, b_conv_w, rows, True)
    u = jax.nn.silu(layer_norm(u + a_conv_b, a_ln_g, a_ln_b))
    return jnp.concatenate([u, b_b * z], axis=-1) @ w_out


def conv_ffn(h, w_up, conv_w, w_down, rows):
    u = h @ w_up
    u = conv_seq(u, conv_w) if rows is None else conv_grid(u, conv_w, rows, False)
    gate, val = jnp.split(u, 2, axis=-1)
    return (jax.nn.silu(gate) * val) @ w_down


def mlstm_project(h, w_in, b_gates):
    b, L, _ = h.shape
    q, k, v, o, g = jnp.split(
        h @ w_in, [Q_COLS, 2 * Q_COLS, 2 * Q_COLS + V_COLS, 2 * Q_COLS + 2 * V_COLS], axis=-1)
    heads = lambda t, d: jnp.transpose(t.reshape(b, L, M_HEADS, d), (0, 2, 1, 3)).astype(jnp.float32)
    q = heads(q, M_DQK)
    k = heads(k, M_DQK) * (M_DQK ** -0.5)
    v = heads(v, M_DV)
    g = (g + b_gates).astype(jnp.float32).reshape(b, L, N_DIR, 2, M_HEADS)
    ig = jnp.transpose(g[:, :, :, 0], (2, 0, 3, 1))
    lf = jax.nn.log_sigmoid(jnp.transpose(g[:, :, :, 1], (2, 0, 3, 1)))
    return q, k, v, o, ig, lf


def mlstm_scan(q, k, v, ig, lf, state):
    b, hh, L, _ = q.shape
    nc = L // M_CHUNK
    tril = jnp.tril(jnp.ones((M_CHUNK, M_CHUNK), bool))

    def to_chunks(t):
        return jnp.moveaxis(t.reshape(b, hh, nc, M_CHUNK, *t.shape[3:]), 2, 0)

    def step(carry, inp):
        C, n, m = carry
        qc, kc, vc, ic, fc = inp
        bcum = jnp.cumsum(fc, axis=-1)
        dmat = jnp.where(tril, bcum[..., :, None] - bcum[..., None, :] + ic[..., None, :], -jnp.inf)
        inter = bcum + m[..., None]
        m_t = jnp.maximum(inter, jnp.max(dmat, axis=-1))
        w_inter = jnp.exp(inter - m_t)
        s = jnp.einsum('bhtd,bhsd->bhts', qc, kc) * jnp.exp(dmat - m_t[..., None])
        num = jnp.einsum('bhts,bhsv->bhtv', s, vc) + w_inter[..., None] * jnp.einsum('bhvd,bhtd->bhtv', C, qc)
        den = jnp.sum(s, axis=-1) + w_inter * jnp.einsum('bhd,bhtd->bht', n, qc)
        h = num / jnp.maximum(jnp.abs(den), jnp.exp(-m_t))[..., None]
        g = bcum[..., -1]
        dend = g[..., None] - bcum + ic
        m_new = jnp.maximum(g + m, jnp.max(dend, axis=-1))
        w_old = jnp.exp(g + m - m_new)
        w_s = jnp.exp(dend - m_new[..., None])
        C_new = w_old[..., None, None] * C + jnp.einsum('bhsv,bhsd->bhvd', vc * w_s[..., None], kc)
        n_new = w_old[..., None] * n + jnp.einsum('bhs,bhsd->bhd', w_s, kc)
        return (C_new, n_new, m_new), h

    state, hs = lax.scan(step, state, tuple(map(to_chunks, (q, k, v, ig, lf))))
    return jnp.moveaxis(hs, 0, 2).reshape(b, hh, L, M_DV), state


def mlstm_out(hs, o, hn_g, w_out):
    b, hh, L, dv = hs.shape
    ht = jnp.transpose(hs, (0, 2, 1, 3))
    ht = ht * lax.rsqrt(jnp.mean(ht * ht, axis=-1, keepdims=True) + EPS)
    ht = (ht.reshape(b, L, hh * dv) * hn_g).astype(o.dtype) * jax.nn.sigmoid(o)
    return ht @ w_out


def mixer_c(h_lat, h_ctx, w_in, b_gates, hn_g, w_out, need_ctx):
    ql, kl, vl, ol, igl, lfl = mlstm_project(h_lat, w_in, b_gates)
    qc, kc, vc, oc, igc, lfc = mlstm_project(h_ctx, w_in, b_gates)
    b = h_lat.shape[0]
    zero = (jnp.zeros((b, M_HEADS, M_DV, M_DQK), jnp.float32),
            jnp.zeros((b, M_HEADS, M_DQK), jnp.float32),
            jnp.zeros((b, M_HEADS), jnp.float32))
    fl = lambda t: jnp.flip(t, axis=2)
    hc_f, st_f = mlstm_scan(qc, kc, vc, igc[0], lfc[0], zero)
    hl_f, _ = mlstm_scan(ql, kl, vl, igl[0], lfl[0], st_f)
    hc_b, st_b = mlstm_scan(fl(qc), fl(kc), fl(vc), fl(igc[1]), fl(lfc[1]), zero)
    hl_b, _ = mlstm_scan(fl(ql), fl(kl), fl(vl), fl(igl[1]), fl(lfl[1]), st_b)
    y_lat = mlstm_out(hl_f + fl(hl_b), ol, hn_g, w_out)
    y_ctx = mlstm_out(hc_f + fl(hc_b), oc, hn_g, w_out) if need_ctx else None
    return y_lat, y_ctx


def setup_inputs(seed: int = 0) -> dict:
    key = jax.random.key(seed)
    ks = iter(jax.random.split(key, 32))
    D = D_MODEL

    def nrm(shape, scale):
        return scale * jax.random.normal(next(ks), shape, jnp.float32)

    gates_i = nrm((N_ODD, N_DIR, M_HEADS), 0.1)
    gates_f = 3.0 + nrm((N_ODD, N_DIR, M_HEADS), 0.5)
    m_b_gates = jnp.stack([gates_i, gates_f], axis=2).reshape(N_ODD, G_COLS)
    return {
        'x': nrm((BATCH, SEQ, D), 1.0),
        'c': nrm((BATCH, D), 1.0),
        'ctx': nrm((BATCH, CTX_LEN, D), 1.0),
        'c_ctx': nrm((D,), 1.0),
        'ada_w': nrm((DEPTH, D, N_MOD * D), 0.5 * D ** -0.5),
        'ada_b': nrm((DEPTH, N_MOD * D), 0.02),
        'norm_g': 1.0 + nrm((DEPTH, 4, D), 0.1),
        'ab_w_in': nrm((N_EVEN, D, AB_IN), D ** -0.5),
        'a_conv_w': nrm((N_EVEN, A_CONV, W_A), A_CONV ** -0.5),
        'a_conv_b': nrm((N_EVEN, W_A), 0.02),
        'a_ln_g': 1.0 + nrm((N_EVEN, W_A), 0.1),
        'a_ln_b': nrm((N_EVEN, W_A), 0.02),
        'b_conv_w': nrm((N_EVEN, B_CONV, W_B), B_CONV ** -0.5),
        'ab_w_out': nrm((N_EVEN, W_A + W_B, D), (W_A + W_B) ** -0.5),
        'm_w_in': nrm((N_ODD, D, C_IN), D ** -0.5),
        'm_b_gates': m_b_gates,
        'm_hn_g': 1.0 + nrm((N_ODD, V_COLS), 0.1),
        'm_w_out': nrm((N_ODD, V_COLS, D), V_COLS ** -0.5),
        'ffn_w_up': nrm((DEPTH, D, 2 * D_FF), D ** -0.5),
        'ffn_conv_w': nrm((DEPTH, FFN_CONV, 2 * D_FF), FFN_CONV ** -0.5),
        'ffn_w_down': nrm((DEPTH, D_FF, D), D_FF ** -0.5),
    }


def reference(x, c, ctx, c_ctx, ada_w, ada_b, norm_g, ab_w_in, a_conv_w, a_conv_b, a_ln_g, a_ln_b,
              b_conv_w, ab_w_out, m_w_in, m_b_gates, m_hn_g, m_w_out, ffn_w_up, ffn_conv_w, ffn_w_down):
    rows = x.shape[1] // GRID_W
    for layer in range(DEPTH):
        last = layer == DEPTH - 1
        j = layer // 2
        mx = jnp.split(jax.nn.silu(c) @ ada_w[layer] + ada_b[layer], N_MOD, axis=-1)
        mc = jnp.split(jax.nn.silu(c_ctx)[None] @ ada_w[layer] + ada_b[layer], N_MOD, axis=-1)
        g_pre, g_post, f_pre, f_post = norm_g[layer][0], norm_g[layer][1], norm_g[layer][2], norm_g[layer][3]
        hx = modulate(rms_norm(x, g_pre), mx[0], mx[1])
        hc = modulate(rms_norm(ctx, g_pre), mc[0], mc[1])
        if layer % 2 == 0:
            yx = mixer_ab(hx, ab_w_in[j], a_conv_w[j], a_conv_b[j], a_ln_g[j], a_ln_b[j],
                          b_conv_w[j], ab_w_out[j], rows)
            yc = None if last else mixer_ab(hc, ab_w_in[j], a_conv_w[j], a_conv_b[j], a_ln_g[j],
                                            a_ln_b[j], b_conv_w[j], ab_w_out[j], None)
        else:
            yx, yc = mixer_c(hx, hc, m_w_in[j], m_b_gates[j], m_hn_g[j], m_w_out[j], not last)
        x = x + mx[2][:, None, :] * rms_norm(yx, g_post)
        fx = conv_ffn(modulate(rms_norm(x, f_pre), mx[3], mx[4]),
                      ffn_w_up[layer], ffn_conv_w[layer], ffn_w_down[layer], rows)
        x = x + mx[5][:, None, :] * rms_norm(fx, f_post)
        if not last:
            ctx = ctx + mc[2][:, None, :] * rms_norm(yc, g_post)
            fc = conv_ffn(modulate(rms_norm(ctx, f_pre), mc[3], mc[4]),
                          ffn_w_up[layer], ffn_conv_w[layer], ffn_w_down[layer], None)
            ctx = ctx + mc[5][:, None, :] * rms_norm(fc, f_post)
    return x
```

```python
import contextlib
import numpy as np
import concourse.bass as bass
import concourse.mybir as mybir
from concourse.bass_utils import run_bass_kernel_spmd

F32 = mybir.dt.float32
BF16 = mybir.dt.bfloat16
ALU = mybir.AluOpType
AF = mybir.ActivationFunctionType
AX = mybir.AxisListType

D = 4096
KC = 32
DFF = 11008
NX = 2048
NCTX = 256
TM = NX + NCTX
TE = 64 + NX + 64 + NCTX
EPS = 1e-6
WA = 2048
ABIN = 10240
CIN = 12320
NCORES = 8


class Buf:
    __slots__ = ("name", "w", "r", "dsem", "dcnt", "weng", "rec")

    def __init__(self, name):
        self.name = name
        self.w = None
        self.r = {}
        self.dsem = None
        self.dcnt = 0
        self.weng = None


class SemPool:
    _by_nc = {}

    @classmethod
    def of(cls, nc):
        p = cls._by_nc.get(id(nc))
        if p is None:
            p = cls(nc)
            cls._by_nc[id(nc)] = p
        return p

    def __init__(self, nc):
        self.nc = nc
        self.es = contextlib.ExitStack()
        self.ce = {e: [self.es.enter_context(nc.semaphore(f"ce_{e}")), 0] for e in ("pe", "act", "dve", "pool")}
        self.free = {"hw": [], "sw": [], "cc": []}
        self.n = 0

    def get(self, kind):
        if self.free[kind]:
            return self.free[kind].pop()
        self.n += 1
        return [self.es.enter_context(self.nc.semaphore(f"dq_{kind}{self.n}")), 0, kind]

    def put(self, rec):
        self.free[rec[2]].append(rec)


class Phase:
    CE = ("pe", "act", "dve", "pool")

    def __init__(self, nc, name):
        self.nc = nc
        self.name = name
        self.es = contextlib.ExitStack()
        self.pool = SemPool.of(nc)
        self.q = {e: [] for e in ("pe", "act", "dve", "pool", "sp")}
        self.sem = {e: self.pool.ce[e][0] for e in self.CE}
        self.cnt = {e: self.pool.ce[e][1] for e in self.CE}
        self.cnt0 = dict(self.cnt)
        self.pend = {e: False for e in self.CE}
        self.seen = {e: {id(self.sem[c]): self.cnt[c] for c in self.CE} for e in self.q}
        self.dma_bufs = []
        self.nsb = 0

    def sb(self, shape, dtype, name=None):
        self.nsb += 1
        return self.es.enter_context(self.nc.sbuf_tensor(f"{self.name}_{name or 't'}{self.nsb}", list(shape), dtype))

    def ps(self, shape, dtype=F32, name=None):
        self.nsb += 1
        return self.es.enter_context(self.nc.psum_tensor(f"{self.name}_{name or 'p'}{self.nsb}", list(shape), dtype))

    def _wait(self, eng, dep):
        sem, val = dep
        k = id(sem)
        if self.seen[eng].get(k, 0) >= val:
            return
        if eng in self.CE and sem is self.sem[eng]:
            assert val <= self.cnt[eng], f"self-wait on pending {eng} {val} {self.cnt[eng]}"
        self.seen[eng][k] = val
        self.q[eng].append(lambda h, sem=sem, val=val: h.wait_ge(sem, val))

    def _hazards(self, eng, reads, writes, pe_acc=False):
        for b in reads:
            if b.w is not None:
                self._wait(eng, b.w)
        for b in writes:
            if b.w is not None and not (pe_acc and b.weng == "pe" and eng == "pe"):
                self._wait(eng, b.w)
            for d in b.r.values():
                self._wait(eng, d)

    def op(self, eng, fn, reads=(), writes=(), sig=True, pe_acc=False):
        self._hazards(eng, reads, writes, pe_acc)
        c = self.cnt[eng] + 1
        sem = self.sem[eng]
        dep = (sem, c)
        for b in reads:
            b.r[id(sem)] = dep
        for b in writes:
            b.w = dep
            b.weng = eng
            b.r = {}
        if sig:
            self.cnt[eng] = c
            self.pend[eng] = False
            self.q[eng].append(lambda h, fn=fn, sem=sem: fn(h).then_inc(sem, 1))
        else:
            self.pend[eng] = True
            self.q[eng].append(lambda h, fn=fn: fn(h))

    def dma(self, qeng, out, in_, reads=(), writes=(), noncontig=False):
        self._hazards(qeng, reads, writes)
        tgt = (list(writes) + list(reads))[0]
        if tgt.dsem is None:
            rec = self.pool.get("sw" if qeng == "pool" else "hw")
            tgt.dsem = rec[0]
            tgt.dcnt = rec[1]
            tgt.rec = rec
            self.dma_bufs.append(tgt)
        assert tgt.rec[2] == ("sw" if qeng == "pool" else "hw"), "buffer DMA'd from both queue kinds"
        tgt.dcnt += 16
        sem = tgt.dsem
        dep = (sem, tgt.dcnt)
        for b in writes:
            b.w = dep
            b.weng = "dma"
            b.r = {}
        for b in reads:
            b.r[id(sem)] = dep
        if noncontig:
            self.q[qeng].append(lambda h, o=out, i=in_, sem=sem: h.dma_start(
                out=o, in_=i, allow_slow_non_contiguous=True).then_inc(sem, 16))
        else:
            self.q[qeng].append(lambda h, o=out, i=in_, sem=sem: h.dma_start(out=o, in_=i).then_inc(sem, 16))

    def run(self):
        for e in self.CE:
            assert not self.pend[e], f"pending unsignaled op on {e}"
        for b in self.dma_bufs:
            self._wait("sp", (b.dsem, b.dcnt))
        for e in self.CE:
            if self.cnt[e] > self.cnt0[e]:
                self._wait("sp", (self.sem[e], self.cnt[e]))
            self.pool.ce[e][1] = self.cnt[e]
        for b in self.dma_bufs:
            b.rec[1] = b.dcnt
            self.pool.put(b.rec)
        nc = self.nc
        q = self.q
        with nc.Block() as block:
            if q["pe"]:
                @block.tensor
                def _(h):
                    for f in q["pe"]:
                        f(h)
            if q["act"]:
                @block.scalar
                def _(h):
                    for f in q["act"]:
                        f(h)
            if q["dve"]:
                @block.vector
                def _(h):
                    for f in q["dve"]:
                        f(h)
            if q["pool"]:
                @block.gpsimd
                def _(h):
                    for f in q["pool"]:
                        f(h)
            if q["sp"]:
                @block.sync
                def _(h):
                    for f in q["sp"]:
                        f(h)
        self.es.close()


def col_view(row_ap, n=None):
    return row_ap.rearrange("(c p) -> p c", p=128)


def make_ident(P):
    ident = P.sb([128, 128], F32, "ident")
    bid = Buf("ident")
    P.op("pool", lambda h: h.memset(ident[:, :], 0.0), writes=[bid])
    P.op("pool", lambda h: h.affine_select(out=ident[:, :], in_=ident[:, :], pattern=[[-1, 128]],
                                           compare_op=ALU.not_equal, fill=1.0, base=0, channel_multiplier=1),
         reads=[bid], writes=[bid])
    return ident, bid


def load_cols(P, rows, ident, bid, ps_ap, pbuf, name="cols"):
    ncs = [r.shape[0] // 128 for r in rows]
    ntot = sum(ncs)
    assert ntot <= 128
    stg = P.sb([128, 128], F32, name + "s")
    bs = [Buf(name + "s") for _ in rows]
    off = 0
    for r, n, b in zip(rows, ncs, bs):
        P.dma("sp", stg[off:off + n, :], r.rearrange("(c p) -> c p", p=128), writes=[b])
        off += n
    t = P.sb([128, ntot], F32, name)
    tb = Buf(name)
    P.op("pe", lambda h: h.transpose(out=ps_ap[:, 0:ntot], in_=stg[0:ntot, :], identity=ident[0:ntot, 0:ntot]),
         reads=bs + [bid], writes=[pbuf])
    P.op("dve", lambda h: h.tensor_copy(out=t[:, :], in_=ps_ap[:, 0:ntot]), reads=[pbuf], writes=[tb])
    return t, tb


def load_bcast(P, row_ap, n, name="bc", q="sp", parts=128):
    t = P.sb([parts, n], F32, name)
    b = Buf(name)
    P.dma(q, t[:, :], row_ap.partition_broadcast(parts), writes=[b])
    return t, b


def rstd_ops(P, s_, sb_, ci, ct, co, scale, n=1):
    P.op("act", lambda h: h.activation(out=s_[:, ct:ct + n], in_=s_[:, ci:ci + n], func=AF.Sqrt, scale=scale, bias=EPS),
         reads=[sb_], writes=[sb_])
    P.op("dve", lambda h: h.reciprocal(out=s_[:, co:co + n], in_=s_[:, ct:ct + n]), reads=[sb_], writes=[sb_])


def phase_ada(nc, csel, ada_w, ada_b, mods, layers=(0, 1), NM=6 * D):
    P = Phase(nc, "ada")
    ident, bid = make_ident(P)
    pst = P.ps([128, 8, 512], F32, "ps")
    pbufs = [Buf(f"ps{i}") for i in range(8)]
    cT, bcT = load_cols(P, [csel[0], csel[1]], ident, bid, pst[:, 7, :], pbufs[7], "cT")
    sT = P.sb([128, 2, KC], F32, "sT")
    bsT = Buf("sT")
    P.op("act", lambda h: h.activation(out=sT[:, :, :], in_=cT[:, :].rearrange("p (r c) -> p r c", r=2), func=AF.Silu),
         reads=[bcT], writes=[bsT])
    NB = 1024
    KG = 4
    ring = [(P.sb([128, KG, NB], F32, "w"), Buf("w")) for _ in range(4)]
    bias = [(P.sb([2, NB], F32, "bias"), Buf("bias")) for _ in range(2)]
    osb = [(P.sb([2, NB], F32, "osb"), Buf("osb")) for _ in range(2)]
    wi = 0
    pi = 0
    ni = 0
    for l in layers:
        wv = ada_w[l].rearrange("(c p) n -> p c n", p=128)
        for nb in range(NM // NB):
            bt, bb = bias[ni % 2]
            ot, bo = osb[ni % 2]
            ni += 1
            P.dma("sp", bt[:, :], ada_b[l, nb * NB:(nb + 1) * NB].partition_broadcast(2), writes=[bb])
            pb = [(pi + j) % 8 for j in range(NB // 512)]
            pi += NB // 512
            for kg in range(KC // KG):
                wt, wb = ring[wi % 4]
                wi += 1
                P.dma("sp", wt[:, :, :], wv[:, kg * KG:(kg + 1) * KG, nb * NB:(nb + 1) * NB], writes=[wb])
                for kk in range(KG):
                    kc = kg * KG + kk
                    for j, bk in enumerate(pb):
                        last = (kc == KC - 1)
                        P.op("pe", lambda h, bk=bk, kc=kc, kk=kk, j=j, wt=wt: h.matmul(
                            pst[0:2, bk, :], lhsT=sT[:, :, kc], rhs=wt[:, kk, j * 512:(j + 1) * 512],
                            start=(kc == 0), stop=(kc == KC - 1)),
                            reads=[bsT, wb], writes=[pbufs[bk]], sig=(last or (kk == KG - 1 and j == len(pb) - 1)), pe_acc=True)
            for j, bk in enumerate(pb):
                P.op("dve", lambda h, bk=bk, j=j, ot=ot, bt=bt: h.tensor_tensor(
                    out=ot[:, j * 512:(j + 1) * 512], in0=pst[0:2, bk, :], in1=bt[:, j * 512:(j + 1) * 512], op=ALU.add),
                    reads=[pbufs[bk], bb], writes=[bo])
            P.dma("sp", mods[l, :, nb * NB:(nb + 1) * NB], ot[:, :], reads=[bo])
    P.run()


def phase_prenorm(nc, name, x_in, groups, y_in=None, gg_rows=None, x_out=None, ab_rows=None, hT_out=None,
                  xin_rows=None):
    P = Phase(nc, name)
    NG = len(groups)
    vsets = sorted(set(groups))
    ident, bid = make_ident(P)
    pst = P.ps([128, 8, 512], F32, "ps")
    pbufs = [Buf(f"ps{i}") for i in range(8)]
    GG = {}
    if y_in is not None:
        for v in vsets:
            gt, gb = load_bcast(P, gg_rows[v][0], D, "gate")
            nt, nb_ = load_bcast(P, gg_rows[v][1], D, "gn")
            P.op("pool", lambda h, gt=gt, nt=nt: h.tensor_tensor(out=gt[:, :], in0=gt[:, :], in1=nt[:, :], op=ALU.mult),
                 reads=[gb, nb_], writes=[gb])
            GG[v] = (gt, gb)
    AB = {}
    if hT_out is not None:
        for v in vsets:
            cl, clb = load_cols(P, list(ab_rows[v]), ident, bid, pst[:, v, :], pbufs[v], "abc")
            A_ = P.sb([128, KC], F32, "A")
            Ab = Buf("A")
            P.op("dve", lambda h, A_=A_, cl=cl: h.scalar_tensor_tensor(
                out=A_[:, :], in0=cl[:, KC:2 * KC], scalar=1.0, in1=cl[:, 0:KC], op0=ALU.add, op1=ALU.mult),
                reads=[clb], writes=[Ab])
            B_, Bb = cl[:, 2 * KC:3 * KC], clb
            AB[v] = (A_, Ab, B_, Bb)
    xt = [(P.sb([128, D], F32, "xt"), Buf("xt")) for _ in range(2)]
    yt = [(P.sb([128, D], F32, "yt"), Buf("yt")) for _ in range(2)]
    sq = P.sb([128, D], BF16, "sq")
    bsq = Buf("sq")
    st = [(P.sb([128, 8], F32, "st"), Buf("st")) for _ in range(2)]
    NHB = 1 if y_in is not None else 2
    hb = [(P.sb([128, KC, 512], BF16, "hb"), Buf("hb")) for _ in range(NHB)] if hT_out is not None else None
    hTv = hT_out.rearrange("(c p) t -> p c t", p=128) if hT_out is not None else None

    def rstd(h, s, i, o):
        return None

    for g in range(NG):
        v = groups[g]
        x_, xb = xt[g % 2]
        y_, yb = yt[g % 2]
        s_, sb_ = st[g % 2]
        r0 = xin_rows[g] if xin_rows is not None else g * 128
        P.dma("sp", x_[:, :], x_in[r0:r0 + 128, :], writes=[xb])
        if y_in is not None:
            P.dma("sp", y_[:, :], y_in[g * 128:(g + 1) * 128, :], writes=[yb])
            P.op("act", lambda h, y_=y_, s_=s_: h.activation(out=sq[:, :], in_=y_[:, :], func=AF.Square,
                                                             accum_out=s_[:, 0:1]),
                 reads=[yb], writes=[bsq, sb_])
            rstd_ops(P, s_, sb_, 0, 1, 2, 1.0 / D)
            gt, gb = GG[v]
            P.op("dve", lambda h, y_=y_, s_=s_, gt=gt: h.scalar_tensor_tensor(
                out=y_[:, :], in0=y_[:, :], scalar=s_[:, 2:3], in1=gt[:, :], op0=ALU.mult, op1=ALU.mult),
                reads=[yb, sb_, gb], writes=[yb])
            P.op("pool", lambda h, x_=x_, y_=y_: h.tensor_tensor(out=x_[:, :], in0=x_[:, :], in1=y_[:, :], op=ALU.add),
                 reads=[xb, yb], writes=[xb])
            if x_out is not None:
                P.dma("sp", x_out[g * 128:(g + 1) * 128, :], x_[:, :], reads=[xb])
        if hT_out is None:
            continue
        P.op("act", lambda h, x_=x_, s_=s_: h.activation(out=sq[:, :], in_=x_[:, :], func=AF.Square,
                                                         accum_out=s_[:, 3:4]),
             reads=[xb], writes=[bsq, sb_])
        rstd_ops(P, s_, sb_, 3, 4, 5, 1.0 / D)
        P.op("act", lambda h, x_=x_, y_=y_, s_=s_: h.activation(out=y_[:, :], in_=x_[:, :], func=AF.Copy,
                                                                scale=s_[:, 5:6]),
             reads=[xb, sb_], writes=[yb])
        h_, hbb = hb[(g // 4) % NHB]
        A_, Ab, B_, Bb = AB[v]
        tcol = (g % 4) * 128
        for c in range(KC):
            bk = c // 4
            j = c % 4
            P.op("pe", lambda h, bk=bk, j=j, c=c, y_=y_: h.transpose(
                out=pst[:, bk, j * 128:(j + 1) * 128], in_=y_[:, c * 128:(c + 1) * 128], identity=ident[:, :]),
                reads=[yb, bid], writes=[pbufs[bk]], sig=(j == 3), pe_acc=True)
            if j == 3:
                for jj in range(4):
                    cc = bk * 4 + jj
                    if jj % 2 == 0:
                        P.op("act", lambda h, bk=bk, jj=jj, cc=cc, h_=h_, A_=A_, B_=B_, tcol=tcol: h.activation(
                            out=h_[:, cc, tcol:tcol + 128], in_=pst[:, bk, jj * 128:(jj + 1) * 128],
                            func=AF.Identity, scale=A_[:, cc:cc + 1], bias=B_[:, cc:cc + 1]),
                            reads=[pbufs[bk], Ab, Bb], writes=[hbb])
                    else:
                        P.op("dve", lambda h, bk=bk, jj=jj, cc=cc, h_=h_, A_=A_, B_=B_, tcol=tcol: h.tensor_scalar(
                            out=h_[:, cc, tcol:tcol + 128], in0=pst[:, bk, jj * 128:(jj + 1) * 128],
                            scalar1=A_[:, cc:cc + 1], scalar2=B_[:, cc:cc + 1], op0=ALU.mult, op1=ALU.add),
                            reads=[pbufs[bk], Ab, Bb], writes=[hbb])
        if g % 4 == 3 or g == NG - 1:
            t0 = (g // 4) * 512
            nt = (g % 4 + 1) * 128
            P.dma("sp", hTv[:, :, t0:t0 + nt], h_[:, :, 0:nt], reads=[hbb])
    P.run()


def phase_gemm_f(nc, name, hT, W, kc_n, passes, groups, hook, hook_init=None):
    P = Phase(nc, name)
    TMAX = max(t for _, t in passes)
    hsb = P.sb([128, kc_n, TMAX], BF16, "hT")
    NQ = 4
    kq = kc_n // NQ
    hbufs = [Buf(f"h{i}") for i in range(NQ)]
    NWR = 4
    ring = [(P.sb([128, kc_n, 128], BF16, "w"), Buf("w")) for _ in range(NWR)]
    pst = P.ps([128, 8, 512], F32, "ps")
    pbufs = [Buf(f"ps{i}") for i in range(8)]
    hTv = hT.rearrange("(c p) t -> p c t", p=128)
    Wv = W.rearrange("(c p) n -> p c n", p=128)
    P.ps_shared, P.pb_shared = pst, pbufs
    ctx = hook_init(P) if hook_init else None
    wi = 0
    si = 0
    for pi_, (t0, T) in enumerate(passes):
        for qd in range(NQ):
            P.dma("sp", hsb[:, qd * kq:(qd + 1) * kq, 0:T], hTv[:, qd * kq:(qd + 1) * kq, t0:t0 + T],
                  writes=[hbufs[qd]])
        nblk = (T + 511) // 512
        blks = [(b * 512, min(512, T - b * 512)) for b in range(nblk)]
        for gi, grp in enumerate(groups):
            slots = []
            for n0 in grp:
                wt, wb = ring[wi % NWR]
                wi += 1
                P.dma("pool", wt[:, :, :], Wv[:, :, n0:n0 + 128], writes=[wb])
                slot = si % 4
                si += 1
                for kc in range(kc_n):
                    for b, (c0, cn) in enumerate(blks):
                        bk = slot * 2 + b
                        last = (kc == kc_n - 1)
                        P.op("pe", lambda h, bk=bk, kc=kc, c0=c0, cn=cn, wt=wt: h.matmul(
                            pst[:, bk, 0:cn], lhsT=wt[:, kc, :], rhs=hsb[:, kc, c0:c0 + cn],
                            start=(kc == 0), stop=(kc == kc_n - 1)),
                            reads=[wb, hbufs[kc // kq]], writes=[pbufs[bk]], sig=(last and b == nblk - 1),
                            pe_acc=True)
                slots.append([(pst, slot * 2 + b, pbufs[slot * 2 + b], c0, cn) for b, (c0, cn) in enumerate(blks)])
            hook(P, ctx, gi, (pi_, t0, T), slots)
    P.run()


def phase_gemm_t(nc, name, aT, W, kc_n, passes, nblocks, outs):
    P = Phase(nc, name)
    TMAX = max(t for _, t in passes)
    asb = P.sb([128, kc_n, TMAX], BF16, "aT")
    NQ = 2
    kq = (kc_n + NQ - 1) // NQ
    abufs = [Buf(f"a{i}") for i in range(NQ)]
    KG = 8
    NWR = 4
    ring = [(P.sb([128, KG, 512], BF16, "w"), Buf("w")) for _ in range(NWR)]
    pst = P.ps([128, 8, 512], F32, "ps")
    pbufs = [Buf(f"ps{i}") for i in range(8)]
    osbs = [(P.sb([128, 4, 512], outs[0].dtype if len(set(o.dtype for o in outs)) == 1 else F32, "o"), Buf("o"))
            for _ in range(2)]
    aTv = aT.rearrange("(c p) t -> p c t", p=128)
    Wv = W.rearrange("(c p) n -> p c n", p=128)
    wi = 0
    si = 0
    oi = 0
    for (t0, T) in passes:
        for qd in range(NQ):
            k0, k1 = qd * kq, min(kc_n, (qd + 1) * kq)
            P.dma("sp", asb[:, k0:k1, 0:T], aTv[:, k0:k1, t0:t0 + T], writes=[abufs[qd]])
        ntg = T // 128
        for (n0, ncol, oidx, oc0) in nblocks:
            slot = si % 2
            si += 1
            for kg in range((kc_n + KG - 1) // KG):
                k0, k1 = kg * KG, min(kc_n, (kg + 1) * KG)
                wt, wb = ring[wi % NWR]
                wi += 1
                P.dma("pool", wt[:, 0:k1 - k0, 0:ncol], Wv[:, k0:k1, n0:n0 + ncol], writes=[wb])
                for kc in range(k0, k1):
                    for tg in range(ntg):
                        bk = slot * 4 + tg
                        last = (kc == kc_n - 1)
                        P.op("pe", lambda h, bk=bk, kc=kc, tg=tg, wt=wt, k0=k0, ncol=ncol: h.matmul(
                            pst[:, bk, 0:ncol], lhsT=asb[:, kc, tg * 128:(tg + 1) * 128], rhs=wt[:, kc - k0, 0:ncol],
                            start=(kc == 0), stop=(kc == kc_n - 1)),
                            reads=[wb, abufs[kc // kq]], writes=[pbufs[bk]], sig=((last or kc == k1 - 1) and tg == ntg - 1),
                            pe_acc=True)
            o_, ob = osbs[oi % 2]
            oi += 1
            for tg in range(ntg):
                bk = slot * 4 + tg
                if tg % 2 == 0:
                    P.op("act", lambda h, bk=bk, tg=tg, o_=o_, ncol=ncol: h.activation(
                        out=o_[:, tg, 0:ncol], in_=pst[:, bk, 0:ncol], func=AF.Copy),
                        reads=[pbufs[bk]], writes=[ob])
                else:
                    P.op("dve", lambda h, bk=bk, tg=tg, o_=o_, ncol=ncol: h.tensor_copy(
                        out=o_[:, tg, 0:ncol], in_=pst[:, bk, 0:ncol]),
                        reads=[pbufs[bk]], writes=[ob])
            ov = outs[oidx][t0:t0 + T, oc0:oc0 + ncol].rearrange("(g p) n -> p g n", p=128)
            P.dma("sp", ov, o_[:, 0:ntg, 0:ncol], reads=[ob])
    P.run()


def make_store_hook(dst_rows, func=None, dtype=F32, tmax=1024):
    def hinit(P):
        return [(P.sb([128, tmax], dtype, "o"), Buf("o")) for _ in range(2)]

    def hook(P, ctx, gi, pinfo, slots):
        _, t0, T = pinfo
        o_, ob = ctx[gi % 2]
        for k, (pst, bk, pb, c0, cn) in enumerate(slots[0]):
            if func is None and k % 2 == 1:
                P.op("dve", lambda h, bk=bk, c0=c0, cn=cn, pst=pst: h.tensor_copy(out=o_[:, c0:c0 + cn], in_=pst[:, bk, 0:cn]),
                     reads=[pb], writes=[ob])
            else:
                fn_ = func(gi) if callable(func) else (func or AF.Copy)
                P.op("act", lambda h, bk=bk, c0=c0, cn=cn, pst=pst, fn_=fn_: h.activation(
                    out=o_[:, c0:c0 + cn], in_=pst[:, bk, 0:cn], func=fn_), reads=[pb], writes=[ob])
        P.dma("sp", dst_rows(gi)[:, t0:t0 + T], o_[:, 0:T], reads=[ob])
    return hook, hinit


def make_ffn_hook(conv_w, actT, row_len):
    NCH = DFF // 128

    def hinit(P):
        ident, bid = make_ident(P)
        pst = P.ps_shared
        cw = []
        for j in range(3):
            for half in range(2):
                t, tb = load_cols(P, [conv_w[j, half * DFF:(half + 1) * DFF]], ident, bid, pst[:, 7, :], P.pb_shared[7], "cw")
                cw.append((t, tb))
        gs = [(P.sb([128, 1024], F32, "g"), Buf("g")) for _ in range(2)]
        vs = [(P.sb([128, 1024], F32, "v"), Buf("v")) for _ in range(2)]
        as_ = [(P.sb([128, 1024], BF16, "a"), Buf("a")) for _ in range(2)]
        return dict(cw=cw, gs=gs, vs=vs, as_=as_)

    def hook(P, ctx, gi, pinfo, slots):
        pi_, t0, T = pinfo
        L = row_len(pi_)
        g_, gb = ctx["gs"][gi % 2]
        v_, vb = ctx["vs"][gi % 2]
        a_, ab = ctx["as_"][gi % 2]
        cw = ctx["cw"]
        for half, (s_, sb_) in enumerate(((g_, gb), (v_, vb))):
            w0, w0b = cw[0 * 2 + half]
            w1, w1b = cw[1 * 2 + half]
            w2, w2b = cw[2 * 2 + half]
            for (pst, bk, pb, c0, cn) in slots[half]:
                R = cn // L
                P.op("act", lambda h, s_=s_, bk=bk, c0=c0, cn=cn, pst=pst, w1=w1: h.activation(
                    out=s_[:, c0:c0 + cn], in_=pst[:, bk, 0:cn], func=AF.Copy, scale=w1[:, gi:gi + 1]),
                    reads=[pb, w1b], writes=[sb_])
                sv = s_[:, c0:c0 + cn].rearrange("p (r l) -> p r l", l=L)
                pv = pst[:, bk, 0:cn].rearrange("p (r l) -> p r l", l=L)
                P.op("dve", lambda h, sv=sv, pv=pv, w0=w0: h.scalar_tensor_tensor(
                    out=sv[:, :, 1:L], in0=pv[:, :, 0:L - 1], scalar=w0[:, gi:gi + 1], in1=sv[:, :, 1:L],
                    op0=ALU.mult, op1=ALU.add), reads=[pb, w0b, sb_], writes=[sb_])
                P.op("dve", lambda h, sv=sv, pv=pv, w2=w2: h.scalar_tensor_tensor(
                    out=sv[:, :, 0:L - 1], in0=pv[:, :, 1:L], scalar=w2[:, gi:gi + 1], in1=sv[:, :, 0:L - 1],
                    op0=ALU.mult, op1=ALU.add), reads=[pb, w2b, sb_], writes=[sb_])
        P.op("act", lambda h: h.activation(out=g_[:, 0:T], in_=g_[:, 0:T], func=AF.Silu), reads=[gb], writes=[gb])
        P.op("dve", lambda h: h.tensor_tensor(out=a_[:, 0:T], in0=g_[:, 0:T], in1=v_[:, 0:T], op=ALU.mult),
             reads=[gb, vb], writes=[ab])
        P.dma("sp", actT[gi * 128:(gi + 1) * 128, t0:t0 + T], a_[:, 0:T], reads=[ab])
    return hook, hinit


def phase_abmix_a(nc, PT, a_conv_w, a_conv_b, a_ln_g, a_ln_b, catT):
    P = Phase(nc, "abA")
    ident, bid = make_ident(P)
    pst = P.ps([128, 8, 512], F32, "ps")
    pbufs = [Buf(f"ps{i}") for i in range(8)]
    NA = WA // 128
    cwa, cwab = load_cols(P, [a_conv_w[j] for j in range(0, 8)], ident, bid, pst[:, 0, :], pbufs[0], "cwa")
    cwb, cwbb = load_cols(P, [a_conv_w[j] for j in range(8, 16)], ident, bid, pst[:, 1, :], pbufs[1], "cwb")
    cwc, cwcb = load_cols(P, [a_conv_w[j] for j in range(16, 24)], ident, bid, pst[:, 2, :], pbufs[2], "cwc")
    cwd, cwdb = load_cols(P, [a_conv_w[j] for j in range(24, 31)], ident, bid, pst[:, 3, :], pbufs[3], "cwd")
    prm, prmb = load_cols(P, [a_conv_b, a_ln_g, a_ln_b], ident, bid, pst[:, 4, :], pbufs[4], "prm")
    cws = [(cwa, cwab), (cwb, cwbb), (cwc, cwcb), (cwd, cwdb)]

    def wcol(j, i):
        t, tb = cws[j // 8]
        return t[:, (j % 8) * NA + i:(j % 8) * NA + i + 1], tb
    ones = P.sb([128, 128], F32, "ones")
    bon = Buf("ones")
    P.op("pool", lambda h: h.memset(ones[:, :], 1.0), writes=[bon])
    uc = P.sb([128, NA, 512], F32, "uc")
    ucb = [Buf(f"uc{i}") for i in range(NA)]
    vt = [(P.sb([128, 512], F32, "val"), Buf("val")) for _ in range(2)]
    gt = [(P.sb([128, 512], F32, "gate"), Buf("gate")) for _ in range(2)]
    sqt = [(P.sb([128, 512], F32, "sq"), Buf("sq")) for _ in range(2)]
    mean = P.sb([128, 512], F32, "mean"); bmean = Buf("mean")
    rstd = P.sb([128, 512], F32, "rstd"); brstd = Buf("rstd")
    tmp = [(P.sb([128, 512], F32, "tmp"), Buf("tmp")) for _ in range(2)]
    ob = [(P.sb([128, 512], BF16, "o"), Buf("o")) for _ in range(2)]
    pieces = [(64 + k * 512, k * 512, 512, 64) for k in range(4)] + [(64 + NX + 64, NX, NCTX, NCTX)]
    k = 0
    for (e0, m0, T, L) in pieces:
        for i in range(NA):
            v_, vb = vt[k % 2]
            g_, gb = gt[k % 2]
            q_, qb = sqt[k % 2]
            k += 1
            P.dma("sp", v_[:, 0:T], PT[i * 128:(i + 1) * 128, e0:e0 + T], writes=[vb])
            P.dma("sp", g_[:, 0:T], PT[WA + i * 128:WA + (i + 1) * 128, e0:e0 + T], writes=[gb])
            P.op("act", lambda h, g_=g_, T=T: h.activation(out=g_[:, 0:T], in_=g_[:, 0:T], func=AF.Sigmoid),
                 reads=[gb], writes=[gb])
            P.op("dve", lambda h, g_=g_, v_=v_, T=T: h.tensor_tensor(out=v_[:, 0:T], in0=v_[:, 0:T], in1=g_[:, 0:T], op=ALU.mult),
                 reads=[gb, vb], writes=[vb])
            wc, wcb = wcol(15, i)
            P.op("act", lambda h, v_=v_, i=i, T=T, wc=wc: h.activation(
                out=uc[:, i, 0:T], in_=v_[:, 0:T], func=AF.Identity, scale=wc, bias=prm[:, i:i + 1]),
                reads=[vb, wcb, prmb], writes=[ucb[i]])
            uv = uc[:, i, 0:T].rearrange("p (r l) -> p r l", l=L)
            vv = v_[:, 0:T].rearrange("p (r l) -> p r l", l=L)
            n = 0
            for dd in range(1, 16):
                for d in (dd, -dd):
                    wc, wcb = wcol(15 + d, i)
                    if d > 0:
                        o_ap, i_ap = uv[:, :, 0:L - d], vv[:, :, d:L]
                    else:
                        o_ap, i_ap = uv[:, :, -d:L], vv[:, :, 0:L + d]
                    eng = "dve"
                    n += 1
                    P.op(eng, lambda h, o_ap=o_ap, i_ap=i_ap, wc=wc: h.scalar_tensor_tensor(
                        out=o_ap, in0=i_ap, scalar=wc, in1=o_ap, op0=ALU.mult, op1=ALU.add),
                        reads=[vb, wcb, ucb[i]], writes=[ucb[i]])
            P.op("act", lambda h, q_=q_, i=i, T=T: h.activation(out=q_[:, 0:T], in_=uc[:, i, 0:T], func=AF.Square),
                 reads=[ucb[i]], writes=[qb])
            P.op("pe", lambda h, i=i, T=T: h.matmul(pst[:, 6, 0:T], lhsT=ones[:, :], rhs=uc[:, i, 0:T],
                                                    start=(i == 0), stop=(i == NA - 1)),
                 reads=[bon, ucb[i]], writes=[pbufs[6]], pe_acc=True)
            P.op("pe", lambda h, q_=q_, i=i, T=T: h.matmul(pst[:, 7, 0:T], lhsT=ones[:, :], rhs=q_[:, 0:T],
                                                           start=(i == 0), stop=(i == NA - 1)),
                 reads=[bon, qb], writes=[pbufs[7]], pe_acc=True)
        P.op("act", lambda h, T=T: h.activation(out=mean[:, 0:T], in_=pst[:, 6, 0:T], func=AF.Copy, scale=1.0 / WA),
             reads=[pbufs[6]], writes=[bmean])
        t_, tb = tmp[0]
        P.op("dve", lambda h, t_=t_, T=T: h.tensor_tensor(out=t_[:, 0:T], in0=mean[:, 0:T], in1=mean[:, 0:T], op=ALU.mult),
             reads=[bmean], writes=[tb])
        P.op("dve", lambda h, t_=t_, T=T: h.scalar_tensor_tensor(
            out=t_[:, 0:T], in0=pst[:, 7, 0:T], scalar=1.0 / WA, in1=t_[:, 0:T], op0=ALU.mult, op1=ALU.subtract),
            reads=[pbufs[7], tb], writes=[tb])
        P.op("act", lambda h, t_=t_, T=T: h.activation(out=t_[:, 0:T], in_=t_[:, 0:T], func=AF.Sqrt, bias=EPS),
             reads=[tb], writes=[tb])
        P.op("dve", lambda h, t_=t_, T=T: h.reciprocal(out=rstd[:, 0:T], in_=t_[:, 0:T]), reads=[tb], writes=[brstd])
        for i in range(NA):
            t_, tb = tmp[i % 2]
            o_, obb = ob[i % 2]
            P.op("dve", lambda h, t_=t_, i=i, T=T: h.tensor_tensor(out=t_[:, 0:T], in0=uc[:, i, 0:T], in1=mean[:, 0:T], op=ALU.subtract),
                 reads=[ucb[i], bmean], writes=[tb])
            P.op("pool", lambda h, t_=t_, T=T: h.tensor_tensor(out=t_[:, 0:T], in0=t_[:, 0:T], in1=rstd[:, 0:T], op=ALU.mult),
                 reads=[tb, brstd], writes=[tb])
            P.op("act", lambda h, t_=t_, o_=o_, i=i, T=T: h.activation(
                out=o_[:, 0:T], in_=t_[:, 0:T], func=AF.Silu, scale=prm[:, NA + i:NA + i + 1], bias=prm[:, 2 * NA + i:2 * NA + i + 1]),
                reads=[tb, prmb], writes=[obb])
            P.dma("sp", catT[i * 128:(i + 1) * 128, m0:m0 + T], o_[:, 0:T], reads=[obb])
    P.run()


def phase_abmix_b(nc, PT, b_conv_w, hmask, catT):
    P = Phase(nc, "abB")
    ident, bid = make_ident(P)
    pst = P.ps([128, 8, 512], F32, "ps")
    pbufs = [Buf(f"ps{i}") for i in range(8)]
    NB_ = WA // 128
    cw, cwb = load_cols(P, [b_conv_w[0], b_conv_w[1], b_conv_w[2]], ident, bid, pst[:, 0, :], pbufs[0], "cw")
    hm = P.sb([128, 2], F32, "hm"); hmb = Buf("hm")
    P.dma("sp", hm[:, :], hmask[:, :], writes=[hmb])
    XE = 64 + NX + 64
    bb_ = [(P.sb([128, TM], F32, "bb"), Buf("bb")) for _ in range(2)]
    bc_ = [(P.sb([128, TE], F32, "bc"), Buf("bc")) for _ in range(2)]
    bx_ = [(P.sb([128, TE], F32, "bx"), Buf("bx")) for _ in range(2)]
    zc_ = [(P.sb([128, TM], F32, "zc"), Buf("zc")) for _ in range(2)]
    o_ = [(P.sb([128, TM], BF16, "o"), Buf("o")) for _ in range(2)]
    for i in range(NB_):
        b_, bbb = bb_[i % 2]
        c_, cb = bc_[i % 2]
        x_, xb = bx_[i % 2]
        z_, zb = zc_[i % 2]
        oo, ob = o_[i % 2]
        r_b = 2 * WA + i * 128
        r_c = 2 * WA + WA + i * 128
        r_x = 2 * WA + 2 * WA + i * 128
        P.dma("sp", b_[:, 0:NX], PT[r_b:r_b + 128, 64:64 + NX], writes=[bbb])
        bbb2 = Buf("bb2")
        P.dma("sp", b_[:, NX:TM], PT[r_b:r_b + 128, XE:TE], writes=[bbb2])
        P.dma("sp", c_[:, :], PT[r_c:r_c + 128, :], writes=[cb])
        P.dma("sp", x_[:, :], PT[r_x:r_x + 128, :], writes=[xb])
        P.op("dve", lambda h, c_=c_, x_=x_: h.tensor_tensor(out=c_[:, :], in0=c_[:, :], in1=x_[:, :], op=ALU.mult),
             reads=[cb, xb], writes=[cb])
        P.op("dve", lambda h, c_=c_: h.tensor_scalar(out=c_[:, 0:64], in0=c_[:, 0:64], scalar1=hm[:, 0:1], scalar2=None, op0=ALU.mult),
             reads=[cb, hmb], writes=[cb])
        P.op("dve", lambda h, c_=c_: h.tensor_scalar(out=c_[:, 64 + NX:XE], in0=c_[:, 64 + NX:XE], scalar1=hm[:, 1:2], scalar2=None, op0=ALU.mult),
             reads=[cb, hmb], writes=[cb])
        w0, w1, w2 = cw[:, i:i + 1], cw[:, NB_ + i:NB_ + i + 1], cw[:, 2 * NB_ + i:2 * NB_ + i + 1]
        P.op("act", lambda h, z_=z_, c_=c_, w1=w1: h.activation(out=z_[:, 0:NX], in_=c_[:, 64:64 + NX], func=AF.Copy, scale=w1),
             reads=[cb, cwb], writes=[zb])
        P.op("dve", lambda h, z_=z_, c_=c_, w0=w0: h.scalar_tensor_tensor(
            out=z_[:, 0:NX], in0=c_[:, 0:NX], scalar=w0, in1=z_[:, 0:NX], op0=ALU.mult, op1=ALU.add),
            reads=[cb, cwb, zb], writes=[zb])
        P.op("dve", lambda h, z_=z_, c_=c_, w2=w2: h.scalar_tensor_tensor(
            out=z_[:, 0:NX], in0=c_[:, 128:128 + NX], scalar=w2, in1=z_[:, 0:NX], op0=ALU.mult, op1=ALU.add),
            reads=[cb, cwb, zb], writes=[zb])
        P.op("act", lambda h, z_=z_, c_=c_, w1=w1: h.activation(out=z_[:, NX:TM], in_=c_[:, XE:TE], func=AF.Copy, scale=w1),
             reads=[cb, cwb], writes=[zb])
        P.op("dve", lambda h, z_=z_, c_=c_, w0=w0: h.scalar_tensor_tensor(
            out=z_[:, NX + 1:TM], in0=c_[:, XE:TE - 1], scalar=w0, in1=z_[:, NX + 1:TM], op0=ALU.mult, op1=ALU.add),
            reads=[cb, cwb, zb], writes=[zb])
        P.op("dve", lambda h, z_=z_, c_=c_, w2=w2: h.scalar_tensor_tensor(
            out=z_[:, NX:TM - 1], in0=c_[:, XE + 1:TE], scalar=w2, in1=z_[:, NX:TM - 1], op0=ALU.mult, op1=ALU.add),
            reads=[cb, cwb, zb], writes=[zb])
        P.op("pool", lambda h, z_=z_, b_=b_, oo=oo: h.tensor_tensor(out=oo[:, :], in0=z_[:, :], in1=b_[:, :], op=ALU.mult),
             reads=[zb, bbb, bbb2], writes=[ob])
        P.dma("sp", catT[WA + i * 128:WA + (i + 1) * 128, :], oo[:, :], reads=[ob])
    P.run()


def phase_scan(nc, qT, kT, kk, vv, gg, bg, hf, hb, NCH=66, NCTXC=2, npairs=2):
    import math
    P = Phase(nc, "scan")
    ident, bid = make_ident(P)
    pst = P.ps([128, 8, 512], F32, "ps")
    pbufs = [Buf(f"ps{i}") for i in range(8)]
    TT = NCH * 128
    ones = P.sb([128, 128], F32, "ones"); bon = Buf("ones")
    P.op("pool", lambda h: h.memset(ones[:, :], 1.0), writes=[bon])
    onesb = P.sb([128, 2], BF16, "onesb"); bonb = Buf("onesb")
    P.op("pool", lambda h: h.memset(onesb[:, :], 1.0), writes=[bonb])
    tri = []
    for d in range(2):
        t = P.sb([128, 128], F32, f"tri{d}"); tb = Buf(f"tri{d}")
        P.op("pool", lambda h, t=t: h.memset(t[:, :], 1.0), writes=[tb])
        sgn = 1 if d == 0 else -1
        P.op("pool", lambda h, t=t, sgn=sgn: h.affine_select(out=t[:, :], in_=t[:, :], pattern=[[sgn, 128]],
                                                            compare_op=ALU.is_ge, fill=0.0, base=0, channel_multiplier=-sgn),
             reads=[tb], writes=[tb])
        tri.append((t, tb))
    bgt = P.sb([128, 4 * npairs], F32, "bg"); bgb = Buf("bg")
    P.dma("sp", bgt[:, :], bg[:, :], writes=[bgb])
    nbg = P.sb([128, 4 * npairs], F32, "nbg"); nbgb = Buf("nbg")
    P.op("dve", lambda h: h.tensor_scalar(out=nbg[:, :], in0=bgt[:, :], scalar1=-1.0, scalar2=None, op0=ALU.mult),
         reads=[bgb], writes=[nbgb])
    qsb = P.sb([128, 2, TT], BF16, "qT"); qb = Buf("qT")
    ksb = P.sb([128, 2, TT], BF16, "kT"); kb = Buf("kT")
    G = P.sb([128, NCH, 4], F32, "G"); Gb = Buf("G")
    NR = 4
    vr = [(P.sb([128, 512], BF16, "v"), Buf("v")) for _ in range(NR)]
    kr = [(P.sb([128, 256], BF16, "k"), Buf("k")) for _ in range(NR)]
    kpr = [(P.sb([128, 256], BF16, "kp"), Buf("kp")) for _ in range(NR)]
    spr = [(P.sb([128, 128], BF16, "sp"), Buf("sp")) for _ in range(2)]
    hr = [(P.sb([128, 512], F32, "h"), Buf("h")) for _ in range(2)]
    dsc = [(P.sb([128, 4], F32, "dsc"), Buf("dsc")) for _ in range(2)]
    ri = 0
    for pr in range(npairs):
        P.dma("sp", qsb[:, :, :], qT[pr].rearrange("(c p) t -> p c t", p=128), writes=[qb])
        P.dma("sp", ksb[:, :, :], kT[pr].rearrange("(c p) t -> p c t", p=128), writes=[kb])
        P.dma("sp", G[:, :, :], gg[pr].rearrange("(c p) f -> p c f", p=128), writes=[Gb], noncontig=True)
        chains = []
        for d in range(2):
            ig = P.sb([128, NCH], F32, "ig"); lf = P.sb([128, NCH], F32, "lf")
            E = P.sb([128, NCH], F32, "E"); R = P.sb([128, NCH], F32, "R"); GC = P.sb([128, NCH], F32, "GC")
            gb_ = Buf("gprep")
            c_i = pr * 4 + 2 * d
            P.op("dve", lambda h, ig=ig, d=d, c_i=c_i: h.tensor_scalar(out=ig[:, :], in0=G[:, :, 2 * d], scalar1=bgt[:, c_i:c_i + 1],
                                                                      scalar2=None, op0=ALU.add), reads=[Gb, bgb], writes=[gb_])
            P.op("act", lambda h, lf=lf, d=d, c_i=c_i: h.activation(out=lf[:, :], in_=G[:, :, 2 * d + 1], func=AF.Exp, scale=-1.0,
                                                                   bias=nbg[:, c_i + 1:c_i + 2]), reads=[Gb, nbgb], writes=[gb_])
            P.op("act", lambda h, lf=lf: h.activation(out=lf[:, :], in_=lf[:, :], func=AF.Ln, bias=1.0), reads=[gb_], writes=[gb_])
            P.op("dve", lambda h, lf=lf: h.tensor_scalar(out=lf[:, :], in0=lf[:, :], scalar1=-1.0, scalar2=None, op0=ALU.mult),
                 reads=[gb_], writes=[gb_])
            t, tb = tri[d]
            P.op("pe", lambda h, t=t, lf=lf: h.matmul(pst[:, 0, 0:NCH], lhsT=t[:, :], rhs=lf[:, :], start=True, stop=True),
                 reads=[tb, gb_], writes=[pbufs[0]])
            P.op("pe", lambda h, lf=lf: h.matmul(pst[:, 1, 0:NCH], lhsT=ones[:, :], rhs=lf[:, :], start=True, stop=True),
                 reads=[bon, gb_], writes=[pbufs[1]])
            P.op("act", lambda h, R=R: h.activation(out=R[:, :], in_=pst[:, 0, 0:NCH], func=AF.Exp), reads=[pbufs[0]], writes=[gb_])
            P.op("act", lambda h, GC=GC: h.activation(out=GC[:, :], in_=pst[:, 1, 0:NCH], func=AF.Exp), reads=[pbufs[1]], writes=[gb_])
            P.op("dve", lambda h, ig=ig: h.tensor_tensor(out=ig[:, :], in0=ig[:, :], in1=pst[:, 0, 0:NCH], op=ALU.subtract),
                 reads=[pbufs[0], gb_], writes=[gb_])
            P.op("act", lambda h, E=E, ig=ig: h.activation(out=E[:, :], in_=ig[:, :], func=AF.Exp, bias=-math.log(16.0)),
                 reads=[gb_], writes=[gb_])
            C = P.sb([128, 2, 512], F32, "C"); Cb = P.sb([128, 2, 512], BF16, "Cb")
            n_ = P.sb([128, 2], F32, "n"); nb_ = P.sb([128, 2], BF16, "nb")
            cb_ = Buf("C"); cbb = Buf("Cb")
            P.op("pool", lambda h, C=C: h.memset(C[:, :, :], 0.0), writes=[cb_])
            P.op("pool", lambda h, n_=n_: h.memset(n_[:, :], 0.0), writes=[cb_])
            P.op("pool", lambda h, Cb=Cb: h.memset(Cb[:, :, :], 0.0), writes=[cbb])
            P.op("pool", lambda h, nb_=nb_: h.memset(nb_[:, :], 0.0), writes=[cbb])
            order = list(range(NCTXC)) + list(range(NCTXC, NCH)) if d == 0 else \
                list(range(NCTXC - 1, -1, -1)) + list(range(NCH - 1, NCTXC - 1, -1))
            chains.append(dict(d=d, E=E, R=R, GC=GC, gb=gb_, C=C, Cb=Cb, n=n_, nb=nb_, cb=cb_, cbb=cbb, order=order,
                               out=(hf if d == 0 else hb)))
        for step in range(NCH):
            for ch in chains:
                c = ch["order"][step]
                d = ch["d"]
                base = d * 4
                SB, NB_, UB0, UB1 = base, base + 1, base + 2, base + 3
                E, R, GC, gb_ = ch["E"], ch["R"], ch["GC"], ch["gb"]
                C, Cb, n_, nb_, cb_, cbb = ch["C"], ch["Cb"], ch["n"], ch["nb"], ch["cb"], ch["cbb"]
                cols = slice(c * 128, (c + 1) * 128)
                v_, vb = vr[ri % NR]
                k_, kb_ = kr[ri % NR]
                kp, kpb = kpr[ri % NR]
                s_, sb_ = spr[ri % 2]
                h_, hb_ = hr[ri % 2]
                ds, dsb = dsc[ri % 2]
                ri += 1
                P.dma("sp", v_[:, :], vv[pr, c * 128:(c + 1) * 128, :], writes=[vb])
                P.dma("sp", k_[:, :], kk[pr, c * 128:(c + 1) * 128, :], writes=[kb_])
                if c >= NCTXC:
                    for dc in range(2):
                        P.op("pe", lambda h, dc=dc, cols=cols, SB=SB: h.matmul(
                            pst[:, SB, 0:128], lhsT=ksb[:, dc, cols], rhs=qsb[:, dc, cols], start=(dc == 0), stop=(dc == 1)),
                            reads=[kb, qb], writes=[pbufs[SB]], sig=(dc == 1), pe_acc=True)
                    t, tb = tri[d]
                    P.op("dve", lambda h, s_=s_, SB=SB, E=E, c=c, t=t: h.scalar_tensor_tensor(
                        out=s_[:, :], in0=pst[:, SB, 0:128], scalar=E[:, c:c + 1], in1=t[:, :], op0=ALU.mult, op1=ALU.mult),
                        reads=[pbufs[SB], gb_, tb], writes=[sb_])
                    P.op("pe", lambda h, s_=s_, v_=v_, NB_=NB_: h.matmul(pst[:, NB_, :], lhsT=s_[:, :], rhs=v_[:, :], start=True, stop=False),
                         reads=[sb_, vb], writes=[pbufs[NB_]], sig=False, pe_acc=True)
                    for dc in range(2):
                        P.op("pe", lambda h, dc=dc, cols=cols, Cb=Cb, NB_=NB_: h.matmul(
                            pst[:, NB_, :], lhsT=qsb[:, dc, cols], rhs=Cb[:, dc, :], start=False, stop=(dc == 1)),
                            reads=[qb, cbb], writes=[pbufs[NB_]], sig=(dc == 1), pe_acc=True)
                    P.op("pe", lambda h, s_=s_, SB=SB: h.matmul(pst[:, SB, 256:257], lhsT=s_[:, :], rhs=onesb[:, 0:1], start=True, stop=False),
                         reads=[sb_, bonb], writes=[pbufs[SB]], sig=False, pe_acc=True)
                    for dc in range(2):
                        P.op("pe", lambda h, dc=dc, cols=cols, nb_=nb_, SB=SB: h.matmul(
                            pst[:, SB, 256:257], lhsT=qsb[:, dc, cols], rhs=nb_[:, dc:dc + 1], start=False, stop=(dc == 1)),
                            reads=[qb, cbb], writes=[pbufs[SB]], sig=(dc == 1), pe_acc=True)
                    P.op("dve", lambda h, ds=ds, SB=SB, R=R, c=c: h.tensor_tensor(out=ds[:, 0:1], in0=pst[:, SB, 256:257], in1=R[:, c:c + 1], op=ALU.mult),
                         reads=[pbufs[SB], gb_], writes=[dsb])
                    P.op("act", lambda h, ds=ds: h.activation(out=ds[:, 1:2], in_=ds[:, 0:1], func=AF.Abs), reads=[dsb], writes=[dsb])
                    P.op("dve", lambda h, ds=ds: h.tensor_single_scalar(out=ds[:, 2:3], in_=ds[:, 1:2], scalar=1.0, op=ALU.max), reads=[dsb], writes=[dsb])
                    P.op("dve", lambda h, ds=ds: h.reciprocal(out=ds[:, 1:2], in_=ds[:, 2:3]), reads=[dsb], writes=[dsb])
                    P.op("dve", lambda h, ds=ds, R=R, c=c: h.tensor_tensor(out=ds[:, 3:4], in0=ds[:, 1:2], in1=R[:, c:c + 1], op=ALU.mult),
                         reads=[dsb, gb_], writes=[dsb])
                    P.op("act", lambda h, h_=h_, ds=ds, NB_=NB_: h.activation(out=h_[:, :], in_=pst[:, NB_, :], func=AF.Copy, scale=ds[:, 3:4]),
                         reads=[pbufs[NB_], dsb], writes=[hb_])
                    P.dma("sp", ch["out"][pr, (c - NCTXC) * 128:(c - NCTXC + 1) * 128, :], h_[:, :], reads=[hb_])
                P.op("dve", lambda h, kp=kp, k_=k_, E=E, c=c: h.tensor_scalar(out=kp[:, :], in0=k_[:, :], scalar1=E[:, c:c + 1], scalar2=None, op0=ALU.mult),
                     reads=[kb_, gb_], writes=[kpb])
                for dc, UB in enumerate((UB0, UB1)):
                    P.op("pe", lambda h, dc=dc, UB=UB, kp=kp, v_=v_: h.matmul(pst[:, UB, :], lhsT=kp[:, dc * 128:(dc + 1) * 128], rhs=v_[:, :], start=True, stop=True),
                         reads=[kpb, vb], writes=[pbufs[UB]], sig=False, pe_acc=True)
                for dc in range(2):
                    P.op("pe", lambda h, dc=dc, kp=kp, SB=SB: h.matmul(pst[:, SB, 300 + dc:301 + dc], lhsT=kp[:, dc * 128:(dc + 1) * 128], rhs=onesb[:, 0:1], start=True, stop=True),
                         reads=[kpb, bonb], writes=[pbufs[SB]], sig=(dc == 1), pe_acc=True)
                P.op("dve", lambda h, C=C, UB0=UB0: h.tensor_tensor(out=C[:, :, :], in0=C[:, :, :], in1=pst[:, UB0:UB0 + 2, :], op=ALU.add),
                     reads=[pbufs[UB0], pbufs[UB1], cbb], writes=[cb_])
                P.op("dve", lambda h, n_=n_, SB=SB: h.tensor_tensor(out=n_[:, :], in0=n_[:, :], in1=pst[:, SB, 300:302], op=ALU.add),
                     reads=[pbufs[SB], cbb], writes=[cb_])
                P.op("dve", lambda h, C=C, GC=GC, c=c: h.tensor_scalar(out=C[:, :, :], in0=C[:, :, :], scalar1=GC[:, c:c + 1], scalar2=None, op0=ALU.mult),
                     reads=[gb_], writes=[cb_])
                P.op("dve", lambda h, n_=n_, GC=GC, c=c: h.tensor_scalar(out=n_[:, :], in0=n_[:, :], scalar1=GC[:, c:c + 1], scalar2=None, op0=ALU.mult),
                     reads=[gb_], writes=[cb_])
                P.op("act", lambda h, C=C, Cb=Cb: h.activation(out=Cb[:, :, :], in_=C[:, :, :], func=AF.Copy), reads=[cb_], writes=[cbb])
                P.op("act", lambda h, n_=n_, nb_=nb_: h.activation(out=nb_[:, :], in_=n_[:, :], func=AF.Copy), reads=[cb_], writes=[cbb])
    P.run()


def phase_mpost(nc, hf, hb, hn_g, oT, hoT, NG=16):
    P = Phase(nc, "mpost")
    ident, bid = make_ident(P)
    pst = P.ps([128, 8, 512], F32, "ps")
    pbufs = [Buf(f"ps{i}") for i in range(8)]
    hg, hgb = load_cols(P, [hn_g], ident, bid, pst[:, 0, :], pbufs[0], "hg")
    xt = [(P.sb([128, D], F32, "xt"), Buf("xt")) for _ in range(2)]
    yt = [(P.sb([128, D], F32, "yt"), Buf("yt")) for _ in range(2)]
    sq = P.sb([128, 512], BF16, "sq"); bsq = Buf("sq")
    st = [(P.sb([128, 24], F32, "st"), Buf("st")) for _ in range(2)]
    ot = [(P.sb([128, KC, 128], BF16, "ot"), Buf("ot")) for _ in range(2)]
    hbk = [(P.sb([128, KC, 512], BF16, "hb"), Buf("hb")) for _ in range(1)]
    oTv = oT.rearrange("(c p) t -> p c t", p=128)
    hoTv = hoT.rearrange("(c p) t -> p c t", p=128)
    for g in range(NG):
        x_, xb = xt[g % 2]
        y_, yb = yt[g % 2]
        s_, sb_ = st[g % 2]
        o_, ob = ot[g % 2]
        P.dma("sp", x_[:, :], hf[g * 128:(g + 1) * 128, :], writes=[xb])
        P.dma("sp", y_[:, :], hb[g * 128:(g + 1) * 128, :], writes=[yb])
        P.dma("sp", o_[:, :, :], oTv[:, :, g * 128:(g + 1) * 128], writes=[ob])
        P.op("pool", lambda h, x_=x_, y_=y_: h.tensor_tensor(out=x_[:, :], in0=x_[:, :], in1=y_[:, :], op=ALU.add),
             reads=[xb, yb], writes=[xb])
        for hd in range(8):
            P.op("act", lambda h, x_=x_, s_=s_, hd=hd: h.activation(out=sq[:, :], in_=x_[:, hd * 512:(hd + 1) * 512], func=AF.Square,
                                                                    accum_out=s_[:, hd:hd + 1]), reads=[xb], writes=[bsq, sb_])
        rstd_ops(P, s_, sb_, 0, 8, 16, 1.0 / 512, n=8)
        for hd in range(8):
            P.op("act", lambda h, x_=x_, y_=y_, s_=s_, hd=hd: h.activation(
                out=y_[:, hd * 512:(hd + 1) * 512], in_=x_[:, hd * 512:(hd + 1) * 512], func=AF.Copy, scale=s_[:, 16 + hd:17 + hd]),
                reads=[xb, sb_], writes=[yb])
        h_, hbb = hbk[0]
        tcol = (g % 4) * 128
        for c in range(KC):
            bk = c // 4
            j = c % 4
            P.op("pe", lambda h, bk=bk, j=j, c=c, y_=y_: h.transpose(
                out=pst[:, bk, j * 128:(j + 1) * 128], in_=y_[:, c * 128:(c + 1) * 128], identity=ident[:, :]),
                reads=[yb, bid], writes=[pbufs[bk]], sig=(j == 3), pe_acc=True)
            if j == 3:
                for jj in range(4):
                    cc = bk * 4 + jj
                    P.op("dve", lambda h, bk=bk, jj=jj, cc=cc, h_=h_, o_=o_, tcol=tcol: h.scalar_tensor_tensor(
                        out=h_[:, cc, tcol:tcol + 128], in0=pst[:, bk, jj * 128:(jj + 1) * 128], scalar=hg[:, cc:cc + 1],
                        in1=o_[:, cc, :], op0=ALU.mult, op1=ALU.mult), reads=[pbufs[bk], hgb, ob], writes=[hbb])
        if g % 4 == 3:
            t0 = (g // 4) * 512
            P.dma("sp", hoTv[:, :, t0:t0 + 512], h_[:, :, :], reads=[hbb])
    P.run()


PASS_F_TE = [(0, 1024), (1024, 1024), (2048, 384)]
PASS_F_TM = [(0, 1024), (1024, 1024), (2048, 256)]
PASS_T_TM = [(0, 512), (512, 512), (1024, 512), (1536, 512), (2048, 256)]
PASS_F_X = [(0, 1024), (1024, 1024)]
PASS_T_X = [(0, 512), (512, 512), (1024, 512), (1536, 512)]
FFN_GROUPS = [[i * 128, DFF + i * 128] for i in range(DFF // 128)]
NBLK_D = [(n * 512, 512, 0, n * 512) for n in range(8)]


def _dt(nc, name, shape, dtype=F32, kind="Internal"):
    return nc.dram_tensor(name, list(shape), dtype, kind=kind).ap()


def build_l1(upto=99):
    nc = bass.Bass("TRN2", target_bir_lowering=False)
    I = lambda n, s, d=F32: _dt(nc, n, s, d, "ExternalInput")
    O = lambda n, s, d=F32: _dt(nc, n, s, d, "ExternalOutput")
    T = lambda n, s, d=F32: _dt(nc, n, s, d, "Internal")
    csel = I("csel", [2, D]); ada_w = I("ada_w", [2, D, 6 * D]); ada_b = I("ada_b", [2, 6 * D])
    xext = I("xext", [TE, D]); hmask = I("hmask", [128, 2]); norm_g = I("norm_g", [2, 4, D])
    w_in = I("ab_w_in", [D, ABIN]); acw = I("a_conv_w", [31, WA]); acb = I("a_conv_b", [WA])
    alg = I("a_ln_g", [WA]); alb = I("a_ln_b", [WA]); bcw = I("b_conv_w", [3, WA]); w_out = I("ab_w_out", [D, D])
    w_up = I("ffn_w_up", [D, 2 * DFF]); fcw = I("ffn_conv_w", [3, 2 * DFF]); w_dn = I("ffn_w_down", [DFF, D])
    m_in = I("m_w_in", [D, CIN])
    mods = O("mods", [2, 2, 6 * D]); x2 = O("x2", [TM, D])
    qT = O("qT", [2048, TM], BF16); kT = O("kT", [2048, TM], BF16); oT = O("oT", [D, TM], BF16)
    vk = O("vk", [TM, 6144], BF16); gates = O("gates", [TM, 32])
    hT1 = T("hT1", [D, TE], BF16); PT = T("PT", [ABIN, TE]); catT = T("catT", [D, TM], BF16)
    y1 = T("y1", [TM, D]); x1 = T("x1", [TM, D]); h2T = T("h2T", [D, TM], BF16)
    actT = T("actT", [DFF, TM], BF16); y2 = T("y2", [TM, D]); h3T = T("h3T", [D, TM], BF16)
    m = lambda l, r, i: mods[l, r, i * D:(i + 1) * D]
    phase_ada(nc, csel, ada_w, ada_b, mods)
    if upto < 1:
        return nc
    phase_prenorm(nc, "pn1", xext, [0] * 17 + [1] * 2,
                  ab_rows={v: (norm_g[0, 0], m(0, v, 1), m(0, v, 0)) for v in (0, 1)}, hT_out=hT1)
    if upto < 2:
        return nc
    hook, hinit = make_store_hook(lambda gi: PT[gi * 128:(gi + 1) * 128, :])
    phase_gemm_f(nc, "g1", hT1, w_in, KC, PASS_F_TE, [[i * 128] for i in range(ABIN // 128)], hook, hinit)
    if upto < 3:
        return nc
    phase_abmix_a(nc, PT, acw, acb, alg, alb, catT)
    if upto < 4:
        return nc
    phase_abmix_b(nc, PT, bcw, hmask, catT)
    if upto < 5:
        return nc
    phase_gemm_t(nc, "g2", catT, w_out, KC, PASS_T_TM, NBLK_D, [y1])
    if upto < 6:
        return nc
    grp = [0] * 16 + [1] * 2
    xrows = [64 + g * 128 for g in range(16)] + [64 + NX + 64, 64 + NX + 64 + 128]
    phase_prenorm(nc, "pn2", xext, grp, y_in=y1, gg_rows={v: (m(0, v, 2), norm_g[0, 1]) for v in (0, 1)}, x_out=x1,
                  ab_rows={v: (norm_g[0, 2], m(0, v, 4), m(0, v, 3)) for v in (0, 1)}, hT_out=h2T, xin_rows=xrows)
    if upto < 7:
        return nc
    hook, hinit = make_ffn_hook(fcw, actT, lambda pi: 64 if pi < 2 else 256)
    phase_gemm_f(nc, "g3", h2T, w_up, KC, PASS_F_TM, FFN_GROUPS, hook, hinit)
    if upto < 8:
        return nc
    phase_gemm_t(nc, "g4", actT, w_dn, DFF // 128, PASS_T_TM, NBLK_D, [y2])
    if upto < 9:
        return nc
    phase_prenorm(nc, "pn3", x1, grp, y_in=y2, gg_rows={v: (m(0, v, 5), norm_g[0, 3]) for v in (0, 1)}, x_out=x2,
                  ab_rows={v: (norm_g[1, 0], m(1, v, 1), m(1, v, 0)) for v in (0, 1)}, hT_out=h3T)

    if upto < 10:
        return nc

    def dst(gi):
        if gi < 16:
            return qT[gi * 128:(gi + 1) * 128, :]
        if gi < 32:
            return kT[(gi - 16) * 128:(gi - 15) * 128, :]
        return oT[(gi - 32) * 128:(gi - 31) * 128, :]
    hook, hinit = make_store_hook(dst, func=lambda gi: (AF.Sigmoid if gi >= 32 else AF.Copy), dtype=BF16)
    groups = [[i * 128] for i in range(32)] + [[8192 + i * 128] for i in range(32)]
    phase_gemm_f(nc, "g5f", h3T, m_in, KC, PASS_F_TM, groups, hook, hinit)
    nb = [(4096 + n * 512, 512, 0, n * 512) for n in range(8)] + [(2048 + n * 512, 512, 0, 4096 + n * 512) for n in range(4)]
    phase_gemm_t(nc, "g5t", h3T, m_in, KC, PASS_T_TM, nb, [vk])
    phase_gemm_t(nc, "g5g", h3T, m_in, KC, PASS_T_TM, [(12288, 32, 0, 0)], [gates])
    return nc


def build_l2():
    nc = bass.Bass("TRN2", target_bir_lowering=False)
    I = lambda n, s, d=F32: _dt(nc, n, s, d, "ExternalInput")
    O = lambda n, s, d=F32: _dt(nc, n, s, d, "ExternalOutput")
    TT = 66 * 128
    qT = I("qT", [2, 256, TT], BF16); kT = I("kT", [2, 256, TT], BF16); kk = I("kk", [2, TT, 256], BF16)
    vv = I("vv", [2, TT, 512], BF16); gg = I("gg", [2, TT, 4]); bg = I("bg", [128, 8])
    hf = O("hf", [2, 8192, 512]); hb = O("hb", [2, 8192, 512])
    phase_scan(nc, qT, kT, kk, vv, gg, bg, hf, hb)
    return nc


def build_l3():
    nc = bass.Bass("TRN2", target_bir_lowering=False)
    I = lambda n, s, d=F32: _dt(nc, n, s, d, "ExternalInput")
    O = lambda n, s, d=F32: _dt(nc, n, s, d, "ExternalOutput")
    T = lambda n, s, d=F32: _dt(nc, n, s, d, "Internal")
    hf = I("hf", [NX, D]); hb = I("hb", [NX, D]); oT = I("oT", [D, NX], BF16); hn_g = I("hn_g", [D])
    m_out = I("m_w_out", [D, D]); x2 = I("x2", [NX, D]); mods = I("mods", [2, 2, 6 * D]); norm_g = I("norm_g", [2, 4, D])
    w_up = I("ffn_w_up", [D, 2 * DFF]); fcw = I("ffn_conv_w", [3, 2 * DFF]); w_dn = I("ffn_w_down", [DFF, D])
    out = O("out", [NX, D])
    hoT = T("hoT", [D, NX], BF16); y3 = T("y3", [NX, D]); x3 = T("x3", [NX, D]); h4T = T("h4T", [D, NX], BF16)
    actT = T("actT", [DFF, NX], BF16); y4 = T("y4", [NX, D])
    m = lambda l, r, i: mods[l, r, i * D:(i + 1) * D]
    phase_mpost(nc, hf, hb, hn_g, oT, hoT)
    phase_gemm_t(nc, "g6", hoT, m_out, KC, PASS_T_X, NBLK_D, [y3])
    grp = [0] * 16
    phase_prenorm(nc, "pn4", x2, grp, y_in=y3, gg_rows={0: (m(1, 0, 2), norm_g[1, 1])}, x_out=x3,
                  ab_rows={0: (norm_g[1, 2], m(1, 0, 4), m(1, 0, 3))}, hT_out=h4T)
    hook, hinit = make_ffn_hook(fcw, actT, lambda pi: 64)
    phase_gemm_f(nc, "g7", h4T, w_up, KC, PASS_F_X, FFN_GROUPS, hook, hinit)
    phase_gemm_t(nc, "g8", actT, w_dn, DFF // 128, PASS_T_X, NBLK_D, [y4])
    phase_prenorm(nc, "fin", x3, grp, y_in=y4, gg_rows={0: (m(1, 0, 5), norm_g[1, 3])}, x_out=out)
    return nc


def phase_mods_gather(nc, msend, mrecv, mods):
    P = Phase(nc, "mg")
    rec = P.pool.get("cc")
    sem = rec[0]
    rec[1] += 1
    val = rec[1]
    P.q["pool"].append(lambda h: h.collective_compute(
        "AllGather", ALU.bypass, replica_groups=[[0, 1, 2, 3], [4, 5, 6, 7]],
        ins=[msend.rearrange("l r n -> (l r) n")], outs=[mrecv[:, :]]).then_inc(sem, 1))
    P.q["pool"].append(lambda h: h.wait_ge(sem, val))
    P.pool.put(rec)
    P.run()
    P = Phase(nc, "mg2")
    rv = mrecv.rearrange("(j q) (i k) -> q i j k", q=4, k=1024)
    for l in range(2):
        for r in range(2):
            b = Buf("mcopy")
            P.dma("sp", mods[l, r].rearrange("(i j k) -> i j k", j=4, k=1024), rv[l * 2 + r], writes=[b])
    P.run()


def build_all():
    nc = bass.Bass("TRN2", target_bir_lowering=False)
    I = lambda n, s, d=F32: _dt(nc, n, s, d, "ExternalInput")
    O = lambda n, s, d=F32: _dt(nc, n, s, d, "ExternalOutput")
    T = lambda n, s, d=F32: _dt(nc, n, s, d, "Internal")
    csel = I("csel", [2, D]); ada_w = I("ada_w", [2, D, 6 * D // 4]); ada_b = I("ada_b", [2, 6 * D // 4])
    xext = I("xext", [TE, D]); hmask = I("hmask", [128, 2]); norm_g = I("norm_g", [2, 4, D])
    w_in = I("ab_w_in", [D, ABIN]); acw = I("a_conv_w", [31, WA]); acb = I("a_conv_b", [WA])
    alg = I("a_ln_g", [WA]); alb = I("a_ln_b", [WA]); bcw = I("b_conv_w", [3, WA]); w_out = I("ab_w_out", [D, D])
    w_up = I("ffn_w_up", [2, D, 2 * DFF]); fcw = I("ffn_conv_w", [2, 3, 2 * DFF]); w_dn = I("ffn_w_down", [2, DFF, D])
    m_in = I("m_w_in", [D, CIN]); bgx = I("bgx", [128, NCHT * 32]); flags = I("flags", [128, 8])
    hn_g = I("hn_g", [D]); m_out = I("m_w_out", [D, D])
    out = O("out", [NX, D])
    mods = T("mods", [2, 2, 6 * D]); x2 = T("x2", [TM, D])
    qT = T("qT", [2048, TM], BF16); kT = T("kT", [2048, TM], BF16); oT = T("oT", [D, TM], BF16)
    vk = T("vk", [TM, 6144], BF16); gates = T("gates", [TM, 32])
    hT1 = T("hT1", [D, TE], BF16); PT = T("PT", [ABIN, TE]); catT = T("catT", [D, TM], BF16)
    y1 = T("y1", [TM, D]); x1 = T("x1", [TM, D]); h2T = T("h2T", [D, TM], BF16)
    actT = T("actT", [DFF, TM], BF16); y2 = T("y2", [TM, D]); h3T = T("h3T", [D, TM], BF16)
    prep = T("prep", [2, 3, 128, NCHT * 8]); sctx = T("sctx", [16, 128, SROW]); send = T("send", [16, 128, SROW])
    recv = T("recv", [16, 4 * 128, SROW]); hf = T("hf", [NX, D]); hb = T("hb", [NX, D])
    hoT = T("hoT", [D, NX], BF16); y3 = T("y3", [NX, D]); x3 = T("x3", [NX, D]); h4T = T("h4T", [D, NX], BF16)
    actT1 = T("actT1", [DFF, NX], BF16); y4 = T("y4", [NX, D])
    m = lambda l, r, i: mods[l, r, i * D:(i + 1) * D]
    msend = T("msend", [2, 2, 6 * D // 4]); mrecv = T("mrecv", [16, 6 * D // 4])
    phase_ada(nc, csel, ada_w, ada_b, msend, NM=6 * D // 4)
    phase_mods_gather(nc, msend, mrecv, mods)
    phase_prenorm(nc, "pn1", xext, [0] * 17 + [1] * 2,
                  ab_rows={v: (norm_g[0, 0], m(0, v, 1), m(0, v, 0)) for v in (0, 1)}, hT_out=hT1)
    hook, hinit = make_store_hook(lambda gi: PT[gi * 128:(gi + 1) * 128, :])
    phase_gemm_f(nc, "g1", hT1, w_in, KC, PASS_F_TE, [[i * 128] for i in range(ABIN // 128)], hook, hinit)
    phase_abmix_a(nc, PT, acw, acb, alg, alb, catT)
    phase_abmix_b(nc, PT, bcw, hmask, catT)
    phase_gemm_t(nc, "g2", catT, w_out, KC, PASS_T_TM, NBLK_D, [y1])
    grp = [0] * 16 + [1] * 2
    xrows = [64 + g * 128 for g in range(16)] + [64 + NX + 64, 64 + NX + 64 + 128]
    phase_prenorm(nc, "pn2", xext, grp, y_in=y1, gg_rows={v: (m(0, v, 2), norm_g[0, 1]) for v in (0, 1)}, x_out=x1,
                  ab_rows={v: (norm_g[0, 2], m(0, v, 4), m(0, v, 3)) for v in (0, 1)}, hT_out=h2T, xin_rows=xrows)
    hook, hinit = make_ffn_hook(fcw[0], actT, lambda pi: 64 if pi < 2 else 256)
    phase_gemm_f(nc, "g3", h2T, w_up[0], KC, PASS_F_TM, FFN_GROUPS, hook, hinit)
    phase_gemm_t(nc, "g4", actT, w_dn[0], DFF // 128, PASS_T_TM, NBLK_D, [y2])
    phase_prenorm(nc, "pn3", x1, grp, y_in=y2, gg_rows={v: (m(0, v, 5), norm_g[0, 3]) for v in (0, 1)}, x_out=x2,
                  ab_rows={v: (norm_g[1, 0], m(1, v, 1), m(1, v, 0)) for v in (0, 1)}, hT_out=h3T)

    def dst(gi):
        if gi < 16:
            return qT[gi * 128:(gi + 1) * 128, :]
        if gi < 32:
            return kT[(gi - 16) * 128:(gi - 15) * 128, :]
        return oT[(gi - 32) * 128:(gi - 31) * 128, :]
    hook, hinit = make_store_hook(dst, func=lambda gi: (AF.Sigmoid if gi >= 32 else AF.Copy), dtype=BF16)
    groups = [[i * 128] for i in range(32)] + [[8192 + i * 128] for i in range(32)]
    phase_gemm_f(nc, "g5f", h3T, m_in, KC, PASS_F_TM, groups, hook, hinit)
    nb = [(4096 + n * 512, 512, 0, n * 512) for n in range(8)] + [(2048 + n * 512, 512, 0, 4096 + n * 512) for n in range(4)]
    phase_gemm_t(nc, "g5t", h3T, m_in, KC, PASS_T_TM, nb, [vk])
    phase_gemm_t(nc, "g5g", h3T, m_in, KC, PASS_T_TM, [(12288, 32, 0, 0)], [gates])
    phase_scan1(nc, vk, gates, bgx, prep, sctx, send)
    phase_allgather(nc, send, recv)
    phase_scan2(nc, qT, kT, vk, prep, sctx, recv, flags, hf, hb)
    phase_mpost(nc, hf, hb, hn_g, oT[:, 0:NX], hoT)
    phase_gemm_t(nc, "g6", hoT, m_out, KC, PASS_T_X, NBLK_D, [y3])
    grp1 = [0] * 16
    phase_prenorm(nc, "pn4", x2[0:NX, :], grp1, y_in=y3, gg_rows={0: (m(1, 0, 2), norm_g[1, 1])}, x_out=x3,
                  ab_rows={0: (norm_g[1, 2], m(1, 0, 4), m(1, 0, 3))}, hT_out=h4T)
    hook, hinit = make_ffn_hook(fcw[1], actT1, lambda pi: 64)
    phase_gemm_f(nc, "g7", h4T, w_up[1], KC, PASS_F_X, FFN_GROUPS, hook, hinit)
    phase_gemm_t(nc, "g8", actT1, w_dn[1], DFF // 128, PASS_T_X, NBLK_D, [y4])
    phase_prenorm(nc, "fin", x3, grp1, y_in=y4, gg_rows={0: (m(1, 0, 5), norm_g[1, 3])}, x_out=out)
    return nc


def kernel(x, c, ctx, c_ctx, ada_w, ada_b, norm_g, ab_w_in, a_conv_w, a_conv_b, a_ln_g, a_ln_b,
           b_conv_w, ab_w_out, m_w_in, m_b_gates, m_hn_g, m_w_out, ffn_w_up, ffn_conv_w, ffn_w_down):
    f = lambda a: np.ascontiguousarray(np.asarray(a))
    x, c, ctx, c_ctx = f(x), f(c), f(ctx), f(c_ctx)
    cores = list(range(NCORES))
    mbg = f(m_b_gates)[0].astype(np.float32)
    bgx = np.ascontiguousarray(np.broadcast_to(np.tile(mbg, NCHT)[None, :], (128, NCHT * 32)))
    ada_w, ada_b = np.asarray(ada_w), np.asarray(ada_b)
    acols = [np.concatenate([np.arange(i * D + jj * 1024, i * D + (jj + 1) * 1024) for i in range(6)]) for jj in range(4)]
    ada_ws = [f(ada_w[:, :, cc]) for cc in acols]
    ada_bs = [f(ada_b[:, cc]) for cc in acols]
    shared = dict(norm_g=f(norm_g), ab_w_in=f(ab_w_in[0]), a_conv_w=f(a_conv_w[0]),
                  a_conv_b=f(a_conv_b[0]), a_ln_g=f(a_ln_g[0]), a_ln_b=f(a_ln_b[0]), b_conv_w=f(b_conv_w[0]),
                  ab_w_out=f(ab_w_out[0]), ffn_w_up=f(ffn_w_up), ffn_conv_w=f(ffn_conv_w), ffn_w_down=f(ffn_w_down),
                  m_w_in=f(m_w_in[0]), bgx=bgx, hn_g=f(m_hn_g[0]), m_w_out=f(m_w_out[0]))
    ins = []
    for r in cores:
        b, j = r // 4, r % 4
        xe = np.zeros((TE, D), np.float32)
        t0 = j * NX
        if j > 0:
            xe[0:64] = x[b, t0 - 64:t0]
        xe[64:64 + NX] = x[b, t0:t0 + NX]
        if j < 3:
            xe[64 + NX:128 + NX] = x[b, t0 + NX:t0 + NX + 64]
        xe[128 + NX:] = ctx[b]
        hm = np.zeros((128, 2), np.float32)
        hm[:, 0] = 1.0 if j > 0 else 0.0
        hm[:, 1] = 1.0 if j < 3 else 0.0
        fl = np.zeros((128, 8), np.float32)
        for i in range(4):
            fl[:, i] = 1.0 if i < j else 0.0
            fl[:, 4 + i] = 1.0 if i > j else 0.0
        d = dict(shared)
        d.update(csel=np.stack([c[b], c_ctx]), xext=xe, hmask=hm, flags=fl, ada_w=ada_ws[j], ada_b=ada_bs[j])
        ins.append(d)
    res = run_bass_kernel_spmd(build_all(), ins, core_ids=cores).results
    out = np.zeros((2, 8192, D), np.float32)
    for r in cores:
        b, j = r // 4, r % 4
        out[b, j * NX:(j + 1) * NX] = res[r]["out"]
    return out


NCHT = TM // 128
SROW = 1032


def _tri_consts(P):
    ones = P.sb([128, 128], F32, "ones"); bon = Buf("ones")
    P.op("pool", lambda h: h.memset(ones[:, :], 1.0), writes=[bon])
    onesb = P.sb([128, 2], BF16, "onesb"); bonb = Buf("onesb")
    P.op("pool", lambda h: h.memset(onesb[:, :], 1.0), writes=[bonb])
    tri = []
    for d in range(2):
        t = P.sb([128, 128], F32, f"tri{d}"); tb = Buf(f"tri{d}")
        P.op("pool", lambda h, t=t: h.memset(t[:, :], 1.0), writes=[tb])
        sgn = 1 if d == 0 else -1
        P.op("pool", lambda h, t=t, sgn=sgn: h.affine_select(out=t[:, :], in_=t[:, :], pattern=[[sgn, 128]],
                                                            compare_op=ALU.is_ge, fill=0.0, base=0, channel_multiplier=-sgn),
             reads=[tb], writes=[tb])
        tri.append((t, tb))
    return ones, bon, onesb, bonb, tri


def _state_update(P, pst, pbufs, UB0, SBK, kp, kpb, k_ap, kb_, v_, vb, e_ap, gc_ap, gb_, C, n_, cb_, extra_reads, onesb, bonb):
    P.op("dve", lambda h: h.tensor_scalar(out=kp[:, :], in0=k_ap, scalar1=e_ap, scalar2=None, op0=ALU.mult),
         reads=[kb_, gb_], writes=[kpb])
    for dc in range(2):
        P.op("pe", lambda h, dc=dc: h.matmul(pst[:, UB0 + dc, :], lhsT=kp[:, dc * 128:(dc + 1) * 128], rhs=v_, start=True, stop=True),
             reads=[kpb, vb], writes=[pbufs[UB0 + dc]], sig=False, pe_acc=True)
    for dc in range(2):
        P.op("pe", lambda h, dc=dc: h.matmul(pst[:, SBK, 300 + dc:301 + dc], lhsT=kp[:, dc * 128:(dc + 1) * 128], rhs=onesb[:, 0:1], start=True, stop=True),
             reads=[kpb, bonb], writes=[pbufs[SBK]], sig=(dc == 1), pe_acc=True)
    P.op("dve", lambda h: h.tensor_tensor(out=C, in0=C, in1=pst[:, UB0:UB0 + 2, :], op=ALU.add),
         reads=[pbufs[UB0], pbufs[UB0 + 1]] + extra_reads, writes=[cb_])
    P.op("dve", lambda h: h.tensor_tensor(out=n_, in0=n_, in1=pst[:, SBK, 300:302], op=ALU.add),
         reads=[pbufs[SBK]] + extra_reads, writes=[cb_])
    P.op("dve", lambda h: h.tensor_scalar(out=C, in0=C, scalar1=gc_ap, scalar2=None, op0=ALU.mult), reads=[gb_], writes=[cb_])
    P.op("dve", lambda h: h.tensor_scalar(out=n_, in0=n_, scalar1=gc_ap, scalar2=None, op0=ALU.mult), reads=[gb_], writes=[cb_])


def phase_scan1(nc, vk, gates, bgx, prep, sctx, send):
    import math
    P = Phase(nc, "scan1")
    pst = P.ps([128, 8, 512], F32, "ps")
    pbufs = [Buf(f"ps{i}") for i in range(8)]
    ones, bon, onesb, bonb, tri = _tri_consts(P)
    NC8 = NCHT * 8
    G = P.sb([128, NCHT, 32], F32, "G"); Gb = Buf("G")
    bgt = P.sb([128, NCHT, 32], F32, "bgx"); bgb = Buf("bgx")
    P.dma("sp", G[:, :, :], gates.rearrange("(c p) f -> p c f", p=128), writes=[Gb], noncontig=True)
    P.dma("sp", bgt[:, :, :], bgx.rearrange("p (c f) -> p c f", f=32), writes=[bgb])
    P.op("dve", lambda h: h.tensor_tensor(out=G[:, :, :], in0=G[:, :, :], in1=bgt[:, :, :], op=ALU.add), reads=[Gb, bgb], writes=[Gb])
    ERG = []
    for d in range(2):
        ig = P.sb([128, NCHT, 8], F32, "ig"); lf = P.sb([128, NCHT, 8], F32, "lf")
        E = P.sb([128, NCHT, 8], F32, "E"); R = P.sb([128, NCHT, 8], F32, "R"); GC = P.sb([128, NCHT, 8], F32, "GC")
        gx = P.sb([128, 8], F32, "gx")
        gb_ = Buf("gprep")
        P.op("act", lambda h, lf=lf, d=d: h.activation(out=lf[:, :, :], in_=G[:, :, d * 16 + 8:d * 16 + 16], func=AF.Exp, scale=-1.0),
             reads=[Gb], writes=[gb_])
        P.op("act", lambda h, lf=lf: h.activation(out=lf[:, :, :], in_=lf[:, :, :], func=AF.Ln, bias=1.0), reads=[gb_], writes=[gb_])
        P.op("dve", lambda h, lf=lf: h.tensor_scalar(out=lf[:, :, :], in0=lf[:, :, :], scalar1=-1.0, scalar2=None, op0=ALU.mult),
             reads=[gb_], writes=[gb_])
        t, tb = tri[d]
        lff = lf[:, :, :].rearrange("p c h -> p (c h)")
        P.op("pe", lambda h, t=t, lff=lff: h.matmul(pst[:, 0, 0:NC8], lhsT=t[:, :], rhs=lff, start=True, stop=True),
             reads=[tb, gb_], writes=[pbufs[0]])
        P.op("pe", lambda h, lff=lff: h.matmul(pst[:, 1, 0:NC8], lhsT=ones[:, :], rhs=lff, start=True, stop=True),
             reads=[bon, gb_], writes=[pbufs[1]])
        flat = lambda T_: T_[:, :, :].rearrange("p c h -> p (c h)")
        P.op("act", lambda h, R=R: h.activation(out=flat(R), in_=pst[:, 0, 0:NC8], func=AF.Exp), reads=[pbufs[0]], writes=[gb_])
        P.op("act", lambda h, GC=GC: h.activation(out=flat(GC), in_=pst[:, 1, 0:NC8], func=AF.Exp), reads=[pbufs[1]], writes=[gb_])
        P.op("dve", lambda h, ig=ig, d=d: h.tensor_tensor(out=ig[:, :, :], in0=G[:, :, d * 16:d * 16 + 8],
                                                        in1=pst[:, 0, 0:NC8].rearrange("p (c h) -> p c h", h=8), op=ALU.subtract),
             reads=[pbufs[0], Gb], writes=[gb_])
        P.op("act", lambda h, E=E, ig=ig: h.activation(out=E[:, :, :], in_=ig[:, :, :], func=AF.Exp, bias=-math.log(16.0)),
             reads=[gb_], writes=[gb_])
        P.op("dve", lambda h, gx=gx: h.reduce_sum(out=gx[:, :], in_=pst[:, 1, 0:16 * 8].rearrange("p (c h) -> p h c", h=8), axis=AX.X),
             reads=[pbufs[1]], writes=[gb_])
        P.op("act", lambda h, gx=gx: h.activation(out=gx[:, :], in_=gx[:, :], func=AF.Exp), reads=[gb_], writes=[gb_])
        for j, T_ in enumerate((E, R, GC)):
            P.dma("sp", prep[d, j], flat(T_), reads=[gb_])
        ERG.append((E, R, GC, gx, gb_))
    Call = P.sb([128, 16, 1026], F32, "Call"); cbs = [Buf(f"C{i}") for i in range(16)]
    rows = [(P.sb([128, 6144], BF16, "vkrow"), Buf("vkrow")) for _ in range(3)]
    kpr = [(P.sb([128, 256], BF16, "kp"), Buf("kp")) for _ in range(3)]
    ri = 0
    ki = 0
    ui = 0
    for which in ("ctx", "x"):
        for ch in range(16):
            P.op("pool", lambda h, ch=ch: h.memset(Call[:, ch, :], 0.0), writes=[cbs[ch]])
        orders = ([16, 17], [17, 16]) if which == "ctx" else (list(range(16)), list(range(15, -1, -1)))
        for step in range(len(orders[0])):
            for d in range(2):
                c = orders[d][step]
                E, R, GC, gx, gb_ = ERG[d]
                row, rb = rows[ri % 3]
                ri += 1
                P.dma("sp", row[:, :], vk[c * 128:(c + 1) * 128, :], writes=[rb])
                for hd in range(8):
                    ch = d * 8 + hd
                    kp, kpb = kpr[ki % 3]
                    ki += 1
                    UB0 = 2 + 2 * (ui % 3)
                    SBK = ui % 2
                    ui += 1
                    Cc = Call[:, ch, 0:1024].rearrange("p (a b) -> p a b", a=2)
                    nn = Call[:, ch, 1024:1026]
                    _state_update(P, pst, pbufs, UB0, SBK, kp, kpb, row[:, 4096 + hd * 256:4096 + (hd + 1) * 256], rb,
                                  row[:, hd * 512:(hd + 1) * 512], rb, E[:, c, hd:hd + 1], GC[:, c, hd:hd + 1], gb_,
                                  Cc, nn, cbs[ch], [], onesb, bonb)
        dst = sctx if which == "ctx" else send
        dv = dst.rearrange("ch p f -> p ch f")
        if which == "x":
            for d in range(2):
                gx, gb_ = ERG[d][3], ERG[d][4]
                P.dma("sp", dv[:, d * 8:(d + 1) * 8, 1026:1027], gx[:, :].rearrange("p (h o) -> p h o", o=1), reads=[gb_], noncontig=True)
        P.dma("sp", dv[:, :, 0:1026], Call[:, :, :], reads=cbs)
    P.run()


def phase_allgather(nc, send, recv):
    P = Phase(nc, "ag")
    rec = P.pool.get("cc")
    sem = rec[0]
    for ch in range(16):
        rec[1] += 1
        P.q["pool"].append(lambda h, ch=ch: h.collective_compute(
            "AllGather", ALU.bypass, replica_groups=[[0, 1, 2, 3], [4, 5, 6, 7]], ins=[send[ch]], outs=[recv[ch]]).then_inc(sem, 1))
    val = rec[1]
    P.q["pool"].append(lambda h: h.wait_ge(sem, val))
    P.pool.put(rec)
    P.run()


def phase_scan2(nc, qT, kT, vk, prep, sctx, recv, flags, hf, hb):
    P = Phase(nc, "scan2")
    pst = P.ps([128, 8, 512], F32, "ps")
    pbufs = [Buf(f"ps{i}") for i in range(8)]
    ones, bon, onesb, bonb, tri = _tri_consts(P)
    fl = P.sb([128, 8], F32, "flags"); flb = Buf("flags")
    P.dma("sp", fl[:, :], flags[:, :], writes=[flb])
    ERG = []
    for d in range(2):
        ts = []
        for j in range(3):
            T_ = P.sb([128, NCHT, 8], F32, "erg")
            bj = Buf("ergl")
            P.dma("sp", T_[:, :, :].rearrange("p c h -> p (c h)"), prep[d, j], writes=[bj])
            ts.append((T_, bj))
        ERG.append(ts)
    qsb = [(P.sb([128, 2, TM], BF16, "qT"), Buf("qT")) for _ in range(2)]
    ksb = [(P.sb([128, 2, TM], BF16, "kT"), Buf("kT")) for _ in range(2)]
    NR = 4
    vr = [(P.sb([128, 512], BF16, "v"), Buf("v")) for _ in range(NR)]
    kr = [(P.sb([128, 256], BF16, "k"), Buf("k")) for _ in range(NR)]
    kpr = [(P.sb([128, 256], BF16, "kp"), Buf("kp")) for _ in range(NR)]
    spr = [(P.sb([128, 128], BF16, "sp"), Buf("sp")) for _ in range(2)]
    hr = [(P.sb([128, 512], F32, "h"), Buf("h")) for _ in range(2)]
    dsc = [(P.sb([128, 4], F32, "dsc"), Buf("dsc")) for _ in range(2)]
    Lt = [(P.sb([128, SROW], F32, "L"), Buf("L")) for _ in range(2)]
    St = [(P.sb([128, 1026], F32, "S"), Buf("S")) for _ in range(4)]
    Cbt = [(P.sb([128, 1026], BF16, "Cb"), Buf("Cb")) for _ in range(4)]
    av = [(P.sb([128, 4], F32, "av"), Buf("av")) for _ in range(2)]
    sv = sctx.rearrange("ch p f -> p ch f")
    rv = recv.rearrange("ch (r p) f -> p r ch f", p=128)
    ri = 0
    li = 0
    for hd in range(8):
        q_, qb = qsb[hd % 2]
        k_T, kb = ksb[hd % 2]
        P.dma("sp", q_[:, :, :], qT[hd * 256:(hd + 1) * 256, :].rearrange("(c p) t -> p c t", p=128), writes=[qb])
        P.dma("sp", k_T[:, :, :], kT[hd * 256:(hd + 1) * 256, :].rearrange("(c p) t -> p c t", p=128), writes=[kb])
        chains = []
        for d in range(2):
            ch = d * 8 + hd
            S, Sb = St[(hd % 2) * 2 + d]
            Cb, Cbb = Cbt[(hd % 2) * 2 + d]
            P.dma("sp", S[:, :], sv[:, ch, 0:1026], writes=[Sb])
            for i in (range(4) if d == 0 else range(3, -1, -1)):
                L, Lb = Lt[li % 2]
                a_, ab = av[li % 2]
                li += 1
                P.dma("sp", L[:, 0:1027], rv[:, i, ch, 0:1027], writes=[Lb])
                fcol = fl[:, d * 4 + i:d * 4 + i + 1]
                P.op("dve", lambda h, a_=a_, L=L: h.tensor_scalar(out=a_[:, 0:1], in0=L[:, 1026:1027], scalar1=-1.0, scalar2=None, op0=ALU.add),
                     reads=[Lb], writes=[ab])
                P.op("dve", lambda h, a_=a_, fcol=fcol: h.tensor_scalar(out=a_[:, 1:2], in0=a_[:, 0:1], scalar1=fcol, scalar2=1.0, op0=ALU.mult, op1=ALU.add),
                     reads=[ab, flb], writes=[ab])
                P.op("dve", lambda h, L=L, fcol=fcol: h.tensor_scalar(out=L[:, 0:1026], in0=L[:, 0:1026], scalar1=fcol, scalar2=None, op0=ALU.mult),
                     reads=[Lb, flb], writes=[Lb])
                P.op("dve", lambda h, S=S, L=L, a_=a_: h.scalar_tensor_tensor(out=S[:, :], in0=S[:, :], scalar=a_[:, 1:2], in1=L[:, 0:1026],
                                                                            op0=ALU.mult, op1=ALU.add), reads=[Sb, Lb, ab], writes=[Sb])
            P.op("act", lambda h, S=S, Cb=Cb: h.activation(out=Cb[:, :], in_=S[:, :], func=AF.Copy), reads=[Sb], writes=[Cbb])
            order = list(range(16)) if d == 0 else list(range(15, -1, -1))
            chains.append(dict(d=d, S=S, Sb=Sb, Cb=Cb, Cbb=Cbb, order=order, out=(hf if d == 0 else hb)))
        for step in range(16):
            for chn in chains:
                d = chn["d"]
                c = chn["order"][step]
                (E, Eb), (R, Rb), (GC, GCb) = ERG[d]
                S, Sb, Cb, Cbb = chn["S"], chn["Sb"], chn["Cb"], chn["Cbb"]
                base = d * 4
                SB, NB_, UB0 = base, base + 1, base + 2
                cols = slice(c * 128, (c + 1) * 128)
                v_, vb = vr[ri % NR]
                k_, kb_ = kr[ri % NR]
                kp, kpb = kpr[ri % NR]
                s_, sb_ = spr[ri % 2]
                h_, hb_ = hr[ri % 2]
                ds, dsb = dsc[ri % 2]
                ri += 1
                P.dma("sp", v_[:, :], vk[c * 128:(c + 1) * 128, hd * 512:(hd + 1) * 512], writes=[vb])
                P.dma("sp", k_[:, :], vk[c * 128:(c + 1) * 128, 4096 + hd * 256:4096 + (hd + 1) * 256], writes=[kb_])
                Cbv = Cb[:, 0:1024].rearrange("p (a b) -> p a b", a=2)
                for dc in range(2):
                    P.op("pe", lambda h, dc=dc, cols=cols, SB=SB, k_T=k_T, q_=q_: h.matmul(
                        pst[:, SB, 0:128], lhsT=k_T[:, dc, cols], rhs=q_[:, dc, cols], start=(dc == 0), stop=(dc == 1)),
                        reads=[kb, qb], writes=[pbufs[SB]], sig=(dc == 1), pe_acc=True)
                t, tb = tri[d]
                P.op("dve", lambda h, s_=s_, SB=SB, E=E, c=c, t=t, hd=hd: h.scalar_tensor_tensor(
                    out=s_[:, :], in0=pst[:, SB, 0:128], scalar=E[:, c, hd:hd + 1], in1=t[:, :], op0=ALU.mult, op1=ALU.mult),
                    reads=[pbufs[SB], Eb, tb], writes=[sb_])
                P.op("pe", lambda h, s_=s_, v_=v_, NB_=NB_: h.matmul(pst[:, NB_, :], lhsT=s_[:, :], rhs=v_[:, :], start=True, stop=False),
                     reads=[sb_, vb], writes=[pbufs[NB_]], sig=False, pe_acc=True)
                for dc in range(2):
                    P.op("pe", lambda h, dc=dc, cols=cols, Cbv=Cbv, NB_=NB_, q_=q_: h.matmul(
                        pst[:, NB_, :], lhsT=q_[:, dc, cols], rhs=Cbv[:, dc, :], start=False, stop=(dc == 1)),
                        reads=[qb, Cbb], writes=[pbufs[NB_]], sig=(dc == 1), pe_acc=True)
                P.op("pe", lambda h, s_=s_, SB=SB: h.matmul(pst[:, SB, 256:257], lhsT=s_[:, :], rhs=onesb[:, 0:1], start=True, stop=False),
                     reads=[sb_, bonb], writes=[pbufs[SB]], sig=False, pe_acc=True)
                for dc in range(2):
                    P.op("pe", lambda h, dc=dc, cols=cols, Cb=Cb, SB=SB, q_=q_: h.matmul(
                        pst[:, SB, 256:257], lhsT=q_[:, dc, cols], rhs=Cb[:, 1024 + dc:1025 + dc], start=False, stop=(dc == 1)),
                        reads=[qb, Cbb], writes=[pbufs[SB]], sig=(dc == 1), pe_acc=True)
                rcol = R[:, c, hd:hd + 1]
                P.op("dve", lambda h, ds=ds, SB=SB, rcol=rcol: h.tensor_tensor(out=ds[:, 0:1], in0=pst[:, SB, 256:257], in1=rcol, op=ALU.mult),
                     reads=[pbufs[SB], Rb], writes=[dsb])
                P.op("act", lambda h, ds=ds: h.activation(out=ds[:, 1:2], in_=ds[:, 0:1], func=AF.Abs), reads=[dsb], writes=[dsb])
                P.op("dve", lambda h, ds=ds: h.tensor_single_scalar(out=ds[:, 2:3], in_=ds[:, 1:2], scalar=1.0, op=ALU.max), reads=[dsb], writes=[dsb])
                P.op("dve", lambda h, ds=ds: h.reciprocal(out=ds[:, 1:2], in_=ds[:, 2:3]), reads=[dsb], writes=[dsb])
                P.op("dve", lambda h, ds=ds, rcol=rcol: h.tensor_tensor(out=ds[:, 3:4], in0=ds[:, 1:2], in1=rcol, op=ALU.mult),
                     reads=[dsb, Rb], writes=[dsb])
                P.op("act", lambda h, h_=h_, ds=ds, NB_=NB_: h.activation(out=h_[:, :], in_=pst[:, NB_, :], func=AF.Copy, scale=ds[:, 3:4]),
                     reads=[pbufs[NB_], dsb], writes=[hb_])
                P.dma("sp", chn["out"][c * 128:(c + 1) * 128, hd * 512:(hd + 1) * 512], h_[:, :], reads=[hb_])
                if step < 15:
                    Sv = S[:, 0:1024].rearrange("p (a b) -> p a b", a=2)
                    _state_update(P, pst, pbufs, UB0, SB, kp, kpb, k_[:, :], kb_, v_[:, :], vb, E[:, c, hd:hd + 1], GC[:, c, hd:hd + 1],
                                  Buf("dummy"), Sv, S[:, 1024:1026], Sb, [Eb, GCb], onesb, bonb)
                    P.op("act", lambda h, S=S, Cb=Cb: h.activation(out=Cb[:, :], in_=S[:, :], func=AF.Copy), reads=[Sb], writes=[Cbb])
    P.run()
```

```python
import contextlib
import numpy as np
import concourse.bass as bass
import concourse.mybir as mybir
from concourse.bass_utils import run_bass_kernel_spmd

F32 = mybir.dt.float32
BF16 = mybir.dt.bfloat16
ALU = mybir.AluOpType
AF = mybir.ActivationFunctionType
AX = mybir.AxisListType

D = 4096
KC = 32
DFF = 11008
NX = 2048
NCTX = 256
TM = NX + NCTX
TE = 64 + NX + 64 + NCTX
EPS = 1e-6
WA = 2048
ABIN = 10240
CIN = 12320
NCORES = 8


class Buf:
    __slots__ = ("name", "w", "r", "dsem", "dcnt", "weng", "rec")

    def __init__(self, name):
        self.name = name
        self.w = None
        self.r = {}
        self.dsem = None
        self.dcnt = 0
        self.weng = None


class SemPool:
    _by_nc = {}

    @classmethod
    def of(cls, nc):
        p = cls._by_nc.get(id(nc))
        if p is None:
            p = cls(nc)
            cls._by_nc[id(nc)] = p
        return p

    def __init__(self, nc):
        self.nc = nc
        self.es = contextlib.ExitStack()
        self.ce = {e: [self.es.enter_context(nc.semaphore(f"ce_{e}")), 0] for e in ("pe", "act", "dve", "pool")}
        self.free = {"hw": [], "sw": [], "cc": []}
        self.n = 0

    def get(self, kind):
        if self.free[kind]:
            return self.free[kind].pop()
        self.n += 1
        return [self.es.enter_context(self.nc.semaphore(f"dq_{kind}{self.n}")), 0, kind]

    def put(self, rec):
        self.free[rec[2]].append(rec)


class Phase:
    CE = ("pe", "act", "dve", "pool")

    def __init__(self, nc, name):
        self.nc = nc
        self.name = name
        self.es = contextlib.ExitStack()
        self.pool = SemPool.of(nc)
        self.q = {e: [] for e in ("pe", "act", "dve", "pool", "sp")}
        self.sem = {e: self.pool.ce[e][0] for e in self.CE}
        self.cnt = {e: self.pool.ce[e][1] for e in self.CE}
        self.cnt0 = dict(self.cnt)
        self.pend = {e: False for e in self.CE}
        self.seen = {e: {id(self.sem[c]): self.cnt[c] for c in self.CE} for e in self.q}
        self.dma_bufs = []
        self.nsb = 0

    def sb(self, shape, dtype, name=None):
        self.nsb += 1
        return self.es.enter_context(self.nc.sbuf_tensor(f"{self.name}_{name or 't'}{self.nsb}", list(shape), dtype))

    def ps(self, shape, dtype=F32, name=None):
        self.nsb += 1
        return self.es.enter_context(self.nc.psum_tensor(f"{self.name}_{name or 'p'}{self.nsb}", list(shape), dtype))

    def _wait(self, eng, dep):
        sem, val = dep
        k = id(sem)
        if self.seen[eng].get(k, 0) >= val:
            return
        if eng in self.CE and sem is self.sem[eng]:
            assert val <= self.cnt[eng], f"self-wait on pending {eng} {val} {self.cnt[eng]}"
        self.seen[eng][k] = val
        self.q[eng].append(lambda h, sem=sem, val=val: h.wait_ge(sem, val))

    def _hazards(self, eng, reads, writes, pe_acc=False):
        for b in reads:
            if b.w is not None:
                self._wait(eng, b.w)
        for b in writes:
            if b.w is not None and not (pe_acc and b.weng == "pe" and eng == "pe"):
                self._wait(eng, b.w)
            for d in b.r.values():
                self._wait(eng, d)

    def op(self, eng, fn, reads=(), writes=(), sig=True, pe_acc=False):
        self._hazards(eng, reads, writes, pe_acc)
        c = self.cnt[eng] + 1
        sem = self.sem[eng]
        dep = (sem, c)
        for b in reads:
            b.r[id(sem)] = dep
        for b in writes:
            b.w = dep
            b.weng = eng
            b.r = {}
        if sig:
            self.cnt[eng] = c
            self.pend[eng] = False
            self.q[eng].append(lambda h, fn=fn, sem=sem: fn(h).then_inc(sem, 1))
        else:
            self.pend[eng] = True
            self.q[eng].append(lambda h, fn=fn: fn(h))

    def dma(self, qeng, out, in_, reads=(), writes=(), noncontig=False):
        self._hazards(qeng, reads, writes)
        tgt = (list(writes) + list(reads))[0]
        if tgt.dsem is None:
            rec = self.pool.get("sw" if qeng == "pool" else "hw")
            tgt.dsem = rec[0]
            tgt.dcnt = rec[1]
            tgt.rec = rec
            self.dma_bufs.append(tgt)
        assert tgt.rec[2] == ("sw" if qeng == "pool" else "hw"), "buffer DMA'd from both queue kinds"
        tgt.dcnt += 16
        sem = tgt.dsem
        dep = (sem, tgt.dcnt)
        for b in writes:
            b.w = dep
            b.weng = "dma"
            b.r = {}
        for b in reads:
            b.r[id(sem)] = dep
        if noncontig:
            self.q[qeng].append(lambda h, o=out, i=in_, sem=sem: h.dma_start(
                out=o, in_=i, allow_slow_non_contiguous=True).then_inc(sem, 16))
        else:
            self.q[qeng].append(lambda h, o=out, i=in_, sem=sem: h.dma_start(out=o, in_=i).then_inc(sem, 16))

    def run(self):
        for e in self.CE:
            assert not self.pend[e], f"pending unsignaled op on {e}"
        for b in self.dma_bufs:
            self._wait("sp", (b.dsem, b.dcnt))
        for e in self.CE:
            if self.cnt[e] > self.cnt0[e]:
                self._wait("sp", (self.sem[e], self.cnt[e]))
            self.pool.ce[e][1] = self.cnt[e]
        for b in self.dma_bufs:
            b.rec[1] = b.dcnt
            self.pool.put(b.rec)
        nc = self.nc
        q = self.q
        with nc.Block() as block:
            if q["pe"]:
                @block.tensor
                def _(h):
                    for f in q["pe"]:
                        f(h)
            if q["act"]:
                @block.scalar
                def _(h):
                    for f in q["act"]:
                        f(h)
            if q["dve"]:
                @block.vector
                def _(h):
                    for f in q["dve"]:
                        f(h)
            if q["pool"]:
                @block.gpsimd
                def _(h):
                    for f in q["pool"]:
                        f(h)
            if q["sp"]:
                @block.sync
                def _(h):
                    for f in q["sp"]:
                        f(h)
        self.es.close()


def col_view(row_ap, n=None):
    return row_ap.rearrange("(c p) -> p c", p=128)


def make_ident(P):
    ident = P.sb([128, 128], F32, "ident")
    bid = Buf("ident")
    P.op("pool", lambda h: h.memset(ident[:, :], 0.0), writes=[bid])
    P.op("pool", lambda h: h.affine_select(out=ident[:, :], in_=ident[:, :], pattern=[[-1, 128]],
                                           compare_op=ALU.not_equal, fill=1.0, base=0, channel_multiplier=1),
         reads=[bid], writes=[bid])
    return ident, bid


def load_cols(P, rows, ident, bid, ps_ap, pbuf, name="cols"):
    ncs = [r.shape[0] // 128 for r in rows]
    ntot = sum(ncs)
    assert ntot <= 128
    stg = P.sb([128, 128], F32, name + "s")
    bs = [Buf(name + "s") for _ in rows]
    off = 0
    for r, n, b in zip(rows, ncs, bs):
        P.dma("sp", stg[off:off + n, :], r.rearrange("(c p) -> c p", p=128), writes=[b])
        off += n
    t = P.sb([128, ntot], F32, name)
    tb = Buf(name)
    P.op("pe", lambda h: h.transpose(out=ps_ap[:, 0:ntot], in_=stg[0:ntot, :], identity=ident[0:ntot, 0:ntot]),
         reads=bs + [bid], writes=[pbuf])
    P.op("dve", lambda h: h.tensor_copy(out=t[:, :], in_=ps_ap[:, 0:ntot]), reads=[pbuf], writes=[tb])
    return t, tb


def load_bcast(P, row_ap, n, name="bc", q="sp", parts=128):
    t = P.sb([parts, n], F32, name)
    b = Buf(name)
    P.dma(q, t[:, :], row_ap.partition_broadcast(parts), writes=[b])
    return t, b


def rstd_ops(P, s_, sb_, ci, ct, co, scale, n=1):
    P.op("act", lambda h: h.activation(out=s_[:, ct:ct + n], in_=s_[:, ci:ci + n], func=AF.Sqrt, scale=scale, bias=EPS),
         reads=[sb_], writes=[sb_])
    P.op("dve", lambda h: h.reciprocal(out=s_[:, co:co + n], in_=s_[:, ct:ct + n]), reads=[sb_], writes=[sb_])


def phase_ada(nc, csel, ada_w, ada_b, mods, layers=(0, 1), NM=6 * D):
    P = Phase(nc, "ada")
    ident, bid = make_ident(P)
    pst = P.ps([128, 8, 512], F32, "ps")
    pbufs = [Buf(f"ps{i}") for i in range(8)]
    cT, bcT = load_cols(P, [csel[0], csel[1]], ident, bid, pst[:, 7, :], pbufs[7], "cT")
    sT = P.sb([128, 2, KC], F32, "sT")
    bsT = Buf("sT")
    P.op("act", lambda h: h.activation(out=sT[:, :, :], in_=cT[:, :].rearrange("p (r c) -> p r c", r=2), func=AF.Silu),
         reads=[bcT], writes=[bsT])
    NB = 1024
    KG = 4
    ring = [(P.sb([128, KG, NB], F32, "w"), Buf("w")) for _ in range(4)]
    bias = [(P.sb([2, NB], F32, "bias"), Buf("bias")) for _ in range(2)]
    osb = [(P.sb([2, NB], F32, "osb"), Buf("osb")) for _ in range(2)]
    wi = 0
    pi = 0
    ni = 0
    for l in layers:
        wv = ada_w[l].rearrange("(c p) n -> p c n", p=128)
        for nb in range(NM // NB):
            bt, bb = bias[ni % 2]
            ot, bo = osb[ni % 2]
            ni += 1
            P.dma("sp", bt[:, :], ada_b[l, nb * NB:(nb + 1) * NB].partition_broadcast(2), writes=[bb])
            pb = [(pi + j) % 8 for j in range(NB // 512)]
            pi += NB // 512
            for kg in range(KC // KG):
                wt, wb = ring[wi % 4]
                wi += 1
                P.dma("sp", wt[:, :, :], wv[:, kg * KG:(kg + 1) * KG, nb * NB:(nb + 1) * NB], writes=[wb])
                for kk in range(KG):
                    kc = kg * KG + kk
                    for j, bk in enumerate(pb):
                        last = (kc == KC - 1)
                        P.op("pe", lambda h, bk=bk, kc=kc, kk=kk, j=j, wt=wt: h.matmul(
                            pst[0:2, bk, :], lhsT=sT[:, :, kc], rhs=wt[:, kk, j * 512:(j + 1) * 512],
                            start=(kc == 0), stop=(kc == KC - 1)),
                            reads=[bsT, wb], writes=[pbufs[bk]], sig=(last or (kk == KG - 1 and j == len(pb) - 1)), pe_acc=True)
            for j, bk in enumerate(pb):
                P.op("dve", lambda h, bk=bk, j=j, ot=ot, bt=bt: h.tensor_tensor(
                    out=ot[:, j * 512:(j + 1) * 512], in0=pst[0:2, bk, :], in1=bt[:, j * 512:(j + 1) * 512], op=ALU.add),
                    reads=[pbufs[bk], bb], writes=[bo])
            P.dma("sp", mods[l, :, nb * NB:(nb + 1) * NB], ot[:, :], reads=[bo])
    P.run()


def phase_prenorm(nc, name, x_in, groups, y_in=None, gg_rows=None, x_out=None, ab_rows=None, hT_out=None,
                  xin_rows=None):
    P = Phase(nc, name)
    NG = len(groups)
    vsets = sorted(set(groups))
    ident, bid = make_ident(P)
    pst = P.ps([128, 8, 512], F32, "ps")
    pbufs = [Buf(f"ps{i}") for i in range(8)]
    GG = {}
    if y_in is not None:
        for v in vsets:
            gt, gb = load_bcast(P, gg_rows[v][0], D, "gate")
            nt, nb_ = load_bcast(P, gg_rows[v][1], D, "gn")
            P.op("pool", lambda h, gt=gt, nt=nt: h.tensor_tensor(out=gt[:, :], in0=gt[:, :], in1=nt[:, :], op=ALU.mult),
                 reads=[gb, nb_], writes=[gb])
            GG[v] = (gt, gb)
    AB = {}
    if hT_out is not None:
        for v in vsets:
            cl, clb = load_cols(P, list(ab_rows[v]), ident, bid, pst[:, v, :], pbufs[v], "abc")
            A_ = P.sb([128, KC], F32, "A")
            Ab = Buf("A")
            P.op("dve", lambda h, A_=A_, cl=cl: h.scalar_tensor_tensor(
                out=A_[:, :], in0=cl[:, KC:2 * KC], scalar=1.0, in1=cl[:, 0:KC], op0=ALU.add, op1=ALU.mult),
                reads=[clb], writes=[Ab])
            B_, Bb = cl[:, 2 * KC:3 * KC], clb
            AB[v] = (A_, Ab, B_, Bb)
    xt = [(P.sb([128, D], F32, "xt"), Buf("xt")) for _ in range(2)]
    yt = [(P.sb([128, D], F32, "yt"), Buf("yt")) for _ in range(2)]
    sq = P.sb([128, D], BF16, "sq")
    bsq = Buf("sq")
    st = [(P.sb([128, 8], F32, "st"), Buf("st")) for _ in range(2)]
    NHB = 1 if y_in is not None else 2
    hb = [(P.sb([128, KC, 512], BF16, "hb"), Buf("hb")) for _ in range(NHB)] if hT_out is not None else None
    hTv = hT_out.rearrange("(c p) t -> p c t", p=128) if hT_out is not None else None

    def rstd(h, s, i, o):
        return None

    for g in range(NG):
        v = groups[g]
        x_, xb = xt[g % 2]
        y_, yb = yt[g % 2]
        s_, sb_ = st[g % 2]
        r0 = xin_rows[g] if xin_rows is not None else g * 128
        P.dma("sp", x_[:, :], x_in[r0:r0 + 128, :], writes=[xb])
        if y_in is not None:
            P.dma("sp", y_[:, :], y_in[g * 128:(g + 1) * 128, :], writes=[yb])
            P.op("act", lambda h, y_=y_, s_=s_: h.activation(out=sq[:, :], in_=y_[:, :], func=AF.Square,
                                                             accum_out=s_[:, 0:1]),
                 reads=[yb], writes=[bsq, sb_])
            rstd_ops(P, s_, sb_, 0, 1, 2, 1.0 / D)
            gt, gb = GG[v]
            P.op("dve", lambda h, y_=y_, s_=s_, gt=gt: h.scalar_tensor_tensor(
                out=y_[:, :], in0=y_[:, :], scalar=s_[:, 2:3], in1=gt[:, :], op0=ALU.mult, op1=ALU.mult),
                reads=[yb, sb_, gb], writes=[yb])
            P.op("pool", lambda h, x_=x_, y_=y_: h.tensor_tensor(out=x_[:, :], in0=x_[:, :], in1=y_[:, :], op=ALU.add),
                 reads=[xb, yb], writes=[xb])
            if x_out is not None:
                P.dma("sp", x_out[g * 128:(g + 1) * 128, :], x_[:, :], reads=[xb])
        if hT_out is None:
            continue
        P.op("act", lambda h, x_=x_, s_=s_: h.activation(out=sq[:, :], in_=x_[:, :], func=AF.Square,
                                                         accum_out=s_[:, 3:4]),
             reads=[xb], writes=[bsq, sb_])
        rstd_ops(P, s_, sb_, 3, 4, 5, 1.0 / D)
        P.op("act", lambda h, x_=x_, y_=y_, s_=s_: h.activation(out=y_[:, :], in_=x_[:, :], func=AF.Copy,
                                                                scale=s_[:, 5:6]),
             reads=[xb, sb_], writes=[yb])
        h_, hbb = hb[(g // 4) % NHB]
        A_, Ab, B_, Bb = AB[v]
        tcol = (g % 4) * 128
        for c in range(KC):
            bk = c // 4
            j = c % 4
            P.op("pe", lambda h, bk=bk, j=j, c=c, y_=y_: h.transpose(
                out=pst[:, bk, j * 128:(j + 1) * 128], in_=y_[:, c * 128:(c + 1) * 128], identity=ident[:, :]),
                reads=[yb, bid], writes=[pbufs[bk]], sig=(j == 3), pe_acc=True)
            if j == 3:
                for jj in range(4):
                    cc = bk * 4 + jj
                    if jj % 2 == 0:
                        P.op("act", lambda h, bk=bk, jj=jj, cc=cc, h_=h_, A_=A_, B_=B_, tcol=tcol: h.activation(
                            out=h_[:, cc, tcol:tcol + 128], in_=pst[:, bk, jj * 128:(jj + 1) * 128],
                            func=AF.Identity, scale=A_[:, cc:cc + 1], bias=B_[:, cc:cc + 1]),
                            reads=[pbufs[bk], Ab, Bb], writes=[hbb])
                    else:
                        P.op("dve", lambda h, bk=bk, jj=jj, cc=cc, h_=h_, A_=A_, B_=B_, tcol=tcol: h.tensor_scalar(
                            out=h_[:, cc, tcol:tcol + 128], in0=pst[:, bk, jj * 128:(jj + 1) * 128],
                            scalar1=A_[:, cc:cc + 1], scalar2=B_[:, cc:cc + 1], op0=ALU.mult, op1=ALU.add),
                            reads=[pbufs[bk], Ab, Bb], writes=[hbb])
        if g % 4 == 3 or g == NG - 1:
            t0 = (g // 4) * 512
            nt = (g % 4 + 1) * 128
            P.dma("sp", hTv[:, :, t0:t0 + nt], h_[:, :, 0:nt], reads=[hbb])
    P.run()


def phase_gemm_f(nc, name, hT, W, kc_n, passes, groups, hook, hook_init=None):
    P = Phase(nc, name)
    passes = [(p[0], p[1], (p[2] if len(p) > 2 else None)) for p in passes]
    TMAX = max(p[1] for p in passes)
    hsb = P.sb([128, kc_n, TMAX], BF16, "hT")
    NQ = 4
    kq = kc_n // NQ
    hbufs = [Buf(f"h{i}") for i in range(NQ)]
    NWR = 4
    ring = [(P.sb([128, kc_n, 128], BF16, "w"), Buf("w")) for _ in range(NWR)]
    pst = P.ps([128, 8, 512], F32, "ps")
    pbufs = [Buf(f"ps{i}") for i in range(8)]
    hTv = hT.rearrange("(c p) t -> p c t", p=128)
    Wv = W.rearrange("(c p) n -> p c n", p=128)
    P.ps_shared, P.pb_shared = pst, pbufs
    ctx = hook_init(P) if hook_init else None
    wi = 0
    si = 0
    for pi_, (t0, T, blks_) in enumerate(passes):
        for qd in range(NQ):
            P.dma("sp", hsb[:, qd * kq:(qd + 1) * kq, 0:T], hTv[:, qd * kq:(qd + 1) * kq, t0:t0 + T],
                  writes=[hbufs[qd]])
        nblk = (T + 511) // 512
        blks = blks_ if blks_ is not None else [(b * 512, min(512, T - b * 512)) for b in range(nblk)]
        nblk = len(blks)
        assert nblk <= 2 and all(cn <= 512 for _, cn in blks)
        for gi, grp in enumerate(groups):
            slots = []
            for n0 in grp:
                wt, wb = ring[wi % NWR]
                wi += 1
                P.dma("pool", wt[:, :, :], Wv[:, :, n0:n0 + 128], writes=[wb])
                slot = si % 4
                si += 1
                for kc in range(kc_n):
                    for b, (c0, cn) in enumerate(blks):
                        bk = slot * 2 + b
                        last = (kc == kc_n - 1)
                        P.op("pe", lambda h, bk=bk, kc=kc, c0=c0, cn=cn, wt=wt: h.matmul(
                            pst[:, bk, 0:cn], lhsT=wt[:, kc, :], rhs=hsb[:, kc, c0:c0 + cn],
                            start=(kc == 0), stop=(kc == kc_n - 1)),
                            reads=[wb, hbufs[kc // kq]], writes=[pbufs[bk]], sig=(last and b == nblk - 1),
                            pe_acc=True)
                slots.append([(pst, slot * 2 + b, pbufs[slot * 2 + b], c0, cn) for b, (c0, cn) in enumerate(blks)])
            hook(P, ctx, gi, (pi_, t0, T), slots)
    P.run()


def phase_gemm_t(nc, name, aT, W, kc_n, passes, nblocks, outs):
    P = Phase(nc, name)
    TMAX = max(t for _, t in passes)
    asb = P.sb([128, kc_n, TMAX], BF16, "aT")
    NQ = 2
    kq = (kc_n + NQ - 1) // NQ
    abufs = [Buf(f"a{i}") for i in range(NQ)]
    KG = 8
    NWR = 4
    ring = [(P.sb([128, KG, 512], BF16, "w"), Buf("w")) for _ in range(NWR)]
    pst = P.ps([128, 8, 512], F32, "ps")
    pbufs = [Buf(f"ps{i}") for i in range(8)]
    osbs = [(P.sb([128, 4, 512], outs[0].dtype if len(set(o.dtype for o in outs)) == 1 else F32, "o"), Buf("o"))
            for _ in range(2)]
    aTv = aT.rearrange("(c p) t -> p c t", p=128)
    Wv = W.rearrange("(c p) n -> p c n", p=128)
    wi = 0
    si = 0
    oi = 0
    for (t0, T) in passes:
        for qd in range(NQ):
            k0, k1 = qd * kq, min(kc_n, (qd + 1) * kq)
            P.dma("sp", asb[:, k0:k1, 0:T], aTv[:, k0:k1, t0:t0 + T], writes=[abufs[qd]])
        ntg = T // 128
        for (n0, ncol, oidx, oc0) in nblocks:
            slot = si % 2
            si += 1
            for kg in range((kc_n + KG - 1) // KG):
                k0, k1 = kg * KG, min(kc_n, (kg + 1) * KG)
                wt, wb = ring[wi % NWR]
                wi += 1
                P.dma("pool", wt[:, 0:k1 - k0, 0:ncol], Wv[:, k0:k1, n0:n0 + ncol], writes=[wb])
                for kc in range(k0, k1):
                    for tg in range(ntg):
                        bk = slot * 4 + tg
                        last = (kc == kc_n - 1)
                        P.op("pe", lambda h, bk=bk, kc=kc, tg=tg, wt=wt, k0=k0, ncol=ncol: h.matmul(
                            pst[:, bk, 0:ncol], lhsT=asb[:, kc, tg * 128:(tg + 1) * 128], rhs=wt[:, kc - k0, 0:ncol],
                            start=(kc == 0), stop=(kc == kc_n - 1)),
                            reads=[wb, abufs[kc // kq]], writes=[pbufs[bk]], sig=((last or kc == k1 - 1) and tg == ntg - 1),
                            pe_acc=True)
            o_, ob = osbs[oi % 2]
            oi += 1
            for tg in range(ntg):
                bk = slot * 4 + tg
                if tg % 2 == 0:
                    P.op("act", lambda h, bk=bk, tg=tg, o_=o_, ncol=ncol: h.activation(
                        out=o_[:, tg, 0:ncol], in_=pst[:, bk, 0:ncol], func=AF.Copy),
                        reads=[pbufs[bk]], writes=[ob])
                else:
                    P.op("dve", lambda h, bk=bk, tg=tg, o_=o_, ncol=ncol: h.tensor_copy(
                        out=o_[:, tg, 0:ncol], in_=pst[:, bk, 0:ncol]),
                        reads=[pbufs[bk]], writes=[ob])
            ov = outs[oidx][t0:t0 + T, oc0:oc0 + ncol].rearrange("(g p) n -> p g n", p=128)
            P.dma("sp", ov, o_[:, 0:ntg, 0:ncol], reads=[ob])
    P.run()


def make_store_hook(dst_rows, func=None, dtype=F32, tmax=1024):
    def hinit(P):
        return [(P.sb([128, tmax], dtype, "o"), Buf("o")) for _ in range(2)]

    def hook(P, ctx, gi, pinfo, slots):
        _, t0, T = pinfo
        o_, ob = ctx[gi % 2]
        for k, (pst, bk, pb, c0, cn) in enumerate(slots[0]):
            if func is None and k % 2 == 1:
                P.op("dve", lambda h, bk=bk, c0=c0, cn=cn, pst=pst: h.tensor_copy(out=o_[:, c0:c0 + cn], in_=pst[:, bk, 0:cn]),
                     reads=[pb], writes=[ob])
            else:
                fn_ = func(gi) if callable(func) else (func or AF.Copy)
                P.op("act", lambda h, bk=bk, c0=c0, cn=cn, pst=pst, fn_=fn_: h.activation(
                    out=o_[:, c0:c0 + cn], in_=pst[:, bk, 0:cn], func=fn_), reads=[pb], writes=[ob])
        P.dma("sp", dst_rows(gi)[:, t0:t0 + T], o_[:, 0:T], reads=[ob])
    return hook, hinit


def make_ffn_hook(conv_w, actT, row_len):
    NCH = DFF // 128

    def hinit(P):
        ident, bid = make_ident(P)
        pst = P.ps_shared
        cw = []
        for j in range(3):
            for half in range(2):
                t, tb = load_cols(P, [conv_w[j, half * DFF:(half + 1) * DFF]], ident, bid, pst[:, 7, :], P.pb_shared[7], "cw")
                cw.append((t, tb))
        gs = [(P.sb([128, 1024], F32, "g"), Buf("g")) for _ in range(2)]
        vs = [(P.sb([128, 1024], F32, "v"), Buf("v")) for _ in range(2)]
        as_ = [(P.sb([128, 1024], BF16, "a"), Buf("a")) for _ in range(2)]
        return dict(cw=cw, gs=gs, vs=vs, as_=as_)

    def hook(P, ctx, gi, pinfo, slots):
        pi_, t0, T = pinfo
        g_, gb = ctx["gs"][gi % 2]
        v_, vb = ctx["vs"][gi % 2]
        a_, ab = ctx["as_"][gi % 2]
        cw = ctx["cw"]
        for half, (s_, sb_) in enumerate(((g_, gb), (v_, vb))):
            w0, w0b = cw[0 * 2 + half]
            w1, w1b = cw[1 * 2 + half]
            w2, w2b = cw[2 * 2 + half]
            for (pst, bk, pb, c0, cn) in slots[half]:
                L = row_len(pi_, c0)
                assert cn % L == 0
                P.op("act", lambda h, s_=s_, bk=bk, c0=c0, cn=cn, pst=pst, w1=w1: h.activation(
                    out=s_[:, c0:c0 + cn], in_=pst[:, bk, 0:cn], func=AF.Copy, scale=w1[:, gi:gi + 1]),
                    reads=[pb, w1b], writes=[sb_])
                sv = s_[:, c0:c0 + cn].rearrange("p (r l) -> p r l", l=L)
                pv = pst[:, bk, 0:cn].rearrange("p (r l) -> p r l", l=L)
                P.op("dve", lambda h, sv=sv, pv=pv, w0=w0, L=L: h.scalar_tensor_tensor(
                    out=sv[:, :, 1:L], in0=pv[:, :, 0:L - 1], scalar=w0[:, gi:gi + 1], in1=sv[:, :, 1:L],
                    op0=ALU.mult, op1=ALU.add), reads=[pb, w0b, sb_], writes=[sb_])
                P.op("dve", lambda h, sv=sv, pv=pv, w2=w2, L=L: h.scalar_tensor_tensor(
                    out=sv[:, :, 0:L - 1], in0=pv[:, :, 1:L], scalar=w2[:, gi:gi + 1], in1=sv[:, :, 0:L - 1],
                    op0=ALU.mult, op1=ALU.add), reads=[pb, w2b, sb_], writes=[sb_])
        P.op("act", lambda h: h.activation(out=g_[:, 0:T], in_=g_[:, 0:T], func=AF.Silu), reads=[gb], writes=[gb])
        P.op("dve", lambda h: h.tensor_tensor(out=a_[:, 0:T], in0=g_[:, 0:T], in1=v_[:, 0:T], op=ALU.mult),
             reads=[gb, vb], writes=[ab])
        P.dma("sp", actT[gi * 128:(gi + 1) * 128, t0:t0 + T], a_[:, 0:T], reads=[ab])
    return hook, hinit


def phase_abmix_a(nc, PT, a_conv_w, a_conv_b, a_ln_g, a_ln_b, catT):
    P = Phase(nc, "abA")
    ident, bid = make_ident(P)
    pst = P.ps([128, 8, 512], F32, "ps")
    pbufs = [Buf(f"ps{i}") for i in range(8)]
    NA = WA // 128
    cwa, cwab = load_cols(P, [a_conv_w[j] for j in range(0, 8)], ident, bid, pst[:, 0, :], pbufs[0], "cwa")
    cwb, cwbb = load_cols(P, [a_conv_w[j] for j in range(8, 16)], ident, bid, pst[:, 1, :], pbufs[1], "cwb")
    cwc, cwcb = load_cols(P, [a_conv_w[j] for j in range(16, 24)], ident, bid, pst[:, 2, :], pbufs[2], "cwc")
    cwd, cwdb = load_cols(P, [a_conv_w[j] for j in range(24, 31)], ident, bid, pst[:, 3, :], pbufs[3], "cwd")
    prm, prmb = load_cols(P, [a_conv_b, a_ln_g, a_ln_b], ident, bid, pst[:, 4, :], pbufs[4], "prm")
    cws = [(cwa, cwab), (cwb, cwbb), (cwc, cwcb), (cwd, cwdb)]

    def wcol(j, i):
        t, tb = cws[j // 8]
        return t[:, (j % 8) * NA + i:(j % 8) * NA + i + 1], tb
    ones = P.sb([128, 128], F32, "ones")
    bon = Buf("ones")
    P.op("pool", lambda h: h.memset(ones[:, :], 1.0), writes=[bon])
    uc = P.sb([128, NA, 512], F32, "uc")
    ucb = [Buf(f"uc{i}") for i in range(NA)]
    vt = [(P.sb([128, 512], F32, "val"), Buf("val")) for _ in range(2)]
    gt = [(P.sb([128, 512], F32, "gate"), Buf("gate")) for _ in range(2)]
    sqt = [(P.sb([128, 512], F32, "sq"), Buf("sq")) for _ in range(2)]
    mean = P.sb([128, 512], F32, "mean"); bmean = Buf("mean")
    rstd = P.sb([128, 512], F32, "rstd"); brstd = Buf("rstd")
    tmp = [(P.sb([128, 512], F32, "tmp"), Buf("tmp")) for _ in range(2)]
    ob = [(P.sb([128, 512], BF16, "o"), Buf("o")) for _ in range(2)]
    pieces = [(64 + k * 512, k * 512, 512, 64) for k in range(4)] + [(64 + NX + 64, NX, NCTX, NCTX)]
    k = 0
    for (e0, m0, T, L) in pieces:
        for i in range(NA):
            v_, vb = vt[k % 2]
            g_, gb = gt[k % 2]
            q_, qb = sqt[k % 2]
            k += 1
            P.dma("sp", v_[:, 0:T], PT[i * 128:(i + 1) * 128, e0:e0 + T], writes=[vb])
            P.dma("sp", g_[:, 0:T], PT[WA + i * 128:WA + (i + 1) * 128, e0:e0 + T], writes=[gb])
            P.op("act", lambda h, g_=g_, T=T: h.activation(out=g_[:, 0:T], in_=g_[:, 0:T], func=AF.Sigmoid),
                 reads=[gb], writes=[gb])
            P.op("dve", lambda h, g_=g_, v_=v_, T=T: h.tensor_tensor(out=v_[:, 0:T], in0=v_[:, 0:T], in1=g_[:, 0:T], op=ALU.mult),
                 reads=[gb, vb], writes=[vb])
            wc, wcb = wcol(15, i)
            P.op("act", lambda h, v_=v_, i=i, T=T, wc=wc: h.activation(
                out=uc[:, i, 0:T], in_=v_[:, 0:T], func=AF.Identity, scale=wc, bias=prm[:, i:i + 1]),
                reads=[vb, wcb, prmb], writes=[ucb[i]])
            uv = uc[:, i, 0:T].rearrange("p (r l) -> p r l", l=L)
            vv = v_[:, 0:T].rearrange("p (r l) -> p r l", l=L)
            n = 0
            for dd in range(1, 16):
                for d in (dd, -dd):
                    wc, wcb = wcol(15 + d, i)
                    if d > 0:
                        o_ap, i_ap = uv[:, :, 0:L - d], vv[:, :, d:L]
                    else:
                        o_ap, i_ap = uv[:, :, -d:L], vv[:, :, 0:L + d]
                    eng = "dve"
                    n += 1
                    P.op(eng, lambda h, o_ap=o_ap, i_ap=i_ap, wc=wc: h.scalar_tensor_tensor(
                        out=o_ap, in0=i_ap, scalar=wc, in1=o_ap, op0=ALU.mult, op1=ALU.add),
                        reads=[vb, wcb, ucb[i]], writes=[ucb[i]])
            P.op("act", lambda h, q_=q_, i=i, T=T: h.activation(out=q_[:, 0:T], in_=uc[:, i, 0:T], func=AF.Square),
                 reads=[ucb[i]], writes=[qb])
            P.op("pe", lambda h, i=i, T=T: h.matmul(pst[:, 6, 0:T], lhsT=ones[:, :], rhs=uc[:, i, 0:T],
                                                    start=(i == 0), stop=(i == NA - 1)),
                 reads=[bon, ucb[i]], writes=[pbufs[6]], pe_acc=True)
            P.op("pe", lambda h, q_=q_, i=i, T=T: h.matmul(pst[:, 7, 0:T], lhsT=ones[:, :], rhs=q_[:, 0:T],
                                                           start=(i == 0), stop=(i == NA - 1)),
                 reads=[bon, qb], writes=[pbufs[7]], pe_acc=True)
        P.op("act", lambda h, T=T: h.activation(out=mean[:, 0:T], in_=pst[:, 6, 0:T], func=AF.Copy, scale=1.0 / WA),
             reads=[pbufs[6]], writes=[bmean])
        t_, tb = tmp[0]
        P.op("dve", lambda h, t_=t_, T=T: h.tensor_tensor(out=t_[:, 0:T], in0=mean[:, 0:T], in1=mean[:, 0:T], op=ALU.mult),
             reads=[bmean], writes=[tb])
        P.op("dve", lambda h, t_=t_, T=T: h.scalar_tensor_tensor(
            out=t_[:, 0:T], in0=pst[:, 7, 0:T], scalar=1.0 / WA, in1=t_[:, 0:T], op0=ALU.mult, op1=ALU.subtract),
            reads=[pbufs[7], tb], writes=[tb])
        P.op("act", lambda h, t_=t_, T=T: h.activation(out=t_[:, 0:T], in_=t_[:, 0:T], func=AF.Sqrt, bias=EPS),
             reads=[tb], writes=[tb])
        P.op("dve", lambda h, t_=t_, T=T: h.reciprocal(out=rstd[:, 0:T], in_=t_[:, 0:T]), reads=[tb], writes=[brstd])
        for i in range(NA):
            t_, tb = tmp[i % 2]
            o_, obb = ob[i % 2]
            P.op("dve", lambda h, t_=t_, i=i, T=T: h.tensor_tensor(out=t_[:, 0:T], in0=uc[:, i, 0:T], in1=mean[:, 0:T], op=ALU.subtract),
                 reads=[ucb[i], bmean], writes=[tb])
            P.op("pool", lambda h, t_=t_, T=T: h.tensor_tensor(out=t_[:, 0:T], in0=t_[:, 0:T], in1=rstd[:, 0:T], op=ALU.mult),
                 reads=[tb, brstd], writes=[tb])
            P.op("act", lambda h, t_=t_, o_=o_, i=i, T=T: h.activation(
                out=o_[:, 0:T], in_=t_[:, 0:T], func=AF.Silu, scale=prm[:, NA + i:NA + i + 1], bias=prm[:, 2 * NA + i:2 * NA + i + 1]),
                reads=[tb, prmb], writes=[obb])
            P.dma("sp", catT[i * 128:(i + 1) * 128, m0:m0 + T], o_[:, 0:T], reads=[obb])
    P.run()


def phase_abmix_b(nc, PT, b_conv_w, hmask, catT):
    P = Phase(nc, "abB")
    ident, bid = make_ident(P)
    pst = P.ps([128, 8, 512], F32, "ps")
    pbufs = [Buf(f"ps{i}") for i in range(8)]
    NB_ = WA // 128
    cw, cwb = load_cols(P, [b_conv_w[0], b_conv_w[1], b_conv_w[2]], ident, bid, pst[:, 0, :], pbufs[0], "cw")
    hm = P.sb([128, 2], F32, "hm"); hmb = Buf("hm")
    P.dma("sp", hm[:, :], hmask[:, :], writes=[hmb])
    XE = 64 + NX + 64
    bb_ = [(P.sb([128, TM], F32, "bb"), Buf("bb")) for _ in range(2)]
    bc_ = [(P.sb([128, TE], F32, "bc"), Buf("bc")) for _ in range(2)]
    bx_ = [(P.sb([128, TE], F32, "bx"), Buf("bx")) for _ in range(2)]
    zc_ = [(P.sb([128, TM], F32, "zc"), Buf("zc")) for _ in range(2)]
    o_ = [(P.sb([128, TM], BF16, "o"), Buf("o")) for _ in range(2)]
    for i in range(NB_):
        b_, bbb = bb_[i % 2]
        c_, cb = bc_[i % 2]
        x_, xb = bx_[i % 2]
        z_, zb = zc_[i % 2]
        oo, ob = o_[i % 2]
        r_b = 2 * WA + i * 128
        r_c = 2 * WA + WA + i * 128
        r_x = 2 * WA + 2 * WA + i * 128
        P.dma("sp", b_[:, 0:NX], PT[r_b:r_b + 128, 64:64 + NX], writes=[bbb])
        bbb2 = Buf("bb2")
        P.dma("sp", b_[:, NX:TM], PT[r_b:r_b + 128, XE:TE], writes=[bbb2])
        P.dma("sp", c_[:, :], PT[r_c:r_c + 128, :], writes=[cb])
        P.dma("sp", x_[:, :], PT[r_x:r_x + 128, :], writes=[xb])
        P.op("dve", lambda h, c_=c_, x_=x_: h.tensor_tensor(out=c_[:, :], in0=c_[:, :], in1=x_[:, :], op=ALU.mult),
             reads=[cb, xb], writes=[cb])
        P.op("dve", lambda h, c_=c_: h.tensor_scalar(out=c_[:, 0:64], in0=c_[:, 0:64], scalar1=hm[:, 0:1], scalar2=None, op0=ALU.mult),
             reads=[cb, hmb], writes=[cb])
        P.op("dve", lambda h, c_=c_: h.tensor_scalar(out=c_[:, 64 + NX:XE], in0=c_[:, 64 + NX:XE], scalar1=hm[:, 1:2], scalar2=None, op0=ALU.mult),
             reads=[cb, hmb], writes=[cb])
        w0, w1, w2 = cw[:, i:i + 1], cw[:, NB_ + i:NB_ + i + 1], cw[:, 2 * NB_ + i:2 * NB_ + i + 1]
        P.op("act", lambda h, z_=z_, c_=c_, w1=w1: h.activation(out=z_[:, 0:NX], in_=c_[:, 64:64 + NX], func=AF.Copy, scale=w1),
             reads=[cb, cwb], writes=[zb])
        P.op("dve", lambda h, z_=z_, c_=c_, w0=w0: h.scalar_tensor_tensor(
            out=z_[:, 0:NX], in0=c_[:, 0:NX], scalar=w0, in1=z_[:, 0:NX], op0=ALU.mult, op1=ALU.add),
            reads=[cb, cwb, zb], writes=[zb])
        P.op("dve", lambda h, z_=z_, c_=c_, w2=w2: h.scalar_tensor_tensor(
            out=z_[:, 0:NX], in0=c_[:, 128:128 + NX], scalar=w2, in1=z_[:, 0:NX], op0=ALU.mult, op1=ALU.add),
            reads=[cb, cwb, zb], writes=[zb])
        P.op("act", lambda h, z_=z_, c_=c_, w1=w1: h.activation(out=z_[:, NX:TM], in_=c_[:, XE:TE], func=AF.Copy, scale=w1),
             reads=[cb, cwb], writes=[zb])
        P.op("dve", lambda h, z_=z_, c_=c_, w0=w0: h.scalar_tensor_tensor(
            out=z_[:, NX + 1:TM], in0=c_[:, XE:TE - 1], scalar=w0, in1=z_[:, NX + 1:TM], op0=ALU.mult, op1=ALU.add),
            reads=[cb, cwb, zb], writes=[zb])
        P.op("dve", lambda h, z_=z_, c_=c_, w2=w2: h.scalar_tensor_tensor(
            out=z_[:, NX:TM - 1], in0=c_[:, XE + 1:TE], scalar=w2, in1=z_[:, NX:TM - 1], op0=ALU.mult, op1=ALU.add),
            reads=[cb, cwb, zb], writes=[zb])
        P.op("pool", lambda h, z_=z_, b_=b_, oo=oo: h.tensor_tensor(out=oo[:, :], in0=z_[:, :], in1=b_[:, :], op=ALU.mult),
             reads=[zb, bbb, bbb2], writes=[ob])
        P.dma("sp", catT[WA + i * 128:WA + (i + 1) * 128, :], oo[:, :], reads=[ob])
    P.run()


def phase_scan(nc, qT, kT, kk, vv, gg, bg, hf, hb, NCH=66, NCTXC=2, npairs=2):
    import math
    P = Phase(nc, "scan")
    ident, bid = make_ident(P)
    pst = P.ps([128, 8, 512], F32, "ps")
    pbufs = [Buf(f"ps{i}") for i in range(8)]
    TT = NCH * 128
    ones = P.sb([128, 128], F32, "ones"); bon = Buf("ones")
    P.op("pool", lambda h: h.memset(ones[:, :], 1.0), writes=[bon])
    onesb = P.sb([128, 2], BF16, "onesb"); bonb = Buf("onesb")
    P.op("pool", lambda h: h.memset(onesb[:, :], 1.0), writes=[bonb])
    tri = []
    for d in range(2):
        t = P.sb([128, 128], F32, f"tri{d}"); tb = Buf(f"tri{d}")
        P.op("pool", lambda h, t=t: h.memset(t[:, :], 1.0), writes=[tb])
        sgn = 1 if d == 0 else -1
        P.op("pool", lambda h, t=t, sgn=sgn: h.affine_select(out=t[:, :], in_=t[:, :], pattern=[[sgn, 128]],
                                                            compare_op=ALU.is_ge, fill=0.0, base=0, channel_multiplier=-sgn),
             reads=[tb], writes=[tb])
        tri.append((t, tb))
    bgt = P.sb([128, 4 * npairs], F32, "bg"); bgb = Buf("bg")
    P.dma("sp", bgt[:, :], bg[:, :], writes=[bgb])
    nbg = P.sb([128, 4 * npairs], F32, "nbg"); nbgb = Buf("nbg")
    P.op("dve", lambda h: h.tensor_scalar(out=nbg[:, :], in0=bgt[:, :], scalar1=-1.0, scalar2=None, op0=ALU.mult),
         reads=[bgb], writes=[nbgb])
    qsb = P.sb([128, 2, TT], BF16, "qT"); qb = Buf("qT")
    ksb = P.sb([128, 2, TT], BF16, "kT"); kb = Buf("kT")
    G = P.sb([128, NCH, 4], F32, "G"); Gb = Buf("G")
    NR = 4
    vr = [(P.sb([128, 512], BF16, "v"), Buf("v")) for _ in range(NR)]
    kr = [(P.sb([128, 256], BF16, "k"), Buf("k")) for _ in range(NR)]
    kpr = [(P.sb([128, 256], BF16, "kp"), Buf("kp")) for _ in range(NR)]
    spr = [(P.sb([128, 128], BF16, "sp"), Buf("sp")) for _ in range(2)]
    hr = [(P.sb([128, 512], F32, "h"), Buf("h")) for _ in range(2)]
    dsc = [(P.sb([128, 4], F32, "dsc"), Buf("dsc")) for _ in range(2)]
    ri = 0
    for pr in range(npairs):
        P.dma("sp", qsb[:, :, :], qT[pr].rearrange("(c p) t -> p c t", p=128), writes=[qb])
        P.dma("sp", ksb[:, :, :], kT[pr].rearrange("(c p) t -> p c t", p=128), writes=[kb])
        P.dma("sp", G[:, :, :], gg[pr].rearrange("(c p) f -> p c f", p=128), writes=[Gb], noncontig=True)
        chains = []
        for d in range(2):
            ig = P.sb([128, NCH], F32, "ig"); lf = P.sb([128, NCH], F32, "lf")
            E = P.sb([128, NCH], F32, "E"); R = P.sb([128, NCH], F32, "R"); GC = P.sb([128, NCH], F32, "GC")
            gb_ = Buf("gprep")
            c_i = pr * 4 + 2 * d
            P.op("dve", lambda h, ig=ig, d=d, c_i=c_i: h.tensor_scalar(out=ig[:, :], in0=G[:, :, 2 * d], scalar1=bgt[:, c_i:c_i + 1],
                                                                      scalar2=None, op0=ALU.add), reads=[Gb, bgb], writes=[gb_])
            P.op("act", lambda h, lf=lf, d=d, c_i=c_i: h.activation(out=lf[:, :], in_=G[:, :, 2 * d + 1], func=AF.Exp, scale=-1.0,
                                                                   bias=nbg[:, c_i + 1:c_i + 2]), reads=[Gb, nbgb], writes=[gb_])
            P.op("act", lambda h, lf=lf: h.activation(out=lf[:, :], in_=lf[:, :], func=AF.Ln, bias=1.0), reads=[gb_], writes=[gb_])
            P.op("dve", lambda h, lf=lf: h.tensor_scalar(out=lf[:, :], in0=lf[:, :], scalar1=-1.0, scalar2=None, op0=ALU.mult),
                 reads=[gb_], writes=[gb_])
            t, tb = tri[d]
            P.op("pe", lambda h, t=t, lf=lf: h.matmul(pst[:, 0, 0:NCH], lhsT=t[:, :], rhs=lf[:, :], start=True, stop=True),
                 reads=[tb, gb_], writes=[pbufs[0]])
            P.op("pe", lambda h, lf=lf: h.matmul(pst[:, 1, 0:NCH], lhsT=ones[:, :], rhs=lf[:, :], start=True, stop=True),
                 reads=[bon, gb_], writes=[pbufs[1]])
            P.op("act", lambda h, R=R: h.activation(out=R[:, :], in_=pst[:, 0, 0:NCH], func=AF.Exp), reads=[pbufs[0]], writes=[gb_])
            P.op("act", lambda h, GC=GC: h.activation(out=GC[:, :], in_=pst[:, 1, 0:NCH], func=AF.Exp), reads=[pbufs[1]], writes=[gb_])
            P.op("dve", lambda h, ig=ig: h.tensor_tensor(out=ig[:, :], in0=ig[:, :], in1=pst[:, 0, 0:NCH], op=ALU.subtract),
                 reads=[pbufs[0], gb_], writes=[gb_])
            P.op("act", lambda h, E=E, ig=ig: h.activation(out=E[:, :], in_=ig[:, :], func=AF.Exp, bias=-math.log(16.0)),
                 reads=[gb_], writes=[gb_])
            C = P.sb([128, 2, 512], F32, "C"); Cb = P.sb([128, 2, 512], BF16, "Cb")
            n_ = P.sb([128, 2], F32, "n"); nb_ = P.sb([128, 2], BF16, "nb")
            cb_ = Buf("C"); cbb = Buf("Cb")
            P.op("pool", lambda h, C=C: h.memset(C[:, :, :], 0.0), writes=[cb_])
            P.op("pool", lambda h, n_=n_: h.memset(n_[:, :], 0.0), writes=[cb_])
            P.op("pool", lambda h, Cb=Cb: h.memset(Cb[:, :, :], 0.0), writes=[cbb])
            P.op("pool", lambda h, nb_=nb_: h.memset(nb_[:, :], 0.0), writes=[cbb])
            order = list(range(NCTXC)) + list(range(NCTXC, NCH)) if d == 0 else \
                list(range(NCTXC - 1, -1, -1)) + list(range(NCH - 1, NCTXC - 1, -1))
            chains.append(dict(d=d, E=E, R=R, GC=GC, gb=gb_, C=C, Cb=Cb, n=n_, nb=nb_, cb=cb_, cbb=cbb, order=order,
                               out=(hf if d == 0 else hb)))
        for step in range(NCH):
            for ch in chains:
                c = ch["order"][step]
                d = ch["d"]
                base = d * 4
                SB, NB_, UB0, UB1 = base, base + 1, base + 2, base + 3
                E, R, GC, gb_ = ch["E"], ch["R"], ch["GC"], ch["gb"]
                C, Cb, n_, nb_, cb_, cbb = ch["C"], ch["Cb"], ch["n"], ch["nb"], ch["cb"], ch["cbb"]
                cols = slice(c * 128, (c + 1) * 128)
                v_, vb = vr[ri % NR]
                k_, kb_ = kr[ri % NR]
                kp, kpb = kpr[ri % NR]
                s_, sb_ = spr[ri % 2]
                h_, hb_ = hr[ri % 2]
                ds, dsb = dsc[ri % 2]
                ri += 1
                P.dma("sp", v_[:, :], vv[pr, c * 128:(c + 1) * 128, :], writes=[vb])
                P.dma("sp", k_[:, :], kk[pr, c * 128:(c + 1) * 128, :], writes=[kb_])
                if c >= NCTXC:
                    for dc in range(2):
                        P.op("pe", lambda h, dc=dc, cols=cols, SB=SB: h.matmul(
                            pst[:, SB, 0:128], lhsT=ksb[:, dc, cols], rhs=qsb[:, dc, cols], start=(dc == 0), stop=(dc == 1)),
                            reads=[kb, qb], writes=[pbufs[SB]], sig=(dc == 1), pe_acc=True)
                    t, tb = tri[d]
                    P.op("dve", lambda h, s_=s_, SB=SB, E=E, c=c, t=t: h.scalar_tensor_tensor(
                        out=s_[:, :], in0=pst[:, SB, 0:128], scalar=E[:, c:c + 1], in1=t[:, :], op0=ALU.mult, op1=ALU.mult),
                        reads=[pbufs[SB], gb_, tb], writes=[sb_])
                    P.op("pe", lambda h, s_=s_, v_=v_, NB_=NB_: h.matmul(pst[:, NB_, :], lhsT=s_[:, :], rhs=v_[:, :], start=True, stop=False),
                         reads=[sb_, vb], writes=[pbufs[NB_]], sig=False, pe_acc=True)
                    for dc in range(2):
                        P.op("pe", lambda h, dc=dc, cols=cols, Cb=Cb, NB_=NB_: h.matmul(
                            pst[:, NB_, :], lhsT=qsb[:, dc, cols], rhs=Cb[:, dc, :], start=False, stop=(dc == 1)),
                            reads=[qb, cbb], writes=[pbufs[NB_]], sig=(dc == 1), pe_acc=True)
                    P.op("pe", lambda h, s_=s_, SB=SB: h.matmul(pst[:, SB, 256:257], lhsT=s_[:, :], rhs=onesb[:, 0:1], start=True, stop=False),
                         reads=[sb_, bonb], writes=[pbufs[SB]], sig=False, pe_acc=True)
                    for dc in range(2):
                        P.op("pe", lambda h, dc=dc, cols=cols, nb_=nb_, SB=SB: h.matmul(
                            pst[:, SB, 256:257], lhsT=qsb[:, dc, cols], rhs=nb_[:, dc:dc + 1], start=False, stop=(dc == 1)),
                            reads=[qb, cbb], writes=[pbufs[SB]], sig=(dc == 1), pe_acc=True)
                    P.op("dve", lambda h, ds=ds, SB=SB, R=R, c=c: h.tensor_tensor(out=ds[:, 0:1], in0=pst[:, SB, 256:257], in1=R[:, c:c + 1], op=ALU.mult),
                         reads=[pbufs[SB], gb_], writes=[dsb])
                    P.op("act", lambda h, ds=ds: h.activation(out=ds[:, 1:2], in_=ds[:, 0:1], func=AF.Abs), reads=[dsb], writes=[dsb])
                    P.op("dve", lambda h, ds=ds: h.tensor_single_scalar(out=ds[:, 2:3], in_=ds[:, 1:2], scalar=1.0, op=ALU.max), reads=[dsb], writes=[dsb])
                    P.op("dve", lambda h, ds=ds: h.reciprocal(out=ds[:, 1:2], in_=ds[:, 2:3]), reads=[dsb], writes=[dsb])
                    P.op("dve", lambda h, ds=ds, R=R, c=c: h.tensor_tensor(out=ds[:, 3:4], in0=ds[:, 1:2], in1=R[:, c:c + 1], op=ALU.mult),
                         reads=[dsb, gb_], writes=[dsb])
                    P.op("act", lambda h, h_=h_, ds=ds, NB_=NB_: h.activation(out=h_[:, :], in_=pst[:, NB_, :], func=AF.Copy, scale=ds[:, 3:4]),
                         reads=[pbufs[NB_], dsb], writes=[hb_])
                    P.dma("sp", ch["out"][pr, (c - NCTXC) * 128:(c - NCTXC + 1) * 128, :], h_[:, :], reads=[hb_])
                P.op("dve", lambda h, kp=kp, k_=k_, E=E, c=c: h.tensor_scalar(out=kp[:, :], in0=k_[:, :], scalar1=E[:, c:c + 1], scalar2=None, op0=ALU.mult),
                     reads=[kb_, gb_], writes=[kpb])
                for dc, UB in enumerate((UB0, UB1)):
                    P.op("pe", lambda h, dc=dc, UB=UB, kp=kp, v_=v_: h.matmul(pst[:, UB, :], lhsT=kp[:, dc * 128:(dc + 1) * 128], rhs=v_[:, :], start=True, stop=True),
                         reads=[kpb, vb], writes=[pbufs[UB]], sig=False, pe_acc=True)
                for dc in range(2):
                    P.op("pe", lambda h, dc=dc, kp=kp, SB=SB: h.matmul(pst[:, SB, 300 + dc:301 + dc], lhsT=kp[:, dc * 128:(dc + 1) * 128], rhs=onesb[:, 0:1], start=True, stop=True),
                         reads=[kpb, bonb], writes=[pbufs[SB]], sig=(dc == 1), pe_acc=True)
                P.op("dve", lambda h, C=C, UB0=UB0: h.tensor_tensor(out=C[:, :, :], in0=C[:, :, :], in1=pst[:, UB0:UB0 + 2, :], op=ALU.add),
                     reads=[pbufs[UB0], pbufs[UB1], cbb], writes=[cb_])
                P.op("dve", lambda h, n_=n_, SB=SB: h.tensor_tensor(out=n_[:, :], in0=n_[:, :], in1=pst[:, SB, 300:302], op=ALU.add),
                     reads=[pbufs[SB], cbb], writes=[cb_])
                P.op("dve", lambda h, C=C, GC=GC, c=c: h.tensor_scalar(out=C[:, :, :], in0=C[:, :, :], scalar1=GC[:, c:c + 1], scalar2=None, op0=ALU.mult),
                     reads=[gb_], writes=[cb_])
                P.op("dve", lambda h, n_=n_, GC=GC, c=c: h.tensor_scalar(out=n_[:, :], in0=n_[:, :], scalar1=GC[:, c:c + 1], scalar2=None, op0=ALU.mult),
                     reads=[gb_], writes=[cb_])
                P.op("act", lambda h, C=C, Cb=Cb: h.activation(out=Cb[:, :, :], in_=C[:, :, :], func=AF.Copy), reads=[cb_], writes=[cbb])
                P.op("act", lambda h, n_=n_, nb_=nb_: h.activation(out=nb_[:, :], in_=n_[:, :], func=AF.Copy), reads=[cb_], writes=[cbb])
    P.run()


def phase_mpost(nc, hf, hb, hn_g, oT, hoT, NG=16):
    P = Phase(nc, "mpost")
    ident, bid = make_ident(P)
    pst = P.ps([128, 8, 512], F32, "ps")
    pbufs = [Buf(f"ps{i}") for i in range(8)]
    hg, hgb = load_cols(P, [hn_g], ident, bid, pst[:, 0, :], pbufs[0], "hg")
    xt = [(P.sb([128, D], F32, "xt"), Buf("xt")) for _ in range(2)]
    yt = [(P.sb([128, D], F32, "yt"), Buf("yt")) for _ in range(2)]
    sq = P.sb([128, 512], BF16, "sq"); bsq = Buf("sq")
    st = [(P.sb([128, 24], F32, "st"), Buf("st")) for _ in range(2)]
    ot = [(P.sb([128, KC, 128], BF16, "ot"), Buf("ot")) for _ in range(2)]
    hbk = [(P.sb([128, KC, 512], BF16, "hb"), Buf("hb")) for _ in range(1)]
    oTv = oT.rearrange("(c p) t -> p c t", p=128)
    hoTv = hoT.rearrange("(c p) t -> p c t", p=128)
    for g in range(NG):
        x_, xb = xt[g % 2]
        y_, yb = yt[g % 2]
        s_, sb_ = st[g % 2]
        o_, ob = ot[g % 2]
        P.dma("sp", x_[:, :], hf[g * 128:(g + 1) * 128, :], writes=[xb])
        P.dma("sp", y_[:, :], hb[g * 128:(g + 1) * 128, :], writes=[yb])
        P.dma("sp", o_[:, :, :], oTv[:, :, g * 128:(g + 1) * 128], writes=[ob])
        P.op("pool", lambda h, x_=x_, y_=y_: h.tensor_tensor(out=x_[:, :], in0=x_[:, :], in1=y_[:, :], op=ALU.add),
             reads=[xb, yb], writes=[xb])
        for hd in range(8):
            P.op("act", lambda h, x_=x_, s_=s_, hd=hd: h.activation(out=sq[:, :], in_=x_[:, hd * 512:(hd + 1) * 512], func=AF.Square,
                                                                    accum_out=s_[:, hd:hd + 1]), reads=[xb], writes=[bsq, sb_])
        rstd_ops(P, s_, sb_, 0, 8, 16, 1.0 / 512, n=8)
        for hd in range(8):
            P.op("act", lambda h, x_=x_, y_=y_, s_=s_, hd=hd: h.activation(
                out=y_[:, hd * 512:(hd + 1) * 512], in_=x_[:, hd * 512:(hd + 1) * 512], func=AF.Copy, scale=s_[:, 16 + hd:17 + hd]),
                reads=[xb, sb_], writes=[yb])
        h_, hbb = hbk[0]
        tcol = (g % 4) * 128
        for c in range(KC):
            bk = c // 4
            j = c % 4
            P.op("pe", lambda h, bk=bk, j=j, c=c, y_=y_: h.transpose(
                out=pst[:, bk, j * 128:(j + 1) * 128], in_=y_[:, c * 128:(c + 1) * 128], identity=ident[:, :]),
                reads=[yb, bid], writes=[pbufs[bk]], sig=(j == 3), pe_acc=True)
            if j == 3:
                for jj in range(4):
                    cc = bk * 4 + jj
                    P.op("dve", lambda h, bk=bk, jj=jj, cc=cc, h_=h_, o_=o_, tcol=tcol: h.scalar_tensor_tensor(
                        out=h_[:, cc, tcol:tcol + 128], in0=pst[:, bk, jj * 128:(jj + 1) * 128], scalar=hg[:, cc:cc + 1],
                        in1=o_[:, cc, :], op0=ALU.mult, op1=ALU.mult), reads=[pbufs[bk], hgb, ob], writes=[hbb])
        if g % 4 == 3:
            t0 = (g // 4) * 512
            P.dma("sp", hoTv[:, :, t0:t0 + 512], h_[:, :, :], reads=[hbb])
    P.run()


PASS_F_TE = [(0, 832, [(0, 512), (512, 320)]), (832, 832, [(0, 512), (512, 320)]), (1664, 768, [(0, 384), (384, 384)])]
PASS_F_TM = [(0, 768, [(0, 384), (384, 384)]), (768, 768, [(0, 384), (384, 384)]), (1536, 768, [(0, 512), (512, 256)])]
PASS_T_TM = [(0, 512), (512, 512), (1024, 512), (1536, 512), (2048, 256)]
PASS_F_X = [(0, 1024), (1024, 1024)]
PASS_T_X = [(0, 512), (512, 512), (1024, 512), (1536, 512)]
FFN_GROUPS = [[i * 128, DFF + i * 128] for i in range(DFF // 128)]
NBLK_D = [(n * 512, 512, 0, n * 512) for n in range(8)]


def _dt(nc, name, shape, dtype=F32, kind="Internal"):
    return nc.dram_tensor(name, list(shape), dtype, kind=kind).ap()


def build_l1(upto=99):
    nc = bass.Bass("TRN2", target_bir_lowering=False)
    I = lambda n, s, d=F32: _dt(nc, n, s, d, "ExternalInput")
    O = lambda n, s, d=F32: _dt(nc, n, s, d, "ExternalOutput")
    T = lambda n, s, d=F32: _dt(nc, n, s, d, "Internal")
    csel = I("csel", [2, D]); ada_w = I("ada_w", [2, D, 6 * D]); ada_b = I("ada_b", [2, 6 * D])
    xext = I("xext", [TE, D]); hmask = I("hmask", [128, 2]); norm_g = I("norm_g", [2, 4, D])
    w_in = I("ab_w_in", [D, ABIN]); acw = I("a_conv_w", [31, WA]); acb = I("a_conv_b", [WA])
    alg = I("a_ln_g", [WA]); alb = I("a_ln_b", [WA]); bcw = I("b_conv_w", [3, WA]); w_out = I("ab_w_out", [D, D])
    w_up = I("ffn_w_up", [D, 2 * DFF]); fcw = I("ffn_conv_w", [3, 2 * DFF]); w_dn = I("ffn_w_down", [DFF, D])
    m_in = I("m_w_in", [D, CIN])
    mods = O("mods", [2, 2, 6 * D]); x2 = O("x2", [TM, D])
    qT = O("qT", [2048, TM], BF16); kT = O("kT", [2048, TM], BF16); oT = O("oT", [D, TM], BF16)
    vk = O("vk", [TM, 6144], BF16); gates = O("gates", [TM, 32])
    hT1 = T("hT1", [D, TE], BF16); PT = T("PT", [ABIN, TE]); catT = T("catT", [D, TM], BF16)
    y1 = T("y1", [TM, D]); x1 = T("x1", [TM, D]); h2T = T("h2T", [D, TM], BF16)
    actT = T("actT", [DFF, TM], BF16); y2 = T("y2", [TM, D]); h3T = T("h3T", [D, TM], BF16)
    m = lambda l, r, i: mods[l, r, i * D:(i + 1) * D]
    phase_ada(nc, csel, ada_w, ada_b, mods)
    if upto < 1:
        return nc
    phase_prenorm(nc, "pn1", xext, [0] * 17 + [1] * 2,
                  ab_rows={v: (norm_g[0, 0], m(0, v, 1), m(0, v, 0)) for v in (0, 1)}, hT_out=hT1)
    if upto < 2:
        return nc
    hook, hinit = make_store_hook(lambda gi: PT[gi * 128:(gi + 1) * 128, :])
    phase_gemm_f(nc, "g1", hT1, w_in, KC, PASS_F_TE, [[i * 128] for i in range(ABIN // 128)], hook, hinit)
    if upto < 3:
        return nc
    phase_abmix_a(nc, PT, acw, acb, alg, alb, catT)
    if upto < 4:
        return nc
    phase_abmix_b(nc, PT, bcw, hmask, catT)
    if upto < 5:
        return nc
    phase_gemm_t(nc, "g2", catT, w_out, KC, PASS_T_TM, NBLK_D, [y1])
    if upto < 6:
        return nc
    grp = [0] * 16 + [1] * 2
    xrows = [64 + g * 128 for g in range(16)] + [64 + NX + 64, 64 + NX + 64 + 128]
    phase_prenorm(nc, "pn2", xext, grp, y_in=y1, gg_rows={v: (m(0, v, 2), norm_g[0, 1]) for v in (0, 1)}, x_out=x1,
                  ab_rows={v: (norm_g[0, 2], m(0, v, 4), m(0, v, 3)) for v in (0, 1)}, hT_out=h2T, xin_rows=xrows)
    if upto < 7:
        return nc
    hook, hinit = make_ffn_hook(fcw, actT, lambda pi, c0: 256 if (pi == 2 and c0 >= 512) else 64)
    phase_gemm_f(nc, "g3", h2T, w_up, KC, PASS_F_TM, FFN_GROUPS, hook, hinit)
    if upto < 8:
        return nc
    phase_gemm_t(nc, "g4", actT, w_dn, DFF // 128, PASS_T_TM, NBLK_D, [y2])
    if upto < 9:
        return nc
    phase_prenorm(nc, "pn3", x1, grp, y_in=y2, gg_rows={v: (m(0, v, 5), norm_g[0, 3]) for v in (0, 1)}, x_out=x2,
                  ab_rows={v: (norm_g[1, 0], m(1, v, 1), m(1, v, 0)) for v in (0, 1)}, hT_out=h3T)

    if upto < 10:
        return nc

    def dst(gi):
        if gi < 16:
            return qT[gi * 128:(gi + 1) * 128, :]
        if gi < 32:
            return kT[(gi - 16) * 128:(gi - 15) * 128, :]
        return oT[(gi - 32) * 128:(gi - 31) * 128, :]
    hook, hinit = make_store_hook(dst, func=lambda gi: (AF.Sigmoid if gi >= 32 else AF.Copy), dtype=BF16)
    groups = [[i * 128] for i in range(32)] + [[8192 + i * 128] for i in range(32)]
    phase_gemm_f(nc, "g5f", h3T, m_in, KC, PASS_F_TM, groups, hook, hinit)
    nb = [(4096 + n * 512, 512, 0, n * 512) for n in range(8)] + [(2048 + n * 512, 512, 0, 4096 + n * 512) for n in range(4)]
    phase_gemm_t(nc, "g5t", h3T, m_in, KC, PASS_T_TM, nb, [vk])
    phase_gemm_t(nc, "g5g", h3T, m_in, KC, PASS_T_TM, [(12288, 32, 0, 0)], [gates])
    return nc


def build_l2():
    nc = bass.Bass("TRN2", target_bir_lowering=False)
    I = lambda n, s, d=F32: _dt(nc, n, s, d, "ExternalInput")
    O = lambda n, s, d=F32: _dt(nc, n, s, d, "ExternalOutput")
    TT = 66 * 128
    qT = I("qT", [2, 256, TT], BF16); kT = I("kT", [2, 256, TT], BF16); kk = I("kk", [2, TT, 256], BF16)
    vv = I("vv", [2, TT, 512], BF16); gg = I("gg", [2, TT, 4]); bg = I("bg", [128, 8])
    hf = O("hf", [2, 8192, 512]); hb = O("hb", [2, 8192, 512])
    phase_scan(nc, qT, kT, kk, vv, gg, bg, hf, hb)
    return nc


def build_l3():
    nc = bass.Bass("TRN2", target_bir_lowering=False)
    I = lambda n, s, d=F32: _dt(nc, n, s, d, "ExternalInput")
    O = lambda n, s, d=F32: _dt(nc, n, s, d, "ExternalOutput")
    T = lambda n, s, d=F32: _dt(nc, n, s, d, "Internal")
    hf = I("hf", [NX, D]); hb = I("hb", [NX, D]); oT = I("oT", [D, NX], BF16); hn_g = I("hn_g", [D])
    m_out = I("m_w_out", [D, D]); x2 = I("x2", [NX, D]); mods = I("mods", [2, 2, 6 * D]); norm_g = I("norm_g", [2, 4, D])
    w_up = I("ffn_w_up", [D, 2 * DFF]); fcw = I("ffn_conv_w", [3, 2 * DFF]); w_dn = I("ffn_w_down", [DFF, D])
    out = O("out", [NX, D])
    hoT = T("hoT", [D, NX], BF16); y3 = T("y3", [NX, D]); x3 = T("x3", [NX, D]); h4T = T("h4T", [D, NX], BF16)
    actT = T("actT", [DFF, NX], BF16); y4 = T("y4", [NX, D])
    m = lambda l, r, i: mods[l, r, i * D:(i + 1) * D]
    phase_mpost(nc, hf, hb, hn_g, oT, hoT)
    phase_gemm_t(nc, "g6", hoT, m_out, KC, PASS_T_X, NBLK_D, [y3])
    grp = [0] * 16
    phase_prenorm(nc, "pn4", x2, grp, y_in=y3, gg_rows={0: (m(1, 0, 2), norm_g[1, 1])}, x_out=x3,
                  ab_rows={0: (norm_g[1, 2], m(1, 0, 4), m(1, 0, 3))}, hT_out=h4T)
    hook, hinit = make_ffn_hook(fcw, actT, lambda pi, c0: 64)
    phase_gemm_f(nc, "g7", h4T, w_up, KC, PASS_F_X, FFN_GROUPS, hook, hinit)
    phase_gemm_t(nc, "g8", actT, w_dn, DFF // 128, PASS_T_X, NBLK_D, [y4])
    phase_prenorm(nc, "fin", x3, grp, y_in=y4, gg_rows={0: (m(1, 0, 5), norm_g[1, 3])}, x_out=out)
    return nc


def phase_mods_gather(nc, msend, mrecv, mods):
    P = Phase(nc, "mg")
    rec = P.pool.get("cc")
    sem = rec[0]
    rec[1] += 1
    val = rec[1]
    P.q["pool"].append(lambda h: h.collective_compute(
        "AllGather", ALU.bypass, replica_groups=[[0, 1, 2, 3], [4, 5, 6, 7]],
        ins=[msend.rearrange("l r n -> (l r) n")], outs=[mrecv[:, :]]).then_inc(sem, 1))
    P.q["pool"].append(lambda h: h.wait_ge(sem, val))
    P.pool.put(rec)
    P.run()
    P = Phase(nc, "mg2")
    rv = mrecv.rearrange("(j q) (i k) -> q i j k", q=4, k=1024)
    for l in range(2):
        for r in range(2):
            b = Buf("mcopy")
            P.dma("sp", mods[l, r].rearrange("(i j k) -> i j k", j=4, k=1024), rv[l * 2 + r], writes=[b])
    P.run()


def build_all():
    nc = bass.Bass("TRN2", target_bir_lowering=False)
    I = lambda n, s, d=F32: _dt(nc, n, s, d, "ExternalInput")
    O = lambda n, s, d=F32: _dt(nc, n, s, d, "ExternalOutput")
    T = lambda n, s, d=F32: _dt(nc, n, s, d, "Internal")
    csel = I("csel", [2, D]); ada_w = I("ada_w", [2, D, 6 * D // 4]); ada_b = I("ada_b", [2, 6 * D // 4])
    xext = I("xext", [TE, D]); hmask = I("hmask", [128, 2]); norm_g = I("norm_g", [2, 4, D])
    w_in = I("ab_w_in", [D, ABIN]); acw = I("a_conv_w", [31, WA]); acb = I("a_conv_b", [WA])
    alg = I("a_ln_g", [WA]); alb = I("a_ln_b", [WA]); bcw = I("b_conv_w", [3, WA]); w_out = I("ab_w_out", [D, D])
    w_up = I("ffn_w_up", [2, D, 2 * DFF]); fcw = I("ffn_conv_w", [2, 3, 2 * DFF]); w_dn = I("ffn_w_down", [2, DFF, D])
    m_in = I("m_w_in", [D, CIN]); bgx = I("bgx", [128, NCHT * 32]); flags = I("flags", [128, 8])
    hn_g = I("hn_g", [D]); m_out = I("m_w_out", [D, D])
    out = O("out", [NX, D])
    mods = T("mods", [2, 2, 6 * D]); x2 = T("x2", [TM, D])
    qT = T("qT", [2048, TM], BF16); kT = T("kT", [2048, TM], BF16); oT = T("oT", [D, TM], BF16)
    vk = T("vk", [TM, 6144], BF16); gates = T("gates", [TM, 32])
    hT1 = T("hT1", [D, TE], BF16); PT = T("PT", [ABIN, TE]); catT = T("catT", [D, TM], BF16)
    y1 = T("y1", [TM, D]); x1 = T("x1", [TM, D]); h2T = T("h2T", [D, TM], BF16)
    actT = T("actT", [DFF, TM], BF16); y2 = T("y2", [TM, D]); h3T = T("h3T", [D, TM], BF16)
    prep = T("prep", [2, 3, 128, NCHT * 8]); sctx = T("sctx", [16, 128, SROW]); send = T("send", [16, 128, SROW])
    recv = T("recv", [16, 4 * 128, SROW]); hf = T("hf", [NX, D]); hb = T("hb", [NX, D])
    hoT = T("hoT", [D, NX], BF16); y3 = T("y3", [NX, D]); x3 = T("x3", [NX, D]); h4T = T("h4T", [D, NX], BF16)
    actT1 = T("actT1", [DFF, NX], BF16); y4 = T("y4", [NX, D])
    m = lambda l, r, i: mods[l, r, i * D:(i + 1) * D]
    msend = T("msend", [2, 2, 6 * D // 4]); mrecv = T("mrecv", [16, 6 * D // 4])
    phase_ada(nc, csel, ada_w, ada_b, msend, NM=6 * D // 4)
    phase_mods_gather(nc, msend, mrecv, mods)
    phase_prenorm(nc, "pn1", xext, [0] * 17 + [1] * 2,
                  ab_rows={v: (norm_g[0, 0], m(0, v, 1), m(0, v, 0)) for v in (0, 1)}, hT_out=hT1)
    hook, hinit = make_store_hook(lambda gi: PT[gi * 128:(gi + 1) * 128, :])
    phase_gemm_f(nc, "g1", hT1, w_in, KC, PASS_F_TE, [[i * 128] for i in range(ABIN // 128)], hook, hinit)
    phase_abmix_a(nc, PT, acw, acb, alg, alb, catT)
    phase_abmix_b(nc, PT, bcw, hmask, catT)
    phase_gemm_t(nc, "g2", catT, w_out, KC, PASS_T_TM, NBLK_D, [y1])
    grp = [0] * 16 + [1] * 2
    xrows = [64 + g * 128 for g in range(16)] + [64 + NX + 64, 64 + NX + 64 + 128]
    phase_prenorm(nc, "pn2", xext, grp, y_in=y1, gg_rows={v: (m(0, v, 2), norm_g[0, 1]) for v in (0, 1)}, x_out=x1,
                  ab_rows={v: (norm_g[0, 2], m(0, v, 4), m(0, v, 3)) for v in (0, 1)}, hT_out=h2T, xin_rows=xrows)
    hook, hinit = make_ffn_hook(fcw[0], actT, lambda pi, c0: 256 if (pi == 2 and c0 >= 512) else 64)
    phase_gemm_f(nc, "g3", h2T, w_up[0], KC, PASS_F_TM, FFN_GROUPS, hook, hinit)
    phase_gemm_t(nc, "g4", actT, w_dn[0], DFF // 128, PASS_T_TM, NBLK_D, [y2])
    phase_prenorm(nc, "pn3", x1, grp, y_in=y2, gg_rows={v: (m(0, v, 5), norm_g[0, 3]) for v in (0, 1)}, x_out=x2,
                  ab_rows={v: (norm_g[1, 0], m(1, v, 1), m(1, v, 0)) for v in (0, 1)}, hT_out=h3T)

    def dst(gi):
        if gi < 16:
            return qT[gi * 128:(gi + 1) * 128, :]
        if gi < 32:
            return kT[(gi - 16) * 128:(gi - 15) * 128, :]
        return oT[(gi - 32) * 128:(gi - 31) * 128, :]
    hook, hinit = make_store_hook(dst, func=lambda gi: (AF.Sigmoid if gi >= 32 else AF.Copy), dtype=BF16)
    groups = [[i * 128] for i in range(32)] + [[8192 + i * 128] for i in range(32)]
    phase_gemm_f(nc, "g5f", h3T, m_in, KC, PASS_F_TM, groups, hook, hinit)
    nb = [(4096 + n * 512, 512, 0, n * 512) for n in range(8)] + [(2048 + n * 512, 512, 0, 4096 + n * 512) for n in range(4)]
    phase_gemm_t(nc, "g5t", h3T, m_in, KC, PASS_T_TM, nb, [vk])
    phase_gemm_t(nc, "g5g", h3T, m_in, KC, PASS_T_TM, [(12288, 32, 0, 0)], [gates])
    phase_scan1(nc, vk, gates, bgx, prep, sctx, send)
    phase_allgather(nc, send, recv)
    phase_scan2(nc, qT, kT, vk, prep, sctx, recv, flags, hf, hb)
    phase_mpost(nc, hf, hb, hn_g, oT[:, 0:NX], hoT)
    phase_gemm_t(nc, "g6", hoT, m_out, KC, PASS_T_X, NBLK_D, [y3])
    grp1 = [0] * 16
    phase_prenorm(nc, "pn4", x2[0:NX, :], grp1, y_in=y3, gg_rows={0: (m(1, 0, 2), norm_g[1, 1])}, x_out=x3,
                  ab_rows={0: (norm_g[1, 2], m(1, 0, 4), m(1, 0, 3))}, hT_out=h4T)
    hook, hinit = make_ffn_hook(fcw[1], actT1, lambda pi, c0: 64)
    phase_gemm_f(nc, "g7", h4T, w_up[1], KC, PASS_F_X, FFN_GROUPS, hook, hinit)
    phase_gemm_t(nc, "g8", actT1, w_dn[1], DFF // 128, PASS_T_X, NBLK_D, [y4])
    phase_prenorm(nc, "fin", x3, grp1, y_in=y4, gg_rows={0: (m(1, 0, 5), norm_g[1, 3])}, x_out=out)
    return nc


def kernel(x, c, ctx, c_ctx, ada_w, ada_b, norm_g, ab_w_in, a_conv_w, a_conv_b, a_ln_g, a_ln_b,
           b_conv_w, ab_w_out, m_w_in, m_b_gates, m_hn_g, m_w_out, ffn_w_up, ffn_conv_w, ffn_w_down):
    f = lambda a: np.ascontiguousarray(np.asarray(a))
    x, c, ctx, c_ctx = f(x), f(c), f(ctx), f(c_ctx)
    cores = list(range(NCORES))
    mbg = f(m_b_gates)[0].astype(np.float32)
    bgx = np.ascontiguousarray(np.broadcast_to(np.tile(mbg, NCHT)[None, :], (128, NCHT * 32)))
    ada_w, ada_b = np.asarray(ada_w), np.asarray(ada_b)
    acols = [np.concatenate([np.arange(i * D + jj * 1024, i * D + (jj + 1) * 1024) for i in range(6)]) for jj in range(4)]
    ada_ws = [f(ada_w[:, :, cc]) for cc in acols]
    ada_bs = [f(ada_b[:, cc]) for cc in acols]
    shared = dict(norm_g=f(norm_g), ab_w_in=f(ab_w_in[0]), a_conv_w=f(a_conv_w[0]),
                  a_conv_b=f(a_conv_b[0]), a_ln_g=f(a_ln_g[0]), a_ln_b=f(a_ln_b[0]), b_conv_w=f(b_conv_w[0]),
                  ab_w_out=f(ab_w_out[0]), ffn_w_up=f(ffn_w_up), ffn_conv_w=f(ffn_conv_w), ffn_w_down=f(ffn_w_down),
                  m_w_in=f(m_w_in[0]), bgx=bgx, hn_g=f(m_hn_g[0]), m_w_out=f(m_w_out[0]))
    ins = []
    for r in cores:
        b, j = r // 4, r % 4
        xe = np.zeros((TE, D), np.float32)
        t0 = j * NX
        if j > 0:
            xe[0:64] = x[b, t0 - 64:t0]
        xe[64:64 + NX] = x[b, t0:t0 + NX]
        if j < 3:
            xe[64 + NX:128 + NX] = x[b, t0 + NX:t0 + NX + 64]
        xe[128 + NX:] = ctx[b]
        hm = np.zeros((128, 2), np.float32)
        hm[:, 0] = 1.0 if j > 0 else 0.0
        hm[:, 1] = 1.0 if j < 3 else 0.0
        fl = np.zeros((128, 8), np.float32)
        for i in range(4):
            fl[:, i] = 1.0 if i < j else 0.0
            fl[:, 4 + i] = 1.0 if i > j else 0.0
        d = dict(shared)
        d.update(csel=np.stack([c[b], c_ctx]), xext=xe, hmask=hm, flags=fl, ada_w=ada_ws[j], ada_b=ada_bs[j])
        ins.append(d)
    res = run_bass_kernel_spmd(build_all(), ins, core_ids=cores).results
    out = np.zeros((2, 8192, D), np.float32)
    for r in cores:
        b, j = r // 4, r % 4
        out[b, j * NX:(j + 1) * NX] = res[r]["out"]
    return out


NCHT = TM // 128
SROW = 1032


def _tri_consts(P):
    ones = P.sb([128, 128], F32, "ones"); bon = Buf("ones")
    P.op("pool", lambda h: h.memset(ones[:, :], 1.0), writes=[bon])
    onesb = P.sb([128, 2], BF16, "onesb"); bonb = Buf("onesb")
    P.op("pool", lambda h: h.memset(onesb[:, :], 1.0), writes=[bonb])
    tri = []
    for d in range(2):
        t = P.sb([128, 128], F32, f"tri{d}"); tb = Buf(f"tri{d}")
        P.op("pool", lambda h, t=t: h.memset(t[:, :], 1.0), writes=[tb])
        sgn = 1 if d == 0 else -1
        P.op("pool", lambda h, t=t, sgn=sgn: h.affine_select(out=t[:, :], in_=t[:, :], pattern=[[sgn, 128]],
                                                            compare_op=ALU.is_ge, fill=0.0, base=0, channel_multiplier=-sgn),
             reads=[tb], writes=[tb])
        tri.append((t, tb))
    return ones, bon, onesb, bonb, tri


def _state_update(P, pst, pbufs, UB0, SBK, kp, kpb, k_ap, kb_, v_, vb, e_ap, gc_ap, gb_, C, n_, cb_, extra_reads, onesb, bonb):
    P.op("dve", lambda h: h.tensor_scalar(out=kp[:, :], in0=k_ap, scalar1=e_ap, scalar2=None, op0=ALU.mult),
         reads=[kb_, gb_], writes=[kpb])
    for dc in range(2):
        P.op("pe", lambda h, dc=dc: h.matmul(pst[:, UB0 + dc, :], lhsT=kp[:, dc * 128:(dc + 1) * 128], rhs=v_, start=True, stop=True),
             reads=[kpb, vb], writes=[pbufs[UB0 + dc]], sig=False, pe_acc=True)
    for dc in range(2):
        P.op("pe", lambda h, dc=dc: h.matmul(pst[:, SBK, 300 + dc:301 + dc], lhsT=kp[:, dc * 128:(dc + 1) * 128], rhs=onesb[:, 0:1], start=True, stop=True),
             reads=[kpb, bonb], writes=[pbufs[SBK]], sig=(dc == 1), pe_acc=True)
    P.op("dve", lambda h: h.tensor_tensor(out=C, in0=C, in1=pst[:, UB0:UB0 + 2, :], op=ALU.add),
         reads=[pbufs[UB0], pbufs[UB0 + 1]] + extra_reads, writes=[cb_])
    P.op("dve", lambda h: h.tensor_tensor(out=n_, in0=n_, in1=pst[:, SBK, 300:302], op=ALU.add),
         reads=[pbufs[SBK]] + extra_reads, writes=[cb_])
    P.op("dve", lambda h: h.tensor_scalar(out=C, in0=C, scalar1=gc_ap, scalar2=None, op0=ALU.mult), reads=[gb_], writes=[cb_])
    P.op("dve", lambda h: h.tensor_scalar(out=n_, in0=n_, scalar1=gc_ap, scalar2=None, op0=ALU.mult), reads=[gb_], writes=[cb_])


def phase_scan1(nc, vk, gates, bgx, prep, sctx, send):
    import math
    P = Phase(nc, "scan1")
    pst = P.ps([128, 8, 512], F32, "ps")
    pbufs = [Buf(f"ps{i}") for i in range(8)]
    ones, bon, onesb, bonb, tri = _tri_consts(P)
    NC8 = NCHT * 8
    G = P.sb([128, NCHT, 32], F32, "G"); Gb = Buf("G")
    bgt = P.sb([128, NCHT, 32], F32, "bgx"); bgb = Buf("bgx")
    P.dma("sp", G[:, :, :], gates.rearrange("(c p) f -> p c f", p=128), writes=[Gb], noncontig=True)
    P.dma("sp", bgt[:, :, :], bgx.rearrange("p (c f) -> p c f", f=32), writes=[bgb])
    P.op("dve", lambda h: h.tensor_tensor(out=G[:, :, :], in0=G[:, :, :], in1=bgt[:, :, :], op=ALU.add), reads=[Gb, bgb], writes=[Gb])
    ERG = []
    for d in range(2):
        ig = P.sb([128, NCHT, 8], F32, "ig"); lf = P.sb([128, NCHT, 8], F32, "lf")
        E = P.sb([128, NCHT, 8], F32, "E"); R = P.sb([128, NCHT, 8], F32, "R"); GC = P.sb([128, NCHT, 8], F32, "GC")
        gx = P.sb([128, 8], F32, "gx")
        gb_ = Buf("gprep")
        P.op("act", lambda h, lf=lf, d=d: h.activation(out=lf[:, :, :], in_=G[:, :, d * 16 + 8:d * 16 + 16], func=AF.Exp, scale=-1.0),
             reads=[Gb], writes=[gb_])
        P.op("act", lambda h, lf=lf: h.activation(out=lf[:, :, :], in_=lf[:, :, :], func=AF.Ln, bias=1.0), reads=[gb_], writes=[gb_])
        P.op("dve", lambda h, lf=lf: h.tensor_scalar(out=lf[:, :, :], in0=lf[:, :, :], scalar1=-1.0, scalar2=None, op0=ALU.mult),
             reads=[gb_], writes=[gb_])
        t, tb = tri[d]
        lff = lf[:, :, :].rearrange("p c h -> p (c h)")
        P.op("pe", lambda h, t=t, lff=lff: h.matmul(pst[:, 0, 0:NC8], lhsT=t[:, :], rhs=lff, start=True, stop=True),
             reads=[tb, gb_], writes=[pbufs[0]])
        P.op("pe", lambda h, lff=lff: h.matmul(pst[:, 1, 0:NC8], lhsT=ones[:, :], rhs=lff, start=True, stop=True),
             reads=[bon, gb_], writes=[pbufs[1]])
        flat = lambda T_: T_[:, :, :].rearrange("p c h -> p (c h)")
        P.op("act", lambda h, R=R: h.activation(out=flat(R), in_=pst[:, 0, 0:NC8], func=AF.Exp), reads=[pbufs[0]], writes=[gb_])
        P.op("act", lambda h, GC=GC: h.activation(out=flat(GC), in_=pst[:, 1, 0:NC8], func=AF.Exp), reads=[pbufs[1]], writes=[gb_])
        P.op("dve", lambda h, ig=ig, d=d: h.tensor_tensor(out=ig[:, :, :], in0=G[:, :, d * 16:d * 16 + 8],
                                                        in1=pst[:, 0, 0:NC8].rearrange("p (c h) -> p c h", h=8), op=ALU.subtract),
             reads=[pbufs[0], Gb], writes=[gb_])
        P.op("act", lambda h, E=E, ig=ig: h.activation(out=E[:, :, :], in_=ig[:, :, :], func=AF.Exp, bias=-math.log(16.0)),
             reads=[gb_], writes=[gb_])
        P.op("dve", lambda h, gx=gx: h.reduce_sum(out=gx[:, :], in_=pst[:, 1, 0:16 * 8].rearrange("p (c h) -> p h c", h=8), axis=AX.X),
             reads=[pbufs[1]], writes=[gb_])
        P.op("act", lambda h, gx=gx: h.activation(out=gx[:, :], in_=gx[:, :], func=AF.Exp), reads=[gb_], writes=[gb_])
        for j, T_ in enumerate((E, R, GC)):
            P.dma("sp", prep[d, j], flat(T_), reads=[gb_])
        ERG.append((E, R, GC, gx, gb_))
    Call = P.sb([128, 16, 1026], F32, "Call"); cbs = [Buf(f"C{i}") for i in range(16)]
    rows = [(P.sb([128, 6144], BF16, "vkrow"), Buf("vkrow")) for _ in range(3)]
    kpr = [(P.sb([128, 256], BF16, "kp"), Buf("kp")) for _ in range(3)]
    ri = 0
    ki = 0
    ui = 0
    for which in ("ctx", "x"):
        for ch in range(16):
            P.op("pool", lambda h, ch=ch: h.memset(Call[:, ch, :], 0.0), writes=[cbs[ch]])
        orders = ([16, 17], [17, 16]) if which == "ctx" else (list(range(16)), list(range(15, -1, -1)))
        for step in range(len(orders[0])):
            for d in range(2):
                c = orders[d][step]
                E, R, GC, gx, gb_ = ERG[d]
                row, rb = rows[ri % 3]
                ri += 1
                P.dma("sp", row[:, :], vk[c * 128:(c + 1) * 128, :], writes=[rb])
                for hd in range(8):
                    ch = d * 8 + hd
                    kp, kpb = kpr[ki % 3]
                    ki += 1
                    UB0 = 2 + 2 * (ui % 3)
                    SBK = ui % 2
                    ui += 1
                    Cc = Call[:, ch, 0:1024].rearrange("p (a b) -> p a b", a=2)
                    nn = Call[:, ch, 1024:1026]
                    _state_update(P, pst, pbufs, UB0, SBK, kp, kpb, row[:, 4096 + hd * 256:4096 + (hd + 1) * 256], rb,
                                  row[:, hd * 512:(hd + 1) * 512], rb, E[:, c, hd:hd + 1], GC[:, c, hd:hd + 1], gb_,
                                  Cc, nn, cbs[ch], [], onesb, bonb)
        dst = sctx if which == "ctx" else send
        dv = dst.rearrange("ch p f -> p ch f")
        if which == "x":
            for d in range(2):
                gx, gb_ = ERG[d][3], ERG[d][4]
                P.dma("sp", dv[:, d * 8:(d + 1) * 8, 1026:1027], gx[:, :].rearrange("p (h o) -> p h o", o=1), reads=[gb_], noncontig=True)
        P.dma("sp", dv[:, :, 0:1026], Call[:, :, :], reads=cbs)
    P.run()


def phase_allgather(nc, send, recv):
    P = Phase(nc, "ag")
    rec = P.pool.get("cc")
    sem = rec[0]
    for ch in range(16):
        rec[1] += 1
        P.q["pool"].append(lambda h, ch=ch: h.collective_compute(
            "AllGather", ALU.bypass, replica_groups=[[0, 1, 2, 3], [4, 5, 6, 7]], ins=[send[ch]], outs=[recv[ch]]).then_inc(sem, 1))
    val = rec[1]
    P.q["pool"].append(lambda h: h.wait_ge(sem, val))
    P.pool.put(rec)
    P.run()


def phase_scan2(nc, qT, kT, vk, prep, sctx, recv, flags, hf, hb):
    P = Phase(nc, "scan2")
    pst = P.ps([128, 8, 512], F32, "ps")
    pbufs = [Buf(f"ps{i}") for i in range(8)]
    ones, bon, onesb, bonb, tri = _tri_consts(P)
    fl = P.sb([128, 8], F32, "flags"); flb = Buf("flags")
    P.dma("sp", fl[:, :], flags[:, :], writes=[flb])
    ERG = []
    for d in range(2):
        ts = []
        for j in range(3):
            T_ = P.sb([128, NCHT, 8], F32, "erg")
            bj = Buf("ergl")
            P.dma("sp", T_[:, :, :].rearrange("p c h -> p (c h)"), prep[d, j], writes=[bj])
            ts.append((T_, bj))
        ERG.append(ts)
    qsb = [(P.sb([128, 2, TM], BF16, "qT"), Buf("qT")) for _ in range(2)]
    ksb = [(P.sb([128, 2, TM], BF16, "kT"), Buf("kT")) for _ in range(2)]
    NR = 4
    vr = [(P.sb([128, 512], BF16, "v"), Buf("v")) for _ in range(NR)]
    kr = [(P.sb([128, 256], BF16, "k"), Buf("k")) for _ in range(NR)]
    kpr = [(P.sb([128, 256], BF16, "kp"), Buf("kp")) for _ in range(NR)]
    spr = [(P.sb([128, 128], BF16, "sp"), Buf("sp")) for _ in range(2)]
    hr = [(P.sb([128, 512], F32, "h"), Buf("h")) for _ in range(2)]
    dsc = [(P.sb([128, 4], F32, "dsc"), Buf("dsc")) for _ in range(2)]
    Lt = [(P.sb([128, SROW], F32, "L"), Buf("L")) for _ in range(2)]
    St = [(P.sb([128, 1026], F32, "S"), Buf("S")) for _ in range(4)]
    Cbt = [(P.sb([128, 1026], BF16, "Cb"), Buf("Cb")) for _ in range(4)]
    av = [(P.sb([128, 4], F32, "av"), Buf("av")) for _ in range(2)]
    sv = sctx.rearrange("ch p f -> p ch f")
    rv = recv.rearrange("ch (r p) f -> p r ch f", p=128)
    ri = 0
    li = 0
    for hd in range(8):
        q_, qb = qsb[hd % 2]
        k_T, kb = ksb[hd % 2]
        P.dma("sp", q_[:, :, :], qT[hd * 256:(hd + 1) * 256, :].rearrange("(c p) t -> p c t", p=128), writes=[qb])
        P.dma("sp", k_T[:, :, :], kT[hd * 256:(hd + 1) * 256, :].rearrange("(c p) t -> p c t", p=128), writes=[kb])
        chains = []
        for d in range(2):
            ch = d * 8 + hd
            S, Sb = St[(hd % 2) * 2 + d]
            Cb, Cbb = Cbt[(hd % 2) * 2 + d]
            P.dma("sp", S[:, :], sv[:, ch, 0:1026], writes=[Sb])
            for i in (range(4) if d == 0 else range(3, -1, -1)):
                L, Lb = Lt[li % 2]
                a_, ab = av[li % 2]
                li += 1
                P.dma("sp", L[:, 0:1027], rv[:, i, ch, 0:1027], writes=[Lb])
                fcol = fl[:, d * 4 + i:d * 4 + i + 1]
                P.op("dve", lambda h, a_=a_, L=L: h.tensor_scalar(out=a_[:, 0:1], in0=L[:, 1026:1027], scalar1=-1.0, scalar2=None, op0=ALU.add),
                     reads=[Lb], writes=[ab])
                P.op("dve", lambda h, a_=a_, fcol=fcol: h.tensor_scalar(out=a_[:, 1:2], in0=a_[:, 0:1], scalar1=fcol, scalar2=1.0, op0=ALU.mult, op1=ALU.add),
                     reads=[ab, flb], writes=[ab])
                P.op("dve", lambda h, L=L, fcol=fcol: h.tensor_scalar(out=L[:, 0:1026], in0=L[:, 0:1026], scalar1=fcol, scalar2=None, op0=ALU.mult),
                     reads=[Lb, flb], writes=[Lb])
                P.op("dve", lambda h, S=S, L=L, a_=a_: h.scalar_tensor_tensor(out=S[:, :], in0=S[:, :], scalar=a_[:, 1:2], in1=L[:, 0:1026],
                                                                            op0=ALU.mult, op1=ALU.add), reads=[Sb, Lb, ab], writes=[Sb])
            P.op("act", lambda h, S=S, Cb=Cb: h.activation(out=Cb[:, :], in_=S[:, :], func=AF.Copy), reads=[Sb], writes=[Cbb])
            order = list(range(16)) if d == 0 else list(range(15, -1, -1))
            chains.append(dict(d=d, S=S, Sb=Sb, Cb=Cb, Cbb=Cbb, order=order, out=(hf if d == 0 else hb)))
        for step in range(16):
            for chn in chains:
                d = chn["d"]
                c = chn["order"][step]
                (E, Eb), (R, Rb), (GC, GCb) = ERG[d]
                S, Sb, Cb, Cbb = chn["S"], chn["Sb"], chn["Cb"], chn["Cbb"]
                base = d * 4
                SB, NB_, UB0 = base, base + 1, base + 2
                cols = slice(c * 128, (c + 1) * 128)
                v_, vb = vr[ri % NR]
                k_, kb_ = kr[ri % NR]
                kp, kpb = kpr[ri % NR]
                s_, sb_ = spr[ri % 2]
                h_, hb_ = hr[ri % 2]
                ds, dsb = dsc[ri % 2]
                ri += 1
                P.dma("sp", v_[:, :], vk[c * 128:(c + 1) * 128, hd * 512:(hd + 1) * 512], writes=[vb])
                P.dma("sp", k_[:, :], vk[c * 128:(c + 1) * 128, 4096 + hd * 256:4096 + (hd + 1) * 256], writes=[kb_])
                Cbv = Cb[:, 0:1024].rearrange("p (a b) -> p a b", a=2)
                for dc in range(2):
                    P.op("pe", lambda h, dc=dc, cols=cols, SB=SB, k_T=k_T, q_=q_: h.matmul(
                        pst[:, SB, 0:128], lhsT=k_T[:, dc, cols], rhs=q_[:, dc, cols], start=(dc == 0), stop=(dc == 1)),
                        reads=[kb, qb], writes=[pbufs[SB]], sig=(dc == 1), pe_acc=True)
                t, tb = tri[d]
                P.op("dve", lambda h, s_=s_, SB=SB, E=E, c=c, t=t, hd=hd: h.scalar_tensor_tensor(
                    out=s_[:, :], in0=pst[:, SB, 0:128], scalar=E[:, c, hd:hd + 1], in1=t[:, :], op0=ALU.mult, op1=ALU.mult),
                    reads=[pbufs[SB], Eb, tb], writes=[sb_])
                P.op("pe", lambda h, s_=s_, v_=v_, NB_=NB_: h.matmul(pst[:, NB_, :], lhsT=s_[:, :], rhs=v_[:, :], start=True, stop=False),
                     reads=[sb_, vb], writes=[pbufs[NB_]], sig=False, pe_acc=True)
                for dc in range(2):
                    P.op("pe", lambda h, dc=dc, cols=cols, Cbv=Cbv, NB_=NB_, q_=q_: h.matmul(
                        pst[:, NB_, :], lhsT=q_[:, dc, cols], rhs=Cbv[:, dc, :], start=False, stop=(dc == 1)),
                        reads=[qb, Cbb], writes=[pbufs[NB_]], sig=(dc == 1), pe_acc=True)
                P.op("pe", lambda h, s_=s_, SB=SB: h.matmul(pst[:, SB, 256:257], lhsT=s_[:, :], rhs=onesb[:, 0:1], start=True, stop=False),
                     reads=[sb_, bonb], writes=[pbufs[SB]], sig=False, pe_acc=True)
                for dc in range(2):
                    P.op("pe", lambda h, dc=dc, cols=cols, Cb=Cb, SB=SB, q_=q_: h.matmul(
                        pst[:, SB, 256:257], lhsT=q_[:, dc, cols], rhs=Cb[:, 1024 + dc:1025 + dc], start=False, stop=(dc == 1)),
                        reads=[qb, Cbb], writes=[pbufs[SB]], sig=(dc == 1), pe_acc=True)
                rcol = R[:, c, hd:hd + 1]
                P.op("dve", lambda h, ds=ds, SB=SB, rcol=rcol: h.tensor_tensor(out=ds[:, 0:1], in0=pst[:, SB, 256:257], in1=rcol, op=ALU.mult),
                     reads=[pbufs[SB], Rb], writes=[dsb])
                P.op("act", lambda h, ds=ds: h.activation(out=ds[:, 1:2], in_=ds[:, 0:1], func=AF.Abs), reads=[dsb], writes=[dsb])
                P.op("dve", lambda h, ds=ds: h.tensor_single_scalar(out=ds[:, 2:3], in_=ds[:, 1:2], scalar=1.0, op=ALU.max), reads=[dsb], writes=[dsb])
                P.op("dve", lambda h, ds=ds: h.reciprocal(out=ds[:, 1:2], in_=ds[:, 2:3]), reads=[dsb], writes=[dsb])
                P.op("dve", lambda h, ds=ds, rcol=rcol: h.tensor_tensor(out=ds[:, 3:4], in0=ds[:, 1:2], in1=rcol, op=ALU.mult),
                     reads=[dsb, Rb], writes=[dsb])
                P.op("act", lambda h, h_=h_, ds=ds, NB_=NB_: h.activation(out=h_[:, :], in_=pst[:, NB_, :], func=AF.Copy, scale=ds[:, 3:4]),
                     reads=[pbufs[NB_], dsb], writes=[hb_])
                P.dma("sp", chn["out"][c * 128:(c + 1) * 128, hd * 512:(hd + 1) * 512], h_[:, :], reads=[hb_])
                if step < 15:
                    Sv = S[:, 0:1024].rearrange("p (a b) -> p a b", a=2)
                    _state_update(P, pst, pbufs, UB0, SB, kp, kpb, k_[:, :], kb_, v_[:, :], vb, E[:, c, hd:hd + 1], GC[:, c, hd:hd + 1],
                                  Buf("dummy"), Sv, S[:, 1024:1026], Sb, [Eb, GCb], onesb, bonb)
                    P.op("act", lambda h, S=S, Cb=Cb: h.activation(out=Cb[:, :], in_=S[:, :], func=AF.Copy), reads=[Sb], writes=[Cbb])
    P.run()
```

```python
import contextlib
import numpy as np
import concourse.bass as bass
import concourse.mybir as mybir
from concourse.bass_utils import run_bass_kernel_spmd

F32 = mybir.dt.float32
BF16 = mybir.dt.bfloat16
ALU = mybir.AluOpType
AF = mybir.ActivationFunctionType
AX = mybir.AxisListType

D = 4096
KC = 32
DFF = 11008
NX = 2048
NCTX = 256
TM = NX + NCTX
TE = 64 + NX + 64 + NCTX
EPS = 1e-6
WA = 2048
ABIN = 10240
CIN = 12320
NCORES = 8


class Buf:
    __slots__ = ("name", "w", "r", "dsem", "dcnt", "weng", "rec")

    def __init__(self, name):
        self.name = name
        self.w = None
        self.r = {}
        self.dsem = None
        self.dcnt = 0
        self.weng = None


class SemPool:
    _by_nc = {}

    @classmethod
    def of(cls, nc):
        p = cls._by_nc.get(id(nc))
        if p is None:
            p = cls(nc)
            cls._by_nc[id(nc)] = p
        return p

    def __init__(self, nc):
        self.nc = nc
        self.es = contextlib.ExitStack()
        self.ce = {e: [self.es.enter_context(nc.semaphore(f"ce_{e}")), 0] for e in ("pe", "act", "dve", "pool")}
        self.free = {"hw": [], "sw": [], "cc": []}
        self.n = 0

    def get(self, kind):
        if self.free[kind]:
            return self.free[kind].pop()
        self.n += 1
        return [self.es.enter_context(self.nc.semaphore(f"dq_{kind}{self.n}")), 0, kind]

    def put(self, rec):
        self.free[rec[2]].append(rec)


class Phase:
    CE = ("pe", "act", "dve", "pool")

    def __init__(self, nc, name):
        self.nc = nc
        self.name = name
        self.es = contextlib.ExitStack()
        self.pool = SemPool.of(nc)
        self.q = {e: [] for e in ("pe", "act", "dve", "pool", "sp")}
        self.sem = {e: self.pool.ce[e][0] for e in self.CE}
        self.cnt = {e: self.pool.ce[e][1] for e in self.CE}
        self.cnt0 = dict(self.cnt)
        self.pend = {e: False for e in self.CE}
        self.seen = {e: {id(self.sem[c]): self.cnt[c] for c in self.CE} for e in self.q}
        self.dma_bufs = []
        self.nsb = 0

    def sb(self, shape, dtype, name=None):
        self.nsb += 1
        return self.es.enter_context(self.nc.sbuf_tensor(f"{self.name}_{name or 't'}{self.nsb}", list(shape), dtype))

    def ps(self, shape, dtype=F32, name=None):
        self.nsb += 1
        return self.es.enter_context(self.nc.psum_tensor(f"{self.name}_{name or 'p'}{self.nsb}", list(shape), dtype))

    def _wait(self, eng, dep):
        sem, val = dep
        k = id(sem)
        if self.seen[eng].get(k, 0) >= val:
            return
        if eng in self.CE and sem is self.sem[eng]:
            assert val <= self.cnt[eng], f"self-wait on pending {eng} {val} {self.cnt[eng]}"
        self.seen[eng][k] = val
        self.q[eng].append(lambda h, sem=sem, val=val: h.wait_ge(sem, val))

    def _hazards(self, eng, reads, writes, pe_acc=False):
        for b in reads:
            if b.w is not None:
                self._wait(eng, b.w)
        for b in writes:
            if b.w is not None and not (pe_acc and b.weng == "pe" and eng == "pe"):
                self._wait(eng, b.w)
            for d in b.r.values():
                self._wait(eng, d)

    def op(self, eng, fn, reads=(), writes=(), sig=True, pe_acc=False):
        self._hazards(eng, reads, writes, pe_acc)
        c = self.cnt[eng] + 1
        sem = self.sem[eng]
        dep = (sem, c)
        for b in reads:
            b.r[id(sem)] = dep
        for b in writes:
            b.w = dep
            b.weng = eng
            b.r = {}
        if sig:
            self.cnt[eng] = c
            self.pend[eng] = False
            self.q[eng].append(lambda h, fn=fn, sem=sem: fn(h).then_inc(sem, 1))
        else:
            self.pend[eng] = True
            self.q[eng].append(lambda h, fn=fn: fn(h))

    def dma(self, qeng, out, in_, reads=(), writes=(), noncontig=False):
        self._hazards(qeng, reads, writes)
        tgt = (list(writes) + list(reads))[0]
        if tgt.dsem is None:
            rec = self.pool.get("sw" if qeng == "pool" else "hw")
            tgt.dsem = rec[0]
            tgt.dcnt = rec[1]
            tgt.rec = rec
            self.dma_bufs.append(tgt)
        assert tgt.rec[2] == ("sw" if qeng == "pool" else "hw"), "buffer DMA'd from both queue kinds"
        tgt.dcnt += 16
        sem = tgt.dsem
        dep = (sem, tgt.dcnt)
        for b in writes:
            b.w = dep
            b.weng = "dma"
            b.r = {}
        for b in reads:
            b.r[id(sem)] = dep
        if noncontig:
            self.q[qeng].append(lambda h, o=out, i=in_, sem=sem: h.dma_start(
                out=o, in_=i, allow_slow_non_contiguous=True).then_inc(sem, 16))
        else:
            self.q[qeng].append(lambda h, o=out, i=in_, sem=sem: h.dma_start(out=o, in_=i).then_inc(sem, 16))

    def run(self):
        for e in self.CE:
            assert not self.pend[e], f"pending unsignaled op on {e}"
        for b in self.dma_bufs:
            self._wait("sp", (b.dsem, b.dcnt))
        for e in self.CE:
            if self.cnt[e] > self.cnt0[e]:
                self._wait("sp", (self.sem[e], self.cnt[e]))
            self.pool.ce[e][1] = self.cnt[e]
        for b in self.dma_bufs:
            b.rec[1] = b.dcnt
            self.pool.put(b.rec)
        nc = self.nc
        q = self.q
        with nc.Block() as block:
            if q["pe"]:
                @block.tensor
                def _(h):
                    for f in q["pe"]:
                        f(h)
            if q["act"]:
                @block.scalar
                def _(h):
                    for f in q["act"]:
                        f(h)
            if q["dve"]:
                @block.vector
                def _(h):
                    for f in q["dve"]:
                        f(h)
            if q["pool"]:
                @block.gpsimd
                def _(h):
                    for f in q["pool"]:
                        f(h)
            if q["sp"]:
                @block.sync
                def _(h):
                    for f in q["sp"]:
                        f(h)
        self.es.close()


def col_view(row_ap, n=None):
    return row_ap.rearrange("(c p) -> p c", p=128)


def make_ident(P):
    ident = P.sb([128, 128], F32, "ident")
    bid = Buf("ident")
    P.op("pool", lambda h: h.memset(ident[:, :], 0.0), writes=[bid])
    P.op("pool", lambda h: h.affine_select(out=ident[:, :], in_=ident[:, :], pattern=[[-1, 128]],
                                           compare_op=ALU.not_equal, fill=1.0, base=0, channel_multiplier=1),
         reads=[bid], writes=[bid])
    return ident, bid


def load_cols(P, rows, ident, bid, ps_ap, pbuf, name="cols"):
    ncs = [r.shape[0] // 128 for r in rows]
    ntot = sum(ncs)
    assert ntot <= 128
    stg = P.sb([128, 128], F32, name + "s")
    bs = [Buf(name + "s") for _ in rows]
    off = 0
    for r, n, b in zip(rows, ncs, bs):
        P.dma("sp", stg[off:off + n, :], r.rearrange("(c p) -> c p", p=128), writes=[b])
        off += n
    t = P.sb([128, ntot], F32, name)
    tb = Buf(name)
    P.op("pe", lambda h: h.transpose(out=ps_ap[:, 0:ntot], in_=stg[0:ntot, :], identity=ident[0:ntot, 0:ntot]),
         reads=bs + [bid], writes=[pbuf])
    P.op("dve", lambda h: h.tensor_copy(out=t[:, :], in_=ps_ap[:, 0:ntot]), reads=[pbuf], writes=[tb])
    return t, tb


def load_bcast(P, row_ap, n, name="bc", q="sp", parts=128):
    t = P.sb([parts, n], F32, name)
    b = Buf(name)
    P.dma(q, t[:, :], row_ap.partition_broadcast(parts), writes=[b])
    return t, b


def rstd_ops(P, s_, sb_, ci, ct, co, scale, n=1):
    P.op("act", lambda h: h.activation(out=s_[:, ct:ct + n], in_=s_[:, ci:ci + n], func=AF.Sqrt, scale=scale, bias=EPS),
         reads=[sb_], writes=[sb_])
    P.op("dve", lambda h: h.reciprocal(out=s_[:, co:co + n], in_=s_[:, ct:ct + n]), reads=[sb_], writes=[sb_])


def phase_ada(nc, csel, ada_w, ada_b, mods, layers=(0, 1), NM=6 * D):
    P = Phase(nc, "ada")
    ident, bid = make_ident(P)
    pst = P.ps([128, 8, 512], F32, "ps")
    pbufs = [Buf(f"ps{i}") for i in range(8)]
    cT, bcT = load_cols(P, [csel[0], csel[1]], ident, bid, pst[:, 7, :], pbufs[7], "cT")
    sT = P.sb([128, 2, KC], F32, "sT")
    bsT = Buf("sT")
    P.op("act", lambda h: h.activation(out=sT[:, :, :], in_=cT[:, :].rearrange("p (r c) -> p r c", r=2), func=AF.Silu),
         reads=[bcT], writes=[bsT])
    NB = 1024
    KG = 4
    ring = [(P.sb([128, KG, NB], F32, "w"), Buf("w")) for _ in range(4)]
    bias = [(P.sb([2, NB], F32, "bias"), Buf("bias")) for _ in range(2)]
    osb = [(P.sb([2, NB], F32, "osb"), Buf("osb")) for _ in range(2)]
    wi = 0
    pi = 0
    ni = 0
    for l in layers:
        wv = ada_w[l].rearrange("(c p) n -> p c n", p=128)
        for nb in range(NM // NB):
            bt, bb = bias[ni % 2]
            ot, bo = osb[ni % 2]
            ni += 1
            P.dma("sp", bt[:, :], ada_b[l, nb * NB:(nb + 1) * NB].partition_broadcast(2), writes=[bb])
            pb = [(pi + j) % 8 for j in range(NB // 512)]
            pi += NB // 512
            for kg in range(KC // KG):
                wt, wb = ring[wi % 4]
                wi += 1
                P.dma("sp", wt[:, :, :], wv[:, kg * KG:(kg + 1) * KG, nb * NB:(nb + 1) * NB], writes=[wb])
                for kk in range(KG):
                    kc = kg * KG + kk
                    for j, bk in enumerate(pb):
                        last = (kc == KC - 1)
                        P.op("pe", lambda h, bk=bk, kc=kc, kk=kk, j=j, wt=wt: h.matmul(
                            pst[0:2, bk, :], lhsT=sT[:, :, kc], rhs=wt[:, kk, j * 512:(j + 1) * 512],
                            start=(kc == 0), stop=(kc == KC - 1)),
                            reads=[bsT, wb], writes=[pbufs[bk]], sig=(last or (kk == KG - 1 and j == len(pb) - 1)), pe_acc=True)
            for j, bk in enumerate(pb):
                P.op("dve", lambda h, bk=bk, j=j, ot=ot, bt=bt: h.tensor_tensor(
                    out=ot[:, j * 512:(j + 1) * 512], in0=pst[0:2, bk, :], in1=bt[:, j * 512:(j + 1) * 512], op=ALU.add),
                    reads=[pbufs[bk], bb], writes=[bo])
            P.dma("sp", mods[l, :, nb * NB:(nb + 1) * NB], ot[:, :], reads=[bo])
    P.run()


def phase_prenorm(nc, name, x_in, groups, y_in=None, gg_rows=None, x_out=None, ab_rows=None, hT_out=None,
                  xin_rows=None):
    P = Phase(nc, name)
    NG = len(groups)
    vsets = sorted(set(groups))
    ident, bid = make_ident(P)
    pst = P.ps([128, 8, 512], F32, "ps")
    pbufs = [Buf(f"ps{i}") for i in range(8)]
    GG = {}
    if y_in is not None:
        for v in vsets:
            gt, gb = load_bcast(P, gg_rows[v][0], D, "gate")
            nt, nb_ = load_bcast(P, gg_rows[v][1], D, "gn")
            P.op("pool", lambda h, gt=gt, nt=nt: h.tensor_tensor(out=gt[:, :], in0=gt[:, :], in1=nt[:, :], op=ALU.mult),
                 reads=[gb, nb_], writes=[gb])
            GG[v] = (gt, gb)
    AB = {}
    if hT_out is not None:
        for v in vsets:
            cl, clb = load_cols(P, list(ab_rows[v]), ident, bid, pst[:, v, :], pbufs[v], "abc")
            A_ = P.sb([128, KC], F32, "A")
            Ab = Buf("A")
            P.op("dve", lambda h, A_=A_, cl=cl: h.scalar_tensor_tensor(
                out=A_[:, :], in0=cl[:, KC:2 * KC], scalar=1.0, in1=cl[:, 0:KC], op0=ALU.add, op1=ALU.mult),
                reads=[clb], writes=[Ab])
            B_, Bb = cl[:, 2 * KC:3 * KC], clb
            AB[v] = (A_, Ab, B_, Bb)
    xt = [(P.sb([128, D], F32, "xt"), Buf("xt")) for _ in range(2)]
    yt = [(P.sb([128, D], F32, "yt"), Buf("yt")) for _ in range(2)]
    sq = P.sb([128, D], BF16, "sq")
    bsq = Buf("sq")
    st = [(P.sb([128, 8], F32, "st"), Buf("st")) for _ in range(2)]
    NHB = 1 if y_in is not None else 2
    hb = [(P.sb([128, KC, 512], BF16, "hb"), Buf("hb")) for _ in range(NHB)] if hT_out is not None else None
    hTv = hT_out.rearrange("(c p) t -> p c t", p=128) if hT_out is not None else None

    def rstd(h, s, i, o):
        return None

    for g in range(NG):
        v = groups[g]
        x_, xb = xt[g % 2]
        y_, yb = yt[g % 2]
        s_, sb_ = st[g % 2]
        r0 = xin_rows[g] if xin_rows is not None else g * 128
        P.dma("sp", x_[:, :], x_in[r0:r0 + 128, :], writes=[xb])
        if y_in is not None:
            P.dma("sp", y_[:, :], y_in[g * 128:(g + 1) * 128, :], writes=[yb])
            P.op("act", lambda h, y_=y_, s_=s_: h.activation(out=sq[:, :], in_=y_[:, :], func=AF.Square,
                                                             accum_out=s_[:, 0:1]),
                 reads=[yb], writes=[bsq, sb_])
            rstd_ops(P, s_, sb_, 0, 1, 2, 1.0 / D)
            gt, gb = GG[v]
            P.op("dve", lambda h, y_=y_, s_=s_, gt=gt: h.scalar_tensor_tensor(
                out=y_[:, :], in0=y_[:, :], scalar=s_[:, 2:3], in1=gt[:, :], op0=ALU.mult, op1=ALU.mult),
                reads=[yb, sb_, gb], writes=[yb])
            P.op("pool", lambda h, x_=x_, y_=y_: h.tensor_tensor(out=x_[:, :], in0=x_[:, :], in1=y_[:, :], op=ALU.add),
                 reads=[xb, yb], writes=[xb])
            if x_out is not None:
                P.dma("sp", x_out[g * 128:(g + 1) * 128, :], x_[:, :], reads=[xb])
        if hT_out is None:
            continue
        P.op("act", lambda h, x_=x_, s_=s_: h.activation(out=sq[:, :], in_=x_[:, :], func=AF.Square,
                                                         accum_out=s_[:, 3:4]),
             reads=[xb], writes=[bsq, sb_])
        rstd_ops(P, s_, sb_, 3, 4, 5, 1.0 / D)
        P.op("act", lambda h, x_=x_, y_=y_, s_=s_: h.activation(out=y_[:, :], in_=x_[:, :], func=AF.Copy,
                                                                scale=s_[:, 5:6]),
             reads=[xb, sb_], writes=[yb])
        h_, hbb = hb[(g // 4) % NHB]
        A_, Ab, B_, Bb = AB[v]
        tcol = (g % 4) * 128
        for c in range(KC):
            bk = c // 4
            j = c % 4
            P.op("pe", lambda h, bk=bk, j=j, c=c, y_=y_: h.transpose(
                out=pst[:, bk, j * 128:(j + 1) * 128], in_=y_[:, c * 128:(c + 1) * 128], identity=ident[:, :]),
                reads=[yb, bid], writes=[pbufs[bk]], sig=(j == 3), pe_acc=True)
            if j == 3:
                for jj in range(4):
                    cc = bk * 4 + jj
                    if jj % 2 == 0:
                        P.op("act", lambda h, bk=bk, jj=jj, cc=cc, h_=h_, A_=A_, B_=B_, tcol=tcol: h.activation(
                            out=h_[:, cc, tcol:tcol + 128], in_=pst[:, bk, jj * 128:(jj + 1) * 128],
                            func=AF.Identity, scale=A_[:, cc:cc + 1], bias=B_[:, cc:cc + 1]),
                            reads=[pbufs[bk], Ab, Bb], writes=[hbb])
                    else:
                        P.op("dve", lambda h, bk=bk, jj=jj, cc=cc, h_=h_, A_=A_, B_=B_, tcol=tcol: h.tensor_scalar(
                            out=h_[:, cc, tcol:tcol + 128], in0=pst[:, bk, jj * 128:(jj + 1) * 128],
                            scalar1=A_[:, cc:cc + 1], scalar2=B_[:, cc:cc + 1], op0=ALU.mult, op1=ALU.add),
                            reads=[pbufs[bk], Ab, Bb], writes=[hbb])
        if g % 4 == 3 or g == NG - 1:
            t0 = (g // 4) * 512
            nt = (g % 4 + 1) * 128
            P.dma("sp", hTv[:, :, t0:t0 + nt], h_[:, :, 0:nt], reads=[hbb])
    P.run()


def phase_gemm_f(nc, name, hT, W, kc_n, passes, groups, hook, hook_init=None):
    P = Phase(nc, name)
    passes = [(p[0], p[1], (p[2] if len(p) > 2 else None)) for p in passes]
    TMAX = max(p[1] for p in passes)
    hsb = P.sb([128, kc_n, TMAX], BF16, "hT")
    NQ = 4
    kq = kc_n // NQ
    hbufs = [Buf(f"h{i}") for i in range(NQ)]
    NWR = 4
    ring = [(P.sb([128, kc_n, 128], BF16, "w"), Buf("w")) for _ in range(NWR)]
    pst = P.ps([128, 8, 512], F32, "ps")
    pbufs = [Buf(f"ps{i}") for i in range(8)]
    hTv = hT.rearrange("(c p) t -> p c t", p=128)
    Wv = W.rearrange("(c p) n -> p c n", p=128)
    P.ps_shared, P.pb_shared = pst, pbufs
    ctx = hook_init(P) if hook_init else None
    wi = 0
    si = 0
    for pi_, (t0, T, blks_) in enumerate(passes):
        for qd in range(NQ):
            P.dma("sp", hsb[:, qd * kq:(qd + 1) * kq, 0:T], hTv[:, qd * kq:(qd + 1) * kq, t0:t0 + T],
                  writes=[hbufs[qd]])
        nblk = (T + 511) // 512
        blks = blks_ if blks_ is not None else [(b * 512, min(512, T - b * 512)) for b in range(nblk)]
        nblk = len(blks)
        assert nblk <= 2 and all(cn <= 512 for _, cn in blks)
        for gi, grp in enumerate(groups):
            slots = []
            for n0 in grp:
                wt, wb = ring[wi % NWR]
                wi += 1
                P.dma("pool", wt[:, :, :], Wv[:, :, n0:n0 + 128], writes=[wb])
                slot = si % 4
                si += 1
                for kc in range(kc_n):
                    for b, (c0, cn) in enumerate(blks):
                        bk = slot * 2 + b
                        last = (kc == kc_n - 1)
                        P.op("pe", lambda h, bk=bk, kc=kc, c0=c0, cn=cn, wt=wt: h.matmul(
                            pst[:, bk, 0:cn], lhsT=wt[:, kc, :], rhs=hsb[:, kc, c0:c0 + cn],
                            start=(kc == 0), stop=(kc == kc_n - 1)),
                            reads=[wb, hbufs[kc // kq]], writes=[pbufs[bk]], sig=(last and b == nblk - 1),
                            pe_acc=True)
                slots.append([(pst, slot * 2 + b, pbufs[slot * 2 + b], c0, cn) for b, (c0, cn) in enumerate(blks)])
            hook(P, ctx, gi, (pi_, t0, T), slots)
    P.run()


def phase_gemm_t(nc, name, aT, W, kc_n, passes, nblocks, outs):
    P = Phase(nc, name)
    TMAX = max(t for _, t in passes)
    asb = P.sb([128, kc_n, TMAX], BF16, "aT")
    NQ = 2
    kq = (kc_n + NQ - 1) // NQ
    abufs = [Buf(f"a{i}") for i in range(NQ)]
    KG = 8
    NWR = 4
    ring = [(P.sb([128, KG, 512], BF16, "w"), Buf("w")) for _ in range(NWR)]
    pst = P.ps([128, 8, 512], F32, "ps")
    pbufs = [Buf(f"ps{i}") for i in range(8)]
    osbs = [(P.sb([128, 4, 512], outs[0].dtype if len(set(o.dtype for o in outs)) == 1 else F32, "o"), Buf("o"))
            for _ in range(2)]
    aTv = aT.rearrange("(c p) t -> p c t", p=128)
    Wv = W.rearrange("(c p) n -> p c n", p=128)
    wi = 0
    si = 0
    oi = 0
    for (t0, T) in passes:
        for qd in range(NQ):
            k0, k1 = qd * kq, min(kc_n, (qd + 1) * kq)
            P.dma("sp", asb[:, k0:k1, 0:T], aTv[:, k0:k1, t0:t0 + T], writes=[abufs[qd]])
        ntg = T // 128
        for (n0, ncol, oidx, oc0) in nblocks:
            slot = si % 2
            si += 1
            for kg in range((kc_n + KG - 1) // KG):
                k0, k1 = kg * KG, min(kc_n, (kg + 1) * KG)
                wt, wb = ring[wi % NWR]
                wi += 1
                P.dma("pool", wt[:, 0:k1 - k0, 0:ncol], Wv[:, k0:k1, n0:n0 + ncol], writes=[wb])
                for kc in range(k0, k1):
                    for tg in range(ntg):
                        bk = slot * 4 + tg
                        last = (kc == kc_n - 1)
                        P.op("pe", lambda h, bk=bk, kc=kc, tg=tg, wt=wt, k0=k0, ncol=ncol: h.matmul(
                            pst[:, bk, 0:ncol], lhsT=asb[:, kc, tg * 128:(tg + 1) * 128], rhs=wt[:, kc - k0, 0:ncol],
                            start=(kc == 0), stop=(kc == kc_n - 1)),
                            reads=[wb, abufs[kc // kq]], writes=[pbufs[bk]], sig=((last or kc == k1 - 1) and tg == ntg - 1),
                            pe_acc=True)
            o_, ob = osbs[oi % 2]
            oi += 1
            for tg in range(ntg):
                bk = slot * 4 + tg
                if tg % 2 == 0:
                    P.op("act", lambda h, bk=bk, tg=tg, o_=o_, ncol=ncol: h.activation(
                        out=o_[:, tg, 0:ncol], in_=pst[:, bk, 0:ncol], func=AF.Copy),
                        reads=[pbufs[bk]], writes=[ob])
                else:
                    P.op("dve", lambda h, bk=bk, tg=tg, o_=o_, ncol=ncol: h.tensor_copy(
                        out=o_[:, tg, 0:ncol], in_=pst[:, bk, 0:ncol]),
                        reads=[pbufs[bk]], writes=[ob])
            ov = outs[oidx][t0:t0 + T, oc0:oc0 + ncol].rearrange("(g p) n -> p g n", p=128)
            P.dma("sp", ov, o_[:, 0:ntg, 0:ncol], reads=[ob])
    P.run()


def make_store_hook(dst_rows, func=None, dtype=F32, tmax=1024):
    def hinit(P):
        return [(P.sb([128, tmax], dtype, "o"), Buf("o")) for _ in range(2)]

    def hook(P, ctx, gi, pinfo, slots):
        _, t0, T = pinfo
        o_, ob = ctx[gi % 2]
        for k, (pst, bk, pb, c0, cn) in enumerate(slots[0]):
            if func is None and k % 2 == 1:
                P.op("dve", lambda h, bk=bk, c0=c0, cn=cn, pst=pst: h.tensor_copy(out=o_[:, c0:c0 + cn], in_=pst[:, bk, 0:cn]),
                     reads=[pb], writes=[ob])
            else:
                fn_ = func(gi) if callable(func) else (func or AF.Copy)
                P.op("act", lambda h, bk=bk, c0=c0, cn=cn, pst=pst, fn_=fn_: h.activation(
                    out=o_[:, c0:c0 + cn], in_=pst[:, bk, 0:cn], func=fn_), reads=[pb], writes=[ob])
        P.dma("sp", dst_rows(gi)[:, t0:t0 + T], o_[:, 0:T], reads=[ob])
    return hook, hinit


def make_ffn_hook(conv_w, actT, row_len):
    NCH = DFF // 128

    def hinit(P):
        ident, bid = make_ident(P)
        pst = P.ps_shared
        cw = []
        for j in range(3):
            for half in range(2):
                t, tb = load_cols(P, [conv_w[j, half * DFF:(half + 1) * DFF]], ident, bid, pst[:, 7, :], P.pb_shared[7], "cw")
                cw.append((t, tb))
        gs = [(P.sb([128, 1024], F32, "g"), Buf("g")) for _ in range(2)]
        vs = [(P.sb([128, 1024], F32, "v"), Buf("v")) for _ in range(2)]
        as_ = [(P.sb([128, 1024], BF16, "a"), Buf("a")) for _ in range(2)]
        return dict(cw=cw, gs=gs, vs=vs, as_=as_)

    def hook(P, ctx, gi, pinfo, slots):
        pi_, t0, T = pinfo
        g_, gb = ctx["gs"][gi % 2]
        v_, vb = ctx["vs"][gi % 2]
        a_, ab = ctx["as_"][gi % 2]
        cw = ctx["cw"]
        for half, (s_, sb_) in enumerate(((g_, gb), (v_, vb))):
            w0, w0b = cw[0 * 2 + half]
            w1, w1b = cw[1 * 2 + half]
            w2, w2b = cw[2 * 2 + half]
            for (pst, bk, pb, c0, cn) in slots[half]:
                L = row_len(pi_, c0)
                assert cn % L == 0
                P.op("act", lambda h, s_=s_, bk=bk, c0=c0, cn=cn, pst=pst, w1=w1: h.activation(
                    out=s_[:, c0:c0 + cn], in_=pst[:, bk, 0:cn], func=AF.Copy, scale=w1[:, gi:gi + 1]),
                    reads=[pb, w1b], writes=[sb_])
                sv = s_[:, c0:c0 + cn].rearrange("p (r l) -> p r l", l=L)
                pv = pst[:, bk, 0:cn].rearrange("p (r l) -> p r l", l=L)
                P.op("dve", lambda h, sv=sv, pv=pv, w0=w0, L=L: h.scalar_tensor_tensor(
                    out=sv[:, :, 1:L], in0=pv[:, :, 0:L - 1], scalar=w0[:, gi:gi + 1], in1=sv[:, :, 1:L],
                    op0=ALU.mult, op1=ALU.add), reads=[pb, w0b, sb_], writes=[sb_])
                P.op("dve", lambda h, sv=sv, pv=pv, w2=w2, L=L: h.scalar_tensor_tensor(
                    out=sv[:, :, 0:L - 1], in0=pv[:, :, 1:L], scalar=w2[:, gi:gi + 1], in1=sv[:, :, 0:L - 1],
                    op0=ALU.mult, op1=ALU.add), reads=[pb, w2b, sb_], writes=[sb_])
        P.op("act", lambda h: h.activation(out=g_[:, 0:T], in_=g_[:, 0:T], func=AF.Silu), reads=[gb], writes=[gb])
        P.op("dve", lambda h: h.tensor_tensor(out=a_[:, 0:T], in0=g_[:, 0:T], in1=v_[:, 0:T], op=ALU.mult),
             reads=[gb, vb], writes=[ab])
        P.dma("sp", actT[gi * 128:(gi + 1) * 128, t0:t0 + T], a_[:, 0:T], reads=[ab])
    return hook, hinit


def phase_abmix_a(nc, PT, a_conv_w, a_conv_b, a_ln_g, a_ln_b, catT):
    P = Phase(nc, "abA")
    ident, bid = make_ident(P)
    pst = P.ps([128, 8, 512], F32, "ps")
    pbufs = [Buf(f"ps{i}") for i in range(8)]
    NA = WA // 128
    cwa, cwab = load_cols(P, [a_conv_w[j] for j in range(0, 8)], ident, bid, pst[:, 0, :], pbufs[0], "cwa")
    cwb, cwbb = load_cols(P, [a_conv_w[j] for j in range(8, 16)], ident, bid, pst[:, 1, :], pbufs[1], "cwb")
    cwc, cwcb = load_cols(P, [a_conv_w[j] for j in range(16, 24)], ident, bid, pst[:, 2, :], pbufs[2], "cwc")
    cwd, cwdb = load_cols(P, [a_conv_w[j] for j in range(24, 31)], ident, bid, pst[:, 3, :], pbufs[3], "cwd")
    prm, prmb = load_cols(P, [a_conv_b, a_ln_g, a_ln_b], ident, bid, pst[:, 4, :], pbufs[4], "prm")
    cws = [(cwa, cwab), (cwb, cwbb), (cwc, cwcb), (cwd, cwdb)]

    def wcol(j, i):
        t, tb = cws[j // 8]
        return t[:, (j % 8) * NA + i:(j % 8) * NA + i + 1], tb
    ones = P.sb([128, 128], F32, "ones")
    bon = Buf("ones")
    P.op("pool", lambda h: h.memset(ones[:, :], 1.0), writes=[bon])
    uc = P.sb([128, NA, 512], F32, "uc")
    ucb = [Buf(f"uc{i}") for i in range(NA)]
    vt = [(P.sb([128, 512], F32, "val"), Buf("val")) for _ in range(2)]
    gt = [(P.sb([128, 512], F32, "gate"), Buf("gate")) for _ in range(2)]
    sqt = [(P.sb([128, 512], F32, "sq"), Buf("sq")) for _ in range(2)]
    mean = P.sb([128, 512], F32, "mean"); bmean = Buf("mean")
    rstd = P.sb([128, 512], F32, "rstd"); brstd = Buf("rstd")
    tmp = [(P.sb([128, 512], F32, "tmp"), Buf("tmp")) for _ in range(2)]
    ob = [(P.sb([128, 512], BF16, "o"), Buf("o")) for _ in range(2)]
    pieces = [(64 + k * 512, k * 512, 512, 64) for k in range(4)] + [(64 + NX + 64, NX, NCTX, NCTX)]
    k = 0
    for (e0, m0, T, L) in pieces:
        for i in range(NA):
            v_, vb = vt[k % 2]
            g_, gb = gt[k % 2]
            q_, qb = sqt[k % 2]
            k += 1
            P.dma("sp", v_[:, 0:T], PT[i * 128:(i + 1) * 128, e0:e0 + T], writes=[vb])
            P.dma("sp", g_[:, 0:T], PT[WA + i * 128:WA + (i + 1) * 128, e0:e0 + T], writes=[gb])
            P.op("act", lambda h, g_=g_, T=T: h.activation(out=g_[:, 0:T], in_=g_[:, 0:T], func=AF.Sigmoid),
                 reads=[gb], writes=[gb])
            P.op("dve", lambda h, g_=g_, v_=v_, T=T: h.tensor_tensor(out=v_[:, 0:T], in0=v_[:, 0:T], in1=g_[:, 0:T], op=ALU.mult),
                 reads=[gb, vb], writes=[vb])
            wc, wcb = wcol(15, i)
            P.op("act", lambda h, v_=v_, i=i, T=T, wc=wc: h.activation(
                out=uc[:, i, 0:T], in_=v_[:, 0:T], func=AF.Identity, scale=wc, bias=prm[:, i:i + 1]),
                reads=[vb, wcb, prmb], writes=[ucb[i]])
            uv = uc[:, i, 0:T].rearrange("p (r l) -> p r l", l=L)
            vv = v_[:, 0:T].rearrange("p (r l) -> p r l", l=L)
            n = 0
            for dd in range(1, 16):
                for d in (dd, -dd):
                    wc, wcb = wcol(15 + d, i)
                    if d > 0:
                        o_ap, i_ap = uv[:, :, 0:L - d], vv[:, :, d:L]
                    else:
                        o_ap, i_ap = uv[:, :, -d:L], vv[:, :, 0:L + d]
                    eng = "dve"
                    n += 1
                    P.op(eng, lambda h, o_ap=o_ap, i_ap=i_ap, wc=wc: h.scalar_tensor_tensor(
                        out=o_ap, in0=i_ap, scalar=wc, in1=o_ap, op0=ALU.mult, op1=ALU.add),
                        reads=[vb, wcb, ucb[i]], writes=[ucb[i]])
            P.op("act", lambda h, q_=q_, i=i, T=T: h.activation(out=q_[:, 0:T], in_=uc[:, i, 0:T], func=AF.Square),
                 reads=[ucb[i]], writes=[qb])
            P.op("pe", lambda h, i=i, T=T: h.matmul(pst[:, 6, 0:T], lhsT=ones[:, :], rhs=uc[:, i, 0:T],
                                                    start=(i == 0), stop=(i == NA - 1)),
                 reads=[bon, ucb[i]], writes=[pbufs[6]], pe_acc=True)
            P.op("pe", lambda h, q_=q_, i=i, T=T: h.matmul(pst[:, 7, 0:T], lhsT=ones[:, :], rhs=q_[:, 0:T],
                                                           start=(i == 0), stop=(i == NA - 1)),
                 reads=[bon, qb], writes=[pbufs[7]], pe_acc=True)
        P.op("act", lambda h, T=T: h.activation(out=mean[:, 0:T], in_=pst[:, 6, 0:T], func=AF.Copy, scale=1.0 / WA),
             reads=[pbufs[6]], writes=[bmean])
        t_, tb = tmp[0]
        P.op("dve", lambda h, t_=t_, T=T: h.tensor_tensor(out=t_[:, 0:T], in0=mean[:, 0:T], in1=mean[:, 0:T], op=ALU.mult),
             reads=[bmean], writes=[tb])
        P.op("dve", lambda h, t_=t_, T=T: h.scalar_tensor_tensor(
            out=t_[:, 0:T], in0=pst[:, 7, 0:T], scalar=1.0 / WA, in1=t_[:, 0:T], op0=ALU.mult, op1=ALU.subtract),
            reads=[pbufs[7], tb], writes=[tb])
        P.op("act", lambda h, t_=t_, T=T: h.activation(out=t_[:, 0:T], in_=t_[:, 0:T], func=AF.Sqrt, bias=EPS),
             reads=[tb], writes=[tb])
        P.op("dve", lambda h, t_=t_, T=T: h.reciprocal(out=rstd[:, 0:T], in_=t_[:, 0:T]), reads=[tb], writes=[brstd])
        for i in range(NA):
            t_, tb = tmp[i % 2]
            o_, obb = ob[i % 2]
            P.op("dve", lambda h, t_=t_, i=i, T=T: h.tensor_tensor(out=t_[:, 0:T], in0=uc[:, i, 0:T], in1=mean[:, 0:T], op=ALU.subtract),
                 reads=[ucb[i], bmean], writes=[tb])
            P.op("pool", lambda h, t_=t_, T=T: h.tensor_tensor(out=t_[:, 0:T], in0=t_[:, 0:T], in1=rstd[:, 0:T], op=ALU.mult),
                 reads=[tb, brstd], writes=[tb])
            P.op("act", lambda h, t_=t_, o_=o_, i=i, T=T: h.activation(
                out=o_[:, 0:T], in_=t_[:, 0:T], func=AF.Silu, scale=prm[:, NA + i:NA + i + 1], bias=prm[:, 2 * NA + i:2 * NA + i + 1]),
                reads=[tb, prmb], writes=[obb])
            P.dma("sp", catT[i * 128:(i + 1) * 128, m0:m0 + T], o_[:, 0:T], reads=[obb])
    P.run()


def phase_abmix_b(nc, PT, b_conv_w, hmask, catT):
    P = Phase(nc, "abB")
    ident, bid = make_ident(P)
    pst = P.ps([128, 8, 512], F32, "ps")
    pbufs = [Buf(f"ps{i}") for i in range(8)]
    NB_ = WA // 128
    cw, cwb = load_cols(P, [b_conv_w[0], b_conv_w[1], b_conv_w[2]], ident, bid, pst[:, 0, :], pbufs[0], "cw")
    hm = P.sb([128, 2], F32, "hm"); hmb = Buf("hm")
    P.dma("sp", hm[:, :], hmask[:, :], writes=[hmb])
    XE = 64 + NX + 64
    bb_ = [(P.sb([128, TM], F32, "bb"), Buf("bb")) for _ in range(2)]
    bc_ = [(P.sb([128, TE], F32, "bc"), Buf("bc")) for _ in range(2)]
    bx_ = [(P.sb([128, TE], F32, "bx"), Buf("bx")) for _ in range(2)]
    zc_ = [(P.sb([128, TM], F32, "zc"), Buf("zc")) for _ in range(2)]
    o_ = [(P.sb([128, TM], BF16, "o"), Buf("o")) for _ in range(2)]
    for i in range(NB_):
        b_, bbb = bb_[i % 2]
        c_, cb = bc_[i % 2]
        x_, xb = bx_[i % 2]
        z_, zb = zc_[i % 2]
        oo, ob = o_[i % 2]
        r_b = 2 * WA + i * 128
        r_c = 2 * WA + WA + i * 128
        r_x = 2 * WA + 2 * WA + i * 128
        P.dma("sp", b_[:, 0:NX], PT[r_b:r_b + 128, 64:64 + NX], writes=[bbb])
        bbb2 = Buf("bb2")
        P.dma("sp", b_[:, NX:TM], PT[r_b:r_b + 128, XE:TE], writes=[bbb2])
        P.dma("sp", c_[:, :], PT[r_c:r_c + 128, :], writes=[cb])
        P.dma("sp", x_[:, :], PT[r_x:r_x + 128, :], writes=[xb])
        P.op("dve", lambda h, c_=c_, x_=x_: h.tensor_tensor(out=c_[:, :], in0=c_[:, :], in1=x_[:, :], op=ALU.mult),
             reads=[cb, xb], writes=[cb])
        P.op("dve", lambda h, c_=c_: h.tensor_scalar(out=c_[:, 0:64], in0=c_[:, 0:64], scalar1=hm[:, 0:1], scalar2=None, op0=ALU.mult),
             reads=[cb, hmb], writes=[cb])
        P.op("dve", lambda h, c_=c_: h.tensor_scalar(out=c_[:, 64 + NX:XE], in0=c_[:, 64 + NX:XE], scalar1=hm[:, 1:2], scalar2=None, op0=ALU.mult),
             reads=[cb, hmb], writes=[cb])
        w0, w1, w2 = cw[:, i:i + 1], cw[:, NB_ + i:NB_ + i + 1], cw[:, 2 * NB_ + i:2 * NB_ + i + 1]
        P.op("act", lambda h, z_=z_, c_=c_, w1=w1: h.activation(out=z_[:, 0:NX], in_=c_[:, 64:64 + NX], func=AF.Copy, scale=w1),
             reads=[cb, cwb], writes=[zb])
        P.op("dve", lambda h, z_=z_, c_=c_, w0=w0: h.scalar_tensor_tensor(
            out=z_[:, 0:NX], in0=c_[:, 0:NX], scalar=w0, in1=z_[:, 0:NX], op0=ALU.mult, op1=ALU.add),
            reads=[cb, cwb, zb], writes=[zb])
        P.op("dve", lambda h, z_=z_, c_=c_, w2=w2: h.scalar_tensor_tensor(
            out=z_[:, 0:NX], in0=c_[:, 128:128 + NX], scalar=w2, in1=z_[:, 0:NX], op0=ALU.mult, op1=ALU.add),
            reads=[cb, cwb, zb], writes=[zb])
        P.op("act", lambda h, z_=z_, c_=c_, w1=w1: h.activation(out=z_[:, NX:TM], in_=c_[:, XE:TE], func=AF.Copy, scale=w1),
             reads=[cb, cwb], writes=[zb])
        P.op("dve", lambda h, z_=z_, c_=c_, w0=w0: h.scalar_tensor_tensor(
            out=z_[:, NX + 1:TM], in0=c_[:, XE:TE - 1], scalar=w0, in1=z_[:, NX + 1:TM], op0=ALU.mult, op1=ALU.add),
            reads=[cb, cwb, zb], writes=[zb])
        P.op("dve", lambda h, z_=z_, c_=c_, w2=w2: h.scalar_tensor_tensor(
            out=z_[:, NX:TM - 1], in0=c_[:, XE + 1:TE], scalar=w2, in1=z_[:, NX:TM - 1], op0=ALU.mult, op1=ALU.add),
            reads=[cb, cwb, zb], writes=[zb])
        P.op("pool", lambda h, z_=z_, b_=b_, oo=oo: h.tensor_tensor(out=oo[:, :], in0=z_[:, :], in1=b_[:, :], op=ALU.mult),
             reads=[zb, bbb, bbb2], writes=[ob])
        P.dma("sp", catT[WA + i * 128:WA + (i + 1) * 128, :], oo[:, :], reads=[ob])
    P.run()


def phase_scan(nc, qT, kT, kk, vv, gg, bg, hf, hb, NCH=66, NCTXC=2, npairs=2):
    import math
    P = Phase(nc, "scan")
    ident, bid = make_ident(P)
    pst = P.ps([128, 8, 512], F32, "ps")
    pbufs = [Buf(f"ps{i}") for i in range(8)]
    TT = NCH * 128
    ones = P.sb([128, 128], F32, "ones"); bon = Buf("ones")
    P.op("pool", lambda h: h.memset(ones[:, :], 1.0), writes=[bon])
    onesb = P.sb([128, 2], BF16, "onesb"); bonb = Buf("onesb")
    P.op("pool", lambda h: h.memset(onesb[:, :], 1.0), writes=[bonb])
    tri = []
    for d in range(2):
        t = P.sb([128, 128], F32, f"tri{d}"); tb = Buf(f"tri{d}")
        P.op("pool", lambda h, t=t: h.memset(t[:, :], 1.0), writes=[tb])
        sgn = 1 if d == 0 else -1
        P.op("pool", lambda h, t=t, sgn=sgn: h.affine_select(out=t[:, :], in_=t[:, :], pattern=[[sgn, 128]],
                                                            compare_op=ALU.is_ge, fill=0.0, base=0, channel_multiplier=-sgn),
             reads=[tb], writes=[tb])
        tri.append((t, tb))
    bgt = P.sb([128, 4 * npairs], F32, "bg"); bgb = Buf("bg")
    P.dma("sp", bgt[:, :], bg[:, :], writes=[bgb])
    nbg = P.sb([128, 4 * npairs], F32, "nbg"); nbgb = Buf("nbg")
    P.op("dve", lambda h: h.tensor_scalar(out=nbg[:, :], in0=bgt[:, :], scalar1=-1.0, scalar2=None, op0=ALU.mult),
         reads=[bgb], writes=[nbgb])
    qsb = P.sb([128, 2, TT], BF16, "qT"); qb = Buf("qT")
    ksb = P.sb([128, 2, TT], BF16, "kT"); kb = Buf("kT")
    G = P.sb([128, NCH, 4], F32, "G"); Gb = Buf("G")
    NR = 4
    vr = [(P.sb([128, 512], BF16, "v"), Buf("v")) for _ in range(NR)]
    kr = [(P.sb([128, 256], BF16, "k"), Buf("k")) for _ in range(NR)]
    kpr = [(P.sb([128, 256], BF16, "kp"), Buf("kp")) for _ in range(NR)]
    spr = [(P.sb([128, 128], BF16, "sp"), Buf("sp")) for _ in range(2)]
    hr = [(P.sb([128, 512], F32, "h"), Buf("h")) for _ in range(2)]
    dsc = [(P.sb([128, 4], F32, "dsc"), Buf("dsc")) for _ in range(2)]
    ri = 0
    for pr in range(npairs):
        P.dma("sp", qsb[:, :, :], qT[pr].rearrange("(c p) t -> p c t", p=128), writes=[qb])
        P.dma("sp", ksb[:, :, :], kT[pr].rearrange("(c p) t -> p c t", p=128), writes=[kb])
        P.dma("sp", G[:, :, :], gg[pr].rearrange("(c p) f -> p c f", p=128), writes=[Gb], noncontig=True)
        chains = []
        for d in range(2):
            ig = P.sb([128, NCH], F32, "ig"); lf = P.sb([128, NCH], F32, "lf")
            E = P.sb([128, NCH], F32, "E"); R = P.sb([128, NCH], F32, "R"); GC = P.sb([128, NCH], F32, "GC")
            gb_ = Buf("gprep")
            c_i = pr * 4 + 2 * d
            P.op("dve", lambda h, ig=ig, d=d, c_i=c_i: h.tensor_scalar(out=ig[:, :], in0=G[:, :, 2 * d], scalar1=bgt[:, c_i:c_i + 1],
                                                                      scalar2=None, op0=ALU.add), reads=[Gb, bgb], writes=[gb_])
            P.op("act", lambda h, lf=lf, d=d, c_i=c_i: h.activation(out=lf[:, :], in_=G[:, :, 2 * d + 1], func=AF.Exp, scale=-1.0,
                                                                   bias=nbg[:, c_i + 1:c_i + 2]), reads=[Gb, nbgb], writes=[gb_])
            P.op("act", lambda h, lf=lf: h.activation(out=lf[:, :], in_=lf[:, :], func=AF.Ln, bias=1.0), reads=[gb_], writes=[gb_])
            P.op("dve", lambda h, lf=lf: h.tensor_scalar(out=lf[:, :], in0=lf[:, :], scalar1=-1.0, scalar2=None, op0=ALU.mult),
                 reads=[gb_], writes=[gb_])
            t, tb = tri[d]
            P.op("pe", lambda h, t=t, lf=lf: h.matmul(pst[:, 0, 0:NCH], lhsT=t[:, :], rhs=lf[:, :], start=True, stop=True),
                 reads=[tb, gb_], writes=[pbufs[0]])
            P.op("pe", lambda h, lf=lf: h.matmul(pst[:, 1, 0:NCH], lhsT=ones[:, :], rhs=lf[:, :], start=True, stop=True),
                 reads=[bon, gb_], writes=[pbufs[1]])
            P.op("act", lambda h, R=R: h.activation(out=R[:, :], in_=pst[:, 0, 0:NCH], func=AF.Exp), reads=[pbufs[0]], writes=[gb_])
            P.op("act", lambda h, GC=GC: h.activation(out=GC[:, :], in_=pst[:, 1, 0:NCH], func=AF.Exp), reads=[pbufs[1]], writes=[gb_])
            P.op("dve", lambda h, ig=ig: h.tensor_tensor(out=ig[:, :], in0=ig[:, :], in1=pst[:, 0, 0:NCH], op=ALU.subtract),
                 reads=[pbufs[0], gb_], writes=[gb_])
            P.op("act", lambda h, E=E, ig=ig: h.activation(out=E[:, :], in_=ig[:, :], func=AF.Exp, bias=-math.log(16.0)),
                 reads=[gb_], writes=[gb_])
            C = P.sb([128, 2, 512], F32, "C"); Cb = P.sb([128, 2, 512], BF16, "Cb")
            n_ = P.sb([128, 2], F32, "n"); nb_ = P.sb([128, 2], BF16, "nb")
            cb_ = Buf("C"); cbb = Buf("Cb")
            P.op("pool", lambda h, C=C: h.memset(C[:, :, :], 0.0), writes=[cb_])
            P.op("pool", lambda h, n_=n_: h.memset(n_[:, :], 0.0), writes=[cb_])
            P.op("pool", lambda h, Cb=Cb: h.memset(Cb[:, :, :], 0.0), writes=[cbb])
            P.op("pool", lambda h, nb_=nb_: h.memset(nb_[:, :], 0.0), writes=[cbb])
            order = list(range(NCTXC)) + list(range(NCTXC, NCH)) if d == 0 else \
                list(range(NCTXC - 1, -1, -1)) + list(range(NCH - 1, NCTXC - 1, -1))
            chains.append(dict(d=d, E=E, R=R, GC=GC, gb=gb_, C=C, Cb=Cb, n=n_, nb=nb_, cb=cb_, cbb=cbb, order=order,
                               out=(hf if d == 0 else hb)))
        for step in range(NCH):
            for ch in chains:
                c = ch["order"][step]
                d = ch["d"]
                base = d * 4
                SB, NB_, UB0, UB1 = base, base + 1, base + 2, base + 3
                E, R, GC, gb_ = ch["E"], ch["R"], ch["GC"], ch["gb"]
                C, Cb, n_, nb_, cb_, cbb = ch["C"], ch["Cb"], ch["n"], ch["nb"], ch["cb"], ch["cbb"]
                cols = slice(c * 128, (c + 1) * 128)
                v_, vb = vr[ri % NR]
                k_, kb_ = kr[ri % NR]
                kp, kpb = kpr[ri % NR]
                s_, sb_ = spr[ri % 2]
                h_, hb_ = hr[ri % 2]
                ds, dsb = dsc[ri % 2]
                ri += 1
                P.dma("sp", v_[:, :], vv[pr, c * 128:(c + 1) * 128, :], writes=[vb])
                P.dma("sp", k_[:, :], kk[pr, c * 128:(c + 1) * 128, :], writes=[kb_])
                if c >= NCTXC:
                    for dc in range(2):
                        P.op("pe", lambda h, dc=dc, cols=cols, SB=SB: h.matmul(
                            pst[:, SB, 0:128], lhsT=ksb[:, dc, cols], rhs=qsb[:, dc, cols], start=(dc == 0), stop=(dc == 1)),
                            reads=[kb, qb], writes=[pbufs[SB]], sig=(dc == 1), pe_acc=True)
                    t, tb = tri[d]
                    P.op("dve", lambda h, s_=s_, SB=SB, E=E, c=c, t=t: h.scalar_tensor_tensor(
                        out=s_[:, :], in0=pst[:, SB, 0:128], scalar=E[:, c:c + 1], in1=t[:, :], op0=ALU.mult, op1=ALU.mult),
                        reads=[pbufs[SB], gb_, tb], writes=[sb_])
                    P.op("pe", lambda h, s_=s_, v_=v_, NB_=NB_: h.matmul(pst[:, NB_, :], lhsT=s_[:, :], rhs=v_[:, :], start=True, stop=False),
                         reads=[sb_, vb], writes=[pbufs[NB_]], sig=False, pe_acc=True)
                    for dc in range(2):
                        P.op("pe", lambda h, dc=dc, cols=cols, Cb=Cb, NB_=NB_: h.matmul(
                            pst[:, NB_, :], lhsT=qsb[:, dc, cols], rhs=Cb[:, dc, :], start=False, stop=(dc == 1)),
                            reads=[qb, cbb], writes=[pbufs[NB_]], sig=(dc == 1), pe_acc=True)
                    P.op("pe", lambda h, s_=s_, SB=SB: h.matmul(pst[:, SB, 256:257], lhsT=s_[:, :], rhs=onesb[:, 0:1], start=True, stop=False),
                         reads=[sb_, bonb], writes=[pbufs[SB]], sig=False, pe_acc=True)
                    for dc in range(2):
                        P.op("pe", lambda h, dc=dc, cols=cols, nb_=nb_, SB=SB: h.matmul(
                            pst[:, SB, 256:257], lhsT=qsb[:, dc, cols], rhs=nb_[:, dc:dc + 1], start=False, stop=(dc == 1)),
                            reads=[qb, cbb], writes=[pbufs[SB]], sig=(dc == 1), pe_acc=True)
                    P.op("dve", lambda h, ds=ds, SB=SB, R=R, c=c: h.tensor_tensor(out=ds[:, 0:1], in0=pst[:, SB, 256:257], in1=R[:, c:c + 1], op=ALU.mult),
                         reads=[pbufs[SB], gb_], writes=[dsb])
                    P.op("act", lambda h, ds=ds: h.activation(out=ds[:, 1:2], in_=ds[:, 0:1], func=AF.Abs), reads=[dsb], writes=[dsb])
                    P.op("dve", lambda h, ds=ds: h.tensor_single_scalar(out=ds[:, 2:3], in_=ds[:, 1:2], scalar=1.0, op=ALU.max), reads=[dsb], writes=[dsb])
                    P.op("dve", lambda h, ds=ds: h.reciprocal(out=ds[:, 1:2], in_=ds[:, 2:3]), reads=[dsb], writes=[dsb])
                    P.op("dve", lambda h, ds=ds, R=R, c=c: h.tensor_tensor(out=ds[:, 3:4], in0=ds[:, 1:2], in1=R[:, c:c + 1], op=ALU.mult),
                         reads=[dsb, gb_], writes=[dsb])
                    P.op("act", lambda h, h_=h_, ds=ds, NB_=NB_: h.activation(out=h_[:, :], in_=pst[:, NB_, :], func=AF.Copy, scale=ds[:, 3:4]),
                         reads=[pbufs[NB_], dsb], writes=[hb_])
                    P.dma("sp", ch["out"][pr, (c - NCTXC) * 128:(c - NCTXC + 1) * 128, :], h_[:, :], reads=[hb_])
                P.op("dve", lambda h, kp=kp, k_=k_, E=E, c=c: h.tensor_scalar(out=kp[:, :], in0=k_[:, :], scalar1=E[:, c:c + 1], scalar2=None, op0=ALU.mult),
                     reads=[kb_, gb_], writes=[kpb])
                for dc, UB in enumerate((UB0, UB1)):
                    P.op("pe", lambda h, dc=dc, UB=UB, kp=kp, v_=v_: h.matmul(pst[:, UB, :], lhsT=kp[:, dc * 128:(dc + 1) * 128], rhs=v_[:, :], start=True, stop=True),
                         reads=[kpb, vb], writes=[pbufs[UB]], sig=False, pe_acc=True)
                for dc in range(2):
                    P.op("pe", lambda h, dc=dc, kp=kp, SB=SB: h.matmul(pst[:, SB, 300 + dc:301 + dc], lhsT=kp[:, dc * 128:(dc + 1) * 128], rhs=onesb[:, 0:1], start=True, stop=True),
                         reads=[kpb, bonb], writes=[pbufs[SB]], sig=(dc == 1), pe_acc=True)
                P.op("dve", lambda h, C=C, UB0=UB0: h.tensor_tensor(out=C[:, :, :], in0=C[:, :, :], in1=pst[:, UB0:UB0 + 2, :], op=ALU.add),
                     reads=[pbufs[UB0], pbufs[UB1], cbb], writes=[cb_])
                P.op("dve", lambda h, n_=n_, SB=SB: h.tensor_tensor(out=n_[:, :], in0=n_[:, :], in1=pst[:, SB, 300:302], op=ALU.add),
                     reads=[pbufs[SB], cbb], writes=[cb_])
                P.op("dve", lambda h, C=C, GC=GC, c=c: h.tensor_scalar(out=C[:, :, :], in0=C[:, :, :], scalar1=GC[:, c:c + 1], scalar2=None, op0=ALU.mult),
                     reads=[gb_], writes=[cb_])
                P.op("dve", lambda h, n_=n_, GC=GC, c=c: h.tensor_scalar(out=n_[:, :], in0=n_[:, :], scalar1=GC[:, c:c + 1], scalar2=None, op0=ALU.mult),
                     reads=[gb_], writes=[cb_])
                P.op("act", lambda h, C=C, Cb=Cb: h.activation(out=Cb[:, :, :], in_=C[:, :, :], func=AF.Copy), reads=[cb_], writes=[cbb])
                P.op("act", lambda h, n_=n_, nb_=nb_: h.activation(out=nb_[:, :], in_=n_[:, :], func=AF.Copy), reads=[cb_], writes=[cbb])
    P.run()


def phase_mpost(nc, hf, hb, hn_g, oT, hoT, NG=16):
    P = Phase(nc, "mpost")
    ident, bid = make_ident(P)
    pst = P.ps([128, 8, 512], F32, "ps")
    pbufs = [Buf(f"ps{i}") for i in range(8)]
    hg, hgb = load_cols(P, [hn_g], ident, bid, pst[:, 0, :], pbufs[0], "hg")
    xt = [(P.sb([128, D], F32, "xt"), Buf("xt")) for _ in range(2)]
    yt = [(P.sb([128, D], F32, "yt"), Buf("yt")) for _ in range(2)]
    sq = P.sb([128, 512], BF16, "sq"); bsq = Buf("sq")
    st = [(P.sb([128, 24], F32, "st"), Buf("st")) for _ in range(2)]
    ot = [(P.sb([128, KC, 128], BF16, "ot"), Buf("ot")) for _ in range(2)]
    hbk = [(P.sb([128, KC, 512], BF16, "hb"), Buf("hb")) for _ in range(1)]
    oTv = oT.rearrange("(c p) t -> p c t", p=128)
    hoTv = hoT.rearrange("(c p) t -> p c t", p=128)
    for g in range(NG):
        x_, xb = xt[g % 2]
        y_, yb = yt[g % 2]
        s_, sb_ = st[g % 2]
        o_, ob = ot[g % 2]
        P.dma("sp", x_[:, :], hf[g * 128:(g + 1) * 128, :], writes=[xb])
        P.dma("sp", y_[:, :], hb[g * 128:(g + 1) * 128, :], writes=[yb])
        P.dma("sp", o_[:, :, :], oTv[:, :, g * 128:(g + 1) * 128], writes=[ob])
        P.op("pool", lambda h, x_=x_, y_=y_: h.tensor_tensor(out=x_[:, :], in0=x_[:, :], in1=y_[:, :], op=ALU.add),
             reads=[xb, yb], writes=[xb])
        for hd in range(8):
            P.op("act", lambda h, x_=x_, s_=s_, hd=hd: h.activation(out=sq[:, :], in_=x_[:, hd * 512:(hd + 1) * 512], func=AF.Square,
                                                                    accum_out=s_[:, hd:hd + 1]), reads=[xb], writes=[bsq, sb_])
        rstd_ops(P, s_, sb_, 0, 8, 16, 1.0 / 512, n=8)
        for hd in range(8):
            P.op("act", lambda h, x_=x_, y_=y_, s_=s_, hd=hd: h.activation(
                out=y_[:, hd * 512:(hd + 1) * 512], in_=x_[:, hd * 512:(hd + 1) * 512], func=AF.Copy, scale=s_[:, 16 + hd:17 + hd]),
                reads=[xb, sb_], writes=[yb])
        h_, hbb = hbk[0]
        tcol = (g % 4) * 128
        for c in range(KC):
            bk = c // 4
            j = c % 4
            P.op("pe", lambda h, bk=bk, j=j, c=c, y_=y_: h.transpose(
                out=pst[:, bk, j * 128:(j + 1) * 128], in_=y_[:, c * 128:(c + 1) * 128], identity=ident[:, :]),
                reads=[yb, bid], writes=[pbufs[bk]], sig=(j == 3), pe_acc=True)
            if j == 3:
                for jj in range(4):
                    cc = bk * 4 + jj
                    P.op("dve", lambda h, bk=bk, jj=jj, cc=cc, h_=h_, o_=o_, tcol=tcol: h.scalar_tensor_tensor(
                        out=h_[:, cc, tcol:tcol + 128], in0=pst[:, bk, jj * 128:(jj + 1) * 128], scalar=hg[:, cc:cc + 1],
                        in1=o_[:, cc, :], op0=ALU.mult, op1=ALU.mult), reads=[pbufs[bk], hgb, ob], writes=[hbb])
        if g % 4 == 3:
            t0 = (g // 4) * 512
            P.dma("sp", hoTv[:, :, t0:t0 + 512], h_[:, :, :], reads=[hbb])
    P.run()


PASS_F_TE = [(0, 832, [(0, 512), (512, 320)]), (832, 832, [(0, 512), (512, 320)]), (1664, 768, [(0, 384), (384, 384)])]
PASS_F_TM = [(0, 768, [(0, 384), (384, 384)]), (768, 768, [(0, 384), (384, 384)]), (1536, 768, [(0, 512), (512, 256)])]
PASS_T_TM = [(0, 512), (512, 512), (1024, 512), (1536, 512), (2048, 256)]
PASS_F_X = [(0, 1024), (1024, 1024)]
PASS_T_X = [(0, 512), (512, 512), (1024, 512), (1536, 512)]
FFN_GROUPS = [[i * 128, DFF + i * 128] for i in range(DFF // 128)]
NBLK_D = [(n * 512, 512, 0, n * 512) for n in range(8)]


def _dt(nc, name, shape, dtype=F32, kind="Internal"):
    return nc.dram_tensor(name, list(shape), dtype, kind=kind).ap()


def build_l1(upto=99):
    nc = bass.Bass("TRN2", target_bir_lowering=False)
    I = lambda n, s, d=F32: _dt(nc, n, s, d, "ExternalInput")
    O = lambda n, s, d=F32: _dt(nc, n, s, d, "ExternalOutput")
    T = lambda n, s, d=F32: _dt(nc, n, s, d, "Internal")
    csel = I("csel", [2, D]); ada_w = I("ada_w", [2, D, 6 * D]); ada_b = I("ada_b", [2, 6 * D])
    xext = I("xext", [TE, D]); hmask = I("hmask", [128, 2]); norm_g = I("norm_g", [2, 4, D])
    w_in = I("ab_w_in", [D, ABIN]); acw = I("a_conv_w", [31, WA]); acb = I("a_conv_b", [WA])
    alg = I("a_ln_g", [WA]); alb = I("a_ln_b", [WA]); bcw = I("b_conv_w", [3, WA]); w_out = I("ab_w_out", [D, D])
    w_up = I("ffn_w_up", [D, 2 * DFF]); fcw = I("ffn_conv_w", [3, 2 * DFF]); w_dn = I("ffn_w_down", [DFF, D])
    m_in = I("m_w_in", [D, CIN])
    mods = O("mods", [2, 2, 6 * D]); x2 = O("x2", [TM, D])
    qT = O("qT", [2048, TM], BF16); kT = O("kT", [2048, TM], BF16); oT = O("oT", [D, TM], BF16)
    vk = O("vk", [TM, 6144], BF16); gates = O("gates", [TM, 32])
    hT1 = T("hT1", [D, TE], BF16); PT = T("PT", [ABIN, TE]); catT = T("catT", [D, TM], BF16)
    y1 = T("y1", [TM, D]); x1 = T("x1", [TM, D]); h2T = T("h2T", [D, TM], BF16)
    actT = T("actT", [DFF, TM], BF16); y2 = T("y2", [TM, D]); h3T = T("h3T", [D, TM], BF16)
    m = lambda l, r, i: mods[l, r, i * D:(i + 1) * D]
    phase_ada(nc, csel, ada_w, ada_b, mods)
    if upto < 1:
        return nc
    phase_prenorm(nc, "pn1", xext, [0] * 17 + [1] * 2,
                  ab_rows={v: (norm_g[0, 0], m(0, v, 1), m(0, v, 0)) for v in (0, 1)}, hT_out=hT1)
    if upto < 2:
        return nc
    hook, hinit = make_store_hook(lambda gi: PT[gi * 128:(gi + 1) * 128, :])
    phase_gemm_f(nc, "g1", hT1, w_in, KC, PASS_F_TE, [[i * 128] for i in range(ABIN // 128)], hook, hinit)
    if upto < 3:
        return nc
    phase_abmix_a(nc, PT, acw, acb, alg, alb, catT)
    if upto < 4:
        return nc
    phase_abmix_b(nc, PT, bcw, hmask, catT)
    if upto < 5:
        return nc
    phase_gemm_t(nc, "g2", catT, w_out, KC, PASS_T_TM, NBLK_D, [y1])
    if upto < 6:
        return nc
    grp = [0] * 16 + [1] * 2
    xrows = [64 + g * 128 for g in range(16)] + [64 + NX + 64, 64 + NX + 64 + 128]
    phase_prenorm(nc, "pn2", xext, grp, y_in=y1, gg_rows={v: (m(0, v, 2), norm_g[0, 1]) for v in (0, 1)}, x_out=x1,
                  ab_rows={v: (norm_g[0, 2], m(0, v, 4), m(0, v, 3)) for v in (0, 1)}, hT_out=h2T, xin_rows=xrows)
    if upto < 7:
        return nc
    hook, hinit = make_ffn_hook(fcw, actT, lambda pi, c0: 256 if (pi == 2 and c0 >= 512) else 64)
    phase_gemm_f(nc, "g3", h2T, w_up, KC, PASS_F_TM, FFN_GROUPS, hook, hinit)
    if upto < 8:
        return nc
    phase_gemm_t(nc, "g4", actT, w_dn, DFF // 128, PASS_T_TM, NBLK_D, [y2])
    if upto < 9:
        return nc
    phase_prenorm(nc, "pn3", x1, grp, y_in=y2, gg_rows={v: (m(0, v, 5), norm_g[0, 3]) for v in (0, 1)}, x_out=x2,
                  ab_rows={v: (norm_g[1, 0], m(1, v, 1), m(1, v, 0)) for v in (0, 1)}, hT_out=h3T)

    if upto < 10:
        return nc

    def dst(gi):
        if gi < 16:
            return qT[gi * 128:(gi + 1) * 128, :]
        if gi < 32:
            return kT[(gi - 16) * 128:(gi - 15) * 128, :]
        return oT[(gi - 32) * 128:(gi - 31) * 128, :]
    hook, hinit = make_store_hook(dst, func=lambda gi: (AF.Sigmoid if gi >= 32 else AF.Copy), dtype=BF16)
    groups = [[i * 128] for i in range(32)] + [[8192 + i * 128] for i in range(32)]
    phase_gemm_f(nc, "g5f", h3T, m_in, KC, PASS_F_TM, groups, hook, hinit)
    nb = [(4096 + n * 512, 512, 0, n * 512) for n in range(8)] + [(2048 + n * 512, 512, 0, 4096 + n * 512) for n in range(4)]
    phase_gemm_t(nc, "g5t", h3T, m_in, KC, PASS_T_TM, nb, [vk])
    phase_gemm_t(nc, "g5g", h3T, m_in, KC, PASS_T_TM, [(12288, 32, 0, 0)], [gates])
    return nc


def build_l2():
    nc = bass.Bass("TRN2", target_bir_lowering=False)
    I = lambda n, s, d=F32: _dt(nc, n, s, d, "ExternalInput")
    O = lambda n, s, d=F32: _dt(nc, n, s, d, "ExternalOutput")
    TT = 66 * 128
    qT = I("qT", [2, 256, TT], BF16); kT = I("kT", [2, 256, TT], BF16); kk = I("kk", [2, TT, 256], BF16)
    vv = I("vv", [2, TT, 512], BF16); gg = I("gg", [2, TT, 4]); bg = I("bg", [128, 8])
    hf = O("hf", [2, 8192, 512]); hb = O("hb", [2, 8192, 512])
    phase_scan(nc, qT, kT, kk, vv, gg, bg, hf, hb)
    return nc


def build_l3():
    nc = bass.Bass("TRN2", target_bir_lowering=False)
    I = lambda n, s, d=F32: _dt(nc, n, s, d, "ExternalInput")
    O = lambda n, s, d=F32: _dt(nc, n, s, d, "ExternalOutput")
    T = lambda n, s, d=F32: _dt(nc, n, s, d, "Internal")
    hf = I("hf", [NX, D]); hb = I("hb", [NX, D]); oT = I("oT", [D, NX], BF16); hn_g = I("hn_g", [D])
    m_out = I("m_w_out", [D, D]); x2 = I("x2", [NX, D]); mods = I("mods", [2, 2, 6 * D]); norm_g = I("norm_g", [2, 4, D])
    w_up = I("ffn_w_up", [D, 2 * DFF]); fcw = I("ffn_conv_w", [3, 2 * DFF]); w_dn = I("ffn_w_down", [DFF, D])
    out = O("out", [NX, D])
    hoT = T("hoT", [D, NX], BF16); y3 = T("y3", [NX, D]); x3 = T("x3", [NX, D]); h4T = T("h4T", [D, NX], BF16)
    actT = T("actT", [DFF, NX], BF16); y4 = T("y4", [NX, D])
    m = lambda l, r, i: mods[l, r, i * D:(i + 1) * D]
    phase_mpost(nc, hf, hb, hn_g, oT, hoT)
    phase_gemm_t(nc, "g6", hoT, m_out, KC, PASS_T_X, NBLK_D, [y3])
    grp = [0] * 16
    phase_prenorm(nc, "pn4", x2, grp, y_in=y3, gg_rows={0: (m(1, 0, 2), norm_g[1, 1])}, x_out=x3,
                  ab_rows={0: (norm_g[1, 2], m(1, 0, 4), m(1, 0, 3))}, hT_out=h4T)
    hook, hinit = make_ffn_hook(fcw, actT, lambda pi, c0: 64)
    phase_gemm_f(nc, "g7", h4T, w_up, KC, PASS_F_X, FFN_GROUPS, hook, hinit)
    phase_gemm_t(nc, "g8", actT, w_dn, DFF // 128, PASS_T_X, NBLK_D, [y4])
    phase_prenorm(nc, "fin", x3, grp, y_in=y4, gg_rows={0: (m(1, 0, 5), norm_g[1, 3])}, x_out=out)
    return nc


def phase_mods_gather(nc, msend, mrecv, mods):
    P = Phase(nc, "mg")
    rec = P.pool.get("cc")
    sem = rec[0]
    rec[1] += 1
    val = rec[1]
    P.q["pool"].append(lambda h: h.collective_compute(
        "AllGather", ALU.bypass, replica_groups=[[0, 1, 2, 3], [4, 5, 6, 7]],
        ins=[msend.rearrange("l r n -> (l r) n")], outs=[mrecv[:, :]]).then_inc(sem, 1))
    P.q["pool"].append(lambda h: h.wait_ge(sem, val))
    P.pool.put(rec)
    P.run()
    P = Phase(nc, "mg2")
    rv = mrecv.rearrange("(j q) (i k) -> q i j k", q=4, k=1024)
    for l in range(2):
        for r in range(2):
            b = Buf("mcopy")
            P.dma("sp", mods[l, r].rearrange("(i j k) -> i j k", j=4, k=1024), rv[l * 2 + r], writes=[b])
    P.run()


def build_all():
    nc = bass.Bass("TRN2", target_bir_lowering=False)
    I = lambda n, s, d=F32: _dt(nc, n, s, d, "ExternalInput")
    O = lambda n, s, d=F32: _dt(nc, n, s, d, "ExternalOutput")
    T = lambda n, s, d=F32: _dt(nc, n, s, d, "Internal")
    csel = I("csel", [2, D]); ada_w = I("ada_w", [2, D, 6 * D // 4]); ada_b = I("ada_b", [2, 6 * D // 4])
    xext = I("xext", [TE, D]); hmask = I("hmask", [128, 2]); norm_g = I("norm_g", [2, 4, D])
    w_in = I("ab_w_in", [D, ABIN]); acw = I("a_conv_w", [31, WA]); acb = I("a_conv_b", [WA])
    alg = I("a_ln_g", [WA]); alb = I("a_ln_b", [WA]); bcw = I("b_conv_w", [3, WA]); w_out = I("ab_w_out", [D, D])
    w_up = I("ffn_w_up", [2, D, 2 * DFF]); fcw = I("ffn_conv_w", [2, 3, 2 * DFF]); w_dn = I("ffn_w_down", [2, DFF, D])
    m_in = I("m_w_in", [D, CIN]); bgx = I("bgx", [128, NCHT * 32]); flags = I("flags", [128, 8])
    hn_g = I("hn_g", [D]); m_out = I("m_w_out", [D, D])
    out = O("out", [NX, D])
    mods = T("mods", [2, 2, 6 * D]); x2 = T("x2", [TM, D])
    qT = T("qT", [2048, TM], BF16); kT = T("kT", [2048, TM], BF16); oT = T("oT", [D, TM], BF16)
    vk = T("vk", [TM, 6144], BF16); gates = T("gates", [TM, 32])
    hT1 = T("hT1", [D, TE], BF16); PT = T("PT", [ABIN, TE]); catT = T("catT", [D, TM], BF16)
    y1 = T("y1", [TM, D]); x1 = T("x1", [TM, D]); h2T = T("h2T", [D, TM], BF16)
    actT = T("actT", [DFF, TM], BF16); y2 = T("y2", [TM, D]); h3T = T("h3T", [D, TM], BF16)
    prep = T("prep", [2, 4, 128, NCHT * 8]); sctx = T("sctx", [16, 128, SROW]); send = T("send", [16, 128, SROW])
    recv = T("recv", [16, 4 * 128, SROW]); hf = T("hf", [NX, D]); hb = T("hb", [NX, D])
    hoT = T("hoT", [D, NX], BF16); y3 = T("y3", [NX, D]); x3 = T("x3", [NX, D]); h4T = T("h4T", [D, NX], BF16)
    actT1 = T("actT1", [DFF, NX], BF16); y4 = T("y4", [NX, D])
    m = lambda l, r, i: mods[l, r, i * D:(i + 1) * D]
    msend = T("msend", [2, 2, 6 * D // 4]); mrecv = T("mrecv", [16, 6 * D // 4])
    phase_ada(nc, csel, ada_w, ada_b, msend, NM=6 * D // 4)
    phase_mods_gather(nc, msend, mrecv, mods)
    phase_prenorm(nc, "pn1", xext, [0] * 17 + [1] * 2,
                  ab_rows={v: (norm_g[0, 0], m(0, v, 1), m(0, v, 0)) for v in (0, 1)}, hT_out=hT1)
    hook, hinit = make_store_hook(lambda gi: PT[gi * 128:(gi + 1) * 128, :])
    phase_gemm_f(nc, "g1", hT1, w_in, KC, PASS_F_TE, [[i * 128] for i in range(ABIN // 128)], hook, hinit)
    phase_abmix_a(nc, PT, acw, acb, alg, alb, catT)
    phase_abmix_b(nc, PT, bcw, hmask, catT)
    phase_gemm_t(nc, "g2", catT, w_out, KC, PASS_T_TM, NBLK_D, [y1])
    grp = [0] * 16 + [1] * 2
    xrows = [64 + g * 128 for g in range(16)] + [64 + NX + 64, 64 + NX + 64 + 128]
    phase_prenorm(nc, "pn2", xext, grp, y_in=y1, gg_rows={v: (m(0, v, 2), norm_g[0, 1]) for v in (0, 1)}, x_out=x1,
                  ab_rows={v: (norm_g[0, 2], m(0, v, 4), m(0, v, 3)) for v in (0, 1)}, hT_out=h2T, xin_rows=xrows)
    hook, hinit = make_ffn_hook(fcw[0], actT, lambda pi, c0: 256 if (pi == 2 and c0 >= 512) else 64)
    phase_gemm_f(nc, "g3", h2T, w_up[0], KC, PASS_F_TM, FFN_GROUPS, hook, hinit)
    phase_gemm_t(nc, "g4", actT, w_dn[0], DFF // 128, PASS_T_TM, NBLK_D, [y2])
    phase_prenorm(nc, "pn3", x1, grp, y_in=y2, gg_rows={v: (m(0, v, 5), norm_g[0, 3]) for v in (0, 1)}, x_out=x2,
                  ab_rows={v: (norm_g[1, 0], m(1, v, 1), m(1, v, 0)) for v in (0, 1)}, hT_out=h3T)

    def dst(gi):
        if gi < 16:
            return qT[gi * 128:(gi + 1) * 128, :]
        if gi < 32:
            return kT[(gi - 16) * 128:(gi - 15) * 128, :]
        return oT[(gi - 32) * 128:(gi - 31) * 128, :]
    hook, hinit = make_store_hook(dst, func=lambda gi: (AF.Sigmoid if gi >= 32 else AF.Copy), dtype=BF16)
    groups = [[i * 128] for i in range(32)] + [[8192 + i * 128] for i in range(32)]
    phase_gemm_f(nc, "g5f", h3T, m_in, KC, PASS_F_TM, groups, hook, hinit)
    nb = [(4096 + n * 512, 512, 0, n * 512) for n in range(8)] + [(2048 + n * 512, 512, 0, 4096 + n * 512) for n in range(4)]
    phase_gemm_t(nc, "g5t", h3T, m_in, KC, PASS_T_TM, nb, [vk])
    phase_gemm_t(nc, "g5g", h3T, m_in, KC, PASS_T_TM, [(12288, 32, 0, 0)], [gates])
    phase_scan1(nc, vk, gates, bgx, prep, sctx, send)
    phase_allgather(nc, send, recv)
    phase_scan2(nc, qT, kT, vk, prep, sctx, recv, flags, hf, hb)
    phase_mpost(nc, hf, hb, hn_g, oT[:, 0:NX], hoT)
    phase_gemm_t(nc, "g6", hoT, m_out, KC, PASS_T_X, NBLK_D, [y3])
    grp1 = [0] * 16
    phase_prenorm(nc, "pn4", x2[0:NX, :], grp1, y_in=y3, gg_rows={0: (m(1, 0, 2), norm_g[1, 1])}, x_out=x3,
                  ab_rows={0: (norm_g[1, 2], m(1, 0, 4), m(1, 0, 3))}, hT_out=h4T)
    hook, hinit = make_ffn_hook(fcw[1], actT1, lambda pi, c0: 64)
    phase_gemm_f(nc, "g7", h4T, w_up[1], KC, PASS_F_X, FFN_GROUPS, hook, hinit)
    phase_gemm_t(nc, "g8", actT1, w_dn[1], DFF // 128, PASS_T_X, NBLK_D, [y4])
    phase_prenorm(nc, "fin", x3, grp1, y_in=y4, gg_rows={0: (m(1, 0, 5), norm_g[1, 3])}, x_out=out)
    return nc


def kernel(x, c, ctx, c_ctx, ada_w, ada_b, norm_g, ab_w_in, a_conv_w, a_conv_b, a_ln_g, a_ln_b,
           b_conv_w, ab_w_out, m_w_in, m_b_gates, m_hn_g, m_w_out, ffn_w_up, ffn_conv_w, ffn_w_down):
    f = lambda a: np.ascontiguousarray(np.asarray(a))
    x, c, ctx, c_ctx = f(x), f(c), f(ctx), f(c_ctx)
    cores = list(range(NCORES))
    mbg = f(m_b_gates)[0].astype(np.float32)
    bgx = np.ascontiguousarray(np.broadcast_to(np.tile(mbg, NCHT)[None, :], (128, NCHT * 32)))
    ada_w, ada_b = np.asarray(ada_w), np.asarray(ada_b)
    acols = [np.concatenate([np.arange(i * D + jj * 1024, i * D + (jj + 1) * 1024) for i in range(6)]) for jj in range(4)]
    ada_ws = [f(ada_w[:, :, cc]) for cc in acols]
    ada_bs = [f(ada_b[:, cc]) for cc in acols]
    shared = dict(norm_g=f(norm_g), ab_w_in=f(ab_w_in[0]), a_conv_w=f(a_conv_w[0]),
                  a_conv_b=f(a_conv_b[0]), a_ln_g=f(a_ln_g[0]), a_ln_b=f(a_ln_b[0]), b_conv_w=f(b_conv_w[0]),
                  ab_w_out=f(ab_w_out[0]), ffn_w_up=f(ffn_w_up), ffn_conv_w=f(ffn_conv_w), ffn_w_down=f(ffn_w_down),
                  m_w_in=f(m_w_in[0]), bgx=bgx, hn_g=f(m_hn_g[0]), m_w_out=f(m_w_out[0]))
    ins = []
    for r in cores:
        b, j = r // 4, r % 4
        xe = np.zeros((TE, D), np.float32)
        t0 = j * NX
        if j > 0:
            xe[0:64] = x[b, t0 - 64:t0]
        xe[64:64 + NX] = x[b, t0:t0 + NX]
        if j < 3:
            xe[64 + NX:128 + NX] = x[b, t0 + NX:t0 + NX + 64]
        xe[128 + NX:] = ctx[b]
        hm = np.zeros((128, 2), np.float32)
        hm[:, 0] = 1.0 if j > 0 else 0.0
        hm[:, 1] = 1.0 if j < 3 else 0.0
        fl = np.zeros((128, 8), np.float32)
        for i in range(4):
            fl[:, i] = 1.0 if i < j else 0.0
            fl[:, 4 + i] = 1.0 if i > j else 0.0
        d = dict(shared)
        d.update(csel=np.stack([c[b], c_ctx]), xext=xe, hmask=hm, flags=fl, ada_w=ada_ws[j], ada_b=ada_bs[j])
        ins.append(d)
    res = run_bass_kernel_spmd(build_all(), ins, core_ids=cores).results
    out = np.zeros((2, 8192, D), np.float32)
    for r in cores:
        b, j = r // 4, r % 4
        out[b, j * NX:(j + 1) * NX] = res[r]["out"]
    return out


NCHT = TM // 128
SROW = 1032


def _tri_consts(P):
    ones = P.sb([128, 128], F32, "ones"); bon = Buf("ones")
    P.op("pool", lambda h: h.memset(ones[:, :], 1.0), writes=[bon])
    onesb = P.sb([128, 2], BF16, "onesb"); bonb = Buf("onesb")
    P.op("pool", lambda h: h.memset(onesb[:, :], 1.0), writes=[bonb])
    tri = []
    for d in range(2):
        t = P.sb([128, 128], F32, f"tri{d}"); tb = Buf(f"tri{d}")
        P.op("pool", lambda h, t=t: h.memset(t[:, :], 1.0), writes=[tb])
        sgn = 1 if d == 0 else -1
        P.op("pool", lambda h, t=t, sgn=sgn: h.affine_select(out=t[:, :], in_=t[:, :], pattern=[[sgn, 128]],
                                                            compare_op=ALU.is_ge, fill=0.0, base=0, channel_multiplier=-sgn),
             reads=[tb], writes=[tb])
        tri.append((t, tb))
    return ones, bon, onesb, bonb, tri


def _state_update(P, pst, pbufs, UB0, SBK, kp, kpb, k_ap, kb_, v_, vb, e_ap, gc_ap, gb_, C, n_, cb_, extra_reads, onesb, bonb):
    P.op("dve", lambda h: h.tensor_scalar(out=kp[:, :], in0=k_ap, scalar1=e_ap, scalar2=None, op0=ALU.mult),
         reads=[kb_, gb_] + extra_reads, writes=[kpb])
    for dc in range(2):
        P.op("pe", lambda h, dc=dc: h.matmul(pst[:, UB0 + dc, :], lhsT=kp[:, dc * 128:(dc + 1) * 128], rhs=v_, start=True, stop=True),
             reads=[kpb, vb], writes=[pbufs[UB0 + dc]], sig=False, pe_acc=True)
    for dc in range(2):
        P.op("pe", lambda h, dc=dc: h.matmul(pst[:, SBK, 300 + dc:301 + dc], lhsT=kp[:, dc * 128:(dc + 1) * 128], rhs=onesb[:, 0:1], start=True, stop=True),
             reads=[kpb, bonb], writes=[pbufs[SBK]], sig=(dc == 1), pe_acc=True)
    P.op("dve", lambda h: h.scalar_tensor_tensor(out=C, in0=C, scalar=gc_ap, in1=pst[:, UB0:UB0 + 2, :], op0=ALU.mult, op1=ALU.add),
         reads=[pbufs[UB0], pbufs[UB0 + 1], gb_] + extra_reads, writes=[cb_])
    P.op("dve", lambda h: h.scalar_tensor_tensor(out=n_, in0=n_, scalar=gc_ap, in1=pst[:, SBK, 300:302], op0=ALU.mult, op1=ALU.add),
         reads=[pbufs[SBK], gb_] + extra_reads, writes=[cb_])


def phase_scan1(nc, vk, gates, bgx, prep, sctx, send):
    import math
    P = Phase(nc, "scan1")
    pst = P.ps([128, 8, 512], F32, "ps")
    pbufs = [Buf(f"ps{i}") for i in range(8)]
    ones, bon, onesb, bonb, tri = _tri_consts(P)
    NC8 = NCHT * 8
    G = P.sb([128, NCHT, 32], F32, "G"); Gb = Buf("G")
    bgt = P.sb([128, NCHT, 32], F32, "bgx"); bgb = Buf("bgx")
    P.dma("sp", G[:, :, :], gates.rearrange("(c p) f -> p c f", p=128), writes=[Gb], noncontig=True)
    P.dma("sp", bgt[:, :, :], bgx.rearrange("p (c f) -> p c f", f=32), writes=[bgb])
    P.op("dve", lambda h: h.tensor_tensor(out=G[:, :, :], in0=G[:, :, :], in1=bgt[:, :, :], op=ALU.add), reads=[Gb, bgb], writes=[Gb])
    ERG = []
    for d in range(2):
        ig = P.sb([128, NCHT, 8], F32, "ig"); lf = P.sb([128, NCHT, 8], F32, "lf")
        E = P.sb([128, NCHT, 8], F32, "E"); R = P.sb([128, NCHT, 8], F32, "R"); GC = P.sb([128, NCHT, 8], F32, "GC")
        gx = P.sb([128, 8], F32, "gx")
        gb_ = Buf("gprep")
        P.op("act", lambda h, lf=lf, d=d: h.activation(out=lf[:, :, :], in_=G[:, :, d * 16 + 8:d * 16 + 16], func=AF.Exp, scale=-1.0),
             reads=[Gb], writes=[gb_])
        P.op("act", lambda h, lf=lf: h.activation(out=lf[:, :, :], in_=lf[:, :, :], func=AF.Ln, bias=1.0), reads=[gb_], writes=[gb_])
        P.op("dve", lambda h, lf=lf: h.tensor_scalar(out=lf[:, :, :], in0=lf[:, :, :], scalar1=-1.0, scalar2=None, op0=ALU.mult),
             reads=[gb_], writes=[gb_])
        t, tb = tri[d]
        lff = lf[:, :, :].rearrange("p c h -> p (c h)")
        P.op("pe", lambda h, t=t, lff=lff: h.matmul(pst[:, 0, 0:NC8], lhsT=t[:, :], rhs=lff, start=True, stop=True),
             reads=[tb, gb_], writes=[pbufs[0]])
        P.op("pe", lambda h, lff=lff: h.matmul(pst[:, 1, 0:NC8], lhsT=ones[:, :], rhs=lff, start=True, stop=True),
             reads=[bon, gb_], writes=[pbufs[1]])
        flat = lambda T_: T_[:, :, :].rearrange("p c h -> p (c h)")
        P.op("act", lambda h, R=R: h.activation(out=flat(R), in_=pst[:, 0, 0:NC8], func=AF.Exp), reads=[pbufs[0]], writes=[gb_])
        P.op("act", lambda h, GC=GC: h.activation(out=flat(GC), in_=pst[:, 1, 0:NC8], func=AF.Exp), reads=[pbufs[1]], writes=[gb_])
        P.op("dve", lambda h, ig=ig, d=d: h.tensor_tensor(out=ig[:, :, :], in0=G[:, :, d * 16:d * 16 + 8],
                                                        in1=pst[:, 0, 0:NC8].rearrange("p (c h) -> p c h", h=8), op=ALU.subtract),
             reads=[pbufs[0], Gb], writes=[gb_])
        P.op("act", lambda h, E=E, ig=ig: h.activation(out=E[:, :, :], in_=ig[:, :, :], func=AF.Exp, bias=-math.log(16.0)),
             reads=[gb_], writes=[gb_])
        P.op("dve", lambda h, gx=gx: h.reduce_sum(out=gx[:, :], in_=pst[:, 1, 0:16 * 8].rearrange("p (c h) -> p h c", h=8), axis=AX.X),
             reads=[pbufs[1]], writes=[gb_])
        P.op("act", lambda h, gx=gx: h.activation(out=gx[:, :], in_=gx[:, :], func=AF.Exp), reads=[gb_], writes=[gb_])
        EG = P.sb([128, NCHT, 8], F32, "EG")
        P.op("dve", lambda h, EG=EG, E=E, GC=GC: h.tensor_tensor(out=EG[:, :, :], in0=E[:, :, :], in1=GC[:, :, :], op=ALU.mult),
             reads=[gb_], writes=[gb_])
        for j, T_ in enumerate((E, R, GC, EG)):
            P.dma("sp", prep[d, j], flat(T_), reads=[gb_])
        ERG.append((EG, R, GC, gx, gb_))
    Call = P.sb([128, 16, 1026], F32, "Call"); cbs = [Buf(f"C{i}") for i in range(16)]
    rows = [(P.sb([128, 6144], BF16, "vkrow"), Buf("vkrow")) for _ in range(3)]
    kpr = [(P.sb([128, 256], BF16, "kp"), Buf("kp")) for _ in range(3)]
    ri = 0
    ki = 0
    ui = 0
    for which in ("ctx", "x"):
        for ch in range(16):
            P.op("pool", lambda h, ch=ch: h.memset(Call[:, ch, :], 0.0), writes=[cbs[ch]])
        orders = ([16, 17], [17, 16]) if which == "ctx" else (list(range(16)), list(range(15, -1, -1)))
        for step in range(len(orders[0])):
            for d in range(2):
                c = orders[d][step]
                E, R, GC, gx, gb_ = ERG[d]
                row, rb = rows[ri % 3]
                ri += 1
                P.dma("sp", row[:, :], vk[c * 128:(c + 1) * 128, :], writes=[rb])
                for hd in range(8):
                    ch = d * 8 + hd
                    kp, kpb = kpr[ki % 3]
                    ki += 1
                    UB0 = 2 + 2 * (ui % 3)
                    SBK = ui % 2
                    ui += 1
                    Cc = Call[:, ch, 0:1024].rearrange("p (a b) -> p a b", a=2)
                    nn = Call[:, ch, 1024:1026]
                    _state_update(P, pst, pbufs, UB0, SBK, kp, kpb, row[:, 4096 + hd * 256:4096 + (hd + 1) * 256], rb,
                                  row[:, hd * 512:(hd + 1) * 512], rb, E[:, c, hd:hd + 1], GC[:, c, hd:hd + 1], gb_,
                                  Cc, nn, cbs[ch], [], onesb, bonb)
        dst = sctx if which == "ctx" else send
        dv = dst.rearrange("ch p f -> p ch f")
        if which == "x":
            for d in range(2):
                gx, gb_ = ERG[d][3], ERG[d][4]
                P.dma("sp", dv[:, d * 8:(d + 1) * 8, 1026:1027], gx[:, :].rearrange("p (h o) -> p h o", o=1), reads=[gb_], noncontig=True)
        P.dma("sp", dv[:, :, 0:1026], Call[:, :, :], reads=cbs)
    P.run()


def phase_allgather(nc, send, recv):
    P = Phase(nc, "ag")
    rec = P.pool.get("cc")
    sem = rec[0]
    for ch in range(16):
        rec[1] += 1
        P.q["pool"].append(lambda h, ch=ch: h.collective_compute(
            "AllGather", ALU.bypass, replica_groups=[[0, 1, 2, 3], [4, 5, 6, 7]], ins=[send[ch]], outs=[recv[ch]]).then_inc(sem, 1))
    val = rec[1]
    P.q["pool"].append(lambda h: h.wait_ge(sem, val))
    P.pool.put(rec)
    P.run()


def phase_scan2(nc, qT, kT, vk, prep, sctx, recv, flags, hf, hb):
    P = Phase(nc, "scan2")
    pst = P.ps([128, 8, 512], F32, "ps")
    pbufs = [Buf(f"ps{i}") for i in range(8)]
    ones, bon, onesb, bonb, tri = _tri_consts(P)
    fl = P.sb([128, 8], F32, "flags"); flb = Buf("flags")
    P.dma("sp", fl[:, :], flags[:, :], writes=[flb])
    ERG = []
    for d in range(2):
        ts = []
        for j in range(4):
            T_ = P.sb([128, NCHT, 8], F32, "erg")
            bj = Buf("ergl")
            P.dma("sp", T_[:, :, :].rearrange("p c h -> p (c h)"), prep[d, j], writes=[bj])
            ts.append((T_, bj))
        ERG.append(ts)
    qsb = [(P.sb([128, 2, TM], BF16, "qT"), Buf("qT")) for _ in range(2)]
    ksb = [(P.sb([128, 2, TM], BF16, "kT"), Buf("kT")) for _ in range(2)]
    NR = 4
    vr = [(P.sb([128, 512], BF16, "v"), Buf("v")) for _ in range(NR)]
    kr = [(P.sb([128, 256], BF16, "k"), Buf("k")) for _ in range(NR)]
    kpr = [(P.sb([128, 256], BF16, "kp"), Buf("kp")) for _ in range(NR)]
    spr = [(P.sb([128, 128], BF16, "sp"), Buf("sp")) for _ in range(2)]
    hr = [(P.sb([128, 512], F32, "h"), Buf("h")) for _ in range(2)]
    dsc = [(P.sb([128, 4], F32, "dsc"), Buf("dsc")) for _ in range(2)]
    Lt = [(P.sb([128, SROW], F32, "L"), Buf("L")) for _ in range(2)]
    St = [(P.sb([128, 1026], F32, "S"), Buf("S")) for _ in range(4)]
    Cbt = [(P.sb([128, 1026], BF16, "Cb"), Buf("Cb")) for _ in range(4)]
    av = [(P.sb([128, 4], F32, "av"), Buf("av")) for _ in range(2)]
    sv = sctx.rearrange("ch p f -> p ch f")
    rv = recv.rearrange("ch (r p) f -> p r ch f", p=128)
    ri = 0
    li = 0
    for hd in range(8):
        q_, qb = qsb[hd % 2]
        k_T, kb = ksb[hd % 2]
        P.dma("sp", q_[:, :, :], qT[hd * 256:(hd + 1) * 256, :].rearrange("(c p) t -> p c t", p=128), writes=[qb])
        P.dma("sp", k_T[:, :, :], kT[hd * 256:(hd + 1) * 256, :].rearrange("(c p) t -> p c t", p=128), writes=[kb])
        chains = []
        for d in range(2):
            ch = d * 8 + hd
            S, Sb = St[(hd % 2) * 2 + d]
            Cb, Cbb = Cbt[(hd % 2) * 2 + d]
            P.dma("sp", S[:, :], sv[:, ch, 0:1026], writes=[Sb])
            for i in (range(4) if d == 0 else range(3, -1, -1)):
                L, Lb = Lt[li % 2]
                a_, ab = av[li % 2]
                li += 1
                P.dma("sp", L[:, 0:1027], rv[:, i, ch, 0:1027], writes=[Lb])
                fcol = fl[:, d * 4 + i:d * 4 + i + 1]
                P.op("dve", lambda h, a_=a_, L=L: h.tensor_scalar(out=a_[:, 0:1], in0=L[:, 1026:1027], scalar1=-1.0, scalar2=None, op0=ALU.add),
                     reads=[Lb], writes=[ab])
                P.op("dve", lambda h, a_=a_, fcol=fcol: h.tensor_scalar(out=a_[:, 1:2], in0=a_[:, 0:1], scalar1=fcol, scalar2=1.0, op0=ALU.mult, op1=ALU.add),
                     reads=[ab, flb], writes=[ab])
                P.op("dve", lambda h, L=L, fcol=fcol: h.tensor_scalar(out=L[:, 0:1026], in0=L[:, 0:1026], scalar1=fcol, scalar2=None, op0=ALU.mult),
                     reads=[Lb, flb], writes=[Lb])
                P.op("dve", lambda h, S=S, L=L, a_=a_: h.scalar_tensor_tensor(out=S[:, :], in0=S[:, :], scalar=a_[:, 1:2], in1=L[:, 0:1026],
                                                                            op0=ALU.mult, op1=ALU.add), reads=[Sb, Lb, ab], writes=[Sb])
            P.op("act", lambda h, S=S, Cb=Cb: h.activation(out=Cb[:, :], in_=S[:, :], func=AF.Copy), reads=[Sb], writes=[Cbb])
            order = list(range(16)) if d == 0 else list(range(15, -1, -1))
            chains.append(dict(d=d, S=S, Sb=Sb, Cb=Cb, Cbb=Cbb, order=order, out=(hf if d == 0 else hb)))
        for step in range(16):
            for chn in chains:
                d = chn["d"]
                c = chn["order"][step]
                (E, Eb), (R, Rb), (GC, GCb), (EG, EGb) = ERG[d]
                S, Sb, Cb, Cbb = chn["S"], chn["Sb"], chn["Cb"], chn["Cbb"]
                base = d * 4
                SB, NB_, UB0 = base, base + 1, base + 2
                cols = slice(c * 128, (c + 1) * 128)
                v_, vb = vr[ri % NR]
                k_, kb_ = kr[ri % NR]
                kp, kpb = kpr[ri % NR]
                s_, sb_ = spr[ri % 2]
                h_, hb_ = hr[ri % 2]
                ds, dsb = dsc[ri % 2]
                ri += 1
                P.dma("sp", v_[:, :], vk[c * 128:(c + 1) * 128, hd * 512:(hd + 1) * 512], writes=[vb])
                P.dma("sp", k_[:, :], vk[c * 128:(c + 1) * 128, 4096 + hd * 256:4096 + (hd + 1) * 256], writes=[kb_])
                Cbv = Cb[:, 0:1024].rearrange("p (a b) -> p a b", a=2)
                for dc in range(2):
                    P.op("pe", lambda h, dc=dc, cols=cols, SB=SB, k_T=k_T, q_=q_: h.matmul(
                        pst[:, SB, 0:128], lhsT=k_T[:, dc, cols], rhs=q_[:, dc, cols], start=(dc == 0), stop=(dc == 1)),
                        reads=[kb, qb], writes=[pbufs[SB]], sig=(dc == 1), pe_acc=True)
                t, tb = tri[d]
                P.op("dve", lambda h, s_=s_, SB=SB, E=E, c=c, t=t, hd=hd: h.scalar_tensor_tensor(
                    out=s_[:, :], in0=pst[:, SB, 0:128], scalar=E[:, c, hd:hd + 1], in1=t[:, :], op0=ALU.mult, op1=ALU.mult),
                    reads=[pbufs[SB], Eb, tb], writes=[sb_])
                P.op("pe", lambda h, s_=s_, v_=v_, NB_=NB_: h.matmul(pst[:, NB_, :], lhsT=s_[:, :], rhs=v_[:, :], start=True, stop=False),
                     reads=[sb_, vb], writes=[pbufs[NB_]], sig=False, pe_acc=True)
                for dc in range(2):
                    P.op("pe", lambda h, dc=dc, cols=cols, Cbv=Cbv, NB_=NB_, q_=q_: h.matmul(
                        pst[:, NB_, :], lhsT=q_[:, dc, cols], rhs=Cbv[:, dc, :], start=False, stop=(dc == 1)),
                        reads=[qb, Cbb], writes=[pbufs[NB_]], sig=(dc == 1), pe_acc=True)
                P.op("pe", lambda h, s_=s_, SB=SB: h.matmul(pst[:, SB, 256:257], lhsT=s_[:, :], rhs=onesb[:, 0:1], start=True, stop=False),
                     reads=[sb_, bonb], writes=[pbufs[SB]], sig=False, pe_acc=True)
                for dc in range(2):
                    P.op("pe", lambda h, dc=dc, cols=cols, Cb=Cb, SB=SB, q_=q_: h.matmul(
                        pst[:, SB, 256:257], lhsT=q_[:, dc, cols], rhs=Cb[:, 1024 + dc:1025 + dc], start=False, stop=(dc == 1)),
                        reads=[qb, Cbb], writes=[pbufs[SB]], sig=(dc == 1), pe_acc=True)
                rcol = R[:, c, hd:hd + 1]
                P.op("dve", lambda h, ds=ds, SB=SB, rcol=rcol: h.tensor_tensor(out=ds[:, 0:1], in0=pst[:, SB, 256:257], in1=rcol, op=ALU.mult),
                     reads=[pbufs[SB], Rb], writes=[dsb])
                P.op("act", lambda h, ds=ds: h.activation(out=ds[:, 1:2], in_=ds[:, 0:1], func=AF.Abs), reads=[dsb], writes=[dsb])
                P.op("dve", lambda h, ds=ds: h.tensor_single_scalar(out=ds[:, 2:3], in_=ds[:, 1:2], scalar=1.0, op=ALU.max), reads=[dsb], writes=[dsb])
                P.op("dve", lambda h, ds=ds: h.reciprocal(out=ds[:, 1:2], in_=ds[:, 2:3]), reads=[dsb], writes=[dsb])
                P.op("dve", lambda h, ds=ds, rcol=rcol: h.tensor_tensor(out=ds[:, 3:4], in0=ds[:, 1:2], in1=rcol, op=ALU.mult),
                     reads=[dsb, Rb], writes=[dsb])
                P.op("act", lambda h, h_=h_, ds=ds, NB_=NB_: h.activation(out=h_[:, :], in_=pst[:, NB_, :], func=AF.Copy, scale=ds[:, 3:4]),
                     reads=[pbufs[NB_], dsb], writes=[hb_])
                P.dma("sp", chn["out"][c * 128:(c + 1) * 128, hd * 512:(hd + 1) * 512], h_[:, :], reads=[hb_])
                if step < 15:
                    Sv = S[:, 0:1024].rearrange("p (a b) -> p a b", a=2)
                    _state_update(P, pst, pbufs, UB0, SB, kp, kpb, k_[:, :], kb_, v_[:, :], vb, EG[:, c, hd:hd + 1], GC[:, c, hd:hd + 1],
                                  Buf("dummy"), Sv, S[:, 1024:1026], Sb, [EGb, GCb], onesb, bonb)
                    P.op("act", lambda h, S=S, Cb=Cb: h.activation(out=Cb[:, :], in_=S[:, :], func=AF.Copy), reads=[Sb], writes=[Cbb])
    P.run()
```

```python
import contextlib
import numpy as np
import concourse.bass as bass
import concourse.mybir as mybir
from concourse.bass_utils import run_bass_kernel_spmd

F32 = mybir.dt.float32
BF16 = mybir.dt.bfloat16
ALU = mybir.AluOpType
AF = mybir.ActivationFunctionType
AX = mybir.AxisListType

D = 4096
KC = 32
DFF = 11008
NX = 2048
NCTX = 256
TM = NX + NCTX
TE = 64 + NX + 64 + NCTX
EPS = 1e-6
WA = 2048
ABIN = 10240
CIN = 12320
NCORES = 8


class Buf:
    __slots__ = ("name", "w", "r", "dsem", "dcnt", "weng", "rec")

    def __init__(self, name):
        self.name = name
        self.w = None
        self.r = {}
        self.dsem = None
        self.dcnt = 0
        self.weng = None


class SemPool:
    _by_nc = {}

    @classmethod
    def of(cls, nc):
        p = cls._by_nc.get(id(nc))
        if p is None:
            p = cls(nc)
            cls._by_nc[id(nc)] = p
        return p

    def __init__(self, nc):
        self.nc = nc
        self.es = contextlib.ExitStack()
        self.ce = {e: [self.es.enter_context(nc.semaphore(f"ce_{e}")), 0] for e in ("pe", "act", "dve", "pool")}
        self.free = {"hw": [], "sw": [], "cc": []}
        self.n = 0

    def get(self, kind):
        if self.free[kind]:
            return self.free[kind].pop()
        self.n += 1
        return [self.es.enter_context(self.nc.semaphore(f"dq_{kind}{self.n}")), 0, kind]

    def put(self, rec):
        self.free[rec[2]].append(rec)


class Phase:
    CE = ("pe", "act", "dve", "pool")

    def __init__(self, nc, name):
        self.nc = nc
        self.name = name
        self.es = contextlib.ExitStack()
        self.pool = SemPool.of(nc)
        self.q = {e: [] for e in ("pe", "act", "dve", "pool", "sp")}
        self.sem = {e: self.pool.ce[e][0] for e in self.CE}
        self.cnt = {e: self.pool.ce[e][1] for e in self.CE}
        self.cnt0 = dict(self.cnt)
        self.pend = {e: False for e in self.CE}
        self.seen = {e: {id(self.sem[c]): self.cnt[c] for c in self.CE} for e in self.q}
        self.dma_bufs = []
        self.nsb = 0

    def sb(self, shape, dtype, name=None):
        self.nsb += 1
        return self.es.enter_context(self.nc.sbuf_tensor(f"{self.name}_{name or 't'}{self.nsb}", list(shape), dtype))

    def ps(self, shape, dtype=F32, name=None):
        self.nsb += 1
        return self.es.enter_context(self.nc.psum_tensor(f"{self.name}_{name or 'p'}{self.nsb}", list(shape), dtype))

    def _wait(self, eng, dep):
        sem, val = dep
        k = id(sem)
        if self.seen[eng].get(k, 0) >= val:
            return
        if eng in self.CE and sem is self.sem[eng]:
            assert val <= self.cnt[eng], f"self-wait on pending {eng} {val} {self.cnt[eng]}"
        self.seen[eng][k] = val
        self.q[eng].append(lambda h, sem=sem, val=val: h.wait_ge(sem, val))

    def _hazards(self, eng, reads, writes, pe_acc=False):
        for b in reads:
            if b.w is not None:
                self._wait(eng, b.w)
        for b in writes:
            if b.w is not None and not (pe_acc and b.weng == "pe" and eng == "pe"):
                self._wait(eng, b.w)
            for d in b.r.values():
                self._wait(eng, d)

    def op(self, eng, fn, reads=(), writes=(), sig=True, pe_acc=False):
        self._hazards(eng, reads, writes, pe_acc)
        c = self.cnt[eng] + 1
        sem = self.sem[eng]
        dep = (sem, c)
        for b in reads:
            b.r[id(sem)] = dep
        for b in writes:
            b.w = dep
            b.weng = eng
            b.r = {}
        if sig:
            self.cnt[eng] = c
            self.pend[eng] = False
            self.q[eng].append(lambda h, fn=fn, sem=sem: fn(h).then_inc(sem, 1))
        else:
            self.pend[eng] = True
            self.q[eng].append(lambda h, fn=fn: fn(h))

    def dma(self, qeng, out, in_, reads=(), writes=(), noncontig=False):
        self._hazards(qeng, reads, writes)
        tgt = (list(writes) + list(reads))[0]
        if tgt.dsem is None:
            rec = self.pool.get("sw" if qeng == "pool" else "hw")
            tgt.dsem = rec[0]
            tgt.dcnt = rec[1]
            tgt.rec = rec
            self.dma_bufs.append(tgt)
        assert tgt.rec[2] == ("sw" if qeng == "pool" else "hw"), "buffer DMA'd from both queue kinds"
        tgt.dcnt += 16
        sem = tgt.dsem
        dep = (sem, tgt.dcnt)
        for b in writes:
            b.w = dep
            b.weng = "dma"
            b.r = {}
        for b in reads:
            b.r[id(sem)] = dep
        if noncontig:
            self.q[qeng].append(lambda h, o=out, i=in_, sem=sem: h.dma_start(
                out=o, in_=i, allow_slow_non_contiguous=True).then_inc(sem, 16))
        else:
            self.q[qeng].append(lambda h, o=out, i=in_, sem=sem: h.dma_start(out=o, in_=i).then_inc(sem, 16))

    def run(self):
        for e in self.CE:
            assert not self.pend[e], f"pending unsignaled op on {e}"
        for b in self.dma_bufs:
            self._wait("sp", (b.dsem, b.dcnt))
        for e in self.CE:
            if self.cnt[e] > self.cnt0[e]:
                self._wait("sp", (self.sem[e], self.cnt[e]))
            self.pool.ce[e][1] = self.cnt[e]
        for b in self.dma_bufs:
            b.rec[1] = b.dcnt
            self.pool.put(b.rec)
        nc = self.nc
        q = self.q
        with nc.Block() as block:
            if q["pe"]:
                @block.tensor
                def _(h):
                    for f in q["pe"]:
                        f(h)
            if q["act"]:
                @block.scalar
                def _(h):
                    for f in q["act"]:
                        f(h)
            if q["dve"]:
                @block.vector
                def _(h):
                    for f in q["dve"]:
                        f(h)
            if q["pool"]:
                @block.gpsimd
                def _(h):
                    for f in q["pool"]:
                        f(h)
            if q["sp"]:
                @block.sync
                def _(h):
                    for f in q["sp"]:
                        f(h)
        self.es.close()


def col_view(row_ap, n=None):
    return row_ap.rearrange("(c p) -> p c", p=128)


def make_ident(P):
    ident = P.sb([128, 128], F32, "ident")
    bid = Buf("ident")
    P.op("pool", lambda h: h.memset(ident[:, :], 0.0), writes=[bid])
    P.op("pool", lambda h: h.affine_select(out=ident[:, :], in_=ident[:, :], pattern=[[-1, 128]],
                                           compare_op=ALU.not_equal, fill=1.0, base=0, channel_multiplier=1),
         reads=[bid], writes=[bid])
    return ident, bid


def load_cols(P, rows, ident, bid, ps_ap, pbuf, name="cols"):
    ncs = [r.shape[0] // 128 for r in rows]
    ntot = sum(ncs)
    assert ntot <= 128
    stg = P.sb([128, 128], F32, name + "s")
    bs = [Buf(name + "s") for _ in rows]
    off = 0
    for r, n, b in zip(rows, ncs, bs):
        P.dma("sp", stg[off:off + n, :], r.rearrange("(c p) -> c p", p=128), writes=[b])
        off += n
    t = P.sb([128, ntot], F32, name)
    tb = Buf(name)
    P.op("pe", lambda h: h.transpose(out=ps_ap[:, 0:ntot], in_=stg[0:ntot, :], identity=ident[0:ntot, 0:ntot]),
         reads=bs + [bid], writes=[pbuf])
    P.op("dve", lambda h: h.tensor_copy(out=t[:, :], in_=ps_ap[:, 0:ntot]), reads=[pbuf], writes=[tb])
    return t, tb


def load_bcast(P, row_ap, n, name="bc", q="sp", parts=128):
    t = P.sb([parts, n], F32, name)
    b = Buf(name)
    P.dma(q, t[:, :], row_ap.partition_broadcast(parts), writes=[b])
    return t, b


def rstd_ops(P, s_, sb_, ci, ct, co, scale, n=1):
    P.op("act", lambda h: h.activation(out=s_[:, ct:ct + n], in_=s_[:, ci:ci + n], func=AF.Sqrt, scale=scale, bias=EPS),
         reads=[sb_], writes=[sb_])
    P.op("dve", lambda h: h.reciprocal(out=s_[:, co:co + n], in_=s_[:, ct:ct + n]), reads=[sb_], writes=[sb_])


def phase_ada(nc, csel, ada_w, ada_b, mods, layers=(0, 1), NM=6 * D):
    P = Phase(nc, "ada")
    ident, bid = make_ident(P)
    pst = P.ps([128, 8, 512], F32, "ps")
    pbufs = [Buf(f"ps{i}") for i in range(8)]
    cT, bcT = load_cols(P, [csel[0], csel[1]], ident, bid, pst[:, 7, :], pbufs[7], "cT")
    sT = P.sb([128, 2, KC], F32, "sT")
    bsT = Buf("sT")
    P.op("act", lambda h: h.activation(out=sT[:, :, :], in_=cT[:, :].rearrange("p (r c) -> p r c", r=2), func=AF.Silu),
         reads=[bcT], writes=[bsT])
    NB = 1024
    KG = 4
    ring = [(P.sb([128, KG, NB], F32, "w"), Buf("w")) for _ in range(4)]
    bias = [(P.sb([2, NB], F32, "bias"), Buf("bias")) for _ in range(2)]
    osb = [(P.sb([2, NB], F32, "osb"), Buf("osb")) for _ in range(2)]
    wi = 0
    pi = 0
    ni = 0
    for l in layers:
        wv = ada_w[l].rearrange("(c p) n -> p c n", p=128)
        for nb in range(NM // NB):
            bt, bb = bias[ni % 2]
            ot, bo = osb[ni % 2]
            ni += 1
            P.dma("sp", bt[:, :], ada_b[l, nb * NB:(nb + 1) * NB].partition_broadcast(2), writes=[bb])
            pb = [(pi + j) % 8 for j in range(NB // 512)]
            pi += NB // 512
            for kg in range(KC // KG):
                wt, wb = ring[wi % 4]
                wi += 1
                P.dma("sp", wt[:, :, :], wv[:, kg * KG:(kg + 1) * KG, nb * NB:(nb + 1) * NB], writes=[wb])
                for kk in range(KG):
                    kc = kg * KG + kk
                    for j, bk in enumerate(pb):
                        last = (kc == KC - 1)
                        P.op("pe", lambda h, bk=bk, kc=kc, kk=kk, j=j, wt=wt: h.matmul(
                            pst[0:2, bk, :], lhsT=sT[:, :, kc], rhs=wt[:, kk, j * 512:(j + 1) * 512],
                            start=(kc == 0), stop=(kc == KC - 1)),
                            reads=[bsT, wb], writes=[pbufs[bk]], sig=(last or (kk == KG - 1 and j == len(pb) - 1)), pe_acc=True)
            for j, bk in enumerate(pb):
                P.op("dve", lambda h, bk=bk, j=j, ot=ot, bt=bt: h.tensor_tensor(
                    out=ot[:, j * 512:(j + 1) * 512], in0=pst[0:2, bk, :], in1=bt[:, j * 512:(j + 1) * 512], op=ALU.add),
                    reads=[pbufs[bk], bb], writes=[bo])
            P.dma("sp", mods[l, :, nb * NB:(nb + 1) * NB], ot[:, :], reads=[bo])
    P.run()


def phase_prenorm(nc, name, x_in, groups, y_in=None, gg_rows=None, x_out=None, ab_rows=None, hT_out=None,
                  xin_rows=None):
    P = Phase(nc, name)
    NG = len(groups)
    vsets = sorted(set(groups))
    ident, bid = make_ident(P)
    pst = P.ps([128, 8, 512], F32, "ps")
    pbufs = [Buf(f"ps{i}") for i in range(8)]
    GG = {}
    if y_in is not None:
        for v in vsets:
            gt, gb = load_bcast(P, gg_rows[v][0], D, "gate")
            nt, nb_ = load_bcast(P, gg_rows[v][1], D, "gn")
            P.op("pool", lambda h, gt=gt, nt=nt: h.tensor_tensor(out=gt[:, :], in0=gt[:, :], in1=nt[:, :], op=ALU.mult),
                 reads=[gb, nb_], writes=[gb])
            GG[v] = (gt, gb)
    AB = {}
    if hT_out is not None:
        for v in vsets:
            cl, clb = load_cols(P, list(ab_rows[v]), ident, bid, pst[:, v, :], pbufs[v], "abc")
            A_ = P.sb([128, KC], F32, "A")
            Ab = Buf("A")
            P.op("dve", lambda h, A_=A_, cl=cl: h.scalar_tensor_tensor(
                out=A_[:, :], in0=cl[:, KC:2 * KC], scalar=1.0, in1=cl[:, 0:KC], op0=ALU.add, op1=ALU.mult),
                reads=[clb], writes=[Ab])
            B_, Bb = cl[:, 2 * KC:3 * KC], clb
            AB[v] = (A_, Ab, B_, Bb)
    xt = [(P.sb([128, D], F32, "xt"), Buf("xt")) for _ in range(2)]
    yt = [(P.sb([128, D], F32, "yt"), Buf("yt")) for _ in range(2)]
    sq = P.sb([128, D], BF16, "sq")
    bsq = Buf("sq")
    st = [(P.sb([128, 8], F32, "st"), Buf("st")) for _ in range(2)]
    NHB = 1 if y_in is not None else 2
    hb = [(P.sb([128, KC, 512], BF16, "hb"), Buf("hb")) for _ in range(NHB)] if hT_out is not None else None
    hTv = hT_out.rearrange("(c p) t -> p c t", p=128) if hT_out is not None else None

    def rstd(h, s, i, o):
        return None

    for g in range(NG):
        v = groups[g]
        x_, xb = xt[g % 2]
        y_, yb = yt[g % 2]
        s_, sb_ = st[g % 2]
        r0 = xin_rows[g] if xin_rows is not None else g * 128
        P.dma("sp", x_[:, :], x_in[r0:r0 + 128, :], writes=[xb])
        if y_in is not None:
            P.dma("sp", y_[:, :], y_in[g * 128:(g + 1) * 128, :], writes=[yb])
            P.op("act", lambda h, y_=y_, s_=s_: h.activation(out=sq[:, :], in_=y_[:, :], func=AF.Square,
                                                             accum_out=s_[:, 0:1]),
                 reads=[yb], writes=[bsq, sb_])
            rstd_ops(P, s_, sb_, 0, 1, 2, 1.0 / D)
            gt, gb = GG[v]
            P.op("dve", lambda h, y_=y_, s_=s_, gt=gt: h.scalar_tensor_tensor(
                out=y_[:, :], in0=y_[:, :], scalar=s_[:, 2:3], in1=gt[:, :], op0=ALU.mult, op1=ALU.mult),
                reads=[yb, sb_, gb], writes=[yb])
            P.op("pool", lambda h, x_=x_, y_=y_: h.tensor_tensor(out=x_[:, :], in0=x_[:, :], in1=y_[:, :], op=ALU.add),
                 reads=[xb, yb], writes=[xb])
            if x_out is not None:
                P.dma("sp", x_out[g * 128:(g + 1) * 128, :], x_[:, :], reads=[xb])
        if hT_out is None:
            continue
        P.op("act", lambda h, x_=x_, s_=s_: h.activation(out=sq[:, :], in_=x_[:, :], func=AF.Square,
                                                         accum_out=s_[:, 3:4]),
             reads=[xb], writes=[bsq, sb_])
        rstd_ops(P, s_, sb_, 3, 4, 5, 1.0 / D)
        P.op("act", lambda h, x_=x_, y_=y_, s_=s_: h.activation(out=y_[:, :], in_=x_[:, :], func=AF.Copy,
                                                                scale=s_[:, 5:6]),
             reads=[xb, sb_], writes=[yb])
        h_, hbb = hb[(g // 4) % NHB]
        A_, Ab, B_, Bb = AB[v]
        tcol = (g % 4) * 128
        for c in range(KC):
            bk = c // 4
            j = c % 4
            P.op("pe", lambda h, bk=bk, j=j, c=c, y_=y_: h.transpose(
                out=pst[:, bk, j * 128:(j + 1) * 128], in_=y_[:, c * 128:(c + 1) * 128], identity=ident[:, :]),
                reads=[yb, bid], writes=[pbufs[bk]], sig=(j == 3), pe_acc=True)
            if j == 3:
                for jj in range(4):
                    cc = bk * 4 + jj
                    if jj % 2 == 0:
                        P.op("act", lambda h, bk=bk, jj=jj, cc=cc, h_=h_, A_=A_, B_=B_, tcol=tcol: h.activation(
                            out=h_[:, cc, tcol:tcol + 128], in_=pst[:, bk, jj * 128:(jj + 1) * 128],
                            func=AF.Identity, scale=A_[:, cc:cc + 1], bias=B_[:, cc:cc + 1]),
                            reads=[pbufs[bk], Ab, Bb], writes=[hbb])
                    else:
                        P.op("dve", lambda h, bk=bk, jj=jj, cc=cc, h_=h_, A_=A_, B_=B_, tcol=tcol: h.tensor_scalar(
                            out=h_[:, cc, tcol:tcol + 128], in0=pst[:, bk, jj * 128:(jj + 1) * 128],
                            scalar1=A_[:, cc:cc + 1], scalar2=B_[:, cc:cc + 1], op0=ALU.mult, op1=ALU.add),
                            reads=[pbufs[bk], Ab, Bb], writes=[hbb])
        if g % 4 == 3 or g == NG - 1:
            t0 = (g // 4) * 512
            nt = (g % 4 + 1) * 128
            P.dma("sp", hTv[:, :, t0:t0 + nt], h_[:, :, 0:nt], reads=[hbb])
    P.run()


def phase_gemm_f(nc, name, hT, W, kc_n, passes, groups, hook, hook_init=None):
    P = Phase(nc, name)
    passes = [(p[0], p[1], (p[2] if len(p) > 2 else None)) for p in passes]
    TMAX = max(p[1] for p in passes)
    hsb = P.sb([128, kc_n, TMAX], BF16, "hT")
    NQ = 4
    kq = kc_n // NQ
    hbufs = [Buf(f"h{i}") for i in range(NQ)]
    NWR = 6
    ring = [(P.sb([128, kc_n, 128], BF16, "w"), Buf("w")) for _ in range(NWR)]
    pst = P.ps([128, 8, 512], F32, "ps")
    pbufs = [Buf(f"ps{i}") for i in range(8)]
    hTv = hT.rearrange("(c p) t -> p c t", p=128)
    Wv = W.rearrange("(c p) n -> p c n", p=128)
    P.ps_shared, P.pb_shared = pst, pbufs
    ctx = hook_init(P) if hook_init else None
    wi = 0
    si = 0
    for pi_, (t0, T, blks_) in enumerate(passes):
        for qd in range(NQ):
            P.dma("sp", hsb[:, qd * kq:(qd + 1) * kq, 0:T], hTv[:, qd * kq:(qd + 1) * kq, t0:t0 + T],
                  writes=[hbufs[qd]])
        nblk = (T + 511) // 512
        blks = blks_ if blks_ is not None else [(b * 512, min(512, T - b * 512)) for b in range(nblk)]
        nblk = len(blks)
        assert nblk <= 2 and all(cn <= 512 for _, cn in blks)
        for gi, grp in enumerate(groups):
            slots = []
            for n0 in grp:
                wt, wb = ring[wi % NWR]
                wi += 1
                P.dma("pool", wt[:, :, :], Wv[:, :, n0:n0 + 128], writes=[wb])
                slot = si % 4
                si += 1
                for kc in range(kc_n):
                    for b, (c0, cn) in enumerate(blks):
                        bk = slot * 2 + b
                        last = (kc == kc_n - 1)
                        P.op("pe", lambda h, bk=bk, kc=kc, c0=c0, cn=cn, wt=wt: h.matmul(
                            pst[:, bk, 0:cn], lhsT=wt[:, kc, :], rhs=hsb[:, kc, c0:c0 + cn],
                            start=(kc == 0), stop=(kc == kc_n - 1)),
                            reads=[wb, hbufs[kc // kq]], writes=[pbufs[bk]], sig=(last and b == nblk - 1),
                            pe_acc=True)
                slots.append([(pst, slot * 2 + b, pbufs[slot * 2 + b], c0, cn) for b, (c0, cn) in enumerate(blks)])
            hook(P, ctx, gi, (pi_, t0, T), slots)
    P.run()


def phase_gemm_t(nc, name, aT, W, kc_n, passes, nblocks, outs):
    P = Phase(nc, name)
    TMAX = max(t for _, t in passes)
    asb = P.sb([128, kc_n, TMAX], BF16, "aT")
    NQ = 2
    kq = (kc_n + NQ - 1) // NQ
    abufs = [Buf(f"a{i}") for i in range(NQ)]
    KG = 8
    NWR = 6
    ring = [(P.sb([128, KG, 512], BF16, "w"), Buf("w")) for _ in range(NWR)]
    pst = P.ps([128, 8, 512], F32, "ps")
    pbufs = [Buf(f"ps{i}") for i in range(8)]
    osbs = [(P.sb([128, 4, 512], outs[0].dtype if len(set(o.dtype for o in outs)) == 1 else F32, "o"), Buf("o"))
            for _ in range(2)]
    aTv = aT.rearrange("(c p) t -> p c t", p=128)
    Wv = W.rearrange("(c p) n -> p c n", p=128)
    wi = 0
    si = 0
    oi = 0
    for (t0, T) in passes:
        for qd in range(NQ):
            k0, k1 = qd * kq, min(kc_n, (qd + 1) * kq)
            P.dma("sp", asb[:, k0:k1, 0:T], aTv[:, k0:k1, t0:t0 + T], writes=[abufs[qd]])
        ntg = T // 128
        for (n0, ncol, oidx, oc0) in nblocks:
            slot = si % 2
            si += 1
            for kg in range((kc_n + KG - 1) // KG):
                k0, k1 = kg * KG, min(kc_n, (kg + 1) * KG)
                wt, wb = ring[wi % NWR]
                wi += 1
                P.dma("pool", wt[:, 0:k1 - k0, 0:ncol], Wv[:, k0:k1, n0:n0 + ncol], writes=[wb])
                for kc in range(k0, k1):
                    for tg in range(ntg):
                        bk = slot * 4 + tg
                        last = (kc == kc_n - 1)
                        P.op("pe", lambda h, bk=bk, kc=kc, tg=tg, wt=wt, k0=k0, ncol=ncol: h.matmul(
                            pst[:, bk, 0:ncol], lhsT=asb[:, kc, tg * 128:(tg + 1) * 128], rhs=wt[:, kc - k0, 0:ncol],
                            start=(kc == 0), stop=(kc == kc_n - 1)),
                            reads=[wb, abufs[kc // kq]], writes=[pbufs[bk]], sig=((last or kc == k1 - 1) and tg == ntg - 1),
                            pe_acc=True)
            o_, ob = osbs[oi % 2]
            oi += 1
            for tg in range(ntg):
                bk = slot * 4 + tg
                if tg % 2 == 0:
                    P.op("act", lambda h, bk=bk, tg=tg, o_=o_, ncol=ncol: h.activation(
                        out=o_[:, tg, 0:ncol], in_=pst[:, bk, 0:ncol], func=AF.Copy),
                        reads=[pbufs[bk]], writes=[ob])
                else:
                    P.op("dve", lambda h, bk=bk, tg=tg, o_=o_, ncol=ncol: h.tensor_copy(
                        out=o_[:, tg, 0:ncol], in_=pst[:, bk, 0:ncol]),
                        reads=[pbufs[bk]], writes=[ob])
            ov = outs[oidx][t0:t0 + T, oc0:oc0 + ncol].rearrange("(g p) n -> p g n", p=128)
            P.dma("sp", ov, o_[:, 0:ntg, 0:ncol], reads=[ob])
    P.run()


def make_store_hook(dst_rows, func=None, dtype=F32, tmax=1024):
    def hinit(P):
        return [(P.sb([128, tmax], dtype, "o"), Buf("o")) for _ in range(2)]

    def hook(P, ctx, gi, pinfo, slots):
        _, t0, T = pinfo
        o_, ob = ctx[gi % 2]
        for k, (pst, bk, pb, c0, cn) in enumerate(slots[0]):
            if func is None and k % 2 == 1:
                P.op("dve", lambda h, bk=bk, c0=c0, cn=cn, pst=pst: h.tensor_copy(out=o_[:, c0:c0 + cn], in_=pst[:, bk, 0:cn]),
                     reads=[pb], writes=[ob])
            else:
                fn_ = func(gi) if callable(func) else (func or AF.Copy)
                P.op("act", lambda h, bk=bk, c0=c0, cn=cn, pst=pst, fn_=fn_: h.activation(
                    out=o_[:, c0:c0 + cn], in_=pst[:, bk, 0:cn], func=fn_), reads=[pb], writes=[ob])
        P.dma("sp", dst_rows(gi)[:, t0:t0 + T], o_[:, 0:T], reads=[ob])
    return hook, hinit


def make_ffn_hook(conv_w, actT, row_len):
    NCH = DFF // 128

    def hinit(P):
        ident, bid = make_ident(P)
        pst = P.ps_shared
        cw = []
        for j in range(3):
            for half in range(2):
                t, tb = load_cols(P, [conv_w[j, half * DFF:(half + 1) * DFF]], ident, bid, pst[:, 7, :], P.pb_shared[7], "cw")
                cw.append((t, tb))
        gs = [(P.sb([128, 1024], F32, "g"), Buf("g")) for _ in range(2)]
        vs = [(P.sb([128, 1024], F32, "v"), Buf("v")) for _ in range(2)]
        as_ = [(P.sb([128, 1024], BF16, "a"), Buf("a")) for _ in range(2)]
        return dict(cw=cw, gs=gs, vs=vs, as_=as_)

    def hook(P, ctx, gi, pinfo, slots):
        pi_, t0, T = pinfo
        g_, gb = ctx["gs"][gi % 2]
        v_, vb = ctx["vs"][gi % 2]
        a_, ab = ctx["as_"][gi % 2]
        cw = ctx["cw"]
        for half, (s_, sb_) in enumerate(((g_, gb), (v_, vb))):
            w0, w0b = cw[0 * 2 + half]
            w1, w1b = cw[1 * 2 + half]
            w2, w2b = cw[2 * 2 + half]
            for (pst, bk, pb, c0, cn) in slots[half]:
                L = row_len(pi_, c0)
                assert cn % L == 0
                P.op("act", lambda h, s_=s_, bk=bk, c0=c0, cn=cn, pst=pst, w1=w1: h.activation(
                    out=s_[:, c0:c0 + cn], in_=pst[:, bk, 0:cn], func=AF.Copy, scale=w1[:, gi:gi + 1]),
                    reads=[pb, w1b], writes=[sb_])
                sv = s_[:, c0:c0 + cn].rearrange("p (r l) -> p r l", l=L)
                pv = pst[:, bk, 0:cn].rearrange("p (r l) -> p r l", l=L)
                P.op("dve", lambda h, sv=sv, pv=pv, w0=w0, L=L: h.scalar_tensor_tensor(
                    out=sv[:, :, 1:L], in0=pv[:, :, 0:L - 1], scalar=w0[:, gi:gi + 1], in1=sv[:, :, 1:L],
                    op0=ALU.mult, op1=ALU.add), reads=[pb, w0b, sb_], writes=[sb_])
                P.op("dve", lambda h, sv=sv, pv=pv, w2=w2, L=L: h.scalar_tensor_tensor(
                    out=sv[:, :, 0:L - 1], in0=pv[:, :, 1:L], scalar=w2[:, gi:gi + 1], in1=sv[:, :, 0:L - 1],
                    op0=ALU.mult, op1=ALU.add), reads=[pb, w2b, sb_], writes=[sb_])
        P.op("act", lambda h: h.activation(out=g_[:, 0:T], in_=g_[:, 0:T], func=AF.Silu), reads=[gb], writes=[gb])
        P.op("dve", lambda h: h.tensor_tensor(out=a_[:, 0:T], in0=g_[:, 0:T], in1=v_[:, 0:T], op=ALU.mult),
             reads=[gb, vb], writes=[ab])
        P.dma("sp", actT[gi * 128:(gi + 1) * 128, t0:t0 + T], a_[:, 0:T], reads=[ab])
    return hook, hinit


def phase_abmix_a(nc, PT, a_conv_w, a_conv_b, a_ln_g, a_ln_b, catT):
    P = Phase(nc, "abA")
    ident, bid = make_ident(P)
    pst = P.ps([128, 8, 512], F32, "ps")
    pbufs = [Buf(f"ps{i}") for i in range(8)]
    NA = WA // 128
    cwa, cwab = load_cols(P, [a_conv_w[j] for j in range(0, 8)], ident, bid, pst[:, 0, :], pbufs[0], "cwa")
    cwb, cwbb = load_cols(P, [a_conv_w[j] for j in range(8, 16)], ident, bid, pst[:, 1, :], pbufs[1], "cwb")
    cwc, cwcb = load_cols(P, [a_conv_w[j] for j in range(16, 24)], ident, bid, pst[:, 2, :], pbufs[2], "cwc")
    cwd, cwdb = load_cols(P, [a_conv_w[j] for j in range(24, 31)], ident, bid, pst[:, 3, :], pbufs[3], "cwd")
    prm, prmb = load_cols(P, [a_conv_b, a_ln_g, a_ln_b], ident, bid, pst[:, 4, :], pbufs[4], "prm")
    cws = [(cwa, cwab), (cwb, cwbb), (cwc, cwcb), (cwd, cwdb)]

    def wcol(j, i):
        t, tb = cws[j // 8]
        return t[:, (j % 8) * NA + i:(j % 8) * NA + i + 1], tb
    ones = P.sb([128, 128], F32, "ones")
    bon = Buf("ones")
    P.op("pool", lambda h: h.memset(ones[:, :], 1.0), writes=[bon])
    uc = P.sb([128, NA, 512], F32, "uc")
    ucb = [Buf(f"uc{i}") for i in range(NA)]
    vt = [(P.sb([128, 512], F32, "val"), Buf("val")) for _ in range(2)]
    gt = [(P.sb([128, 512], F32, "gate"), Buf("gate")) for _ in range(2)]
    sqt = [(P.sb([128, 512], F32, "sq"), Buf("sq")) for _ in range(2)]
    mean = P.sb([128, 512], F32, "mean"); bmean = Buf("mean")
    rstd = P.sb([128, 512], F32, "rstd"); brstd = Buf("rstd")
    tmp = [(P.sb([128, 512], F32, "tmp"), Buf("tmp")) for _ in range(2)]
    ob = [(P.sb([128, 512], BF16, "o"), Buf("o")) for _ in range(2)]
    pieces = [(64 + k * 512, k * 512, 512, 64) for k in range(4)] + [(64 + NX + 64, NX, NCTX, NCTX)]
    k = 0
    for (e0, m0, T, L) in pieces:
        for i in range(NA):
            v_, vb = vt[k % 2]
            g_, gb = gt[k % 2]
            q_, qb = sqt[k % 2]
            k += 1
            P.dma("sp", v_[:, 0:T], PT[i * 128:(i + 1) * 128, e0:e0 + T], writes=[vb])
            P.dma("sp", g_[:, 0:T], PT[WA + i * 128:WA + (i + 1) * 128, e0:e0 + T], writes=[gb])
            P.op("act", lambda h, g_=g_, T=T: h.activation(out=g_[:, 0:T], in_=g_[:, 0:T], func=AF.Sigmoid),
                 reads=[gb], writes=[gb])
            P.op("dve", lambda h, g_=g_, v_=v_, T=T: h.tensor_tensor(out=v_[:, 0:T], in0=v_[:, 0:T], in1=g_[:, 0:T], op=ALU.mult),
                 reads=[gb, vb], writes=[vb])
            wc, wcb = wcol(15, i)
            P.op("act", lambda h, v_=v_, i=i, T=T, wc=wc: h.activation(
                out=uc[:, i, 0:T], in_=v_[:, 0:T], func=AF.Identity, scale=wc, bias=prm[:, i:i + 1]),
                reads=[vb, wcb, prmb], writes=[ucb[i]])
            uv = uc[:, i, 0:T].rearrange("p (r l) -> p r l", l=L)
            vv = v_[:, 0:T].rearrange("p (r l) -> p r l", l=L)
            n = 0
            for dd in range(1, 16):
                for d in (dd, -dd):
                    wc, wcb = wcol(15 + d, i)
                    if d > 0:
                        o_ap, i_ap = uv[:, :, 0:L - d], vv[:, :, d:L]
                    else:
                        o_ap, i_ap = uv[:, :, -d:L], vv[:, :, 0:L + d]
                    eng = "dve"
                    n += 1
                    P.op(eng, lambda h, o_ap=o_ap, i_ap=i_ap, wc=wc: h.scalar_tensor_tensor(
                        out=o_ap, in0=i_ap, scalar=wc, in1=o_ap, op0=ALU.mult, op1=ALU.add),
                        reads=[vb, wcb, ucb[i]], writes=[ucb[i]])
            P.op("act", lambda h, q_=q_, i=i, T=T: h.activation(out=q_[:, 0:T], in_=uc[:, i, 0:T], func=AF.Square),
                 reads=[ucb[i]], writes=[qb])
            P.op("pe", lambda h, i=i, T=T: h.matmul(pst[:, 6, 0:T], lhsT=ones[:, :], rhs=uc[:, i, 0:T],
                                                    start=(i == 0), stop=(i == NA - 1)),
                 reads=[bon, ucb[i]], writes=[pbufs[6]], pe_acc=True)
            P.op("pe", lambda h, q_=q_, i=i, T=T: h.matmul(pst[:, 7, 0:T], lhsT=ones[:, :], rhs=q_[:, 0:T],
                                                           start=(i == 0), stop=(i == NA - 1)),
                 reads=[bon, qb], writes=[pbufs[7]], pe_acc=True)
        P.op("act", lambda h, T=T: h.activation(out=mean[:, 0:T], in_=pst[:, 6, 0:T], func=AF.Copy, scale=1.0 / WA),
             reads=[pbufs[6]], writes=[bmean])
        t_, tb = tmp[0]
        P.op("dve", lambda h, t_=t_, T=T: h.tensor_tensor(out=t_[:, 0:T], in0=mean[:, 0:T], in1=mean[:, 0:T], op=ALU.mult),
             reads=[bmean], writes=[tb])
        P.op("dve", lambda h, t_=t_, T=T: h.scalar_tensor_tensor(
            out=t_[:, 0:T], in0=pst[:, 7, 0:T], scalar=1.0 / WA, in1=t_[:, 0:T], op0=ALU.mult, op1=ALU.subtract),
            reads=[pbufs[7], tb], writes=[tb])
        P.op("act", lambda h, t_=t_, T=T: h.activation(out=t_[:, 0:T], in_=t_[:, 0:T], func=AF.Sqrt, bias=EPS),
             reads=[tb], writes=[tb])
        P.op("dve", lambda h, t_=t_, T=T: h.reciprocal(out=rstd[:, 0:T], in_=t_[:, 0:T]), reads=[tb], writes=[brstd])
        for i in range(NA):
            t_, tb = tmp[i % 2]
            o_, obb = ob[i % 2]
            P.op("dve", lambda h, t_=t_, i=i, T=T: h.tensor_tensor(out=t_[:, 0:T], in0=uc[:, i, 0:T], in1=mean[:, 0:T], op=ALU.subtract),
                 reads=[ucb[i], bmean], writes=[tb])
            P.op("pool", lambda h, t_=t_, T=T: h.tensor_tensor(out=t_[:, 0:T], in0=t_[:, 0:T], in1=rstd[:, 0:T], op=ALU.mult),
                 reads=[tb, brstd], writes=[tb])
            P.op("act", lambda h, t_=t_, o_=o_, i=i, T=T: h.activation(
                out=o_[:, 0:T], in_=t_[:, 0:T], func=AF.Silu, scale=prm[:, NA + i:NA + i + 1], bias=prm[:, 2 * NA + i:2 * NA + i + 1]),
                reads=[tb, prmb], writes=[obb])
            P.dma("sp", catT[i * 128:(i + 1) * 128, m0:m0 + T], o_[:, 0:T], reads=[obb])
    P.run()


def phase_abmix_b(nc, PT, b_conv_w, hmask, catT):
    P = Phase(nc, "abB")
    ident, bid = make_ident(P)
    pst = P.ps([128, 8, 512], F32, "ps")
    pbufs = [Buf(f"ps{i}") for i in range(8)]
    NB_ = WA // 128
    cw, cwb = load_cols(P, [b_conv_w[0], b_conv_w[1], b_conv_w[2]], ident, bid, pst[:, 0, :], pbufs[0], "cw")
    hm = P.sb([128, 2], F32, "hm"); hmb = Buf("hm")
    P.dma("sp", hm[:, :], hmask[:, :], writes=[hmb])
    XE = 64 + NX + 64
    bb_ = [(P.sb([128, TM], F32, "bb"), Buf("bb")) for _ in range(2)]
    bc_ = [(P.sb([128, TE], F32, "bc"), Buf("bc")) for _ in range(2)]
    bx_ = [(P.sb([128, TE], F32, "bx"), Buf("bx")) for _ in range(2)]
    zc_ = [(P.sb([128, TM], F32, "zc"), Buf("zc")) for _ in range(2)]
    o_ = [(P.sb([128, TM], BF16, "o"), Buf("o")) for _ in range(2)]
    for i in range(NB_):
        b_, bbb = bb_[i % 2]
        c_, cb = bc_[i % 2]
        x_, xb = bx_[i % 2]
        z_, zb = zc_[i % 2]
        oo, ob = o_[i % 2]
        r_b = 2 * WA + i * 128
        r_c = 2 * WA + WA + i * 128
        r_x = 2 * WA + 2 * WA + i * 128
        P.dma("sp", b_[:, 0:NX], PT[r_b:r_b + 128, 64:64 + NX], writes=[bbb])
        bbb2 = Buf("bb2")
        P.dma("sp", b_[:, NX:TM], PT[r_b:r_b + 128, XE:TE], writes=[bbb2])
        P.dma("sp", c_[:, :], PT[r_c:r_c + 128, :], writes=[cb])
        P.dma("sp", x_[:, :], PT[r_x:r_x + 128, :], writes=[xb])
        P.op("dve", lambda h, c_=c_, x_=x_: h.tensor_tensor(out=c_[:, :], in0=c_[:, :], in1=x_[:, :], op=ALU.mult),
             reads=[cb, xb], writes=[cb])
        P.op("dve", lambda h, c_=c_: h.tensor_scalar(out=c_[:, 0:64], in0=c_[:, 0:64], scalar1=hm[:, 0:1], scalar2=None, op0=ALU.mult),
             reads=[cb, hmb], writes=[cb])
        P.op("dve", lambda h, c_=c_: h.tensor_scalar(out=c_[:, 64 + NX:XE], in0=c_[:, 64 + NX:XE], scalar1=hm[:, 1:2], scalar2=None, op0=ALU.mult),
             reads=[cb, hmb], writes=[cb])
        w0, w1, w2 = cw[:, i:i + 1], cw[:, NB_ + i:NB_ + i + 1], cw[:, 2 * NB_ + i:2 * NB_ + i + 1]
        P.op("act", lambda h, z_=z_, c_=c_, w1=w1: h.activation(out=z_[:, 0:NX], in_=c_[:, 64:64 + NX], func=AF.Copy, scale=w1),
             reads=[cb, cwb], writes=[zb])
        P.op("dve", lambda h, z_=z_, c_=c_, w0=w0: h.scalar_tensor_tensor(
            out=z_[:, 0:NX], in0=c_[:, 0:NX], scalar=w0, in1=z_[:, 0:NX], op0=ALU.mult, op1=ALU.add),
            reads=[cb, cwb, zb], writes=[zb])
        P.op("dve", lambda h, z_=z_, c_=c_, w2=w2: h.scalar_tensor_tensor(
            out=z_[:, 0:NX], in0=c_[:, 128:128 + NX], scalar=w2, in1=z_[:, 0:NX], op0=ALU.mult, op1=ALU.add),
            reads=[cb, cwb, zb], writes=[zb])
        P.op("act", lambda h, z_=z_, c_=c_, w1=w1: h.activation(out=z_[:, NX:TM], in_=c_[:, XE:TE], func=AF.Copy, scale=w1),
             reads=[cb, cwb], writes=[zb])
        P.op("dve", lambda h, z_=z_, c_=c_, w0=w0: h.scalar_tensor_tensor(
            out=z_[:, NX + 1:TM], in0=c_[:, XE:TE - 1], scalar=w0, in1=z_[:, NX + 1:TM], op0=ALU.mult, op1=ALU.add),
            reads=[cb, cwb, zb], writes=[zb])
        P.op("dve", lambda h, z_=z_, c_=c_, w2=w2: h.scalar_tensor_tensor(
            out=z_[:, NX:TM - 1], in0=c_[:, XE + 1:TE], scalar=w2, in1=z_[:, NX:TM - 1], op0=ALU.mult, op1=ALU.add),
            reads=[cb, cwb, zb], writes=[zb])
        P.op("pool", lambda h, z_=z_, b_=b_, oo=oo: h.tensor_tensor(out=oo[:, :], in0=z_[:, :], in1=b_[:, :], op=ALU.mult),
             reads=[zb, bbb, bbb2], writes=[ob])
        P.dma("sp", catT[WA + i * 128:WA + (i + 1) * 128, :], oo[:, :], reads=[ob])
    P.run()


def phase_scan(nc, qT, kT, kk, vv, gg, bg, hf, hb, NCH=66, NCTXC=2, npairs=2):
    import math
    P = Phase(nc, "scan")
    ident, bid = make_ident(P)
    pst = P.ps([128, 8, 512], F32, "ps")
    pbufs = [Buf(f"ps{i}") for i in range(8)]
    TT = NCH * 128
    ones = P.sb([128, 128], F32, "ones"); bon = Buf("ones")
    P.op("pool", lambda h: h.memset(ones[:, :], 1.0), writes=[bon])
    onesb = P.sb([128, 2], BF16, "onesb"); bonb = Buf("onesb")
    P.op("pool", lambda h: h.memset(onesb[:, :], 1.0), writes=[bonb])
    tri = []
    for d in range(2):
        t = P.sb([128, 128], F32, f"tri{d}"); tb = Buf(f"tri{d}")
        P.op("pool", lambda h, t=t: h.memset(t[:, :], 1.0), writes=[tb])
        sgn = 1 if d == 0 else -1
        P.op("pool", lambda h, t=t, sgn=sgn: h.affine_select(out=t[:, :], in_=t[:, :], pattern=[[sgn, 128]],
                                                            compare_op=ALU.is_ge, fill=0.0, base=0, channel_multiplier=-sgn),
             reads=[tb], writes=[tb])
        tri.append((t, tb))
    bgt = P.sb([128, 4 * npairs], F32, "bg"); bgb = Buf("bg")
    P.dma("sp", bgt[:, :], bg[:, :], writes=[bgb])
    nbg = P.sb([128, 4 * npairs], F32, "nbg"); nbgb = Buf("nbg")
    P.op("dve", lambda h: h.tensor_scalar(out=nbg[:, :], in0=bgt[:, :], scalar1=-1.0, scalar2=None, op0=ALU.mult),
         reads=[bgb], writes=[nbgb])
    qsb = P.sb([128, 2, TT], BF16, "qT"); qb = Buf("qT")
    ksb = P.sb([128, 2, TT], BF16, "kT"); kb = Buf("kT")
    G = P.sb([128, NCH, 4], F32, "G"); Gb = Buf("G")
    NR = 4
    vr = [(P.sb([128, 512], BF16, "v"), Buf("v")) for _ in range(NR)]
    kr = [(P.sb([128, 256], BF16, "k"), Buf("k")) for _ in range(NR)]
    kpr = [(P.sb([128, 256], BF16, "kp"), Buf("kp")) for _ in range(NR)]
    spr = [(P.sb([128, 128], BF16, "sp"), Buf("sp")) for _ in range(2)]
    hr = [(P.sb([128, 512], F32, "h"), Buf("h")) for _ in range(2)]
    dsc = [(P.sb([128, 4], F32, "dsc"), Buf("dsc")) for _ in range(2)]
    ri = 0
    for pr in range(npairs):
        P.dma("sp", qsb[:, :, :], qT[pr].rearrange("(c p) t -> p c t", p=128), writes=[qb])
        P.dma("sp", ksb[:, :, :], kT[pr].rearrange("(c p) t -> p c t", p=128), writes=[kb])
        P.dma("sp", G[:, :, :], gg[pr].rearrange("(c p) f -> p c f", p=128), writes=[Gb], noncontig=True)
        chains = []
        for d in range(2):
            ig = P.sb([128, NCH], F32, "ig"); lf = P.sb([128, NCH], F32, "lf")
            E = P.sb([128, NCH], F32, "E"); R = P.sb([128, NCH], F32, "R"); GC = P.sb([128, NCH], F32, "GC")
            gb_ = Buf("gprep")
            c_i = pr * 4 + 2 * d
            P.op("dve", lambda h, ig=ig, d=d, c_i=c_i: h.tensor_scalar(out=ig[:, :], in0=G[:, :, 2 * d], scalar1=bgt[:, c_i:c_i + 1],
                                                                      scalar2=None, op0=ALU.add), reads=[Gb, bgb], writes=[gb_])
            P.op("act", lambda h, lf=lf, d=d, c_i=c_i: h.activation(out=lf[:, :], in_=G[:, :, 2 * d + 1], func=AF.Exp, scale=-1.0,
                                                                   bias=nbg[:, c_i + 1:c_i + 2]), reads=[Gb, nbgb], writes=[gb_])
            P.op("act", lambda h, lf=lf: h.activation(out=lf[:, :], in_=lf[:, :], func=AF.Ln, bias=1.0), reads=[gb_], writes=[gb_])
            P.op("dve", lambda h, lf=lf: h.tensor_scalar(out=lf[:, :], in0=lf[:, :], scalar1=-1.0, scalar2=None, op0=ALU.mult),
                 reads=[gb_], writes=[gb_])
            t, tb = tri[d]
            P.op("pe", lambda h, t=t, lf=lf: h.matmul(pst[:, 0, 0:NCH], lhsT=t[:, :], rhs=lf[:, :], start=True, stop=True),
                 reads=[tb, gb_], writes=[pbufs[0]])
            P.op("pe", lambda h, lf=lf: h.matmul(pst[:, 1, 0:NCH], lhsT=ones[:, :], rhs=lf[:, :], start=True, stop=True),
                 reads=[bon, gb_], writes=[pbufs[1]])
            P.op("act", lambda h, R=R: h.activation(out=R[:, :], in_=pst[:, 0, 0:NCH], func=AF.Exp), reads=[pbufs[0]], writes=[gb_])
            P.op("act", lambda h, GC=GC: h.activation(out=GC[:, :], in_=pst[:, 1, 0:NCH], func=AF.Exp), reads=[pbufs[1]], writes=[gb_])
            P.op("dve", lambda h, ig=ig: h.tensor_tensor(out=ig[:, :], in0=ig[:, :], in1=pst[:, 0, 0:NCH], op=ALU.subtract),
                 reads=[pbufs[0], gb_], writes=[gb_])
            P.op("act", lambda h, E=E, ig=ig: h.activation(out=E[:, :], in_=ig[:, :], func=AF.Exp, bias=-math.log(16.0)),
                 reads=[gb_], writes=[gb_])
            C = P.sb([128, 2, 512], F32, "C"); Cb = P.sb([128, 2, 512], BF16, "Cb")
            n_ = P.sb([128, 2], F32, "n"); nb_ = P.sb([128, 2], BF16, "nb")
            cb_ = Buf("C"); cbb = Buf("Cb")
            P.op("pool", lambda h, C=C: h.memset(C[:, :, :], 0.0), writes=[cb_])
            P.op("pool", lambda h, n_=n_: h.memset(n_[:, :], 0.0), writes=[cb_])
            P.op("pool", lambda h, Cb=Cb: h.memset(Cb[:, :, :], 0.0), writes=[cbb])
            P.op("pool", lambda h, nb_=nb_: h.memset(nb_[:, :], 0.0), writes=[cbb])
            order = list(range(NCTXC)) + list(range(NCTXC, NCH)) if d == 0 else \
                list(range(NCTXC - 1, -1, -1)) + list(range(NCH - 1, NCTXC - 1, -1))
            chains.append(dict(d=d, E=E, R=R, GC=GC, gb=gb_, C=C, Cb=Cb, n=n_, nb=nb_, cb=cb_, cbb=cbb, order=order,
                               out=(hf if d == 0 else hb)))
        for step in range(NCH):
            for ch in chains:
                c = ch["order"][step]
                d = ch["d"]
                base = d * 4
                SB, NB_, UB0, UB1 = base, base + 1, base + 2, base + 3
                E, R, GC, gb_ = ch["E"], ch["R"], ch["GC"], ch["gb"]
                C, Cb, n_, nb_, cb_, cbb = ch["C"], ch["Cb"], ch["n"], ch["nb"], ch["cb"], ch["cbb"]
                cols = slice(c * 128, (c + 1) * 128)
                v_, vb = vr[ri % NR]
                k_, kb_ = kr[ri % NR]
                kp, kpb = kpr[ri % NR]
                s_, sb_ = spr[ri % 2]
                h_, hb_ = hr[ri % 2]
                ds, dsb = dsc[ri % 2]
                ri += 1
                P.dma("sp", v_[:, :], vv[pr, c * 128:(c + 1) * 128, :], writes=[vb])
                P.dma("sp", k_[:, :], kk[pr, c * 128:(c + 1) * 128, :], writes=[kb_])
                if c >= NCTXC:
                    for dc in range(2):
                        P.op("pe", lambda h, dc=dc, cols=cols, SB=SB: h.matmul(
                            pst[:, SB, 0:128], lhsT=ksb[:, dc, cols], rhs=qsb[:, dc, cols], start=(dc == 0), stop=(dc == 1)),
                            reads=[kb, qb], writes=[pbufs[SB]], sig=(dc == 1), pe_acc=True)
                    t, tb = tri[d]
                    P.op("dve", lambda h, s_=s_, SB=SB, E=E, c=c, t=t: h.scalar_tensor_tensor(
                        out=s_[:, :], in0=pst[:, SB, 0:128], scalar=E[:, c:c + 1], in1=t[:, :], op0=ALU.mult, op1=ALU.mult),
                        reads=[pbufs[SB], gb_, tb], writes=[sb_])
                    P.op("pe", lambda h, s_=s_, v_=v_, NB_=NB_: h.matmul(pst[:, NB_, :], lhsT=s_[:, :], rhs=v_[:, :], start=True, stop=False),
                         reads=[sb_, vb], writes=[pbufs[NB_]], sig=False, pe_acc=True)
                    for dc in range(2):
                        P.op("pe", lambda h, dc=dc, cols=cols, Cb=Cb, NB_=NB_: h.matmul(
                            pst[:, NB_, :], lhsT=qsb[:, dc, cols], rhs=Cb[:, dc, :], start=False, stop=(dc == 1)),
                            reads=[qb, cbb], writes=[pbufs[NB_]], sig=(dc == 1), pe_acc=True)
                    P.op("pe", lambda h, s_=s_, SB=SB: h.matmul(pst[:, SB, 256:257], lhsT=s_[:, :], rhs=onesb[:, 0:1], start=True, stop=False),
                         reads=[sb_, bonb], writes=[pbufs[SB]], sig=False, pe_acc=True)
                    for dc in range(2):
                        P.op("pe", lambda h, dc=dc, cols=cols, nb_=nb_, SB=SB: h.matmul(
                            pst[:, SB, 256:257], lhsT=qsb[:, dc, cols], rhs=nb_[:, dc:dc + 1], start=False, stop=(dc == 1)),
                            reads=[qb, cbb], writes=[pbufs[SB]], sig=(dc == 1), pe_acc=True)
                    P.op("dve", lambda h, ds=ds, SB=SB, R=R, c=c: h.tensor_tensor(out=ds[:, 0:1], in0=pst[:, SB, 256:257], in1=R[:, c:c + 1], op=ALU.mult),
                         reads=[pbufs[SB], gb_], writes=[dsb])
                    P.op("act", lambda h, ds=ds: h.activation(out=ds[:, 1:2], in_=ds[:, 0:1], func=AF.Abs), reads=[dsb], writes=[dsb])
                    P.op("dve", lambda h, ds=ds: h.tensor_single_scalar(out=ds[:, 2:3], in_=ds[:, 1:2], scalar=1.0, op=ALU.max), reads=[dsb], writes=[dsb])
                    P.op("dve", lambda h, ds=ds: h.reciprocal(out=ds[:, 1:2], in_=ds[:, 2:3]), reads=[dsb], writes=[dsb])
                    P.op("dve", lambda h, ds=ds, R=R, c=c: h.tensor_tensor(out=ds[:, 3:4], in0=ds[:, 1:2], in1=R[:, c:c + 1], op=ALU.mult),
                         reads=[dsb, gb_], writes=[dsb])
                    P.op("act", lambda h, h_=h_, ds=ds, NB_=NB_: h.activation(out=h_[:, :], in_=pst[:, NB_, :], func=AF.Copy, scale=ds[:, 3:4]),
                         reads=[pbufs[NB_], dsb], writes=[hb_])
                    P.dma("sp", ch["out"][pr, (c - NCTXC) * 128:(c - NCTXC + 1) * 128, :], h_[:, :], reads=[hb_])
                P.op("dve", lambda h, kp=kp, k_=k_, E=E, c=c: h.tensor_scalar(out=kp[:, :], in0=k_[:, :], scalar1=E[:, c:c + 1], scalar2=None, op0=ALU.mult),
                     reads=[kb_, gb_], writes=[kpb])
                for dc, UB in enumerate((UB0, UB1)):
                    P.op("pe", lambda h, dc=dc, UB=UB, kp=kp, v_=v_: h.matmul(pst[:, UB, :], lhsT=kp[:, dc * 128:(dc + 1) * 128], rhs=v_[:, :], start=True, stop=True),
                         reads=[kpb, vb], writes=[pbufs[UB]], sig=False, pe_acc=True)
                for dc in range(2):
                    P.op("pe", lambda h, dc=dc, kp=kp, SB=SB: h.matmul(pst[:, SB, 300 + dc:301 + dc], lhsT=kp[:, dc * 128:(dc + 1) * 128], rhs=onesb[:, 0:1], start=True, stop=True),
                         reads=[kpb, bonb], writes=[pbufs[SB]], sig=(dc == 1), pe_acc=True)
                P.op("dve", lambda h, C=C, UB0=UB0: h.tensor_tensor(out=C[:, :, :], in0=C[:, :, :], in1=pst[:, UB0:UB0 + 2, :], op=ALU.add),
                     reads=[pbufs[UB0], pbufs[UB1], cbb], writes=[cb_])
                P.op("dve", lambda h, n_=n_, SB=SB: h.tensor_tensor(out=n_[:, :], in0=n_[:, :], in1=pst[:, SB, 300:302], op=ALU.add),
                     reads=[pbufs[SB], cbb], writes=[cb_])
                P.op("dve", lambda h, C=C, GC=GC, c=c: h.tensor_scalar(out=C[:, :, :], in0=C[:, :, :], scalar1=GC[:, c:c + 1], scalar2=None, op0=ALU.mult),
                     reads=[gb_], writes=[cb_])
                P.op("dve", lambda h, n_=n_, GC=GC, c=c: h.tensor_scalar(out=n_[:, :], in0=n_[:, :], scalar1=GC[:, c:c + 1], scalar2=None, op0=ALU.mult),
                     reads=[gb_], writes=[cb_])
                P.op("act", lambda h, C=C, Cb=Cb: h.activation(out=Cb[:, :, :], in_=C[:, :, :], func=AF.Copy), reads=[cb_], writes=[cbb])
                P.op("act", lambda h, n_=n_, nb_=nb_: h.activation(out=nb_[:, :], in_=n_[:, :], func=AF.Copy), reads=[cb_], writes=[cbb])
    P.run()


def phase_mpost(nc, hf, hb, hn_g, oT, hoT, NG=16):
    P = Phase(nc, "mpost")
    ident, bid = make_ident(P)
    pst = P.ps([128, 8, 512], F32, "ps")
    pbufs = [Buf(f"ps{i}") for i in range(8)]
    hg, hgb = load_cols(P, [hn_g], ident, bid, pst[:, 0, :], pbufs[0], "hg")
    xt = [(P.sb([128, D], F32, "xt"), Buf("xt")) for _ in range(2)]
    yt = [(P.sb([128, D], F32, "yt"), Buf("yt")) for _ in range(2)]
    sq = P.sb([128, 512], BF16, "sq"); bsq = Buf("sq")
    st = [(P.sb([128, 24], F32, "st"), Buf("st")) for _ in range(2)]
    ot = [(P.sb([128, KC, 128], BF16, "ot"), Buf("ot")) for _ in range(2)]
    hbk = [(P.sb([128, KC, 512], BF16, "hb"), Buf("hb")) for _ in range(1)]
    oTv = oT.rearrange("(c p) t -> p c t", p=128)
    hoTv = hoT.rearrange("(c p) t -> p c t", p=128)
    for g in range(NG):
        x_, xb = xt[g % 2]
        y_, yb = yt[g % 2]
        s_, sb_ = st[g % 2]
        o_, ob = ot[g % 2]
        P.dma("sp", x_[:, :], hf[g * 128:(g + 1) * 128, :], writes=[xb])
        P.dma("sp", y_[:, :], hb[g * 128:(g + 1) * 128, :], writes=[yb])
        P.dma("sp", o_[:, :, :], oTv[:, :, g * 128:(g + 1) * 128], writes=[ob])
        P.op("pool", lambda h, x_=x_, y_=y_: h.tensor_tensor(out=x_[:, :], in0=x_[:, :], in1=y_[:, :], op=ALU.add),
             reads=[xb, yb], writes=[xb])
        for hd in range(8):
            P.op("act", lambda h, x_=x_, s_=s_, hd=hd: h.activation(out=sq[:, :], in_=x_[:, hd * 512:(hd + 1) * 512], func=AF.Square,
                                                                    accum_out=s_[:, hd:hd + 1]), reads=[xb], writes=[bsq, sb_])
        rstd_ops(P, s_, sb_, 0, 8, 16, 1.0 / 512, n=8)
        for hd in range(8):
            P.op("act", lambda h, x_=x_, y_=y_, s_=s_, hd=hd: h.activation(
                out=y_[:, hd * 512:(hd + 1) * 512], in_=x_[:, hd * 512:(hd + 1) * 512], func=AF.Copy, scale=s_[:, 16 + hd:17 + hd]),
                reads=[xb, sb_], writes=[yb])
        h_, hbb = hbk[0]
        tcol = (g % 4) * 128
        for c in range(KC):
            bk = c // 4
            j = c % 4
            P.op("pe", lambda h, bk=bk, j=j, c=c, y_=y_: h.transpose(
                out=pst[:, bk, j * 128:(j + 1) * 128], in_=y_[:, c * 128:(c + 1) * 128], identity=ident[:, :]),
                reads=[yb, bid], writes=[pbufs[bk]], sig=(j == 3), pe_acc=True)
            if j == 3:
                for jj in range(4):
                    cc = bk * 4 + jj
                    P.op("dve", lambda h, bk=bk, jj=jj, cc=cc, h_=h_, o_=o_, tcol=tcol: h.scalar_tensor_tensor(
                        out=h_[:, cc, tcol:tcol + 128], in0=pst[:, bk, jj * 128:(jj + 1) * 128], scalar=hg[:, cc:cc + 1],
                        in1=o_[:, cc, :], op0=ALU.mult, op1=ALU.mult), reads=[pbufs[bk], hgb, ob], writes=[hbb])
        if g % 4 == 3:
            t0 = (g // 4) * 512
            P.dma("sp", hoTv[:, :, t0:t0 + 512], h_[:, :, :], reads=[hbb])
    P.run()


PASS_F_TE = [(0, 832, [(0, 512), (512, 320)]), (832, 832, [(0, 512), (512, 320)]), (1664, 768, [(0, 384), (384, 384)])]
PASS_F_TM = [(0, 768, [(0, 384), (384, 384)]), (768, 768, [(0, 384), (384, 384)]), (1536, 768, [(0, 512), (512, 256)])]
PASS_T_TM = [(0, 512), (512, 512), (1024, 512), (1536, 512), (2048, 256)]
PASS_F_X = [(0, 1024), (1024, 1024)]
PASS_T_X = [(0, 512), (512, 512), (1024, 512), (1536, 512)]
FFN_GROUPS = [[i * 128, DFF + i * 128] for i in range(DFF // 128)]
NBLK_D = [(n * 512, 512, 0, n * 512) for n in range(8)]


def _dt(nc, name, shape, dtype=F32, kind="Internal"):
    return nc.dram_tensor(name, list(shape), dtype, kind=kind).ap()


def build_l1(upto=99):
    nc = bass.Bass("TRN2", target_bir_lowering=False)
    I = lambda n, s, d=F32: _dt(nc, n, s, d, "ExternalInput")
    O = lambda n, s, d=F32: _dt(nc, n, s, d, "ExternalOutput")
    T = lambda n, s, d=F32: _dt(nc, n, s, d, "Internal")
    csel = I("csel", [2, D]); ada_w = I("ada_w", [2, D, 6 * D]); ada_b = I("ada_b", [2, 6 * D])
    xext = I("xext", [TE, D]); hmask = I("hmask", [128, 2]); norm_g = I("norm_g", [2, 4, D])
    w_in = I("ab_w_in", [D, ABIN]); acw = I("a_conv_w", [31, WA]); acb = I("a_conv_b", [WA])
    alg = I("a_ln_g", [WA]); alb = I("a_ln_b", [WA]); bcw = I("b_conv_w", [3, WA]); w_out = I("ab_w_out", [D, D])
    w_up = I("ffn_w_up", [D, 2 * DFF]); fcw = I("ffn_conv_w", [3, 2 * DFF]); w_dn = I("ffn_w_down", [DFF, D])
    m_in = I("m_w_in", [D, CIN])
    mods = O("mods", [2, 2, 6 * D]); x2 = O("x2", [TM, D])
    qT = O("qT", [2048, TM], BF16); kT = O("kT", [2048, TM], BF16); oT = O("oT", [D, TM], BF16)
    vk = O("vk", [TM, 6144], BF16); gates = O("gates", [TM, 32])
    hT1 = T("hT1", [D, TE], BF16); PT = T("PT", [ABIN, TE]); catT = T("catT", [D, TM], BF16)
    y1 = T("y1", [TM, D]); x1 = T("x1", [TM, D]); h2T = T("h2T", [D, TM], BF16)
    actT = T("actT", [DFF, TM], BF16); y2 = T("y2", [TM, D]); h3T = T("h3T", [D, TM], BF16)
    m = lambda l, r, i: mods[l, r, i * D:(i + 1) * D]
    phase_ada(nc, csel, ada_w, ada_b, mods)
    if upto < 1:
        return nc
    phase_prenorm(nc, "pn1", xext, [0] * 17 + [1] * 2,
                  ab_rows={v: (norm_g[0, 0], m(0, v, 1), m(0, v, 0)) for v in (0, 1)}, hT_out=hT1)
    if upto < 2:
        return nc
    hook, hinit = make_store_hook(lambda gi: PT[gi * 128:(gi + 1) * 128, :])
    phase_gemm_f(nc, "g1", hT1, w_in, KC, PASS_F_TE, [[i * 128] for i in range(ABIN // 128)], hook, hinit)
    if upto < 3:
        return nc
    phase_abmix_a(nc, PT, acw, acb, alg, alb, catT)
    if upto < 4:
        return nc
    phase_abmix_b(nc, PT, bcw, hmask, catT)
    if upto < 5:
        return nc
    phase_gemm_t(nc, "g2", catT, w_out, KC, PASS_T_TM, NBLK_D, [y1])
    if upto < 6:
        return nc
    grp = [0] * 16 + [1] * 2
    xrows = [64 + g * 128 for g in range(16)] + [64 + NX + 64, 64 + NX + 64 + 128]
    phase_prenorm(nc, "pn2", xext, grp, y_in=y1, gg_rows={v: (m(0, v, 2), norm_g[0, 1]) for v in (0, 1)}, x_out=x1,
                  ab_rows={v: (norm_g[0, 2], m(0, v, 4), m(0, v, 3)) for v in (0, 1)}, hT_out=h2T, xin_rows=xrows)
    if upto < 7:
        return nc
    hook, hinit = make_ffn_hook(fcw, actT, lambda pi, c0: 256 if (pi == 2 and c0 >= 512) else 64)
    phase_gemm_f(nc, "g3", h2T, w_up, KC, PASS_F_TM, FFN_GROUPS, hook, hinit)
    if upto < 8:
        return nc
    phase_gemm_t(nc, "g4", actT, w_dn, DFF // 128, PASS_T_TM, NBLK_D, [y2])
    if upto < 9:
        return nc
    phase_prenorm(nc, "pn3", x1, grp, y_in=y2, gg_rows={v: (m(0, v, 5), norm_g[0, 3]) for v in (0, 1)}, x_out=x2,
                  ab_rows={v: (norm_g[1, 0], m(1, v, 1), m(1, v, 0)) for v in (0, 1)}, hT_out=h3T)

    if upto < 10:
        return nc

    def dst(gi):
        if gi < 16:
            return qT[gi * 128:(gi + 1) * 128, :]
        if gi < 32:
            return kT[(gi - 16) * 128:(gi - 15) * 128, :]
        return oT[(gi - 32) * 128:(gi - 31) * 128, :]
    hook, hinit = make_store_hook(dst, func=lambda gi: (AF.Sigmoid if gi >= 32 else AF.Copy), dtype=BF16)
    groups = [[i * 128] for i in range(32)] + [[8192 + i * 128] for i in range(32)]
    phase_gemm_f(nc, "g5f", h3T, m_in, KC, PASS_F_TM, groups, hook, hinit)
    nb = [(4096 + n * 512, 512, 0, n * 512) for n in range(8)] + [(2048 + n * 512, 512, 0, 4096 + n * 512) for n in range(4)]
    phase_gemm_t(nc, "g5t", h3T, m_in, KC, PASS_T_TM, nb, [vk])
    phase_gemm_t(nc, "g5g", h3T, m_in, KC, PASS_T_TM, [(12288, 32, 0, 0)], [gates])
    return nc


def build_l2():
    nc = bass.Bass("TRN2", target_bir_lowering=False)
    I = lambda n, s, d=F32: _dt(nc, n, s, d, "ExternalInput")
    O = lambda n, s, d=F32: _dt(nc, n, s, d, "ExternalOutput")
    TT = 66 * 128
    qT = I("qT", [2, 256, TT], BF16); kT = I("kT", [2, 256, TT], BF16); kk = I("kk", [2, TT, 256], BF16)
    vv = I("vv", [2, TT, 512], BF16); gg = I("gg", [2, TT, 4]); bg = I("bg", [128, 8])
    hf = O("hf", [2, 8192, 512]); hb = O("hb", [2, 8192, 512])
    phase_scan(nc, qT, kT, kk, vv, gg, bg, hf, hb)
    return nc


def build_l3():
    nc = bass.Bass("TRN2", target_bir_lowering=False)
    I = lambda n, s, d=F32: _dt(nc, n, s, d, "ExternalInput")
    O = lambda n, s, d=F32: _dt(nc, n, s, d, "ExternalOutput")
    T = lambda n, s, d=F32: _dt(nc, n, s, d, "Internal")
    hf = I("hf", [NX, D]); hb = I("hb", [NX, D]); oT = I("oT", [D, NX], BF16); hn_g = I("hn_g", [D])
    m_out = I("m_w_out", [D, D]); x2 = I("x2", [NX, D]); mods = I("mods", [2, 2, 6 * D]); norm_g = I("norm_g", [2, 4, D])
    w_up = I("ffn_w_up", [D, 2 * DFF]); fcw = I("ffn_conv_w", [3, 2 * DFF]); w_dn = I("ffn_w_down", [DFF, D])
    out = O("out", [NX, D])
    hoT = T("hoT", [D, NX], BF16); y3 = T("y3", [NX, D]); x3 = T("x3", [NX, D]); h4T = T("h4T", [D, NX], BF16)
    actT = T("actT", [DFF, NX], BF16); y4 = T("y4", [NX, D])
    m = lambda l, r, i: mods[l, r, i * D:(i + 1) * D]
    phase_mpost(nc, hf, hb, hn_g, oT, hoT)
    phase_gemm_t(nc, "g6", hoT, m_out, KC, PASS_T_X, NBLK_D, [y3])
    grp = [0] * 16
    phase_prenorm(nc, "pn4", x2, grp, y_in=y3, gg_rows={0: (m(1, 0, 2), norm_g[1, 1])}, x_out=x3,
                  ab_rows={0: (norm_g[1, 2], m(1, 0, 4), m(1, 0, 3))}, hT_out=h4T)
    hook, hinit = make_ffn_hook(fcw, actT, lambda pi, c0: 64)
    phase_gemm_f(nc, "g7", h4T, w_up, KC, PASS_F_X, FFN_GROUPS, hook, hinit)
    phase_gemm_t(nc, "g8", actT, w_dn, DFF // 128, PASS_T_X, NBLK_D, [y4])
    phase_prenorm(nc, "fin", x3, grp, y_in=y4, gg_rows={0: (m(1, 0, 5), norm_g[1, 3])}, x_out=out)
    return nc


def phase_mods_gather(nc, msend, mrecv, mods):
    P = Phase(nc, "mg")
    rec = P.pool.get("cc")
    sem = rec[0]
    rec[1] += 1
    val = rec[1]
    P.q["pool"].append(lambda h: h.collective_compute(
        "AllGather", ALU.bypass, replica_groups=[[0, 1, 2, 3], [4, 5, 6, 7]],
        ins=[msend.rearrange("l r n -> (l r) n")], outs=[mrecv[:, :]]).then_inc(sem, 1))
    P.q["pool"].append(lambda h: h.wait_ge(sem, val))
    P.pool.put(rec)
    P.run()
    P = Phase(nc, "mg2")
    rv = mrecv.rearrange("(j q) (i k) -> q i j k", q=4, k=1024)
    for l in range(2):
        for r in range(2):
            b = Buf("mcopy")
            P.dma("sp", mods[l, r].rearrange("(i j k) -> i j k", j=4, k=1024), rv[l * 2 + r], writes=[b])
    P.run()


def build_all():
    nc = bass.Bass("TRN2", target_bir_lowering=False)
    I = lambda n, s, d=F32: _dt(nc, n, s, d, "ExternalInput")
    O = lambda n, s, d=F32: _dt(nc, n, s, d, "ExternalOutput")
    T = lambda n, s, d=F32: _dt(nc, n, s, d, "Internal")
    csel = I("csel", [2, D]); ada_w = I("ada_w", [2, D, 6 * D // 4]); ada_b = I("ada_b", [2, 6 * D // 4])
    xext = I("xext", [TE, D]); hmask = I("hmask", [128, 2]); norm_g = I("norm_g", [2, 4, D])
    w_in = I("ab_w_in", [D, ABIN]); acw = I("a_conv_w", [31, WA]); acb = I("a_conv_b", [WA])
    alg = I("a_ln_g", [WA]); alb = I("a_ln_b", [WA]); bcw = I("b_conv_w", [3, WA]); w_out = I("ab_w_out", [D, D])
    w_up = I("ffn_w_up", [2, D, 2 * DFF]); fcw = I("ffn_conv_w", [2, 3, 2 * DFF]); w_dn = I("ffn_w_down", [2, DFF, D])
    m_in = I("m_w_in", [D, CIN]); bgx = I("bgx", [128, NCHT * 32]); flags = I("flags", [128, 8])
    hn_g = I("hn_g", [D]); m_out = I("m_w_out", [D, D])
    out = O("out", [NX, D])
    mods = T("mods", [2, 2, 6 * D]); x2 = T("x2", [TM, D])
    qT = T("qT", [2048, TM], BF16); kT = T("kT", [2048, TM], BF16); oT = T("oT", [D, TM], BF16)
    vk = T("vk", [TM, 6144], BF16); gates = T("gates", [TM, 32])
    hT1 = T("hT1", [D, TE], BF16); PT = T("PT", [ABIN, TE]); catT = T("catT", [D, TM], BF16)
    y1 = T("y1", [TM, D]); x1 = T("x1", [TM, D]); h2T = T("h2T", [D, TM], BF16)
    actT = T("actT", [DFF, TM], BF16); y2 = T("y2", [TM, D]); h3T = T("h3T", [D, TM], BF16)
    prep = T("prep", [2, 4, 128, NCHT * 8]); sctx = T("sctx", [16, 128, SROW]); send = T("send", [16, 128, SROW])
    recv = T("recv", [16, 4 * 128, SROW]); hf = T("hf", [NX, D]); hb = T("hb", [NX, D])
    hoT = T("hoT", [D, NX], BF16); y3 = T("y3", [NX, D]); x3 = T("x3", [NX, D]); h4T = T("h4T", [D, NX], BF16)
    actT1 = T("actT1", [DFF, NX], BF16); y4 = T("y4", [NX, D])
    m = lambda l, r, i: mods[l, r, i * D:(i + 1) * D]
    msend = T("msend", [2, 2, 6 * D // 4]); mrecv = T("mrecv", [16, 6 * D // 4])
    phase_ada(nc, csel, ada_w, ada_b, msend, NM=6 * D // 4)
    phase_mods_gather(nc, msend, mrecv, mods)
    phase_prenorm(nc, "pn1", xext, [0] * 17 + [1] * 2,
                  ab_rows={v: (norm_g[0, 0], m(0, v, 1), m(0, v, 0)) for v in (0, 1)}, hT_out=hT1)
    hook, hinit = make_store_hook(lambda gi: PT[gi * 128:(gi + 1) * 128, :])
    phase_gemm_f(nc, "g1", hT1, w_in, KC, PASS_F_TE, [[i * 128] for i in range(ABIN // 128)], hook, hinit)
    phase_abmix_a(nc, PT, acw, acb, alg, alb, catT)
    phase_abmix_b(nc, PT, bcw, hmask, catT)
    phase_gemm_t(nc, "g2", catT, w_out, KC, PASS_T_TM, NBLK_D, [y1])
    grp = [0] * 16 + [1] * 2
    xrows = [64 + g * 128 for g in range(16)] + [64 + NX + 64, 64 + NX + 64 + 128]
    phase_prenorm(nc, "pn2", xext, grp, y_in=y1, gg_rows={v: (m(0, v, 2), norm_g[0, 1]) for v in (0, 1)}, x_out=x1,
                  ab_rows={v: (norm_g[0, 2], m(0, v, 4), m(0, v, 3)) for v in (0, 1)}, hT_out=h2T, xin_rows=xrows)
    hook, hinit = make_ffn_hook(fcw[0], actT, lambda pi, c0: 256 if (pi == 2 and c0 >= 512) else 64)
    phase_gemm_f(nc, "g3", h2T, w_up[0], KC, PASS_F_TM, FFN_GROUPS, hook, hinit)
    phase_gemm_t(nc, "g4", actT, w_dn[0], DFF // 128, PASS_T_TM, NBLK_D, [y2])
    phase_prenorm(nc, "pn3", x1, grp, y_in=y2, gg_rows={v: (m(0, v, 5), norm_g[0, 3]) for v in (0, 1)}, x_out=x2,
                  ab_rows={v: (norm_g[1, 0], m(1, v, 1), m(1, v, 0)) for v in (0, 1)}, hT_out=h3T)

    def dst(gi):
        if gi < 16:
            return qT[gi * 128:(gi + 1) * 128, :]
        if gi < 32:
            return kT[(gi - 16) * 128:(gi - 15) * 128, :]
        return oT[(gi - 32) * 128:(gi - 31) * 128, :]
    hook, hinit = make_store_hook(dst, func=lambda gi: (AF.Sigmoid if gi >= 32 else AF.Copy), dtype=BF16)
    groups = [[i * 128] for i in range(32)] + [[8192 + i * 128] for i in range(32)]
    phase_gemm_f(nc, "g5f", h3T, m_in, KC, PASS_F_TM, groups, hook, hinit)
    nb = [(4096 + n * 512, 512, 0, n * 512) for n in range(8)] + [(2048 + n * 512, 512, 0, 4096 + n * 512) for n in range(4)]
    phase_gemm_t(nc, "g5t", h3T, m_in, KC, PASS_T_TM, nb, [vk])
    phase_gemm_t(nc, "g5g", h3T, m_in, KC, PASS_T_TM, [(12288, 32, 0, 0)], [gates])
    phase_scan1(nc, vk, gates, bgx, prep, sctx, send)
    phase_allgather(nc, send, recv)
    phase_scan2(nc, qT, kT, vk, prep, sctx, recv, flags, hf, hb)
    phase_mpost(nc, hf, hb, hn_g, oT[:, 0:NX], hoT)
    phase_gemm_t(nc, "g6", hoT, m_out, KC, PASS_T_X, NBLK_D, [y3])
    grp1 = [0] * 16
    phase_prenorm(nc, "pn4", x2[0:NX, :], grp1, y_in=y3, gg_rows={0: (m(1, 0, 2), norm_g[1, 1])}, x_out=x3,
                  ab_rows={0: (norm_g[1, 2], m(1, 0, 4), m(1, 0, 3))}, hT_out=h4T)
    hook, hinit = make_ffn_hook(fcw[1], actT1, lambda pi, c0: 64)
    phase_gemm_f(nc, "g7", h4T, w_up[1], KC, PASS_F_X, FFN_GROUPS, hook, hinit)
    phase_gemm_t(nc, "g8", actT1, w_dn[1], DFF // 128, PASS_T_X, NBLK_D, [y4])
    phase_prenorm(nc, "fin", x3, grp1, y_in=y4, gg_rows={0: (m(1, 0, 5), norm_g[1, 3])}, x_out=out)
    return nc


def kernel(x, c, ctx, c_ctx, ada_w, ada_b, norm_g, ab_w_in, a_conv_w, a_conv_b, a_ln_g, a_ln_b,
           b_conv_w, ab_w_out, m_w_in, m_b_gates, m_hn_g, m_w_out, ffn_w_up, ffn_conv_w, ffn_w_down):
    f = lambda a: np.ascontiguousarray(np.asarray(a))
    x, c, ctx, c_ctx = f(x), f(c), f(ctx), f(c_ctx)
    cores = list(range(NCORES))
    mbg = f(m_b_gates)[0].astype(np.float32)
    bgx = np.ascontiguousarray(np.broadcast_to(np.tile(mbg, NCHT)[None, :], (128, NCHT * 32)))
    ada_w, ada_b = np.asarray(ada_w), np.asarray(ada_b)
    acols = [np.concatenate([np.arange(i * D + jj * 1024, i * D + (jj + 1) * 1024) for i in range(6)]) for jj in range(4)]
    ada_ws = [f(ada_w[:, :, cc]) for cc in acols]
    ada_bs = [f(ada_b[:, cc]) for cc in acols]
    shared = dict(norm_g=f(norm_g), ab_w_in=f(ab_w_in[0]), a_conv_w=f(a_conv_w[0]),
                  a_conv_b=f(a_conv_b[0]), a_ln_g=f(a_ln_g[0]), a_ln_b=f(a_ln_b[0]), b_conv_w=f(b_conv_w[0]),
                  ab_w_out=f(ab_w_out[0]), ffn_w_up=f(ffn_w_up), ffn_conv_w=f(ffn_conv_w), ffn_w_down=f(ffn_w_down),
                  m_w_in=f(m_w_in[0]), bgx=bgx, hn_g=f(m_hn_g[0]), m_w_out=f(m_w_out[0]))
    ins = []
    for r in cores:
        b, j = r // 4, r % 4
        xe = np.zeros((TE, D), np.float32)
        t0 = j * NX
        if j > 0:
            xe[0:64] = x[b, t0 - 64:t0]
        xe[64:64 + NX] = x[b, t0:t0 + NX]
        if j < 3:
            xe[64 + NX:128 + NX] = x[b, t0 + NX:t0 + NX + 64]
        xe[128 + NX:] = ctx[b]
        hm = np.zeros((128, 2), np.float32)
        hm[:, 0] = 1.0 if j > 0 else 0.0
        hm[:, 1] = 1.0 if j < 3 else 0.0
        fl = np.zeros((128, 8), np.float32)
        for i in range(4):
            fl[:, i] = 1.0 if i < j else 0.0
            fl[:, 4 + i] = 1.0 if i > j else 0.0
        d = dict(shared)
        d.update(csel=np.stack([c[b], c_ctx]), xext=xe, hmask=hm, flags=fl, ada_w=ada_ws[j], ada_b=ada_bs[j])
        ins.append(d)
    res = run_bass_kernel_spmd(build_all(), ins, core_ids=cores).results
    out = np.zeros((2, 8192, D), np.float32)
    for r in cores:
        b, j = r // 4, r % 4
        out[b, j * NX:(j + 1) * NX] = res[r]["out"]
    return out


NCHT = TM // 128
SROW = 1032


def _tri_consts(P):
    ones = P.sb([128, 128], F32, "ones"); bon = Buf("ones")
    P.op("pool", lambda h: h.memset(ones[:, :], 1.0), writes=[bon])
    onesb = P.sb([128, 2], BF16, "onesb"); bonb = Buf("onesb")
    P.op("pool", lambda h: h.memset(onesb[:, :], 1.0), writes=[bonb])
    tri = []
    for d in range(2):
        t = P.sb([128, 128], F32, f"tri{d}"); tb = Buf(f"tri{d}")
        P.op("pool", lambda h, t=t: h.memset(t[:, :], 1.0), writes=[tb])
        sgn = 1 if d == 0 else -1
        P.op("pool", lambda h, t=t, sgn=sgn: h.affine_select(out=t[:, :], in_=t[:, :], pattern=[[sgn, 128]],
                                                            compare_op=ALU.is_ge, fill=0.0, base=0, channel_multiplier=-sgn),
             reads=[tb], writes=[tb])
        tri.append((t, tb))
    return ones, bon, onesb, bonb, tri


def _state_update(P, pst, pbufs, UB0, SBK, kp, kpb, k_ap, kb_, v_, vb, e_ap, gc_ap, gb_, C, n_, cb_, extra_reads, onesb, bonb):
    P.op("dve", lambda h: h.tensor_scalar(out=kp[:, :], in0=k_ap, scalar1=e_ap, scalar2=None, op0=ALU.mult),
         reads=[kb_, gb_] + extra_reads, writes=[kpb])
    for dc in range(2):
        P.op("pe", lambda h, dc=dc: h.matmul(pst[:, UB0 + dc, :], lhsT=kp[:, dc * 128:(dc + 1) * 128], rhs=v_, start=True, stop=True),
             reads=[kpb, vb], writes=[pbufs[UB0 + dc]], sig=False, pe_acc=True)
    for dc in range(2):
        P.op("pe", lambda h, dc=dc: h.matmul(pst[:, SBK, 300 + dc:301 + dc], lhsT=kp[:, dc * 128:(dc + 1) * 128], rhs=onesb[:, 0:1], start=True, stop=True),
             reads=[kpb, bonb], writes=[pbufs[SBK]], sig=(dc == 1), pe_acc=True)
    P.op("dve", lambda h: h.scalar_tensor_tensor(out=C, in0=C, scalar=gc_ap, in1=pst[:, UB0:UB0 + 2, :], op0=ALU.mult, op1=ALU.add),
         reads=[pbufs[UB0], pbufs[UB0 + 1], gb_] + extra_reads, writes=[cb_])
    P.op("dve", lambda h: h.scalar_tensor_tensor(out=n_, in0=n_, scalar=gc_ap, in1=pst[:, SBK, 300:302], op0=ALU.mult, op1=ALU.add),
         reads=[pbufs[SBK], gb_] + extra_reads, writes=[cb_])


def phase_scan1(nc, vk, gates, bgx, prep, sctx, send):
    import math
    P = Phase(nc, "scan1")
    pst = P.ps([128, 8, 512], F32, "ps")
    pbufs = [Buf(f"ps{i}") for i in range(8)]
    ones, bon, onesb, bonb, tri = _tri_consts(P)
    NC8 = NCHT * 8
    G = P.sb([128, NCHT, 32], F32, "G"); Gb = Buf("G")
    bgt = P.sb([128, NCHT, 32], F32, "bgx"); bgb = Buf("bgx")
    P.dma("sp", G[:, :, :], gates.rearrange("(c p) f -> p c f", p=128), writes=[Gb], noncontig=True)
    P.dma("sp", bgt[:, :, :], bgx.rearrange("p (c f) -> p c f", f=32), writes=[bgb])
    P.op("dve", lambda h: h.tensor_tensor(out=G[:, :, :], in0=G[:, :, :], in1=bgt[:, :, :], op=ALU.add), reads=[Gb, bgb], writes=[Gb])
    ERG = []
    for d in range(2):
        ig = P.sb([128, NCHT, 8], F32, "ig"); lf = P.sb([128, NCHT, 8], F32, "lf")
        E = P.sb([128, NCHT, 8], F32, "E"); R = P.sb([128, NCHT, 8], F32, "R"); GC = P.sb([128, NCHT, 8], F32, "GC")
        gx = P.sb([128, 8], F32, "gx")
        gb_ = Buf("gprep")
        P.op("act", lambda h, lf=lf, d=d: h.activation(out=lf[:, :, :], in_=G[:, :, d * 16 + 8:d * 16 + 16], func=AF.Exp, scale=-1.0),
             reads=[Gb], writes=[gb_])
        P.op("act", lambda h, lf=lf: h.activation(out=lf[:, :, :], in_=lf[:, :, :], func=AF.Ln, bias=1.0), reads=[gb_], writes=[gb_])
        P.op("dve", lambda h, lf=lf: h.tensor_scalar(out=lf[:, :, :], in0=lf[:, :, :], scalar1=-1.0, scalar2=None, op0=ALU.mult),
             reads=[gb_], writes=[gb_])
        t, tb = tri[d]
        lff = lf[:, :, :].rearrange("p c h -> p (c h)")
        P.op("pe", lambda h, t=t, lff=lff: h.matmul(pst[:, 0, 0:NC8], lhsT=t[:, :], rhs=lff, start=True, stop=True),
             reads=[tb, gb_], writes=[pbufs[0]])
        P.op("pe", lambda h, lff=lff: h.matmul(pst[:, 1, 0:NC8], lhsT=ones[:, :], rhs=lff, start=True, stop=True),
             reads=[bon, gb_], writes=[pbufs[1]])
        flat = lambda T_: T_[:, :, :].rearrange("p c h -> p (c h)")
        P.op("act", lambda h, R=R: h.activation(out=flat(R), in_=pst[:, 0, 0:NC8], func=AF.Exp), reads=[pbufs[0]], writes=[gb_])
        P.op("act", lambda h, GC=GC: h.activation(out=flat(GC), in_=pst[:, 1, 0:NC8], func=AF.Exp), reads=[pbufs[1]], writes=[gb_])
        P.op("dve", lambda h, ig=ig, d=d: h.tensor_tensor(out=ig[:, :, :], in0=G[:, :, d * 16:d * 16 + 8],
                                                        in1=pst[:, 0, 0:NC8].rearrange("p (c h) -> p c h", h=8), op=ALU.subtract),
             reads=[pbufs[0], Gb], writes=[gb_])
        P.op("act", lambda h, E=E, ig=ig: h.activation(out=E[:, :, :], in_=ig[:, :, :], func=AF.Exp, bias=-math.log(16.0)),
             reads=[gb_], writes=[gb_])
        P.op("dve", lambda h, gx=gx: h.reduce_sum(out=gx[:, :], in_=pst[:, 1, 0:16 * 8].rearrange("p (c h) -> p h c", h=8), axis=AX.X),
             reads=[pbufs[1]], writes=[gb_])
        P.op("act", lambda h, gx=gx: h.activation(out=gx[:, :], in_=gx[:, :], func=AF.Exp), reads=[gb_], writes=[gb_])
        EG = P.sb([128, NCHT, 8], F32, "EG")
        P.op("dve", lambda h, EG=EG, E=E, GC=GC: h.tensor_tensor(out=EG[:, :, :], in0=E[:, :, :], in1=GC[:, :, :], op=ALU.mult),
             reads=[gb_], writes=[gb_])
        for j, T_ in enumerate((E, R, GC, EG)):
            P.dma("sp", prep[d, j], flat(T_), reads=[gb_])
        ERG.append((EG, R, GC, gx, gb_))
    Call = P.sb([128, 16, 1026], F32, "Call"); cbs = [Buf(f"C{i}") for i in range(16)]
    rows = [(P.sb([128, 6144], BF16, "vkrow"), Buf("vkrow")) for _ in range(3)]
    kpr = [(P.sb([128, 256], BF16, "kp"), Buf("kp")) for _ in range(3)]
    ri = 0
    ki = 0
    ui = 0
    for which in ("ctx", "x"):
        for ch in range(16):
            P.op("pool", lambda h, ch=ch: h.memset(Call[:, ch, :], 0.0), writes=[cbs[ch]])
        orders = ([16, 17], [17, 16]) if which == "ctx" else (list(range(16)), list(range(15, -1, -1)))
        for step in range(len(orders[0])):
            for d in range(2):
                c = orders[d][step]
                E, R, GC, gx, gb_ = ERG[d]
                row, rb = rows[ri % 3]
                ri += 1
                P.dma("sp", row[:, :], vk[c * 128:(c + 1) * 128, :], writes=[rb])
                for hd in range(8):
                    ch = d * 8 + hd
                    kp, kpb = kpr[ki % 3]
                    ki += 1
                    UB0 = 2 + 2 * (ui % 3)
                    SBK = ui % 2
                    ui += 1
                    Cc = Call[:, ch, 0:1024].rearrange("p (a b) -> p a b", a=2)
                    nn = Call[:, ch, 1024:1026]
                    _state_update(P, pst, pbufs, UB0, SBK, kp, kpb, row[:, 4096 + hd * 256:4096 + (hd + 1) * 256], rb,
                                  row[:, hd * 512:(hd + 1) * 512], rb, E[:, c, hd:hd + 1], GC[:, c, hd:hd + 1], gb_,
                                  Cc, nn, cbs[ch], [], onesb, bonb)
        dst = sctx if which == "ctx" else send
        dv = dst.rearrange("ch p f -> p ch f")
        if which == "x":
            for d in range(2):
                gx, gb_ = ERG[d][3], ERG[d][4]
                P.dma("sp", dv[:, d * 8:(d + 1) * 8, 1026:1027], gx[:, :].rearrange("p (h o) -> p h o", o=1), reads=[gb_], noncontig=True)
        P.dma("sp", dv[:, :, 0:1026], Call[:, :, :], reads=cbs)
    P.run()


def phase_allgather(nc, send, recv):
    P = Phase(nc, "ag")
    rec = P.pool.get("cc")
    sem = rec[0]
    for ch in range(16):
        rec[1] += 1
        P.q["pool"].append(lambda h, ch=ch: h.collective_compute(
            "AllGather", ALU.bypass, replica_groups=[[0, 1, 2, 3], [4, 5, 6, 7]], ins=[send[ch]], outs=[recv[ch]]).then_inc(sem, 1))
    val = rec[1]
    P.q["pool"].append(lambda h: h.wait_ge(sem, val))
    P.pool.put(rec)
    P.run()


def phase_scan2(nc, qT, kT, vk, prep, sctx, recv, flags, hf, hb):
    P = Phase(nc, "scan2")
    pst = P.ps([128, 8, 512], F32, "ps")
    pbufs = [Buf(f"ps{i}") for i in range(8)]
    ones, bon, onesb, bonb, tri = _tri_consts(P)
    fl = P.sb([128, 8], F32, "flags"); flb = Buf("flags")
    P.dma("sp", fl[:, :], flags[:, :], writes=[flb])
    ERG = []
    for d in range(2):
        ts = []
        for j in range(4):
            T_ = P.sb([128, NCHT, 8], F32, "erg")
            bj = Buf("ergl")
            P.dma("sp", T_[:, :, :].rearrange("p c h -> p (c h)"), prep[d, j], writes=[bj])
            ts.append((T_, bj))
        ERG.append(ts)
    qsb = [(P.sb([128, 2, TM], BF16, "qT"), Buf("qT")) for _ in range(2)]
    ksb = [(P.sb([128, 2, TM], BF16, "kT"), Buf("kT")) for _ in range(2)]
    NR = 4
    vr = [(P.sb([128, 512], BF16, "v"), Buf("v")) for _ in range(NR)]
    kr = [(P.sb([128, 256], BF16, "k"), Buf("k")) for _ in range(NR)]
    kpr = [(P.sb([128, 256], BF16, "kp"), Buf("kp")) for _ in range(NR)]
    spr = [(P.sb([128, 128], BF16, "sp"), Buf("sp")) for _ in range(2)]
    hr = [(P.sb([128, 512], F32, "h"), Buf("h")) for _ in range(2)]
    dsc = [(P.sb([128, 4], F32, "dsc"), Buf("dsc")) for _ in range(2)]
    Lt = [(P.sb([128, SROW], F32, "L"), Buf("L")) for _ in range(2)]
    St = [(P.sb([128, 1026], F32, "S"), Buf("S")) for _ in range(4)]
    Cbt = [(P.sb([128, 1026], BF16, "Cb"), Buf("Cb")) for _ in range(4)]
    av = [(P.sb([128, 4], F32, "av"), Buf("av")) for _ in range(2)]
    sv = sctx.rearrange("ch p f -> p ch f")
    rv = recv.rearrange("ch (r p) f -> p r ch f", p=128)
    ri = 0
    li = 0
    for hd in range(8):
        q_, qb = qsb[hd % 2]
        k_T, kb = ksb[hd % 2]
        P.dma("sp", q_[:, :, :], qT[hd * 256:(hd + 1) * 256, :].rearrange("(c p) t -> p c t", p=128), writes=[qb])
        P.dma("sp", k_T[:, :, :], kT[hd * 256:(hd + 1) * 256, :].rearrange("(c p) t -> p c t", p=128), writes=[kb])
        chains = []
        for d in range(2):
            ch = d * 8 + hd
            S, Sb = St[(hd % 2) * 2 + d]
            Cb, Cbb = Cbt[(hd % 2) * 2 + d]
            P.dma("sp", S[:, :], sv[:, ch, 0:1026], writes=[Sb])
            for i in (range(4) if d == 0 else range(3, -1, -1)):
                L, Lb = Lt[li % 2]
                a_, ab = av[li % 2]
                li += 1
                P.dma("sp", L[:, 0:1027], rv[:, i, ch, 0:1027], writes=[Lb])
                fcol = fl[:, d * 4 + i:d * 4 + i + 1]
                P.op("dve", lambda h, a_=a_, L=L: h.tensor_scalar(out=a_[:, 0:1], in0=L[:, 1026:1027], scalar1=-1.0, scalar2=None, op0=ALU.add),
                     reads=[Lb], writes=[ab])
                P.op("dve", lambda h, a_=a_, fcol=fcol: h.tensor_scalar(out=a_[:, 1:2], in0=a_[:, 0:1], scalar1=fcol, scalar2=1.0, op0=ALU.mult, op1=ALU.add),
                     reads=[ab, flb], writes=[ab])
                P.op("dve", lambda h, L=L, fcol=fcol: h.tensor_scalar(out=L[:, 0:1026], in0=L[:, 0:1026], scalar1=fcol, scalar2=None, op0=ALU.mult),
                     reads=[Lb, flb], writes=[Lb])
                P.op("dve", lambda h, S=S, L=L, a_=a_: h.scalar_tensor_tensor(out=S[:, :], in0=S[:, :], scalar=a_[:, 1:2], in1=L[:, 0:1026],
                                                                            op0=ALU.mult, op1=ALU.add), reads=[Sb, Lb, ab], writes=[Sb])
            P.op("act", lambda h, S=S, Cb=Cb: h.activation(out=Cb[:, :], in_=S[:, :], func=AF.Copy), reads=[Sb], writes=[Cbb])
            order = list(range(16)) if d == 0 else list(range(15, -1, -1))
            chains.append(dict(d=d, S=S, Sb=Sb, Cb=Cb, Cbb=Cbb, order=order, out=(hf if d == 0 else hb)))
        for step in range(16):
            for chn in chains:
                d = chn["d"]
                c = chn["order"][step]
                (E, Eb), (R, Rb), (GC, GCb), (EG, EGb) = ERG[d]
                S, Sb, Cb, Cbb = chn["S"], chn["Sb"], chn["Cb"], chn["Cbb"]
                base = d * 4
                SB, NB_, UB0 = base, base + 1, base + 2
                cols = slice(c * 128, (c + 1) * 128)
                v_, vb = vr[ri % NR]
                k_, kb_ = kr[ri % NR]
                kp, kpb = kpr[ri % NR]
                s_, sb_ = spr[ri % 2]
                h_, hb_ = hr[ri % 2]
                ds, dsb = dsc[ri % 2]
                ri += 1
                P.dma("sp", v_[:, :], vk[c * 128:(c + 1) * 128, hd * 512:(hd + 1) * 512], writes=[vb])
                P.dma("sp", k_[:, :], vk[c * 128:(c + 1) * 128, 4096 + hd * 256:4096 + (hd + 1) * 256], writes=[kb_])
                Cbv = Cb[:, 0:1024].rearrange("p (a b) -> p a b", a=2)
                for dc in range(2):
                    P.op("pe", lambda h, dc=dc, cols=cols, SB=SB, k_T=k_T, q_=q_: h.matmul(
                        pst[:, SB, 0:128], lhsT=k_T[:, dc, cols], rhs=q_[:, dc, cols], start=(dc == 0), stop=(dc == 1)),
                        reads=[kb, qb], writes=[pbufs[SB]], sig=(dc == 1), pe_acc=True)
                t, tb = tri[d]
                P.op("dve", lambda h, s_=s_, SB=SB, E=E, c=c, t=t, hd=hd: h.scalar_tensor_tensor(
                    out=s_[:, :], in0=pst[:, SB, 0:128], scalar=E[:, c, hd:hd + 1], in1=t[:, :], op0=ALU.mult, op1=ALU.mult),
                    reads=[pbufs[SB], Eb, tb], writes=[sb_])
                P.op("pe", lambda h, s_=s_, v_=v_, NB_=NB_: h.matmul(pst[:, NB_, :], lhsT=s_[:, :], rhs=v_[:, :], start=True, stop=False),
                     reads=[sb_, vb], writes=[pbufs[NB_]], sig=False, pe_acc=True)
                for dc in range(2):
                    P.op("pe", lambda h, dc=dc, cols=cols, Cbv=Cbv, NB_=NB_, q_=q_: h.matmul(
                        pst[:, NB_, :], lhsT=q_[:, dc, cols], rhs=Cbv[:, dc, :], start=False, stop=(dc == 1)),
                        reads=[qb, Cbb], writes=[pbufs[NB_]], sig=(dc == 1), pe_acc=True)
                P.op("pe", lambda h, s_=s_, SB=SB: h.matmul(pst[:, SB, 256:257], lhsT=s_[:, :], rhs=onesb[:, 0:1], start=True, stop=False),
                     reads=[sb_, bonb], writes=[pbufs[SB]], sig=False, pe_acc=True)
                for dc in range(2):
                    P.op("pe", lambda h, dc=dc, cols=cols, Cb=Cb, SB=SB, q_=q_: h.matmul(
                        pst[:, SB, 256:257], lhsT=q_[:, dc, cols], rhs=Cb[:, 1024 + dc:1025 + dc], start=False, stop=(dc == 1)),
                        reads=[qb, Cbb], writes=[pbufs[SB]], sig=(dc == 1), pe_acc=True)
                rcol = R[:, c, hd:hd + 1]
                P.op("dve", lambda h, ds=ds, SB=SB, rcol=rcol: h.tensor_tensor(out=ds[:, 0:1], in0=pst[:, SB, 256:257], in1=rcol, op=ALU.mult),
                     reads=[pbufs[SB], Rb], writes=[dsb])
                P.op("act", lambda h, ds=ds: h.activation(out=ds[:, 1:2], in_=ds[:, 0:1], func=AF.Abs), reads=[dsb], writes=[dsb])
                P.op("dve", lambda h, ds=ds: h.tensor_single_scalar(out=ds[:, 2:3], in_=ds[:, 1:2], scalar=1.0, op=ALU.max), reads=[dsb], writes=[dsb])
                P.op("dve", lambda h, ds=ds: h.reciprocal(out=ds[:, 1:2], in_=ds[:, 2:3]), reads=[dsb], writes=[dsb])
                P.op("dve", lambda h, ds=ds, rcol=rcol: h.tensor_tensor(out=ds[:, 3:4], in0=ds[:, 1:2], in1=rcol, op=ALU.mult),
                     reads=[dsb, Rb], writes=[dsb])
                P.op("act", lambda h, h_=h_, ds=ds, NB_=NB_: h.activation(out=h_[:, :], in_=pst[:, NB_, :], func=AF.Copy, scale=ds[:, 3:4]),
                     reads=[pbufs[NB_], dsb], writes=[hb_])
                P.dma("sp", chn["out"][c * 128:(c + 1) * 128, hd * 512:(hd + 1) * 512], h_[:, :], reads=[hb_])
                if step < 15:
                    Sv = S[:, 0:1024].rearrange("p (a b) -> p a b", a=2)
                    _state_update(P, pst, pbufs, UB0, SB, kp, kpb, k_[:, :], kb_, v_[:, :], vb, EG[:, c, hd:hd + 1], GC[:, c, hd:hd + 1],
                                  Buf("dummy"), Sv, S[:, 1024:1026], Sb, [EGb, GCb], onesb, bonb)
                    P.op("act", lambda h, S=S, Cb=Cb: h.activation(out=Cb[:, :], in_=S[:, :], func=AF.Copy), reads=[Sb], writes=[Cbb])
    P.run()
```
